# Optimizing a Trainium2 kernel written in Bass

```python
import math
import jax, jax.numpy as jnp
from jax import lax
import numpy as np


D_MODEL = 1024
BATCH = 16
SEQ = 2048
DEPTH = 1
DEC_BATCH = 128
DEC_SEQ = 1
PAST_LEN = 8192
PAGE_SIZE = 128

MIX_WIDTH = D_MODEL
MLA_HEADS = 8
MLA_HEAD_DIM = 64
ROPE_DIM = 32
KV_LORA = D_MODEL // 4
ATTN_WIDTH = MLA_HEADS * MLA_HEAD_DIM
CONV_CH = MIX_WIDTH - ATTN_WIDTH
CONV_K = 31
CONV_STATE = CONV_K - 1
MEM_TOKENS = 256
MEM_HEADS = 4
MEM_HEAD_DIM = D_MODEL // MEM_HEADS
D_FF = 4 * D_MODEL
Q_BLOCK = 128
ROPE_BASE = 10000.0
LN_EPS = 1e-5
ALPHA = (2.0 * DEPTH) ** 0.25
BETA = (8.0 * DEPTH) ** -0.25
Q_COLS = MLA_HEADS * (MLA_HEAD_DIM + ROPE_DIM)
IN_COLS = Q_COLS + KV_LORA + ROPE_DIM + 2 * CONV_CH
MLA_SCALE = (MLA_HEAD_DIM + ROPE_DIM) ** -0.5
MEM_SCALE = MEM_HEAD_DIM ** -0.5

kernel_name = 'hymba_mla_conformer_deepnorm_decoder_step'


def layer_norm(x, g, b):
    xf = x.astype(jnp.float32)
    mu = jnp.mean(xf, axis=-1, keepdims=True)
    var = jnp.mean(jnp.square(xf - mu), axis=-1, keepdims=True)
    y = (xf - mu) * lax.rsqrt(var + LN_EPS) * g.astype(jnp.float32) + b.astype(jnp.float32)
    return y.astype(x.dtype)


def rms_norm(x, g):
    xf = x.astype(jnp.float32)
    y = xf * lax.rsqrt(jnp.mean(jnp.square(xf), axis=-1, keepdims=True) + LN_EPS) * g.astype(jnp.float32)
    return y.astype(x.dtype)


def rope(x, pos):
    half = ROPE_DIM // 2
    inv_freq = jnp.exp(-math.log(ROPE_BASE) * jnp.arange(half, dtype=jnp.float32) / half)
    ang = pos.astype(jnp.float32)[:, None] * inv_freq[None, :]
    shape = (ang.shape[0],) + (1,) * (x.ndim - 3) + (half,)
    cos = jnp.cos(ang).reshape(shape).astype(x.dtype)
    sin = jnp.sin(ang).reshape(shape).astype(x.dtype)
    x1, x2 = x[..., :half], x[..., half:]
    return jnp.concatenate([x1 * cos - x2 * sin, x2 * cos + x1 * sin], axis=-1)


def mixer_inputs(x, pos, w_in, kv_norm_g):
    B, T, _ = x.shape
    z = jnp.einsum('btd,dc->btc', x, w_in)
    q = z[..., :Q_COLS].reshape(B, T, MLA_HEADS, MLA_HEAD_DIM + ROPE_DIM)
    q_nope = q[..., :MLA_HEAD_DIM]
    q_rope = rope(q[..., MLA_HEAD_DIM:], pos)
    c_kv = rms_norm(z[..., Q_COLS:Q_COLS + KV_LORA], kv_norm_g)
    k_rope = rope(z[..., Q_COLS + KV_LORA:Q_COLS + KV_LORA + ROPE_DIM], pos)
    glu = z[..., Q_COLS + KV_LORA + ROPE_DIM:]
    u = glu[..., :CONV_CH] * jax.nn.sigmoid(glu[..., CONV_CH:])
    return q_nope, q_rope, c_kv, k_rope, u


def mla_prompt(q_nope, q_rope, c_kv, k_rope, w_uk, w_uv):
    B, S, H, _ = q_nope.shape
    nb = S // Q_BLOCK
    k_nope = jnp.einsum('bsl,lhd->bshd', c_kv, w_uk)
    v = jnp.einsum('bsl,lhd->bshd', c_kv, w_uv)
    qn_b = q_nope.reshape(B, nb, Q_BLOCK, H, MLA_HEAD_DIM).transpose(1, 0, 2, 3, 4)
    qr_b = q_rope.reshape(B, nb, Q_BLOCK, H, ROPE_DIM).transpose(1, 0, 2, 3, 4)
    k_pos = jnp.arange(S, dtype=jnp.int32)

    def block(args):
        qn, qr, i = args
        s = jnp.einsum('bqhd,bkhd->bhqk', qn, k_nope) + jnp.einsum('bqhr,bkr->bhqk', qr, k_rope)
        q_pos = i * Q_BLOCK + jnp.arange(Q_BLOCK, dtype=jnp.int32)
        mask = k_pos[None, :] <= q_pos[:, None]
        s = jnp.where(mask, s.astype(jnp.float32) * MLA_SCALE, -jnp.inf)
        p = jax.nn.softmax(s, axis=-1).astype(v.dtype)
        return jnp.einsum('bhqk,bkhd->bqhd', p, v)

    o = lax.map(block, (qn_b, qr_b, jnp.arange(nb, dtype=jnp.int32)))
    return o.transpose(1, 0, 2, 3, 4).reshape(B, S, H * MLA_HEAD_DIM)


def mla_sample(q_nope, q_rope, c_new, kr_new, pool_ckv, pool_krope, page_table, w_uk, w_uv):
    Bd, T, H, _ = q_nope.shape
    q_lat = jnp.einsum('bthd,lhd->bthl', q_nope, w_uk)
    past_c = pool_ckv[page_table].reshape(Bd, -1, KV_LORA)
    past_r = pool_krope[page_table].reshape(Bd, -1, ROPE_DIM)
    P = past_c.shape[1]
    s_past = jnp.einsum('bthl,bkl->bhtk', q_lat, past_c) + jnp.einsum('bthr,bkr->bhtk', q_rope, past_r)
    s_new = jnp.einsum('bthl,bkl->bhtk', q_lat, c_new) + jnp.einsum('bthr,bkr->bhtk', q_rope, kr_new)
    causal = jnp.arange(T)[None, :] <= jnp.arange(T)[:, None]
    s_new = jnp.where(causal, s_new.astype(jnp.float32) * MLA_SCALE, -jnp.inf)
    s = jnp.concatenate([s_past.astype(jnp.float32) * MLA_SCALE, s_new], axis=-1)
    p = jax.nn.softmax(s, axis=-1).astype(c_new.dtype)
    o_lat = jnp.einsum('bhtk,bkl->bthl', p[..., :P], past_c) + jnp.einsum('bhtk,bkl->bthl', p[..., P:], c_new)
    return jnp.einsum('bthl,lhd->bthd', o_lat, w_uv).reshape(Bd, T, H * MLA_HEAD_DIM)


def depthwise_causal_conv(u_ext, conv_w):
    return lax.conv_general_dilated(u_ext, conv_w[:, None, :].astype(u_ext.dtype), window_strides=(1,),
                                    padding='VALID', dimension_numbers=('NWC', 'WIO', 'NWC'),
                                    feature_group_count=CONV_CH)


def conv_tail(y, conv_b, g, b):
    return jax.nn.silu(layer_norm(y + conv_b.astype(y.dtype), g, b))


def mem_kv(mem, w_xk, w_xv):
    B, M, _ = mem.shape
    k = jnp.einsum('bmd,de->bme', mem, w_xk).reshape(B, M, MEM_HEADS, MEM_HEAD_DIM)
    v = jnp.einsum('bmd,de->bme', mem, w_xv).reshape(B, M, MEM_HEADS, MEM_HEAD_DIM)
    return k, v


def post_sublayers(x, mix, mem_k, mem_v, ln1_g, ln1_b, w_xq, w_xo, ln2_g, ln2_b, w_up, w_down, ln3_g, ln3_b):
    x = layer_norm(ALPHA * x + mix, ln1_g, ln1_b)
    B, T, _ = x.shape
    q = jnp.einsum('btd,de->bte', x, w_xq).reshape(B, T, MEM_HEADS, MEM_HEAD_DIM)
    s = jnp.einsum('bthd,bmhd->bhtm', q, mem_k).astype(jnp.float32) * MEM_SCALE
    p = jax.nn.softmax(s, axis=-1).astype(mem_v.dtype)
    o = jnp.einsum('bhtm,bmhd->bthd', p, mem_v).reshape(B, T, D_MODEL)
    x = layer_norm(ALPHA * x + o @ w_xo, ln2_g, ln2_b)
    h = jnp.square(jax.nn.relu(x @ w_up))
    return layer_norm(ALPHA * x + h @ w_down, ln3_g, ln3_b)


def setup_inputs(seed: int = 0) -> dict:
    key = jax.random.key(seed)
    k = jax.random.split(key, 30)
    f32 = jnp.float32

    def nrm(i, shape, scale):
        return jax.random.normal(k[i], shape, f32) * scale

    n_pages = PAST_LEN // PAGE_SIZE
    n_used = DEC_BATCH * n_pages
    n_pool = n_used + n_used // 4
    page_table = jax.random.permutation(k[7], n_pool)[:n_used].reshape(DEC_BATCH, n_pages).astype(jnp.int32)
    return {
        'x_prompt': nrm(0, (BATCH, SEQ, D_MODEL), 1.0),
        'x_sample': nrm(1, (DEC_BATCH, DEC_SEQ, D_MODEL), 1.0),
        'cache_ckv': nrm(2, (DEPTH, n_pool, PAGE_SIZE, KV_LORA), 1.0),
        'cache_krope': nrm(3, (DEPTH, n_pool, PAGE_SIZE, ROPE_DIM), 1.0),
        'state_conv': nrm(4, (DEPTH, DEC_BATCH, CONV_STATE, CONV_CH), 0.5),
        'cache_mem_k': nrm(5, (DEPTH, DEC_BATCH, MEM_TOKENS, MEM_HEADS, MEM_HEAD_DIM), 1.0),
        'cache_mem_v': nrm(6, (DEPTH, DEC_BATCH, MEM_TOKENS, MEM_HEADS, MEM_HEAD_DIM), BETA),
        'page_table': page_table,
        'mem_prompt': nrm(8, (BATCH, MEM_TOKENS, D_MODEL), 1.0),
        'w_in': nrm(9, (DEPTH, D_MODEL, IN_COLS), D_MODEL ** -0.5),
        'kv_norm_g': 1.0 + nrm(10, (DEPTH, KV_LORA), 0.01),
        'w_uk': nrm(11, (DEPTH, KV_LORA, MLA_HEADS, MLA_HEAD_DIM), KV_LORA ** -0.5),
        'w_uv': nrm(12, (DEPTH, KV_LORA, MLA_HEADS, MLA_HEAD_DIM), BETA * KV_LORA ** -0.5),
        'conv_w': nrm(13, (DEPTH, CONV_K, CONV_CH), CONV_K ** -0.5),
        'conv_b': nrm(14, (DEPTH, CONV_CH), 0.01),
        'conv_ln_g': 1.0 + nrm(15, (DEPTH, CONV_CH), 0.01),
        'conv_ln_b': nrm(16, (DEPTH, CONV_CH), 0.01),
        'w_o': nrm(17, (DEPTH, MIX_WIDTH, D_MODEL), BETA * MIX_WIDTH ** -0.5),
        'ln1_g': 1.0 + nrm(18, (DEPTH, D_MODEL), 0.01),
        'ln1_b': nrm(19, (DEPTH, D_MODEL), 0.01),
        'w_xq': nrm(20, (DEPTH, D_MODEL, D_MODEL), D_MODEL ** -0.5),
        'w_xk': nrm(21, (DEPTH, D_MODEL, D_MODEL), D_MODEL ** -0.5),
        'w_xv': nrm(22, (DEPTH, D_MODEL, D_MODEL), BETA * D_MODEL ** -0.5),
        'w_xo': nrm(23, (DEPTH, D_MODEL, D_MODEL), BETA * D_MODEL ** -0.5),
        'ln2_g': 1.0 + nrm(24, (DEPTH, D_MODEL), 0.01),
        'ln2_b': nrm(25, (DEPTH, D_MODEL), 0.01),
        'w_up': nrm(26, (DEPTH, D_MODEL, D_FF), D_MODEL ** -0.5),
        'w_down': nrm(27, (DEPTH, D_FF, D_MODEL), BETA * D_FF ** -0.5),
        'ln3_g': 1.0 + nrm(28, (DEPTH, D_MODEL), 0.01),
        'ln3_b': nrm(29, (DEPTH, D_MODEL), 0.01),
    }


def reference(x_prompt, x_sample, cache_ckv, cache_krope, state_conv, cache_mem_k, cache_mem_v, page_table,
              mem_prompt, w_in, kv_norm_g, w_uk, w_uv, conv_w, conv_b, conv_ln_g, conv_ln_b, w_o, ln1_g, ln1_b,
              w_xq, w_xk, w_xv, w_xo, ln2_g, ln2_b, w_up, w_down, ln3_g, ln3_b):
    B, S, _ = x_prompt.shape
    T = x_sample.shape[1]
    past_len = page_table.shape[1] * cache_ckv.shape[2]
    pos_p = jnp.arange(S, dtype=jnp.int32)
    pos_s = past_len + jnp.arange(T, dtype=jnp.int32)

    xp, xs = x_prompt, x_sample
    ckv_p, kr_p, cs_p, mk_p, mv_p = [], [], [], [], []
    ckv_s, kr_s, cs_s = [], [], []
    for l in range(DEPTH):
        qn, qr, c_kv, k_r, u = mixer_inputs(xp, pos_p, w_in[l], kv_norm_g[l])
        attn = mla_prompt(qn, qr, c_kv, k_r, w_uk[l], w_uv[l])
        u_ext = jnp.concatenate([jnp.zeros((B, CONV_STATE, CONV_CH), u.dtype), u], axis=1)
        conv = conv_tail(depthwise_causal_conv(u_ext, conv_w[l]), conv_b[l], conv_ln_g[l], conv_ln_b[l])
        mix = jnp.concatenate([attn, conv], axis=-1) @ w_o[l]
        mk, mv = mem_kv(mem_prompt, w_xk[l], w_xv[l])
        xp = post_sublayers(xp, mix, mk, mv, ln1_g[l], ln1_b[l], w_xq[l], w_xo[l], ln2_g[l], ln2_b[l],
                            w_up[l], w_down[l], ln3_g[l], ln3_b[l])
        ckv_p.append(c_kv)
        kr_p.append(k_r)
        cs_p.append(u_ext[:, -CONV_STATE:])
        mk_p.append(mk)
        mv_p.append(mv)
        qn, qr, c_kv, k_r, u = mixer_inputs(xs, pos_s, w_in[l], kv_norm_g[l])
        attn = mla_sample(qn, qr, c_kv, k_r, cache_ckv[l], cache_krope[l], page_table, w_uk[l], w_uv[l])
        u_ext = jnp.concatenate([state_conv[l].astype(u.dtype), u], axis=1)
        conv = conv_tail(depthwise_causal_conv(u_ext, conv_w[l]), conv_b[l], conv_ln_g[l], conv_ln_b[l])
        mix = jnp.concatenate([attn, conv], axis=-1) @ w_o[l]
        xs = post_sublayers(xs, mix, cache_mem_k[l], cache_mem_v[l], ln1_g[l], ln1_b[l], w_xq[l], w_xo[l],
                            ln2_g[l], ln2_b[l], w_up[l], w_down[l], ln3_g[l], ln3_b[l])
        ckv_s.append(c_kv)
        kr_s.append(k_r)
        cs_s.append(u_ext[:, -CONV_STATE:])

    return (xp, xs, jnp.stack(ckv_p), jnp.stack(kr_p), jnp.stack(cs_p), jnp.stack(mk_p), jnp.stack(mv_p),
            jnp.stack(ckv_s), jnp.stack(kr_s), jnp.stack(cs_s))
```

```python
import math
import numpy as np
import ml_dtypes
import concourse.bass as bass
import concourse.mybir as mybir
from concourse.bass_utils import run_bass_kernel_spmd

F32 = mybir.dt.float32
BF16 = mybir.dt.bfloat16
I32 = mybir.dt.int32
AF = mybir.ActivationFunctionType
ALU = mybir.AluOpType
AX = mybir.AxisListType

ALPHA = 2.0 ** 0.25
MLA_SCALE = 96.0 ** -0.5
MEM_SCALE = 1.0 / 16.0
EPS = 1e-5
NCORE = 8
SEQ = 2048
NBLK = 4
NPOOL = 10240
DEBUG = False


class Eng:
    def __init__(self, name, eng, sem):
        self.name, self.eng, self.sem, self.count, self.seen = name, eng, sem, 0, {}


class DSem:
    def __init__(self, sem):
        self.sem, self.count = sem, 0


class Trk:
    def __init__(self, nc):
        self.nc = nc
        self.last_w = {}
        self.readers = {}
        self.nsem = 0

    def sem(self, name):
        self.nsem += 1
        return self.nc.alloc_semaphore(name)

    def dsem(self, name):
        return DSem(self.sem(name))

    def _wait(self, E, R, W):
        best = {}
        def add(tk, raw=False):
            obj, val = tk
            if obj is E and not (raw and E.name != "pe" and E.name != "sp"):
                return
            if val > best.get(obj, 0):
                best[obj] = val
        for k in R:
            if k in self.last_w:
                add(self.last_w[k], raw=True)
            if isinstance(k, tuple) and k[0] in ("pf", "pb"):
                for obj, val in self.readers.get(k, {}).items():
                    add((obj, val))
        for k in W:
            if k in self.last_w:
                add(self.last_w[k])
            for obj, val in self.readers.get(k, {}).items():
                add((obj, val))
        for obj, val in best.items():
            if E.seen.get(obj, 0) >= val:
                continue
            assert obj.count >= val, "wait on a signal that is not issued yet"
            E.eng.wait_ge(obj.sem, val * (16 if isinstance(obj, DSem) else 1))
            E.seen[obj] = val

    def _commit(self, tk, R, W):
        obj, val = tk
        for k in R:
            d = self.readers.setdefault(k, {})
            if d.get(obj, 0) < val:
                d[obj] = val
        for k in W:
            self.last_w[k] = tk
            self.readers[k] = {}

    def retire(self, old_keys, new_keys):
        merged = {}
        for k in old_keys:
            if k in self.last_w:
                o, v = self.last_w[k]
                merged[o] = max(merged.get(o, 0), v)
            for o, v in self.readers.get(k, {}).items():
                merged[o] = max(merged.get(o, 0), v)
        for nk in new_keys:
            d = self.readers.setdefault(nk, {})
            for o, v in merged.items():
                d[o] = max(d.get(o, 0), v)

    def op(self, E, fn, R=(), W=(), sig=True):
        self._wait(E, R, W)
        ins = fn()
        if sig:
            E.count += 1
            ins.then_inc(E.sem, 1)
            tk = (E, E.count)
        else:
            tk = (E, E.count + 1)
        self._commit(tk, R, W)

    def dma(self, Q, ds, fn, R=(), W=()):
        self._wait(Q, R, W)
        ins = fn()
        ds.count += 1
        ins.then_inc(ds.sem, 16)
        self._commit((ds, ds.count), R, W)


class _Stop(Exception):
    pass


def build(n_pool=NPOOL, do_prompt=True, do_sample=True, nseq=2, nblk=NBLK, stop=None):
    nc = bass.Bass("TRN2", target_bir_lowering=False)

    def chk(n):
        if stop is not None and n >= stop:
            raise _Stop()

    T = Trk(nc)
    _dummy = [T.sem(f"dummy{i}") for i in range(6)]
    PE = Eng("pe", nc.tensor, T.sem("s_pe"))
    ACT = Eng("act", nc.scalar, T.sem("s_act"))
    DVE = Eng("dve", nc.vector, T.sem("s_dve"))
    POOL = Eng("pool", nc.gpsimd, T.sem("s_pool"))
    SP = Eng("sp", nc.sync, T.sem("s_sp"))

    def din(name, shape, dt=F32):
        return nc.dram_tensor(name, list(shape), dt, kind="ExternalInput").ap()

    def dout(name, shape, dt=F32):
        return nc.dram_tensor(name, list(shape), dt, kind="ExternalOutput").ap()

    def dscr(name, shape, dt=BF16):
        return nc.dram_tensor(name, list(shape), dt, kind="Internal").ap()

    xp = din("xp", [2, SEQ, 1024])
    memp = din("memp", [2, 256, 1024])
    xs = din("xs", [16, 1024])
    pool_c = din("pool_c", [n_pool * 16, 2048])
    pool_r = din("pool_r", [n_pool * 16, 256])
    ptrep = din("ptrep", [128, 128], I32)
    gq = din("gq", [128, 1], I32)
    stc = din("stc", [16, 30 * 512])
    cmk = din("cmk", [16, 256, 1024])
    cmv = din("cmv", [16, 256, 1024])
    w_in = din("w_in", [1024, 2080])
    kvg = din("kvg", [1, 256])
    w_uk = din("w_uk", [256, 512])
    w_uv = din("w_uv", [256, 512])
    conv_w = din("conv_w", [31, 512])
    conv_b = din("conv_b", [1, 512])
    conv_g = din("conv_g", [1, 512])
    conv_bb = din("conv_bb", [1, 512])
    w_o = din("w_o", [1024, 1024])
    w_xq = din("w_xq", [1024, 1024])
    w_xk = din("w_xk", [1024, 1024])
    w_xv = din("w_xv", [1024, 1024])
    w_xo = din("w_xo", [1024, 1024])
    w_up = din("w_up", [1024, 4096])
    w_down = din("w_down", [4096, 1024])
    lnv = [din(n, [1, 1024]) for n in ("ln1_g", "ln1_b", "ln2_g", "ln2_b", "ln3_g", "ln3_b")]
    c_ident = din("c_ident", [128, 128], BF16)
    c_mask = din("c_mask", [128, 128], BF16)
    c_ropeC = din("c_ropeC", [128, 16 * 32])
    c_ropeS = din("c_ropeS", [128, 16 * 32])
    c_ropeCs = din("c_ropeCs", [128, 32])
    c_ropeSs = din("c_ropeSs", [128, 32])
    c_injk = din("c_injk", [32, 96], BF16)
    c_sel = din("c_sel", [16, 16 * 128])

    y_p = dout("y_p", [2, SEQ, 1024])
    y_s = dout("y_s", [16, 1024])
    ckv_p = dout("ckv_p", [2, SEQ, 256])
    kr_p = dout("kr_p", [2, SEQ, 32])
    conv_p = dout("conv_p", [2, 30, 512])
    mk_p = dout("mk_p", [2, 256, 1024])
    mv_p = dout("mv_p", [2, 256, 1024])
    ckv_s = dout("ckv_s", [16, 256])
    kr_s = dout("kr_s", [16, 32])
    conv_s = dout("conv_s", [16, 30 * 512])

    slabs = {}
    cvt_sems = [T.dsem(f"cv{i}") for i in range(4)]
    cvt_i = [0]

    def mk_slab(name, src3):
        a, b = src3.shape[1], src3.shape[2]
        scr = dscr("ws_" + name, [128, a * b])
        ds = cvt_sems[cvt_i[0] % 4]
        cvt_i[0] += 1
        T.dma(POOL, ds, lambda: nc.gpsimd.dma_start(out=scr.rearrange("p (a b) -> p a b", b=b), in_=src3),
              W=[("ws", name)])
        slabs[name] = (scr, a, b)

    def cols(w, c0, c1):
        return w[:, c0:c1].rearrange("(c p) n -> p c n", p=128)

    mk_slab("in0", cols(w_in, 0, 512))
    mk_slab("in1", cols(w_in, 512, 1056))
    mk_slab("in2", cols(w_in, 1056, 1568))
    mk_slab("in3", cols(w_in, 1568, 2080))
    for nm, w in (("xk", w_xk), ("xv", w_xv), ("o", w_o), ("xq", w_xq), ("xo", w_xo)):
        mk_slab(nm + "0", cols(w, 0, 512))
        mk_slab(nm + "1", cols(w, 512, 1024))
    for g in range(8):
        mk_slab(f"up{g}", cols(w_up, g * 512, (g + 1) * 512))
    for fh in range(2):
        for qd in range(4):
            mk_slab(f"dn{fh}{qd}", w_down[fh * 2048:(fh + 1) * 2048, qd * 256:(qd + 1) * 256]
                    .rearrange("(f p) n -> p f n", p=128))

    def sb(name, shape, dt):
        return nc.alloc_sbuf_tensor(name, list(shape), dt)

    ident = sb("ident", [128, 128], BF16)
    mask = sb("mask", [128, 128], BF16)
    ones = sb("ones", [128, 128], BF16)
    injk = sb("injk", [32, 96], BF16)
    ropeC = sb("ropeC", [128, 16, 32], F32)
    ropeS = sb("ropeS", [128, 16, 32], F32)
    ropeCs = sb("ropeCs", [128, 32], F32)
    ropeSs = sb("ropeSs", [128, 32], F32)
    kvg_b = sb("kvg_b", [128, 256], F32)
    WukP = sb("WukP", [128, 2, 8, 96], BF16)
    Wuv = sb("Wuv", [128, 2, 512], BF16)
    cwT = sb("cwT", [128, 4, 31], F32)
    cvec = sb("cvec", [128, 3, 4], F32)
    bvec = sb("bvec", [128, 2, 1024], F32)
    WR_N = 3
    WR = sb("WR", [128, WR_N, 4352], BF16)
    KT = sb("KT", [128, 8, SEQ], BF16)
    Vaug = sb("Vaug", [128, 16, 4, 192], BF16)
    mkT = sb("mkT", [128, 8, 256], BF16)
    mvb = sb("mvb", [128, 2, 1024], BF16)
    Rb = sb("Rb", [128, 4, 1024], F32)
    xring = sb("xring", [128, 1, 1024], F32)
    Cb = sb("Cb", [128, 2, 1024], BF16)
    Db = sb("Db", [128, 8, 512], BF16)
    uT = sb("uT", [128, 4, 30 + 512], F32)
    Pr = sb("Pr", [128, 4, 512], BF16)
    QT = sb("QT", [128, 8, 512], BF16)
    mixT = sb("mixT", [128, 8, 512], BF16)
    hT = sb("hT", [128, 16, 512], BF16)
    qtm = hT[:, 0:6, :].rearrange("p a b -> p (a b)").rearrange("p (t e) -> p t e", e=768)
    ckb = sb("ckb", [128, 4, 288], BF16)
    ckT = sb("ckT", [128, 2, 512], BF16)
    krT = sb("krT", [32, 512], BF16)
    tmpA = sb("tmpA", [128, 1024], F32)
    tmpB = sb("tmpB", [128, 1024], F32)
    cacc = Rb[:, :, 0:512]
    csq = Pr
    cyb = QT
    small = sb("small", [128, 64], F32)
    stats = sb("stats", [128, 2, 6], F32)
    mvar = sb("mvar", [128, 2], F32)

    psf = nc.alloc_psum_tensor("psf", [128, 6, 512], F32)
    psb = nc.alloc_psum_tensor("psb", [128, 2, 1024], BF16)
    pf_i = [0]
    pb_i = [0]

    def nf():
        i = pf_i[0] % 4
        pf_i[0] += 1
        return i

    def nf2():
        while pf_i[0] % 2:
            pf_i[0] += 1
        i = pf_i[0] % 4
        pf_i[0] += 2
        return i

    ph_i = [0]

    def nh():
        i = 4 + ph_i[0] % 2
        ph_i[0] += 1
        return i

    def nb():
        i = pb_i[0] % 2
        pb_i[0] += 1
        return i

    ld_sems = [T.dsem(f"ld{i}") for i in range(8)]
    ld_i = [0]
    st_sems = [T.dsem(f"st{i}") for i in range(8)]
    st_i = [0]

    def load(out, in_, W, R=(), q=None):
        ds = ld_sems[ld_i[0] % 8]
        ld_i[0] += 1
        if q is POOL:
            T.dma(POOL, ds, lambda: nc.gpsimd.dma_start(out=out, in_=in_), R=R, W=W)
        else:
            T.dma(SP, ds, lambda: nc.sync.dma_start(out=out, in_=in_), R=R, W=W)

    def store(out, in_, R, W=()):
        ds = st_sems[st_i[0] % 8]
        st_i[0] += 1
        T.dma(SP, ds, lambda: nc.sync.dma_start(out=out, in_=in_), R=R, W=W)

    wr_sems = [T.dsem(f"wr{i}") for i in range(WR_N)]
    wr_i = [0]

    def wslab(name):
        scr, a, b = slabs[name]
        i = wr_i[0] % WR_N
        wr_i[0] += 1
        view = WR[:, i, 0:a * b].rearrange("p (a b) -> p a b", b=b)
        T.dma(SP, wr_sems[i], lambda: nc.sync.dma_start(out=view, in_=scr.rearrange("p (a b) -> p a b", b=b)),
              R=[("ws", name)], W=[("WR", i)])
        return view, ("WR", i)

    def mm(out, lhsT, rhs, start, stop, R, W, sig=False):
        T.op(PE, lambda: nc.tensor.matmul(out, lhsT, rhs, start=start, stop=stop), R=R, W=W, sig=sig)

    def tr(out, in_, R, W, sig=False):
        k = in_.shape[0]
        T.op(PE, lambda: nc.tensor.transpose(out, in_, ident[0:k, 0:k]), R=list(R) + ["ident"], W=W, sig=sig)

    def act(out, in_, func, R, W, bias=0.0, scale=1.0, accum=None):
        if accum is None:
            T.op(ACT, lambda: nc.scalar.activation(out, in_, func, bias=bias, scale=scale), R=R, W=W)
        else:
            T.op(ACT, lambda: nc.scalar.activation(out, in_, func, bias=bias, scale=scale, accum_out=accum), R=R, W=W)

    def tt(E, out, in0, in1, op, R, W):
        T.op(E, lambda: E.eng.tensor_tensor(out, in0, in1, op), R=R, W=W)

    def ts(E, out, in0, s1, s2, op0, op1, R, W):
        if op1 is None:
            T.op(E, lambda: E.eng.tensor_scalar(out, in0, s1, None, op0), R=R, W=W)
        else:
            T.op(E, lambda: E.eng.tensor_scalar(out, in0, s1, s2, op0, op1), R=R, W=W)

    def stt(E, out, in0, scalar, in1, op0, op1, R, W):
        T.op(E, lambda: E.eng.scalar_tensor_tensor(out, in0, scalar, in1, op0, op1), R=R, W=W)

    def cp(E, out, in_, R, W):
        if E is ACT:
            T.op(ACT, lambda: nc.scalar.activation(out, in_, AF.Identity), R=R, W=W)
        else:
            T.op(E, lambda: E.eng.tensor_copy(out, in_), R=R, W=W)

    def bvload(idx, slot):
        load(bvec[:, slot, :], lnv[idx].partition_broadcast(128), W=[("bvec", slot)])

    load(ident[:], c_ident, W=["ident"])
    load(mask[:], c_mask, W=["mask"])
    load(injk[:], c_injk, W=["injk"])
    load(ropeC[:], c_ropeC.rearrange("p (a b) -> p a b", b=32), W=["rope"])
    load(ropeS[:], c_ropeS.rearrange("p (a b) -> p a b", b=32), W=["rope"])
    load(ropeCs[:], c_ropeCs, W=["ropes"])
    load(ropeSs[:], c_ropeSs, W=["ropes"])
    load(kvg_b[:], kvg.partition_broadcast(128), W=["kvg"])
    T.op(DVE, lambda: nc.vector.memset(ones[:], 1.0), W=["ones"])
    T.op(DVE, lambda: nc.vector.memset(WukP[:], 0.0), W=["WukP"])
    T.op(DVE, lambda: nc.vector.memset(Vaug[:], 1.0), W=["Vaug_init"])
    for lc in range(2):
        load(WukP[:, lc, :, 0:64], w_uk[lc * 128:(lc + 1) * 128, :].rearrange("p (h e) -> p h e", e=64),
             R=["WukP"], W=[("WukPl", lc)], q=POOL)
        load(Wuv[:, lc, :], w_uv[lc * 128:(lc + 1) * 128, :], W=[("Wuv", lc)], q=POOL)
    WUK = [("WukPl", 0), ("WukPl", 1), "WukP"]
    WUV = [("Wuv", 0), ("Wuv", 1)]
    with nc.allow_non_contiguous_dma(reason="tiny transposed parameter loads"):
        for cc in range(4):
            load(cwT[:, cc, :], conv_w[:, cc * 128:(cc + 1) * 128].rearrange("k p -> p k"), W=["cwT"])
            for i, v in enumerate((conv_b, conv_g, conv_bb)):
                load(cvec[:, i, cc:cc + 1], v[:, cc * 128:(cc + 1) * 128].rearrange("o p -> p o"), W=[("cvec", i)])
    CV = [("cvec", 0), ("cvec", 1), ("cvec", 2)]

    def rstd_from_var(out, var_ap, R, W):
        act(out, var_ap, AF.Sqrt, R=R, W=W, bias=EPS, scale=1.0)
        T.op(DVE, lambda: nc.vector.reciprocal(out, out), R=W, W=W)

    def layer_norm_rows(t_key, rows, src, dst, gi, bi, bf_out, bf_key):
        for hlf in range(2):
            T.op(DVE, lambda h=hlf: nc.vector.bn_stats(stats[0:rows, h, :], src[:, h * 512:(h + 1) * 512]),
                 R=[t_key], W=[("stats", hlf)])
        T.op(DVE, lambda: nc.vector.bn_aggr(mvar[0:rows, :], stats[0:rows, :, :]),
             R=[("stats", 0), ("stats", 1)], W=["mvar"])
        rstd_from_var(small[0:rows, 0:1], mvar[0:rows, 1:2], R=["mvar"], W=[("small", 0)])
        stt(DVE, small[0:rows, 1:2], mvar[0:rows, 0:1], -1.0, small[0:rows, 0:1], ALU.mult, ALU.mult,
            R=["mvar", ("small", 0)], W=[("small", 1)])
        act(dst, src, AF.Identity, R=[t_key, ("small", 0), ("small", 1)], W=[t_key],
            bias=small[0:rows, 1:2], scale=small[0:rows, 0:1])
        tt(POOL, dst, dst, bvec[0:rows, 0, :], ALU.mult, R=[t_key, ("bvec", 0)], W=[t_key])
        tt(POOL, dst, dst, bvec[0:rows, 1, :], ALU.add, R=[t_key, ("bvec", 1)], W=[t_key])
        if bf_out is not None:
            cp(ACT, bf_out, dst, R=[t_key], W=[bf_key])

    def rope_rows(rows, src3, dst3, C2, S2, tmp3, R, W, tmpkey):
        tt(DVE, tmp3[:, :, 0:16], src3[:, :, 16:32], S2[:, :, 0:16], ALU.mult, R=R, W=[tmpkey])
        tt(DVE, tmp3[:, :, 16:32], src3[:, :, 0:16], S2[:, :, 16:32], ALU.mult, R=R, W=[tmpkey])
        tt(DVE, tmp3[:, :, 32:64], src3, C2, ALU.mult, R=R, W=[tmpkey])
        tt(POOL, dst3, tmp3[:, :, 0:32], tmp3[:, :, 32:64], ALU.add, R=[tmpkey], W=W)

    def mem_kv_seq(s):
        memT = Db
        for mt in range(2):
            load(Cb[:, mt, :], memp[s, mt * 128:(mt + 1) * 128, :], W=[("Cb", mt)], q=POOL)
        for c in range(8):
            b = nb()
            for mt in range(2):
                tr(psb[:, b, mt * 128:(mt + 1) * 128], Cb[:, mt, c * 128:(c + 1) * 128],
                   R=[("Cb", mt)], W=[("pb", b)], sig=(mt == 1))
            cp(DVE, memT[:, c, 0:256], psb[:, b, 0:256], R=[("pb", b)], W=[("Db", c)])
        DBK = [("Db", c) for c in range(8)]
        chk(0.3)
        for which, dst in (("xk", mk_p), ("xv", mv_p)):
            cf = (lambda x: x) if which == "xk" else (lambda x: 0.97 + 0.03 * x)
            views = [wslab(which + "0"), wslab(which + "1")]
            chk(cf(0.5))
            for mt in range(2):
                b2 = nf2()
                for hf in range(2):
                    wv, wk = views[hf]
                    for c in range(8):
                        mm(psf[:, b2 + hf, :], memT[:, c, mt * 128:(mt + 1) * 128], wv[:, c, :], c == 0, c == 7,
                           R=DBK + [wk], W=[("pf", b2 + hf)], sig=(c == 7))
                chk(cf(0.7))
                t_ = tmpA if mt == 0 else tmpB
                tk = "tmpA" if mt == 0 else "tmpB"
                for hf in range(2):
                    cp(ACT, t_[:, hf * 512:(hf + 1) * 512], psf[:, b2 + hf, :], R=[("pf", b2 + hf)], W=[tk])
                chk(cf(0.8))
                store(dst[s, mt * 128:(mt + 1) * 128, :], t_[:, :], R=[tk])
                chk(cf(0.9))
                if which == "xv":
                    for hf in range(2):
                        cp(DVE, mvb[:, mt, hf * 512:(hf + 1) * 512], t_[:, hf * 512:(hf + 1) * 512],
                           R=[tk], W=[("mvb", mt)])
            chk(cf(0.95))
            if which == "xk":
                for e in range(8):
                    wv, wk = views[e // 4]
                    b1 = nf()
                    for c in range(8):
                        mm(psf[:, b1, 0:256], wv[:, c, (e % 4) * 128:(e % 4 + 1) * 128], memT[:, c, 0:256], c == 0, c == 7,
                           R=DBK + [wk], W=[("pf", b1)], sig=(c == 7))
                    cp(DVE, mkT[:, e, :], psf[:, b1, 0:256], R=[("pf", b1)], W=[("mkT", e)])
                chk(0.97)

    MKT = [("mkT", e) for e in range(8)]
    MVB = [("mvb", 0), ("mvb", 1)]

    def prompt_block(s, j):
        t0 = j * 512
        DBK = [("Db", c) for c in range(8)]
        T.retire([("hT", i) for i in range(16)], [("qtm", t) for t in range(4)])
        if j == 0:
            T.op(POOL, lambda: nc.gpsimd.memset(uT[:, :, 0:30], 0.0), W=["uT_head"])
        for t in range(4):
            load(Cb[:, t % 2, :], xp[s, t0 + t * 128:t0 + (t + 1) * 128, :], W=[("Cb", t % 2)], q=POOL)
            for c in range(8):
                if c % 4 == 0:
                    b = nb()
                tr(psb[:, b, (c % 4) * 128:(c % 4 + 1) * 128], Cb[:, t % 2, c * 128:(c + 1) * 128],
                   R=[("Cb", t % 2)], W=[("pb", b)], sig=(c % 4 == 3))
                if c % 4 == 3:
                    c0 = c - 3
                    cp(ACT if c0 else DVE,
                       Db[:, c0:c0 + 4, t * 128:(t + 1) * 128],
                       psb[:, b, 0:512].rearrange("p (a b) -> p a b", b=128),
                       R=[("pb", b)], W=[("Db", cc) for cc in range(c0, c0 + 4)])
        chk(2)
        v0, k0 = wslab("in0")
        v1, k1 = wslab("in1")
        for t in range(4):
            bA = nf()
            bB = nf()
            bC = nf()
            for (ob, o0, o1, vv, kk, c0_, c1_) in (
                    (bA, 0, 480, v0, k0, 0, 480), (bB, 0, 32, v0, k0, 480, 512), (bB, 32, 288, v1, k1, 0, 256),
                    (bB, 288, 320, v1, k1, 512, 544), (bC, 0, 256, v1, k1, 256, 512)):
                for c in range(8):
                    mm(psf[:, ob, o0:o1], Db[:, c, t * 128:(t + 1) * 128], vv[:, c, c0_:c1_], c == 0, c == 7,
                       R=DBK + [kk], W=[("pf", ob)], sig=(c == 7))
            PQ = [("pf", bA), ("pf", bB), ("pf", bC)]
            tile_i = j * 4 + t
            qd3 = qtm[:, t, :].rearrange("p (h e) -> p h e", e=96)
            for (bk, h0, nh_) in ((bA, 0, 5), (bB, 5, 3)):
                q3 = psf[:, bk, 0:nh_ * 96].rearrange("p (h e) -> p h e", e=96)
                cp(ACT, qd3[:, h0:h0 + nh_, 0:64], q3[:, :, 0:64], R=PQ, W=[("qtm", t)])
                tmp3 = tmpA[:, 0:nh_ * 64].rearrange("p (h e) -> p h e", e=64)
                rope_rows(128, q3[:, :, 64:96], qd3[:, h0:h0 + nh_, 64:96],
                          ropeC[:, tile_i:tile_i + 1, :].to_broadcast([128, nh_, 32]),
                          ropeS[:, tile_i:tile_i + 1, :].to_broadcast([128, nh_, 32]),
                          tmp3, R=PQ + ["rope"], W=[("qtm", t)], tmpkey="tmpA")
            pckv = psf[:, bC, 0:256]
            b1 = bB
            cp(DVE, tmpB[:, 768:1024], pckv, R=PQ, W=["tmpB"])
            pckv = tmpB[:, 768:1024]
            act(tmpB[:, 0:256], pckv, AF.Square, R=["tmpB"], W=["tmpB"])
            T.op(DVE, lambda: nc.vector.reduce_sum(small[:, 2:3], tmpB[:, 0:256], axis=AX.X), R=["tmpB"], W=["tmpB"])
            act(small[:, 3:4], small[:, 2:3], AF.Sqrt, R=["tmpB"], W=[("small", 3)], bias=EPS, scale=1.0 / 256.0)
            T.op(DVE, lambda: nc.vector.reciprocal(small[:, 3:4], small[:, 3:4]), R=[("small", 3)], W=[("small", 3)])
            stt(DVE, tmpB[:, 256:512], pckv, small[:, 3:4], kvg_b[:, :], ALU.mult, ALU.mult,
                R=PQ + [("small", 3), "kvg", "tmpB"], W=["tmpB"])
            store(ckv_p[s, t0 + t * 128:t0 + (t + 1) * 128, :], tmpB[:, 256:512], R=["tmpB"])
            if DEBUG and t == 1:
                store(y_p[s, 0:128, 0:64], small[:, :], R=["tmpB", ("small", 3)])
                store(y_p[s, 0:128, 64:320], tmpB[:, 768:1024], R=["tmpB"])
                store(y_p[s, 0:128, 320:576], tmpB[:, 256:512], R=["tmpB"])
            cp(ACT, ckb[:, t, 0:256], tmpB[:, 256:512], R=["tmpB"], W=[("ckb", t)])
            kr3 = tmpB[:, 512:544].rearrange("p (h e) -> p h e", e=32)
            rope_rows(128, psf[:, b1, 288:320].rearrange("p (h e) -> p h e", e=32), kr3,
                      ropeC[:, tile_i:tile_i + 1, :], ropeS[:, tile_i:tile_i + 1, :],
                      tmpB[:, 576:640].rearrange("p (h e) -> p h e", e=64),
                      R=[("pf", b1), "rope"], W=["tmpB"], tmpkey="tmpB")
            store(kr_p[s, t0 + t * 128:t0 + (t + 1) * 128, :], tmpB[:, 512:544], R=["tmpB"])
            cp(ACT, ckb[:, t, 256:288], tmpB[:, 512:544], R=["tmpB"], W=[("ckb", t)])
        chk(3)
        QTM = [("qtm", t) for t in range(4)]
        CKB = [("ckb", t) for t in range(4)]
        for h in range(8):
            b = nb()
            for t in range(4):
                tr(psb[0:96, b, t * 128:(t + 1) * 128], qtm[:, t, h * 96:(h + 1) * 96], R=QTM, W=[("pb", b)], sig=(t == 3))
            cp(ACT if h % 2 else DVE, QT[0:96, h, :], psb[0:96, b, 0:512], R=[("pb", b)], W=[("QT", h)])
        for lc in range(2):
            b = nb()
            for t in range(4):
                tr(psb[:, b, t * 128:(t + 1) * 128], ckb[:, t, lc * 128:(lc + 1) * 128], R=CKB, W=[("pb", b)], sig=(t == 3))
            cp(DVE, ckT[:, lc, :], psb[:, b, 0:512], R=[("pb", b)], W=[("ckT", lc)])
        b = nb()
        for t in range(4):
            tr(psb[0:32, b, t * 128:(t + 1) * 128], ckb[:, t, 256:288], R=CKB, W=[("pb", b)], sig=(t == 3))
        cp(DVE, krT[:, :], psb[0:32, b, 0:512], R=[("pb", b)], W=["krT"])
        CKT = [("ckT", 0), ("ckT", 1)]
        for h in range(8):
            b1 = nf()
            mm(psf[0:96, b1, :], WukP[:, 0, h, :], ckT[:, 0, :], True, False, R=CKT + WUK, W=[("pf", b1)])
            mm(psf[0:96, b1, :], WukP[:, 1, h, :], ckT[:, 1, :], False, False, R=CKT + WUK, W=[("pf", b1)])
            mm(psf[0:96, b1, :], injk[:, :], krT[:, :], False, True, R=["krT", "injk"], W=[("pf", b1)], sig=True)
            cp(ACT if h % 2 else DVE, KT[0:96, h, t0:t0 + 512], psf[0:96, b1, :], R=[("pf", b1)], W=[("KT", h, j)])
        for t in range(4):
            b1 = nf()
            for lc in range(2):
                mm(psf[:, b1, :], ckT[:, lc, t * 128:(t + 1) * 128], Wuv[:, lc, :], lc == 0, lc == 1,
                   R=CKT + WUV, W=[("pf", b1)], sig=(lc == 1))
            kt = j * 4 + t
            pv = psf[:, b1, :].rearrange("p (a two e) -> p a two e", two=2, e=64)
            cp(DVE, Vaug[:, kt, :, 0:64], pv[:, :, 0, :], R=[("pf", b1), "Vaug_init"], W=[("V", kt)])
            cp(ACT, Vaug[:, kt, :, 128:192], pv[:, :, 1, :], R=[("pf", b1), "Vaug_init"], W=[("V", kt)])
        chk(4)
        v2, k2 = wslab("in2")
        v3, k3 = wslab("in3")
        for cc in range(4):
            ba = nf()
            bb = nf()
            for c in range(8):
                mm(psf[:, ba, :], v2[:, c, cc * 128:(cc + 1) * 128], Db[:, c, :], c == 0, c == 7, R=DBK + [k2], W=[("pf", ba)])
                mm(psf[:, bb, :], v3[:, c, cc * 128:(cc + 1) * 128], Db[:, c, :], c == 0, c == 7, R=DBK + [k3], W=[("pf", bb)], sig=(c == 7))
            act(tmpA[:, 0:512], psf[:, bb, :], AF.Sigmoid, R=[("pf", bb)], W=["tmpA"])
            tt(DVE, uT[:, cc, 30:542], psf[:, ba, :], tmpA[:, 0:512], ALU.mult, R=[("pf", ba), "tmpA", "uT_head"], W=[("uT", cc)])
        if j == nblk - 1:
            b2 = nf2()
            for c in range(8):
                xTt = Db[:, c, 384:512]
                mm(psf[:, b2, :], xTt, v2[:, c, :], c == 0, c == 7, R=DBK + [k2], W=[("pf", b2)])
                mm(psf[:, b2 + 1, :], xTt, v3[:, c, :], c == 0, c == 7, R=DBK + [k3], W=[("pf", b2 + 1)], sig=(c == 7))
            act(tmpA[:, 0:512], psf[:, b2 + 1, :], AF.Sigmoid, R=[("pf", b2 + 1)], W=["tmpA"])
            tt(DVE, tmpA[:, 512:1024], psf[:, b2, :], tmpA[:, 0:512], ALU.mult, R=[("pf", b2), "tmpA"], W=["tmpA"])
            store(conv_p[s, :, :], tmpA[98:128, 512:1024], R=["tmpA"])
        chk(5)
        nkt = 4 * j + 4
        for h in range(8):
            bo = nh()
            for kt in range(nkt):
                r = kt - 4 * j
                q0 = 0 if r <= 0 else r * 128
                n = 512 - q0
                bs = nf()
                mm(psf[:, bs, 0:n], KT[0:96, h, kt * 128:(kt + 1) * 128], QT[0:96, h, q0:512], True, True,
                   R=[("KT", h, kt // 4), ("QT", h)], W=[("pf", bs)], sig=True)
                pi = (h * 64 + kt) % 4
                act(Pr[:, pi, 0:n], psf[:, bs, 0:n], AF.Exp, R=[("pf", bs)], W=[("Pr", pi)], scale=MLA_SCALE)
                if r >= 0:
                    tt(POOL, Pr[:, pi, 0:128], Pr[:, pi, 0:128], mask[:, :], ALU.mult, R=[("Pr", pi), "mask"], W=[("Pr", pi)])
                odd = h % 2
                lhs = Vaug[:, kt, h // 2, 64:192] if odd else Vaug[:, kt, h // 2, 0:128]
                mm(psf[:, bo, q0:512], lhs, Pr[:, pi, 0:n], kt == 0, kt == nkt - 1,
                   R=[("Pr", pi), ("V", kt), "Vaug_init"], W=[("pf", bo)], sig=True)
            if h % 2 == 0:
                cp(DVE, tmpA[0:64, 0:512], psf[64:128, bo, :], R=[("pf", bo)], W=["tmpA"])
                T.op(DVE, lambda: nc.vector.reciprocal(tmpA[0:64, 0:512], tmpA[0:64, 0:512]), R=["tmpA"], W=["tmpA"])
                tt(DVE, mixT[0:64, h // 2, :], psf[0:64, bo, :], tmpA[0:64, 0:512], ALU.mult,
                   R=[("pf", bo), "tmpA"], W=[("mixT", h // 2)])
            else:
                cp(DVE, tmpB[64:128, 0:512], psf[0:64, bo, :], R=[("pf", bo)], W=["tmpB"])
                T.op(DVE, lambda: nc.vector.reciprocal(tmpB[64:128, 0:512], tmpB[64:128, 0:512]), R=["tmpB"], W=["tmpB"])
                tt(DVE, mixT[64:128, h // 2, :], psf[64:128, bo, :], tmpB[64:128, 0:512], ALU.mult,
                   R=[("pf", bo), "tmpB"], W=[("mixT", h // 2)])
        chk(6)
        for cc in range(4):
            E = DVE
            for k in range(31):
                src = uT[:, cc, k:k + 512]
                if k == 0:
                    ts(E, cacc[:, cc, :], src, cwT[:, cc, 0:1], None, ALU.mult, None,
                       R=[("uT", cc), "uT_head", "cwT"], W=[("Rb", cc)])
                else:
                    stt(E, cacc[:, cc, :], src, cwT[:, cc, k:k + 1], cacc[:, cc, :], ALU.mult, ALU.add,
                        R=[("uT", cc), "cwT", ("Rb", cc)], W=[("Rb", cc)])
            if j < nblk - 1:
                cp(ACT, tmpA[:, 0:30], uT[:, cc, 512:542], R=[("uT", cc), ("Rb", cc)], W=["tmpA"])
                cp(ACT, uT[:, cc, 0:30], tmpA[:, 0:30], R=["tmpA"], W=["uT_head"])
            act(cacc[:, cc, :], cacc[:, cc, :], AF.Identity, R=[("Rb", cc)] + CV, W=[("Rb", cc)], bias=cvec[:, 0, cc:cc + 1])
            cp(ACT, cyb[:, cc, :], cacc[:, cc, :], R=[("Rb", cc)], W=[("QT", cc)])
            act(csq[:, cc, :], cacc[:, cc, :], AF.Square, R=[("Rb", cc)], W=[("Pr", cc)])
        bm = nf()
        bq = nf()
        for cc in range(4):
            mm(psf[:, bm, :], ones[:, :], cyb[:, cc, :], cc == 0, cc == 3, R=[("QT", cc), "ones"], W=[("pf", bm)])
            mm(psf[:, bq, :], ones[:, :], csq[:, cc, :], cc == 0, cc == 3, R=[("Pr", cc), "ones"], W=[("pf", bq)], sig=(cc == 3))
        ts(DVE, tmpA[:, 0:512], psf[:, bm, :], 1.0 / 512.0, None, ALU.mult, None, R=[("pf", bm)], W=["tmpA"])
        tt(DVE, tmpA[:, 512:1024], tmpA[:, 0:512], tmpA[:, 0:512], ALU.mult, R=["tmpA"], W=["tmpA"])
        stt(DVE, tmpA[:, 512:1024], psf[:, bq, :], 1.0 / 512.0, tmpA[:, 512:1024], ALU.mult, ALU.subtract,
            R=[("pf", bq), "tmpA"], W=["tmpA"])
        act(tmpA[:, 512:1024], tmpA[:, 512:1024], AF.Sqrt, R=["tmpA"], W=["tmpA"], bias=EPS, scale=1.0)
        T.op(DVE, lambda: nc.vector.reciprocal(tmpA[:, 512:1024], tmpA[:, 512:1024]), R=["tmpA"], W=["tmpA"])
        for cc in range(4):
            tt(DVE, cacc[:, cc, :], cacc[:, cc, :], tmpA[:, 0:512], ALU.subtract, R=[("Rb", cc), "tmpA"], W=[("Rb", cc)])
            tt(POOL, cacc[:, cc, :], cacc[:, cc, :], tmpA[:, 512:1024], ALU.mult, R=[("Rb", cc), "tmpA"], W=[("Rb", cc)])
            act(mixT[:, 4 + cc, :], cacc[:, cc, :], AF.Silu, R=[("Rb", cc)] + CV, W=[("mixT", 4 + cc)],
                bias=cvec[:, 2, cc:cc + 1], scale=cvec[:, 1, cc:cc + 1])
        chk(7)
        MIX = [("mixT", c) for c in range(8)]
        post_ln(s, j, MIX, mixT, "o", 0, 1, x_from_dram=True)
        chk(8)
        vq = [wslab("xq0"), wslab("xq1")]
        for e in range(8):
            wv, wk = vq[e // 4]
            b1 = nf()
            for c in range(8):
                mm(psf[:, b1, :], wv[:, c, (e % 4) * 128:(e % 4 + 1) * 128], Db[:, c, :], c == 0, c == 7,
                   R=DBK + [wk], W=[("pf", b1)], sig=(c == 7))
            cp(ACT if e % 2 else DVE, QT[:, e, :], psf[:, b1, :], R=[("pf", b1)], W=[("QT", e)])
        for h in range(4):
            pis = []
            for mc in range(2):
                bs = nf()
                for dc in range(2):
                    mm(psf[:, bs, :], mkT[:, 2 * h + dc, mc * 128:(mc + 1) * 128], QT[:, 2 * h + dc, :], dc == 0, dc == 1,
                       R=MKT + [("QT", 2 * h + dc)], W=[("pf", bs)], sig=(dc == 1))
                pi = (h * 2 + mc) % 4
                pis.append(pi)
                act(Pr[:, pi, :], psf[:, bs, :], AF.Exp, R=[("pf", bs)], W=[("Pr", pi)], scale=MEM_SCALE)
            bd = nf()
            for mc in range(2):
                mm(psf[:, bd, :], ones[:, :], Pr[:, pis[mc], :], mc == 0, mc == 1, R=[("Pr", pis[mc]), "ones"], W=[("pf", bd)], sig=(mc == 1))
            T.op(DVE, lambda: nc.vector.reciprocal(tmpA[:, 0:512], psf[:, bd, :]), R=[("pf", bd)], W=["tmpA"])
            for dc in range(2):
                bo = nf()
                for mc in range(2):
                    mm(psf[:, bo, :], mvb[:, mc, h * 256 + dc * 128:h * 256 + (dc + 1) * 128], Pr[:, pis[mc], :], mc == 0, mc == 1,
                       R=MVB + [("Pr", pis[mc])], W=[("pf", bo)], sig=(mc == 1))
                tt(DVE, mixT[:, 2 * h + dc, :], psf[:, bo, :], tmpA[:, 0:512], ALU.mult, R=[("pf", bo), "tmpA"], W=[("mixT", 2 * h + dc)])
        chk(9)
        post_ln(s, j, MIX, mixT, "xo", 2, 3, x_from_dram=False)
        chk(10)
        T.retire([("qtm", t) for t in range(4)], [("hT", i) for i in range(16)])
        for fh in range(2):
            for g in range(4):
                wv, wk = wslab(f"up{fh * 4 + g}")
                for fc in range(4):
                    b1 = nf()
                    for c in range(8):
                        mm(psf[:, b1, :], wv[:, c, fc * 128:(fc + 1) * 128], Db[:, c, :], c == 0, c == 7,
                           R=DBK + [wk], W=[("pf", b1)], sig=(c == 7))
                    act(tmpA[:, 0:512] if fc % 2 == 0 else tmpB[:, 0:512], psf[:, b1, :], AF.Relu, R=[("pf", b1)],
                        W=["tmpA" if fc % 2 == 0 else "tmpB"])
                    tt(DVE, hT[:, g * 4 + fc, :], tmpA[:, 0:512] if fc % 2 == 0 else tmpB[:, 0:512], psf[:, b1, :], ALU.mult,
                       R=[("pf", b1), "tmpA" if fc % 2 == 0 else "tmpB"], W=[("hT", g * 4 + fc)])
            HK = [("hT", i) for i in range(16)]
            for qd in range(4):
                wv, wk = wslab(f"dn{fh}{qd}")
                for t in range(4):
                    b1 = nf()
                    for fc in range(16):
                        mm(psf[:, b1, 0:256], hT[:, fc, t * 128:(t + 1) * 128], wv[:, fc, :], fc == 0, fc == 15,
                           R=HK + [wk], W=[("pf", b1)], sig=(fc == 15))
                    dstv = Rb[:, t, qd * 256:(qd + 1) * 256]
                    if fh == 0:
                        stt(DVE, dstv, dstv, ALPHA, psf[:, b1, 0:256], ALU.mult, ALU.add, R=[("Rb", t), ("pf", b1)], W=[("Rb", t)])
                    else:
                        tt(DVE, dstv, dstv, psf[:, b1, 0:256], ALU.add, R=[("Rb", t), ("pf", b1)], W=[("Rb", t)])
        bvload(4, 0)
        bvload(5, 1)
        for t in range(4):
            layer_norm_rows(("Rb", t), 128, Rb[:, t, :], Rb[:, t, :], 4, 5, None, None)
            store(y_p[s, t0 + t * 128:t0 + (t + 1) * 128, :], Rb[:, t, :], R=[("Rb", t)])

    def post_ln(s, j, INK, inT, wname, gi, bi, x_from_dram):
        t0 = j * 512
        views = [wslab(wname + "0"), wslab(wname + "1")]
        bvload(gi, 0)
        bvload(bi, 1)
        for t in range(4):
            b2 = nf2()
            for hf in range(2):
                wv, wk = views[hf]
                for c in range(8):
                    mm(psf[:, b2 + hf, :], inT[:, c, t * 128:(t + 1) * 128], wv[:, c, :], c == 0, c == 7,
                       R=INK + [wk], W=[("pf", b2 + hf)], sig=(c == 7))
            if x_from_dram:
                load(xring[:, 0, :], xp[s, t0 + t * 128:t0 + (t + 1) * 128, :], W=[("xring", 0)])
            for hf in range(2):
                src = xring[:, 0, hf * 512:(hf + 1) * 512] if x_from_dram else Rb[:, t, hf * 512:(hf + 1) * 512]
                stt(DVE, Rb[:, t, hf * 512:(hf + 1) * 512], src, ALPHA, psf[:, b2 + hf, :], ALU.mult, ALU.add,
                    R=[("pf", b2 + hf), ("xring", 0), ("Rb", t)], W=[("Rb", t)])
            layer_norm_rows(("Rb", t), 128, Rb[:, t, :], Rb[:, t, :], gi, bi, Cb[:, t % 2, :], ("Cb", t % 2))
            for c in range(8):
                if c % 4 == 0:
                    b = nb()
                tr(psb[:, b, (c % 4) * 128:(c % 4 + 1) * 128], Cb[:, t % 2, c * 128:(c + 1) * 128],
                   R=[("Cb", t % 2)], W=[("pb", b)], sig=(c % 4 == 3))
                if c % 4 == 3:
                    c0 = c - 3
                    cp(ACT if c0 else DVE,
                       Db[:, c0:c0 + 4, t * 128:(t + 1) * 128],
                       psb[:, b, 0:512].rearrange("p (a b) -> p a b", b=128),
                       R=[("pb", b)], W=[("Db", cc) for cc in range(c0, c0 + 4)])


    def barrier():
        engs = (PE, ACT, DVE, POOL, SP)
        dss = st_sems + ld_sems + wr_sems + cvt_sems + g_sems
        for E in engs:
            for F in engs:
                if F is not E and F.count and E.seen.get(F, 0) < F.count:
                    E.eng.wait_ge(F.sem, F.count)
                    E.seen[F] = F.count
            for ds in dss:
                if ds.count and E.seen.get(ds, 0) < ds.count:
                    E.eng.wait_ge(ds.sem, ds.count * 16)
                    E.seen[ds] = ds.count

    g_sems = [T.dsem(f"g{i}") for i in range(4)]

    def sample_phase():
        NS = 16
        KTf = KT[:, :, :].rearrange("p a b -> p (a b)")
        hTf = hT[:, :, :].rearrange("p a b -> p (a b)")
        QTf = QT[:, :, :].rearrange("p a b -> p (a b)")
        Rbf = Rb[:, :, :].rearrange("p a b -> p (a b)")
        uTf = uT[:, :, :].rearrange("p a b -> p (a b)")
        bvf = bvec[:, :, :].rearrange("p a b -> p (a b)")
        mixf = mixT[:, :, :].rearrange("p a b -> p (a b)")
        Dbf = Db[:, :, :].rearrange("p a b -> p (a b)")

        def carver(arena):
            off = [0]
            def carve(n):
                v = arena[:, off[0]:off[0] + n]
                off[0] += n
                return v
            return carve
        ck, ch, cq, cr, cm, cd = carver(KTf), carver(hTf), carver(QTf), carver(Rbf), carver(mixf), carver(Dbf)
        cpg = [ck(2048).rearrange("p (r l) -> p r l", l=256) for _ in range(2)]
        rpg = [ck(256).rearrange("p (r l) -> p r l", l=32) for _ in range(2)]
        cT = [ck(2048).rearrange("p (a b) -> p a b", b=1024) for _ in range(2)]
        rT = [ck(1024) for _ in range(2)]
        cnx = ck(16 * 264).rearrange("p (b l) -> p b l", l=264)
        Pp = ck(512)
        WukT = ch(2048).rearrange("p (h l) -> p h l", l=256)
        ql_bf = ch(2048)
        xs_bf = ch(1024)
        q_s = ch(768)
        cnew_bf = ch(288)
        mix_s = ch(1024)
        xsT = cq(128).rearrange("p (c b) -> p c b", b=16)
        qT_s = cq(128).rearrange("p (h b) -> p h b", b=16)
        qlT = cq(256).rearrange("p (l b h) -> p l b h", b=16, h=8)
        qrT = cq(128).rearrange("p (b h) -> p b h", h=8)
        cnT = cq(32).rearrange("p (l b) -> p l b", b=16)
        krnT = cq(16)
        olT = cq(256).rearrange("p (l h b) -> p l h b", h=8, b=16)
        mixT_s = cq(128).rearrange("p (c b) -> p c b", b=16)
        o2T_s = cq(128).rearrange("p (c b) -> p c b", b=16)
        hT_s = cq(512)
        pnew = cq(8)
        ol_bf = cq(256)
        P2bf = cq(8)
        dhl = cq(16)
        xs_f = cr(1024)
        r_s = cr(1024)
        u_s = cr(512)
        q2_f = cr(1024)
        misc = cr(512)
        xb16 = cm(1024)
        ptb = sb("ptb", [128, 128], I32)
        gqt = sb("gqt", [128, 1], I32)
        idx = sb("idx", [128, 128], I32)
        q2_scr = dscr("q2_scr", [16, 1024], F32)
        onesf = sb("onesf", [128, 1], F32)
        idxc = [sb(f"idxc{i}", [128, 1], I32) for i in range(2)]

        load(ptb[:], ptrep, W=["ptb"])
        load(gqt[:], gq, W=["gqt"])
        T.op(DVE, lambda: nc.vector.memset(onesf[:], 1.0), W=["onesf"])
        ts(DVE, idx[:], ptb[:], 16, gqt[:, 0:1], ALU.mult, ALU.add, R=["ptb", "gqt"], W=["idx"])
        if DEBUG == "idx":
            cp(DVE, tmpA[:, 0:128], idx[:], R=["idx"], W=["tmpA"])
            store(ckv_p[0, 0:128, 0:128], tmpA[:, 0:128], R=["tmpA"])
            barrier()
            raise _Stop()
        T.op(POOL, lambda: nc.gpsimd.memset(cnx[0:1, :, :], 1.0), W=["cnx"])

        def tr16(dst, src16, ncols, key_r, key_w, E=DVE):
            b = nb()
            for c in range(ncols):
                tr(psb[:, b, c * 16:(c + 1) * 16], src16[0:16, c * 128:(c + 1) * 128], R=[key_r], W=[("pb", b)], sig=(c == ncols - 1))
            cp(E, dst, psb[:, b, 0:ncols * 16].rearrange("p (c b) -> p c b", b=16), R=[("pb", b)], W=[key_w])

        load(xs_f[0:16, :], xs, W=["xs_f"])
        load(xs_bf[0:16, :], xs, W=["xs_bf"], q=POOL)
        tr16(xsT, xs_bf, 8, "xs_bf", "xsT")
        v0, k0 = wslab("in0")
        v1, k1 = wslab("in1")
        bA, bB, bC = nf(), nf(), nf()
        for (ob, o0, o1, vv, kk, c0_, c1_) in ((bA, 0, 480, v0, k0, 0, 480), (bB, 0, 32, v0, k0, 480, 512),
                                               (bB, 32, 288, v1, k1, 0, 256), (bB, 288, 320, v1, k1, 512, 544),
                                               (bC, 0, 256, v1, k1, 256, 512)):
            for c in range(8):
                mm(psf[0:16, ob, o0:o1], xsT[:, c, :], vv[:, c, c0_:c1_], c == 0, c == 7, R=["xsT", kk], W=[("pf", ob)], sig=(c == 7))
        PQ = [("pf", bA), ("pf", bB), ("pf", bC)]
        qd3 = q_s[0:16, :].rearrange("p (h e) -> p h e", e=96)
        for (bk, h0, nh_) in ((bA, 0, 5), (bB, 5, 3)):
            q3 = psf[0:16, bk, 0:nh_ * 96].rearrange("p (h e) -> p h e", e=96)
            cp(ACT, qd3[:, h0:h0 + nh_, 0:64], q3[:, :, 0:64], R=PQ, W=["q_s"])
            tmp3 = tmpA[0:16, 0:nh_ * 64].rearrange("p (h e) -> p h e", e=64)
            rope_rows(16, q3[:, :, 64:96], qd3[:, h0:h0 + nh_, 64:96],
                      ropeCs[0:16, :].rearrange("p (o e) -> p o e", o=1).to_broadcast([16, nh_, 32]),
                      ropeSs[0:16, :].rearrange("p (o e) -> p o e", o=1).to_broadcast([16, nh_, 32]),
                      tmp3, R=PQ + ["ropes"], W=["q_s"], tmpkey="tmpA")
        cp(DVE, tmpB[0:16, 768:1024], psf[0:16, bC, 0:256], R=PQ, W=["tmpB"])
        zc = tmpB[0:16, 768:1024]
        act(tmpB[0:16, 0:256], zc, AF.Square, R=["tmpB"], W=["tmpB"])
        T.op(DVE, lambda: nc.vector.reduce_sum(small[0:16, 2:3], tmpB[0:16, 0:256], axis=AX.X), R=["tmpB"], W=["tmpB"])
        act(small[0:16, 3:4], small[0:16, 2:3], AF.Sqrt, R=["tmpB"], W=[("small", 3)], bias=EPS, scale=1.0 / 256.0)
        T.op(DVE, lambda: nc.vector.reciprocal(small[0:16, 3:4], small[0:16, 3:4]), R=[("small", 3)], W=[("small", 3)])
        stt(DVE, tmpB[0:16, 256:512], zc, small[0:16, 3:4], kvg_b[0:16, :], ALU.mult, ALU.mult,
            R=[("small", 3), "kvg", "tmpB"], W=["tmpB"])
        store(ckv_s[:, :], tmpB[0:16, 256:512], R=["tmpB"], W=["ckv_s_dram"])
        cp(ACT, cnew_bf[0:16, 0:256], tmpB[0:16, 256:512], R=["tmpB"], W=["cnew_bf"])
        kr3 = tmpB[0:16, 512:544].rearrange("p (h e) -> p h e", e=32)
        rope_rows(16, psf[0:16, bB, 288:320].rearrange("p (h e) -> p h e", e=32), kr3,
                  ropeCs[0:16, :].rearrange("p (o e) -> p o e", o=1), ropeSs[0:16, :].rearrange("p (o e) -> p o e", o=1),
                  tmpB[0:16, 576:640].rearrange("p (h e) -> p h e", e=64),
                  R=PQ + ["ropes"], W=["tmpB"], tmpkey="tmpB")
        store(kr_s[:, :], tmpB[0:16, 512:544], R=["tmpB"])
        cp(ACT, cnew_bf[0:16, 256:288], tmpB[0:16, 512:544], R=["tmpB"], W=["cnew_bf"])
        v2, k2 = wslab("in2")
        v3, k3 = wslab("in3")
        ba, bb = nf(), nf()
        for c in range(8):
            mm(psf[0:16, ba, :], xsT[:, c, :], v2[:, c, :], c == 0, c == 7, R=["xsT", k2], W=[("pf", ba)])
            mm(psf[0:16, bb, :], xsT[:, c, :], v3[:, c, :], c == 0, c == 7, R=["xsT", k3], W=[("pf", bb)], sig=(c == 7))
        act(tmpA[0:16, 0:512], psf[0:16, bb, :], AF.Sigmoid, R=[("pf", bb)], W=["tmpA"])
        tt(DVE, u_s[0:16, :], psf[0:16, ba, :], tmpA[0:16, 0:512], ALU.mult, R=[("pf", ba), "tmpA"], W=["u_s"])
        T.dma(SP, st_sems[0], lambda: nc.sync.dma_start(out=conv_s[:, 0:29 * 512], in_=stc[:, 512:30 * 512]))
        store(conv_s[:, 29 * 512:30 * 512], u_s[0:16, :], R=["u_s"])
        acc = misc[0:16, :]
        first = True
        for k0_ in range(0, 30, 4):
            n = min(4, 30 - k0_)
            ext = uTf[0:16, 0:n * 512]
            cwb = bvf[0:16, 0:n * 512]
            load(ext, stc[:, k0_ * 512:(k0_ + n) * 512], W=["ext"])
            load(cwb, conv_w[k0_:k0_ + n, :].rearrange("(o k) c -> o (k c)", o=1).partition_broadcast(16), W=[("bvec", 0), ("bvec", 1)])
            tt(DVE, ext, ext, cwb, ALU.mult, R=["ext", ("bvec", 0), ("bvec", 1)], W=["ext"])
            for jx in range(n):
                if first:
                    cp(DVE, acc, ext[:, jx * 512:(jx + 1) * 512], R=["ext"], W=["acc"])
                    first = False
                else:
                    tt(DVE, acc, acc, ext[:, jx * 512:(jx + 1) * 512], ALU.add, R=["ext", "acc"], W=["acc"])
        load(bvf[0:16, 0:512], conv_w[30:31, :].partition_broadcast(16), W=[("bvec", 0), ("bvec", 1)])
        tt(DVE, tmpA[0:16, 0:512], u_s[0:16, :], bvf[0:16, 0:512], ALU.mult, R=["u_s", ("bvec", 0), ("bvec", 1)], W=["tmpA"])
        tt(DVE, acc, acc, tmpA[0:16, 0:512], ALU.add, R=["tmpA", "acc"], W=["acc"])
        load(bvf[0:16, 0:512], conv_b.partition_broadcast(16), W=[("bvec", 0), ("bvec", 1)])
        tt(DVE, acc, acc, bvf[0:16, 0:512], ALU.add, R=["acc", ("bvec", 0), ("bvec", 1)], W=["acc"])
        T.op(DVE, lambda: nc.vector.bn_stats(stats[0:16, 0, :], acc), R=["acc"], W=[("stats", 0)])
        T.op(DVE, lambda: nc.vector.bn_aggr(mvar[0:16, :], stats[0:16, 0:1, :]), R=[("stats", 0)], W=["mvar"])
        rstd_from_var(small[0:16, 0:1], mvar[0:16, 1:2], R=["mvar"], W=[("small", 0)])
        stt(DVE, small[0:16, 1:2], mvar[0:16, 0:1], -1.0, small[0:16, 0:1], ALU.mult, ALU.mult,
            R=["mvar", ("small", 0)], W=[("small", 1)])
        act(acc, acc, AF.Identity, R=["acc", ("small", 0), ("small", 1)], W=["acc"], bias=small[0:16, 1:2], scale=small[0:16, 0:1])
        load(bvf[0:16, 0:512], conv_g.partition_broadcast(16), W=[("bvec", 0)])
        load(bvf[0:16, 1024:1536], conv_bb.partition_broadcast(16), W=[("bvec", 1)])
        tt(DVE, acc, acc, bvf[0:16, 0:512], ALU.mult, R=["acc", ("bvec", 0)], W=["acc"])
        tt(DVE, acc, acc, bvf[0:16, 1024:1536], ALU.add, R=["acc", ("bvec", 1)], W=["acc"])
        act(mix_s[0:16, 512:1024], acc, AF.Silu, R=["acc"], W=["mix_s"])
        for h in range(8):
            b = nb()
            for lc in range(2):
                tr(psb[0:64, b, lc * 128:(lc + 1) * 128], WukP[:, lc, h, 0:64], R=WUK, W=[("pb", b)], sig=(lc == 1))
            cp(DVE, WukT[0:64, h, :], psb[0:64, b, 0:256], R=[("pb", b)], W=["WukT"])
        b = nb()
        for h in range(8):
            tr(psb[0:96, b, h * 16:(h + 1) * 16], q_s[0:16, h * 96:(h + 1) * 96], R=["q_s"], W=[("pb", b)], sig=(h == 7))
        cp(DVE, qT_s[0:96, :, :], psb[0:96, b, 0:128].rearrange("p (h b) -> p h b", b=16), R=[("pb", b)], W=["qT_s"])
        cp(DVE, qrT[0:32, :, :].rearrange("p b h -> p h b"), qT_s[64:96, :, :], R=["qT_s"], W=["qrT"])
        qb = [nf() for _ in range(4)]
        for h in range(8):
            mm(psf[0:16, qb[h // 2], (h % 2) * 256:(h % 2 + 1) * 256], qT_s[0:64, h, :], WukT[0:64, h, :], True, True,
               R=["qT_s", "WukT"], W=[("pf", qb[h // 2])], sig=True)
        for i in range(4):
            cp(ACT if i % 2 else DVE, ql_bf[0:16, i * 512:(i + 1) * 512], psf[0:16, qb[i], :], R=[("pf", qb[i])], W=["ql_bf"])
        if DEBUG == "ql":
            cp(DVE, tmpA[0:16, 0:1024], ql_bf[0:16, 1024:2048], R=["ql_bf"], W=["tmpA"])
            store(y_s[:, :], tmpA[0:16, 0:1024], R=["tmpA"])
            barrier()
            raise _Stop()
        b = nb()
        for lc in range(2):
            for h in range(8):
                o = (lc * 8 + h) * 16
                tr(psb[:, b, o:o + 16], ql_bf[0:16, h * 256 + lc * 128:h * 256 + (lc + 1) * 128], R=["ql_bf"], W=[("pb", b)],
                   sig=(lc == 1 and h == 7))
        cp(DVE, qlT[:, :, :, :].rearrange("p l b h -> p l h b"), psb[:, b, 0:256].rearrange("p (l h b) -> p l h b", h=8, b=16),
           R=[("pb", b)], W=["qlT"])
        tr16(cnT, cnew_bf, 2, "cnew_bf", "cnT")
        b = nb()
        tr(psb[0:32, b, 0:16], cnew_bf[0:16, 256:288], R=["cnew_bf"], W=[("pb", b)], sig=True)
        cp(DVE, krnT[0:32, :], psb[0:32, b, 0:16], R=[("pb", b)], W=["krnT"])
        load(cnx[0:1, :, 0:256], ckv_s.rearrange("(o b) l -> o b l", o=1), R=["ckv_s_dram", "cnx"], W=["cnxl"], q=POOL)
        gi = [0]
        for bsm in range(NS):
            pS = nh()
            pO = nh()
            for pg in range(8):
                sl = gi[0] % 2
                gi[0] += 1
                col = bsm * 8 + pg
                cp(DVE, idxc[sl][:, 0:1], idx[:, col:col + 1], R=["idx"], W=[("idxc", sl)])
                T.dma(POOL, g_sems[sl], lambda sl=sl, col=col: nc.gpsimd.indirect_dma_start(
                    out=cpg[sl][:, :, :].rearrange("p r l -> p (r l)"), out_offset=None, in_=pool_c,
                    in_offset=bass.IndirectOffsetOnAxis(ap=idxc[sl][:, 0:1], axis=0)), R=[("idxc", sl)], W=[("cpg", sl)])
                T.dma(POOL, g_sems[2 + sl], lambda sl=sl, col=col: nc.gpsimd.indirect_dma_start(
                    out=rpg[sl][:, :, :].rearrange("p r l -> p (r l)"), out_offset=None, in_=pool_r,
                    in_offset=bass.IndirectOffsetOnAxis(ap=idxc[sl][:, 0:1], axis=0)), R=[("idxc", sl)], W=[("rpg", sl)])
                if DEBUG == "cpg":
                    for ii, rr in enumerate((0, 7)):
                        cp(DVE, tmpA[:, ii * 256:(ii + 1) * 256], cpg[sl][:, rr, 0:256], R=[("cpg", sl)], W=["tmpA"])
                        store(ckv_p[0, ii * 128:(ii + 1) * 128, :], tmpA[:, ii * 256:(ii + 1) * 256], R=["tmpA"])
                    cp(DVE, tmpA[:, 512:768], rpg[sl][:, :, :].rearrange("p r l -> p (r l)"), R=[("rpg", sl)], W=["tmpA"])
                    store(ckv_p[0, 256:384, :], tmpA[:, 512:768], R=["tmpA"])
                    barrier()
                    raise _Stop()
                for lc in range(2):
                    b = nb()
                    for r in range(8):
                        tr(psb[:, b, r * 128:(r + 1) * 128], cpg[sl][:, r, lc * 128:(lc + 1) * 128], R=[("cpg", sl)], W=[("pb", b)], sig=(r == 7))
                    cp(ACT if lc else DVE, cT[sl][:, lc, :], psb[:, b, :], R=[("pb", b)], W=[("cT", sl)])
                b = nb()
                for r in range(8):
                    tr(psb[0:32, b, r * 128:(r + 1) * 128], rpg[sl][:, r, :], R=[("rpg", sl)], W=[("pb", b)], sig=(r == 7))
                cp(DVE, rT[sl][0:32, :], psb[0:32, b, :], R=[("pb", b)], W=[("rT", sl)])
                for r in range(8):
                    kt = pg * 8 + r
                    o_ = psf[:, pS, kt * 8:(kt + 1) * 8]
                    RR = [("cT", sl), ("rT", sl), "qlT", "qrT"]
                    mm(o_, cT[sl][:, 0, r * 128:(r + 1) * 128], qlT[:, 0, bsm, :], True, False, R=RR, W=[("pf", pS)])
                    mm(o_, cT[sl][:, 1, r * 128:(r + 1) * 128], qlT[:, 1, bsm, :], False, False, R=RR, W=[("pf", pS)])
                    mm(o_, rT[sl][0:32, r * 128:(r + 1) * 128], qrT[0:32, bsm, :], False, True, R=RR, W=[("pf", pS)], sig=(r == 7))
                act(Pp[:, pg * 64:(pg + 1) * 64], psf[:, pS, pg * 64:(pg + 1) * 64], AF.Exp, R=[("pf", pS)], W=["Pp"], scale=MLA_SCALE)
                for r in range(8):
                    kt = pg * 8 + r
                    mm(psf[0:8, pO, 0:256], Pp[:, kt * 8:(kt + 1) * 8], cpg[sl][:, r, :], kt == 0, False,
                       R=["Pp", ("cpg", sl)], W=[("pf", pO)], sig=(r == 7))
            pN = nf()
            mm(psf[0:1, pN, 0:8], cnT[:, 0, bsm:bsm + 1], qlT[:, 0, bsm, :], True, False, R=["cnT", "qlT"], W=[("pf", pN)])
            mm(psf[0:1, pN, 0:8], cnT[:, 1, bsm:bsm + 1], qlT[:, 1, bsm, :], False, False, R=["cnT", "qlT"], W=[("pf", pN)])
            mm(psf[0:1, pN, 0:8], krnT[0:32, bsm:bsm + 1], qrT[0:32, bsm, :], False, True, R=["krnT", "qrT"], W=[("pf", pN)], sig=True)
            act(pnew[0:1, :], psf[0:1, pN, 0:8], AF.Exp, R=[("pf", pN)], W=["pnew"], scale=MLA_SCALE)
            mm(psf[0:8, pO, 0:256], pnew[0:1, :], cnx[0:1, bsm, 0:256], False, True, R=["pnew", "cnx", "cnxl"], W=[("pf", pO)], sig=True)
            T.op(DVE, lambda: nc.vector.reduce_sum(small[:, 40:48], Pp[:, :].rearrange("p (k h) -> p h k", h=8), axis=AX.X),
                 R=["Pp"], W=[("small", 40)])
            cp(DVE, dhl[:, 0:8], small[:, 40:48], R=[("small", 40)], W=["dhl"])
            tt(DVE, small[:, 48:56], small[:, 40:48], dhl[:, 0:8], ALU.subtract, R=[("small", 40), "dhl"], W=[("small", 48)])
            cp(DVE, dhl[:, 8:16], small[:, 48:56], R=[("small", 48)], W=["dhl"])
            pD = nf()
            mm(psf[0:8, pD, 0:1], dhl[:, 0:8], ones[:, 0:1], True, False, R=["dhl", "ones"], W=[("pf", pD)])
            mm(psf[0:8, pD, 0:1], dhl[:, 8:16], ones[:, 0:1], False, False, R=["dhl", "ones"], W=[("pf", pD)])
            mm(psf[0:8, pD, 0:1], pnew[0:1, :], ones[0:1, 0:1], False, True, R=["pnew", "ones"], W=[("pf", pD)], sig=True)
            if DEBUG == "po" and bsm == 0:
                cp(DVE, tmpA[0:8, 0:256], psf[0:8, pO, 0:256], R=[("pf", pO)], W=["tmpA"])
                store(y_s[0:8, 0:256], tmpA[0:8, 0:256], R=["tmpA"])
                cp(DVE, tmpA[:, 512:1024], Pp[:, :], R=["Pp"], W=["tmpA"])
                store(ckv_p[0, 0:128, :], tmpA[:, 512:768], R=["tmpA"])
                store(ckv_p[0, 128:256, :], tmpA[:, 768:1024], R=["tmpA"])
                barrier()
                raise _Stop()
            T.op(DVE, lambda pD=pD: nc.vector.reciprocal(small[0:8, 8:9], psf[0:8, pD, 0:1]), R=[("pf", pD)], W=[("small", 8)])
            ts(DVE, ol_bf[0:8, :], psf[0:8, pO, 0:256], small[0:8, 8:9], None, ALU.mult, None, R=[("pf", pO), ("small", 8)], W=["ol_bf"])
            b = nb()
            for lc in range(2):
                tr(psb[:, b, lc * 8:(lc + 1) * 8], ol_bf[0:8, lc * 128:(lc + 1) * 128], R=["ol_bf"], W=[("pb", b)], sig=(lc == 1))
            cp(DVE, olT[:, :, :, bsm], psb[:, b, 0:16].rearrange("p (l h) -> p l h", h=8), R=[("pb", b)], W=["olT"])
        bk = nf()
        for h in range(8):
            for lc in range(2):
                mm(psf[0:16, bk, h * 64:(h + 1) * 64], olT[:, lc, h, :], Wuv[:, lc, h * 64:(h + 1) * 64], lc == 0, lc == 1,
                   R=["olT"] + WUV, W=[("pf", bk)], sig=(h == 7 and lc == 1))
        cp(DVE, mix_s[0:16, 0:512], psf[0:16, bk, :], R=[("pf", bk)], W=["mix_s"])

        def dbg(name, ap, key):
            if DEBUG == name:
                n = ap.shape[1]
                cp(DVE, tmpA[0:ap.shape[0], 0:n], ap, R=[key], W=["tmpA"])
                store(y_s[0:ap.shape[0], 0:n], tmpA[0:ap.shape[0], 0:n], R=["tmpA"])
                barrier()
                raise _Stop()

        dbg("mix_s", mix_s[0:16, :], "mix_s")

        def post16(inT, inkey, wname, resid, gi_, bi_, outT, outkey):
            views = [wslab(wname + "0"), wslab(wname + "1")]
            b2 = nf2()
            for hf in range(2):
                wv, wk = views[hf]
                for c in range(8):
                    mm(psf[0:16, b2 + hf, :], inT[:, c, :], wv[:, c, :], c == 0, c == 7, R=[inkey, wk], W=[("pf", b2 + hf)], sig=(c == 7))
            for hf in range(2):
                stt(DVE, r_s[0:16, hf * 512:(hf + 1) * 512], resid[0:16, hf * 512:(hf + 1) * 512], ALPHA, psf[0:16, b2 + hf, :],
                    ALU.mult, ALU.add, R=[("pf", b2 + hf), "r_s", "xs_f"], W=["r_s"])
            bvload(gi_, 0)
            bvload(bi_, 1)
            layer_norm_rows("r_s", 16, r_s[0:16, :], r_s[0:16, :], gi_, bi_, xb16[0:16, :] if outT is not None else None, "xb16")
            if outT is not None:
                tr16(outT, xb16, 8, "xb16", outkey)

        cp(ACT, xb16[0:16, :], mix_s[0:16, :], R=["mix_s"], W=["xb16"])
        tr16(mixT_s, xb16, 8, "xb16", "mixT_s")
        x1T_s = cd(128).rearrange("p (c b) -> p c b", b=16)
        x2T_s = cd(128).rearrange("p (c b) -> p c b", b=16)
        post16(mixT_s, "mixT_s", "o", xs_f, 0, 1, x1T_s, "x1T_s")
        dbg("x1", r_s[0:16, :], "r_s")
        vq = [wslab("xq0"), wslab("xq1")]
        b2 = nf2()
        for hf in range(2):
            wv, wk = vq[hf]
            for c in range(8):
                mm(psf[0:16, b2 + hf, :], x1T_s[:, c, :], wv[:, c, :], c == 0, c == 7, R=["x1T_s", wk], W=[("pf", b2 + hf)], sig=(c == 7))
        for hf in range(2):
            cp(DVE, q2_f[0:16, hf * 512:(hf + 1) * 512], psf[0:16, b2 + hf, :], R=[("pf", b2 + hf)], W=["q2_f"])
        store(q2_scr[:, :], q2_f[0:16, :], R=["q2_f"], W=["q2_scr"])
        mkb = uTf[:, 0:2048].rearrange("p (a b) -> p a b", b=1024)
        for bsm in range(NS):
            load(mkb, cmk[bsm].rearrange("(mt p) e -> p mt e", p=128), W=["mkb"])
            load(mvb[:, :, :], cmv[bsm].rearrange("(mt p) e -> p mt e", p=128), W=[("mvb", 0), ("mvb", 1)], q=POOL)
            load(xring[:, 0, :], q2_scr[bsm:bsm + 1, :].partition_broadcast(128), R=["q2_scr"], W=[("xring", 0)])
            for mt in range(2):
                tt(DVE, mkb[:, mt, :], mkb[:, mt, :], xring[:, 0, :], ALU.mult, R=["mkb", ("xring", 0)], W=["mkb"])
            T.op(DVE, lambda: nc.vector.reduce_sum(small[:, 16:24], mkb[:, :, :].rearrange("p a (h d) -> p (a h) d", d=256), axis=AX.X),
                 R=["mkb"], W=[("small", 16)])
            act(P2bf[:, :], small[:, 16:24], AF.Exp, R=[("small", 16)], W=["P2bf"], scale=MEM_SCALE)
            bo = nf()
            for h in range(4):
                for dc in range(2):
                    for mt in range(2):
                        mm(psf[:, bo, (h * 2 + dc):(h * 2 + dc) + 1], mvb[:, mt, h * 256 + dc * 128:h * 256 + (dc + 1) * 128],
                           P2bf[:, mt * 4 + h:mt * 4 + h + 1], mt == 0, mt == 1, R=[("mvb", 0), ("mvb", 1), "P2bf"], W=[("pf", bo)])
            mm(psf[:, bo, 16:24], ones[:, :], P2bf[:, :], True, True, R=["ones", "P2bf"], W=[("pf", bo)], sig=True)
            cp(DVE, small[:, 28:36], psf[:, bo, 16:24], R=[("pf", bo)], W=[("small", 28)])
            tt(DVE, small[:, 24:28], small[:, 28:32], small[:, 32:36], ALU.add, R=[("small", 28)], W=[("small", 24)])
            T.op(DVE, lambda: nc.vector.reciprocal(small[:, 24:28], small[:, 24:28]), R=[("small", 24)], W=[("small", 24)])
            tt(DVE, o2T_s[:, :, bsm].rearrange("p (h d) -> p h d", d=2), psf[:, bo, 0:8].rearrange("p (h d) -> p h d", d=2),
               small[:, 24:28].rearrange("p (h o) -> p h o", o=1).to_broadcast([128, 4, 2]), ALU.mult,
               R=[("pf", bo), ("small", 24)], W=["o2T_s"])
        post16(o2T_s, "o2T_s", "xo", r_s, 2, 3, x2T_s, "x2T_s")
        dbg("x2", r_s[0:16, :], "r_s")
        pH = nh()
        for g in range(8):
            wv, wk = wslab(f"up{g}")
            for fc in range(4):
                o = (g * 4 + fc) * 16
                for c in range(8):
                    mm(psf[:, pH, o:o + 16], wv[:, c, fc * 128:(fc + 1) * 128], x2T_s[:, c, :], c == 0, c == 7,
                       R=["x2T_s", wk], W=[("pf", pH)], sig=(c == 7 and fc == 3))
        act(tmpA[:, 0:512], psf[:, pH, :], AF.Relu, R=[("pf", pH)], W=["tmpA"])
        tt(DVE, hT_s[:, :], tmpA[:, 0:512], psf[:, pH, :], ALU.mult, R=[("pf", pH), "tmpA"], W=["hT_s"])
        b2 = nf2()
        for qd in range(4):
            for fh in range(2):
                wv, wk = wslab(f"dn{fh}{qd}")
                for fc in range(16):
                    o = (fh * 16 + fc) * 16
                    mm(psf[0:16, b2 + qd // 2, (qd % 2) * 256:(qd % 2 + 1) * 256], hT_s[:, o:o + 16], wv[:, fc, :],
                       fh == 0 and fc == 0, fh == 1 and fc == 15, R=["hT_s", wk], W=[("pf", b2 + qd // 2)], sig=(fc == 15))
        for hf in range(2):
            stt(DVE, r_s[0:16, hf * 512:(hf + 1) * 512], r_s[0:16, hf * 512:(hf + 1) * 512], ALPHA, psf[0:16, b2 + hf, :],
                ALU.mult, ALU.add, R=[("pf", b2 + hf), "r_s"], W=["r_s"])
        bvload(4, 0)
        bvload(5, 1)
        layer_norm_rows("r_s", 16, r_s[0:16, :], r_s[0:16, :], 4, 5, None, None)
        store(y_s[:, :], r_s[0:16, :], R=["r_s"])
        barrier()

    try:
        chk(0)
        if do_sample:
            sample_phase()
        chk(0.1)
        if do_prompt:
            for s in range(nseq):
                mem_kv_seq(s)
                chk(1)
                for j in range(nblk):
                    prompt_block(s, j)
    except _Stop:
        pass

    for ds in st_sems + ld_sems + wr_sems + cvt_sems + g_sems:
        if ds.count:
            nc.sync.wait_ge(ds.sem, ds.count * 16)
    for E in (PE, ACT, DVE, POOL):
        if E.count:
            nc.sync.wait_ge(E.sem, E.count)
    return nc


def _consts():
    half = 16
    inv_freq = np.exp(-math.log(10000.0) * np.arange(half, dtype=np.float32) / half).astype(np.float32)
    pos = np.arange(SEQ, dtype=np.float32)
    ang = pos[:, None] * inv_freq[None, :]
    cos, sin = np.cos(ang).astype(np.float32), np.sin(ang).astype(np.float32)
    C2 = np.concatenate([cos, cos], axis=1)
    S2 = np.concatenate([-sin, sin], axis=1)
    ropeC = C2.reshape(16, 128, 32).transpose(1, 0, 2).reshape(128, 512)
    ropeS = S2.reshape(16, 128, 32).transpose(1, 0, 2).reshape(128, 512)
    angs = np.float32(8192.0) * inv_freq
    cs, sn = np.cos(angs).astype(np.float32), np.sin(angs).astype(np.float32)
    ropeCs = np.tile(np.concatenate([cs, cs])[None, :], (128, 1)).astype(np.float32)
    ropeSs = np.tile(np.concatenate([-sn, sn])[None, :], (128, 1)).astype(np.float32)
    ident = np.eye(128, dtype=np.float32).astype(ml_dtypes.bfloat16)
    k = np.arange(128)
    mask = (k[None, :] >= k[:, None]).astype(np.float32).astype(ml_dtypes.bfloat16)
    injk = np.zeros((32, 96), np.float32)
    injk[np.arange(32), 64 + np.arange(32)] = 1.0
    injk = injk.astype(ml_dtypes.bfloat16)
    sel = np.zeros((16, 16, 128), np.float32)
    for b in range(16):
        sel[b, b, :] = 1.0
    gq = (np.arange(128) % 16).astype(np.int32).reshape(128, 1)
    return dict(c_ident=ident, c_mask=mask, c_ropeC=np.ascontiguousarray(ropeC), c_ropeS=np.ascontiguousarray(ropeS),
                c_ropeCs=ropeCs, c_ropeSs=ropeSs, c_injk=injk, c_sel=sel.reshape(16, 16 * 128), gq=gq)


def make_in_maps(inp, cores=range(NCORE), pool_c=None, pool_r=None, page_table=None):
    consts = _consts()
    if pool_c is None:
        pool_c = np.ascontiguousarray(inp["cache_ckv"][0]).reshape(-1, 8 * 256)
        pool_r = np.ascontiguousarray(inp["cache_krope"][0]).reshape(-1, 8 * 32)
        page_table = np.asarray(inp["page_table"])
    shared = dict(
        pool_c=pool_c, pool_r=pool_r,
        w_in=np.ascontiguousarray(inp["w_in"][0]), kvg=np.ascontiguousarray(inp["kv_norm_g"]),
        w_uk=np.ascontiguousarray(inp["w_uk"][0]).reshape(256, 512), w_uv=np.ascontiguousarray(inp["w_uv"][0]).reshape(256, 512),
        conv_w=np.ascontiguousarray(inp["conv_w"][0]), conv_b=np.ascontiguousarray(inp["conv_b"]),
        conv_g=np.ascontiguousarray(inp["conv_ln_g"]), conv_bb=np.ascontiguousarray(inp["conv_ln_b"]),
        w_o=np.ascontiguousarray(inp["w_o"][0]), w_xq=np.ascontiguousarray(inp["w_xq"][0]),
        w_xk=np.ascontiguousarray(inp["w_xk"][0]), w_xv=np.ascontiguousarray(inp["w_xv"][0]),
        w_xo=np.ascontiguousarray(inp["w_xo"][0]), w_up=np.ascontiguousarray(inp["w_up"][0]),
        w_down=np.ascontiguousarray(inp["w_down"][0]),
        ln1_g=np.ascontiguousarray(inp["ln1_g"]), ln1_b=np.ascontiguousarray(inp["ln1_b"]),
        ln2_g=np.ascontiguousarray(inp["ln2_g"]), ln2_b=np.ascontiguousarray(inp["ln2_b"]),
        ln3_g=np.ascontiguousarray(inp["ln3_g"]), ln3_b=np.ascontiguousarray(inp["ln3_b"]),
        **consts,
    )
    maps = []
    for c in cores:
        pt = page_table[c * 16:(c + 1) * 16]
        ptr = pt.reshape(16, 8, 8)[:, :, np.arange(128) // 16]
        ptr = np.ascontiguousarray(ptr.transpose(2, 0, 1).reshape(128, 128)).astype(np.int32)
        m = dict(shared)
        m.update(
            xp=np.ascontiguousarray(inp["x_prompt"][2 * c:2 * c + 2]),
            memp=np.ascontiguousarray(inp["mem_prompt"][2 * c:2 * c + 2]),
            xs=np.ascontiguousarray(inp["x_sample"][16 * c:16 * c + 16, 0]),
            ptrep=ptr,
            stc=np.ascontiguousarray(inp["state_conv"][0, 16 * c:16 * c + 16]).reshape(16, 30 * 512),
            cmk=np.ascontiguousarray(inp["cache_mem_k"][0, 16 * c:16 * c + 16]).reshape(16, 256, 1024),
            cmv=np.ascontiguousarray(inp["cache_mem_v"][0, 16 * c:16 * c + 16]).reshape(16, 256, 1024),
        )
        maps.append(m)
    return maps


def assemble(results):
    cat = lambda k: np.concatenate([r[k] for r in results], axis=0)
    y_p = cat("y_p")
    y_s = cat("y_s")[:, None, :]
    return (y_p, y_s, cat("ckv_p")[None], cat("kr_p")[None], cat("conv_p")[None],
            cat("mk_p").reshape(1, -1, 256, 4, 256), cat("mv_p").reshape(1, -1, 256, 4, 256),
            cat("ckv_s")[None, :, None, :], cat("kr_s")[None, :, None, :], cat("conv_s").reshape(1, -1, 30, 512))


def kernel(**inputs):
    inp = {k: np.asarray(v) for k, v in inputs.items()}
    nc = build()
    maps = make_in_maps(inp)
    res = run_bass_kernel_spmd(nc, maps, core_ids=list(range(NCORE)))
    outs = assemble(res.results)
    return tuple(np.ascontiguousarray(o, dtype=np.float32) for o in outs)
```

```python
import math
import numpy as np
import ml_dtypes
import concourse.bass as bass
import concourse.mybir as mybir
from concourse.bass_utils import run_bass_kernel_spmd

F32 = mybir.dt.float32
BF16 = mybir.dt.bfloat16
I32 = mybir.dt.int32
AF = mybir.ActivationFunctionType
ALU = mybir.AluOpType
AX = mybir.AxisListType

ALPHA = 2.0 ** 0.25
MLA_SCALE = 96.0 ** -0.5
MEM_SCALE = 1.0 / 16.0
EPS = 1e-5
NCORE = 8
SEQ = 2048
NBLK = 4
NPOOL = 10240
DEBUG = False
PROFILE = False


class Eng:
    def __init__(self, name, eng, sem):
        self.name, self.eng, self.sem, self.count, self.seen = name, eng, sem, 0, {}


class DSem:
    def __init__(self, sem):
        self.sem, self.count = sem, 0


_TINY = ("small", "stats", "mvar", "P2bf", "dhl", "pnew", "idx", "idxc", "ptb", "gqt", "onesf", "ol_bf", "krnT", "cnT")


def _tiny(k):
    n = k[0] if isinstance(k, tuple) else k
    return n in _TINY


class Trk:
    def __init__(self, nc):
        self.nc = nc
        self.last_w = {}
        self.readers = {}
        self.nsem = 0
        self.stage = None

    def sem(self, name):
        self.nsem += 1
        return self.nc.alloc_semaphore(name)

    def dsem(self, name):
        return DSem(self.sem(name))

    def _wait(self, E, R, W):
        best = {}
        def add(tk, raw=False):
            obj, val = tk
            if obj is E and not (raw and E.name != "pe" and E.name != "sp"):
                return
            if val > best.get(obj, 0):
                best[obj] = val
        for k in R:
            if k in self.last_w:
                add(self.last_w[k], raw=_tiny(k))
            if isinstance(k, tuple) and k[0] in ("pf", "pb"):
                for obj, val in self.readers.get(k, {}).items():
                    add((obj, val))
        for k in W:
            if k in self.last_w:
                add(self.last_w[k])
            for obj, val in self.readers.get(k, {}).items():
                add((obj, val))
        for obj, val in best.items():
            if E.seen.get(obj, 0) >= val:
                continue
            assert obj.count >= val, "wait on a signal that is not issued yet"
            E.eng.wait_ge(obj.sem, val * (16 if isinstance(obj, DSem) else 1))
            E.seen[obj] = val

    def _commit(self, tk, R, W):
        obj, val = tk
        for k in R:
            d = self.readers.setdefault(k, {})
            if d.get(obj, 0) < val:
                d[obj] = val
        for k in W:
            self.last_w[k] = tk
            self.readers[k] = {}

    def retire(self, old_keys, new_keys):
        merged = {}
        for k in old_keys:
            if k in self.last_w:
                o, v = self.last_w[k]
                merged[o] = max(merged.get(o, 0), v)
            for o, v in self.readers.get(k, {}).items():
                merged[o] = max(merged.get(o, 0), v)
        for nk in new_keys:
            d = self.readers.setdefault(nk, {})
            for o, v in merged.items():
                d[o] = max(d.get(o, 0), v)

    def op(self, E, fn, R=(), W=(), sig=True):
        self._wait(E, R, W)
        ins = fn()
        if self.stage is not None:
            try:
                ins.annotate(self.stage)
            except Exception:
                pass
        if sig:
            E.count += 1
            ins.then_inc(E.sem, 1)
            tk = (E, E.count)
        else:
            tk = (E, E.count + 1)
        self._commit(tk, R, W)

    def dma(self, Q, ds, fn, R=(), W=()):
        self._wait(Q, R, W)
        ins = fn()
        ds.count += 1
        ins.then_inc(ds.sem, 16)
        self._commit((ds, ds.count), R, W)


class _Stop(Exception):
    pass


def build(n_pool=NPOOL, do_prompt=True, do_sample=True, nseq=2, nblk=NBLK, stop=None):
    nc = bass.Bass("TRN2", target_bir_lowering=False)

    def chk(n):
        if stop is not None and n >= stop:
            raise _Stop()

    T = Trk(nc)
    _dummy = [T.sem(f"dummy{i}") for i in range(6)]
    PE = Eng("pe", nc.tensor, T.sem("s_pe"))
    ACT = Eng("act", nc.scalar, T.sem("s_act"))
    DVE = Eng("dve", nc.vector, T.sem("s_dve"))
    POOL = Eng("pool", nc.gpsimd, T.sem("s_pool"))
    SP = Eng("sp", nc.sync, T.sem("s_sp"))

    def din(name, shape, dt=F32):
        return nc.dram_tensor(name, list(shape), dt, kind="ExternalInput").ap()

    def dout(name, shape, dt=F32):
        return nc.dram_tensor(name, list(shape), dt, kind="ExternalOutput").ap()

    def dscr(name, shape, dt=BF16):
        return nc.dram_tensor(name, list(shape), dt, kind="Internal").ap()

    xp = din("xp", [2, SEQ, 1024])
    memp = din("memp", [2, 256, 1024])
    xs = din("xs", [16, 1024])
    pool_c = din("pool_c", [n_pool * 16, 2048])
    pool_r = din("pool_r", [n_pool * 16, 256])
    ptrep = din("ptrep", [128, 128], I32)
    gq = din("gq", [128, 1], I32)
    stc = din("stc", [16, 30 * 512])
    cmk = din("cmk", [16, 256, 1024])
    cmv = din("cmv", [16, 256, 1024])
    w_in = din("w_in", [1024, 2080])
    kvg = din("kvg", [1, 256])
    w_uk = din("w_uk", [256, 512])
    w_uv = din("w_uv", [256, 512])
    conv_w = din("conv_w", [31, 512])
    conv_b = din("conv_b", [1, 512])
    conv_g = din("conv_g", [1, 512])
    conv_bb = din("conv_bb", [1, 512])
    w_o = din("w_o", [1024, 1024])
    w_xq = din("w_xq", [1024, 1024])
    w_xk = din("w_xk", [1024, 1024])
    w_xv = din("w_xv", [1024, 1024])
    w_xo = din("w_xo", [1024, 1024])
    w_up = din("w_up", [1024, 4096])
    w_down = din("w_down", [4096, 1024])
    lnv = [din(n, [1, 1024]) for n in ("ln1_g", "ln1_b", "ln2_g", "ln2_b", "ln3_g", "ln3_b")]
    c_ident = din("c_ident", [128, 128], BF16)
    c_mask = din("c_mask", [128, 128], BF16)
    c_ropeC = din("c_ropeC", [128, 16 * 32])
    c_ropeS = din("c_ropeS", [128, 16 * 32])
    c_ropeCs = din("c_ropeCs", [128, 32])
    c_ropeSs = din("c_ropeSs", [128, 32])
    c_injk = din("c_injk", [32, 96], BF16)
    c_sel = din("c_sel", [16, 16 * 128])

    y_p = dout("y_p", [2, SEQ, 1024])
    y_s = dout("y_s", [16, 1024])
    ckv_p = dout("ckv_p", [2, SEQ, 256])
    kr_p = dout("kr_p", [2, SEQ, 32])
    conv_p = dout("conv_p", [2, 30, 512])
    mk_p = dout("mk_p", [2, 256, 1024])
    mv_p = dout("mv_p", [2, 256, 1024])
    ckv_s = dout("ckv_s", [16, 256])
    kr_s = dout("kr_s", [16, 32])
    conv_s = dout("conv_s", [16, 30 * 512])

    slabs = {}
    cvt_sems = [T.dsem(f"cv{i}") for i in range(4)]
    cvt_i = [0]

    def mk_slab(name, src3):
        a, b = src3.shape[1], src3.shape[2]
        scr = dscr("ws_" + name, [128, a * b])
        ds = cvt_sems[cvt_i[0] % 4]
        cvt_i[0] += 1
        T.dma(POOL, ds, lambda: nc.gpsimd.dma_start(out=scr.rearrange("p (a b) -> p a b", b=b), in_=src3),
              W=[("ws", name)])
        slabs[name] = (scr, a, b)

    def cols(w, c0, c1):
        return w[:, c0:c1].rearrange("(c p) n -> p c n", p=128)

    mk_slab("in0", cols(w_in, 0, 512))
    mk_slab("in1", cols(w_in, 512, 1056))
    mk_slab("in2", cols(w_in, 1056, 1568))
    mk_slab("in3", cols(w_in, 1568, 2080))
    rest_done = [False]

    def mk_rest_slabs():
        if rest_done[0]:
            return
        rest_done[0] = True
        for nm, w in (("o", w_o), ("xq", w_xq), ("xo", w_xo)):
            mk_slab(nm + "0", cols(w, 0, 512))
            mk_slab(nm + "1", cols(w, 512, 1024))
        for g in range(8):
            mk_slab(f"up{g}", cols(w_up, g * 512, (g + 1) * 512))
        for fh in range(2):
            for qd in range(4):
                mk_slab(f"dn{fh}{qd}", w_down[fh * 2048:(fh + 1) * 2048, qd * 256:(qd + 1) * 256]
                        .rearrange("(f p) n -> p f n", p=128))
        for nm, w in (("xk", w_xk), ("xv", w_xv)):
            mk_slab(nm + "0", cols(w, 0, 512))
            mk_slab(nm + "1", cols(w, 512, 1024))

    def sb(name, shape, dt):
        return nc.alloc_sbuf_tensor(name, list(shape), dt)

    ident = sb("ident", [128, 128], BF16)
    mask = sb("mask", [128, 128], BF16)
    ones = sb("ones", [128, 128], BF16)
    injk = sb("injk", [32, 96], BF16)
    ropeC = sb("ropeC", [128, 16, 32], F32)
    ropeS = sb("ropeS", [128, 16, 32], F32)
    ropeCs = sb("ropeCs", [128, 32], F32)
    ropeSs = sb("ropeSs", [128, 32], F32)
    kvg_b = sb("kvg_b", [128, 256], F32)
    WukP = sb("WukP", [128, 2, 8, 96], BF16)
    Wuv = sb("Wuv", [128, 2, 512], BF16)
    cwT = sb("cwT", [128, 4, 31], F32)
    cvec = sb("cvec", [128, 3, 4], F32)
    bvec = sb("bvec", [128, 2, 1024], F32)
    WR_N = 3
    WR = sb("WR", [128, WR_N, 4352], BF16)
    KT = sb("KT", [128, 8, SEQ], BF16)
    Vaug = sb("Vaug", [128, 16, 4, 192], BF16)
    mkT = sb("mkT", [128, 8, 256], BF16)
    mvb = sb("mvb", [128, 2, 1024], BF16)
    Rb = sb("Rb", [128, 4, 1024], F32)
    xring = sb("xring", [128, 1, 1024], F32)
    Cb = sb("Cb", [128, 2, 1024], BF16)
    Db = sb("Db", [128, 8, 512], BF16)
    uT = sb("uT", [128, 4, 30 + 512], F32)
    uTb = sb("uTb", [128, 4, 30 + 512], BF16)
    Pr = sb("Pr", [128, 4, 512], BF16)
    QT = sb("QT", [128, 8, 512], BF16)
    mixT = sb("mixT", [128, 8, 512], BF16)
    hT = sb("hT", [128, 16, 512], BF16)
    qtm = hT[:, 0:6, :].rearrange("p a b -> p (a b)").rearrange("p (t e) -> p t e", e=768)
    ckb = sb("ckb", [128, 4, 288], BF16)
    ckT = sb("ckT", [128, 2, 512], BF16)
    krT = sb("krT", [32, 512], BF16)
    tmpA = sb("tmpA", [128, 1024], F32)
    tmpB = sb("tmpB", [128, 1024], F32)
    cacc = Rb[:, :, 0:512]
    csq = Pr
    cyb = QT
    small = sb("small", [128, 64], F32)
    stats = sb("stats", [128, 2, 6], F32)
    mvar = sb("mvar", [128, 2], F32)

    psf = nc.alloc_psum_tensor("psf", [128, 6, 512], F32)
    psb = nc.alloc_psum_tensor("psb", [128, 2, 1024], BF16)
    pf_i = [0]
    pb_i = [0]

    def nf():
        i = pf_i[0] % 4
        pf_i[0] += 1
        return i

    def nf2():
        while pf_i[0] % 2:
            pf_i[0] += 1
        i = pf_i[0] % 4
        pf_i[0] += 2
        return i

    ph_i = [0]

    def nh():
        i = 4 + ph_i[0] % 2
        ph_i[0] += 1
        return i

    def nb():
        i = pb_i[0] % 2
        pb_i[0] += 1
        return i

    ld_sems = [T.dsem(f"ld{i}") for i in range(8)]
    ld_i = [0]
    st_sems = [T.dsem(f"st{i}") for i in range(8)]
    st_i = [0]

    def load(out, in_, W, R=(), q=None):
        ds = ld_sems[ld_i[0] % 8]
        ld_i[0] += 1
        if q is POOL:
            T.dma(POOL, ds, lambda: nc.gpsimd.dma_start(out=out, in_=in_), R=R, W=W)
        else:
            T.dma(SP, ds, lambda: nc.sync.dma_start(out=out, in_=in_), R=R, W=W)

    def store(out, in_, R, W=()):
        ds = st_sems[st_i[0] % 8]
        st_i[0] += 1
        T.dma(SP, ds, lambda: nc.sync.dma_start(out=out, in_=in_), R=R, W=W)

    wr_sems = [T.dsem(f"wr{i}") for i in range(WR_N)]
    wr_i = [0]

    def wslab(name):
        scr, a, b = slabs[name]
        i = wr_i[0] % WR_N
        wr_i[0] += 1
        view = WR[:, i, 0:a * b].rearrange("p (a b) -> p a b", b=b)
        T.dma(SP, wr_sems[i], lambda: nc.sync.dma_start(out=view, in_=scr.rearrange("p (a b) -> p a b", b=b)),
              R=[("ws", name)], W=[("WR", i)])
        return view, ("WR", i)

    def mm(out, lhsT, rhs, start, stop, R, W, sig=False):
        T.op(PE, lambda: nc.tensor.matmul(out, lhsT, rhs, start=start, stop=stop), R=R, W=W, sig=sig)

    def tr(out, in_, R, W, sig=False):
        k = in_.shape[0]
        T.op(PE, lambda: nc.tensor.transpose(out, in_, ident[0:k, 0:k]), R=list(R) + ["ident"], W=W, sig=sig)

    def act(out, in_, func, R, W, bias=0.0, scale=1.0, accum=None):
        if accum is None:
            T.op(ACT, lambda: nc.scalar.activation(out, in_, func, bias=bias, scale=scale), R=R, W=W)
        else:
            T.op(ACT, lambda: nc.scalar.activation(out, in_, func, bias=bias, scale=scale, accum_out=accum), R=R, W=W)

    def tt(E, out, in0, in1, op, R, W):
        T.op(E, lambda: E.eng.tensor_tensor(out, in0, in1, op), R=R, W=W)

    def ts(E, out, in0, s1, s2, op0, op1, R, W):
        if op1 is None:
            T.op(E, lambda: E.eng.tensor_scalar(out, in0, s1, None, op0), R=R, W=W)
        else:
            T.op(E, lambda: E.eng.tensor_scalar(out, in0, s1, s2, op0, op1), R=R, W=W)

    def stt(E, out, in0, scalar, in1, op0, op1, R, W):
        T.op(E, lambda: E.eng.scalar_tensor_tensor(out, in0, scalar, in1, op0, op1), R=R, W=W)

    def cp(E, out, in_, R, W):
        if E is ACT:
            T.op(ACT, lambda: nc.scalar.activation(out, in_, AF.Identity), R=R, W=W)
        else:
            T.op(E, lambda: E.eng.tensor_copy(out, in_), R=R, W=W)

    def bvload(idx, slot):
        load(bvec[:, slot, :], lnv[idx].partition_broadcast(128), W=[("bvec", slot)])

    load(ident[:], c_ident, W=["ident"])
    load(mask[:], c_mask, W=["mask"])
    load(injk[:], c_injk, W=["injk"])
    load(ropeC[:], c_ropeC.rearrange("p (a b) -> p a b", b=32), W=["rope"])
    load(ropeS[:], c_ropeS.rearrange("p (a b) -> p a b", b=32), W=["rope"])
    load(ropeCs[:], c_ropeCs, W=["ropes"])
    load(ropeSs[:], c_ropeSs, W=["ropes"])
    load(kvg_b[:], kvg.partition_broadcast(128), W=["kvg"])
    T.op(DVE, lambda: nc.vector.memset(ones[:], 1.0), W=["ones"])
    T.op(DVE, lambda: nc.vector.memset(WukP[:], 0.0), W=["WukP"])
    T.op(DVE, lambda: nc.vector.memset(Vaug[:], 1.0), W=["Vaug_init"])
    for lc in range(2):
        load(WukP[:, lc, :, 0:64], w_uk[lc * 128:(lc + 1) * 128, :].rearrange("p (h e) -> p h e", e=64),
             R=["WukP"], W=[("WukPl", lc)], q=POOL)
        load(Wuv[:, lc, :], w_uv[lc * 128:(lc + 1) * 128, :], W=[("Wuv", lc)], q=POOL)
    WUK = [("WukPl", 0), ("WukPl", 1), "WukP"]
    WUV = [("Wuv", 0), ("Wuv", 1)]
    with nc.allow_non_contiguous_dma(reason="tiny transposed parameter loads"):
        for cc in range(4):
            load(cwT[:, cc, :], conv_w[:, cc * 128:(cc + 1) * 128].rearrange("k p -> p k"), W=["cwT"])
            for i, v in enumerate((conv_b, conv_g, conv_bb)):
                load(cvec[:, i, cc:cc + 1], v[:, cc * 128:(cc + 1) * 128].rearrange("o p -> p o"), W=[("cvec", i)])
    CV = [("cvec", 0), ("cvec", 1), ("cvec", 2)]
    for cc in range(4):
        stg = hT[:, (cc % 2) * 8:(cc % 2) * 8 + 8, :].rearrange("p a b -> p (a b)")[:, 0:31 * 128].rearrange("p (k c) -> p k c", c=128)
        for k in range(31):
            ts(DVE, stg[:, k, :], ident[:, :], cwT[:, cc, k:k + 1], None, ALU.mult, None, R=["ident", "cwT"], W=[("cdstg", cc % 2)])
        scr = dscr(f"ws_cd{cc}", [128, 31 * 128])
        store(scr.rearrange("p (k c) -> p k c", c=128), stg, R=[("cdstg", cc % 2)], W=[("ws", f"cd{cc}")])
        slabs[f"cd{cc}"] = (scr, 31, 128)

    def rstd_from_var(out, var_ap, R, W):
        act(out, var_ap, AF.Sqrt, R=R, W=W, bias=EPS, scale=1.0)
        T.op(DVE, lambda: nc.vector.reciprocal(out, out), R=W, W=W)

    def layer_norm_rows(t_key, rows, src, dst, gi, bi, bf_out, bf_key):
        for hlf in range(2):
            T.op(DVE, lambda h=hlf: nc.vector.bn_stats(stats[0:rows, h, :], src[:, h * 512:(h + 1) * 512]),
                 R=[t_key], W=[("stats", hlf)])
        T.op(DVE, lambda: nc.vector.bn_aggr(mvar[0:rows, :], stats[0:rows, :, :]),
             R=[("stats", 0), ("stats", 1)], W=["mvar"])
        rstd_from_var(small[0:rows, 0:1], mvar[0:rows, 1:2], R=["mvar"], W=[("small", 0)])
        stt(DVE, small[0:rows, 1:2], mvar[0:rows, 0:1], -1.0, small[0:rows, 0:1], ALU.mult, ALU.mult,
            R=["mvar", ("small", 0)], W=[("small", 1)])
        act(dst, src, AF.Identity, R=[t_key, ("small", 0), ("small", 1)], W=[t_key],
            bias=small[0:rows, 1:2], scale=small[0:rows, 0:1])
        tt(POOL, dst, dst, bvec[0:rows, 0, :], ALU.mult, R=[t_key, ("bvec", 0)], W=[t_key])
        tt(POOL, dst, dst, bvec[0:rows, 1, :], ALU.add, R=[t_key, ("bvec", 1)], W=[t_key])
        if bf_out is not None:
            cp(ACT, bf_out, dst, R=[t_key], W=[bf_key])

    def rope_rows(rows, src3, dst3, C2, S2, tmp3, R, W, tmpkey):
        tt(DVE, tmp3[:, :, 0:16], src3[:, :, 16:32], S2[:, :, 0:16], ALU.mult, R=R, W=[tmpkey])
        tt(DVE, tmp3[:, :, 16:32], src3[:, :, 0:16], S2[:, :, 16:32], ALU.mult, R=R, W=[tmpkey])
        tt(DVE, tmp3[:, :, 32:64], src3, C2, ALU.mult, R=R, W=[tmpkey])
        tt(POOL, dst3, tmp3[:, :, 0:32], tmp3[:, :, 32:64], ALU.add, R=[tmpkey], W=W)

    def mem_kv_seq(s):
        memT = Db
        for mt in range(2):
            load(Cb[:, mt, :], memp[s, mt * 128:(mt + 1) * 128, :], W=[("Cb", mt)], q=POOL)
        for c in range(8):
            b = nb()
            for mt in range(2):
                tr(psb[:, b, mt * 128:(mt + 1) * 128], Cb[:, mt, c * 128:(c + 1) * 128],
                   R=[("Cb", mt)], W=[("pb", b)], sig=(mt == 1))
            cp(DVE, memT[:, c, 0:256], psb[:, b, 0:256], R=[("pb", b)], W=[("Db", c)])
        DBK = [("Db", c) for c in range(8)]
        chk(0.3)
        for which, dst in (("xk", mk_p), ("xv", mv_p)):
            cf = (lambda x: x) if which == "xk" else (lambda x: 0.97 + 0.03 * x)
            views = [wslab(which + "0"), wslab(which + "1")]
            chk(cf(0.5))
            for mt in range(2):
                b2 = nf2()
                for hf in range(2):
                    wv, wk = views[hf]
                    for c in range(8):
                        mm(psf[:, b2 + hf, :], memT[:, c, mt * 128:(mt + 1) * 128], wv[:, c, :], c == 0, c == 7,
                           R=DBK + [wk], W=[("pf", b2 + hf)], sig=(c == 7))
                chk(cf(0.7))
                t_ = tmpA if mt == 0 else tmpB
                tk = "tmpA" if mt == 0 else "tmpB"
                for hf in range(2):
                    cp(ACT, t_[:, hf * 512:(hf + 1) * 512], psf[:, b2 + hf, :], R=[("pf", b2 + hf)], W=[tk])
                chk(cf(0.8))
                store(dst[s, mt * 128:(mt + 1) * 128, :], t_[:, :], R=[tk])
                chk(cf(0.9))
                if which == "xv":
                    for hf in range(2):
                        cp(DVE, mvb[:, mt, hf * 512:(hf + 1) * 512], t_[:, hf * 512:(hf + 1) * 512],
                           R=[tk], W=[("mvb", mt)])
            chk(cf(0.95))
            if which == "xk":
                for e in range(8):
                    wv, wk = views[e // 4]
                    b1 = nf()
                    for c in range(8):
                        mm(psf[:, b1, 0:256], wv[:, c, (e % 4) * 128:(e % 4 + 1) * 128], memT[:, c, 0:256], c == 0, c == 7,
                           R=DBK + [wk], W=[("pf", b1)], sig=(c == 7))
                    cp(DVE, mkT[:, e, :], psf[:, b1, 0:256], R=[("pf", b1)], W=[("mkT", e)])
                chk(0.97)

    xpref = set()
    MKT = [("mkT", e) for e in range(8)]
    MVB = [("mvb", 0), ("mvb", 1)]

    def prompt_block(s, j):
        t0 = j * 512
        DBK = [("Db", c) for c in range(8)]
        T.retire([("hT", i) for i in range(16)], [("qtm", t) for t in range(4)])
        if PROFILE:
            T.stage = "s01"
        if j == 0:
            T.op(POOL, lambda: nc.gpsimd.memset(uTb[:, :, 0:30], 0.0), W=["uT_head"])
        for t in range(4):
            if (s, j, t) not in xpref:
                load(Cb[:, t % 2, :], xp[s, t0 + t * 128:t0 + (t + 1) * 128, :], W=[("Cb", t % 2)], q=POOL)
            for c in range(8):
                if c % 4 == 0:
                    b = nb()
                tr(psb[:, b, (c % 4) * 128:(c % 4 + 1) * 128], Cb[:, t % 2, c * 128:(c + 1) * 128],
                   R=[("Cb", t % 2)], W=[("pb", b)], sig=(c % 4 == 3))
                if c % 4 == 3:
                    c0 = c - 3
                    cp(ACT if c0 else DVE,
                       Db[:, c0:c0 + 4, t * 128:(t + 1) * 128],
                       psb[:, b, 0:512].rearrange("p (a b) -> p a b", b=128),
                       R=[("pb", b)], W=[("Db", cc) for cc in range(c0, c0 + 4)])
        chk(2)
        if PROFILE:
            T.stage = "s2a"
        v0, k0 = wslab("in0")
        v1, k1 = wslab("in1")
        for t in range(4):
            bA = nf()
            bB = nf()
            bC = nf()
            for (ob, o0, o1, vv, kk, c0_, c1_) in (
                    (bA, 0, 480, v0, k0, 0, 480), (bB, 0, 32, v0, k0, 480, 512), (bB, 32, 288, v1, k1, 0, 256),
                    (bB, 288, 320, v1, k1, 512, 544), (bC, 0, 256, v1, k1, 256, 512)):
                for c in range(8):
                    mm(psf[:, ob, o0:o1], Db[:, c, t * 128:(t + 1) * 128], vv[:, c, c0_:c1_], c == 0, c == 7,
                       R=DBK + [kk], W=[("pf", ob)], sig=(c == 7))
            PQ = [("pf", bA), ("pf", bB), ("pf", bC)]
            tile_i = j * 4 + t
            qd3 = qtm[:, t, :].rearrange("p (h e) -> p h e", e=96)
            for (bk, h0, nh_) in ((bA, 0, 5), (bB, 5, 3)):
                q3 = psf[:, bk, 0:nh_ * 96].rearrange("p (h e) -> p h e", e=96)
                cp(ACT, qd3[:, h0:h0 + nh_, 0:64], q3[:, :, 0:64], R=PQ, W=[("qtm", t)])
                tmp3 = tmpA[:, 0:nh_ * 64].rearrange("p (h e) -> p h e", e=64)
                rope_rows(128, q3[:, :, 64:96], qd3[:, h0:h0 + nh_, 64:96],
                          ropeC[:, tile_i:tile_i + 1, :].to_broadcast([128, nh_, 32]),
                          ropeS[:, tile_i:tile_i + 1, :].to_broadcast([128, nh_, 32]),
                          tmp3, R=PQ + ["rope"], W=[("qtm", t)], tmpkey="tmpA")
            pckv = psf[:, bC, 0:256]
            b1 = bB
            cp(DVE, tmpB[:, 768:1024], pckv, R=PQ, W=["tmpB"])
            pckv = tmpB[:, 768:1024]
            act(tmpB[:, 0:256], pckv, AF.Square, R=["tmpB"], W=["tmpB"])
            T.op(DVE, lambda: nc.vector.reduce_sum(small[:, 2:3], tmpB[:, 0:256], axis=AX.X), R=["tmpB"], W=["tmpB"])
            act(small[:, 3:4], small[:, 2:3], AF.Sqrt, R=["tmpB"], W=[("small", 3)], bias=EPS, scale=1.0 / 256.0)
            T.op(DVE, lambda: nc.vector.reciprocal(small[:, 3:4], small[:, 3:4]), R=[("small", 3)], W=[("small", 3)])
            stt(DVE, tmpB[:, 256:512], pckv, small[:, 3:4], kvg_b[:, :], ALU.mult, ALU.mult,
                R=PQ + [("small", 3), "kvg", "tmpB"], W=["tmpB"])
            store(ckv_p[s, t0 + t * 128:t0 + (t + 1) * 128, :], tmpB[:, 256:512], R=["tmpB"])
            if DEBUG and t == 1:
                store(y_p[s, 0:128, 0:64], small[:, :], R=["tmpB", ("small", 3)])
                store(y_p[s, 0:128, 64:320], tmpB[:, 768:1024], R=["tmpB"])
                store(y_p[s, 0:128, 320:576], tmpB[:, 256:512], R=["tmpB"])
            cp(ACT, ckb[:, t, 0:256], tmpB[:, 256:512], R=["tmpB"], W=[("ckb", t)])
            kr3 = tmpB[:, 512:544].rearrange("p (h e) -> p h e", e=32)
            rope_rows(128, psf[:, b1, 288:320].rearrange("p (h e) -> p h e", e=32), kr3,
                      ropeC[:, tile_i:tile_i + 1, :], ropeS[:, tile_i:tile_i + 1, :],
                      tmpB[:, 576:640].rearrange("p (h e) -> p h e", e=64),
                      R=[("pf", b1), "rope"], W=["tmpB"], tmpkey="tmpB")
            store(kr_p[s, t0 + t * 128:t0 + (t + 1) * 128, :], tmpB[:, 512:544], R=["tmpB"])
            cp(ACT, ckb[:, t, 256:288], tmpB[:, 512:544], R=["tmpB"], W=[("ckb", t)])
        chk(3)
        QTM = [("qtm", t) for t in range(4)]
        CKB = [("ckb", t) for t in range(4)]
        if PROFILE:
            T.stage = "s2b"
        for h in range(8):
            b = nb()
            for t in range(4):
                tr(psb[0:96, b, t * 128:(t + 1) * 128], qtm[:, t, h * 96:(h + 1) * 96], R=QTM, W=[("pb", b)], sig=(t == 3))
            cp(ACT if h % 2 else DVE, QT[0:96, h, :], psb[0:96, b, 0:512], R=[("pb", b)], W=[("QT", h)])
        for lc in range(2):
            b = nb()
            for t in range(4):
                tr(psb[:, b, t * 128:(t + 1) * 128], ckb[:, t, lc * 128:(lc + 1) * 128], R=CKB, W=[("pb", b)], sig=(t == 3))
            cp(DVE, ckT[:, lc, :], psb[:, b, 0:512], R=[("pb", b)], W=[("ckT", lc)])
        b = nb()
        for t in range(4):
            tr(psb[0:32, b, t * 128:(t + 1) * 128], ckb[:, t, 256:288], R=CKB, W=[("pb", b)], sig=(t == 3))
        cp(DVE, krT[:, :], psb[0:32, b, 0:512], R=[("pb", b)], W=["krT"])
        CKT = [("ckT", 0), ("ckT", 1)]
        if PROFILE:
            T.stage = "s2c"
        for h in range(8):
            b1 = nf()
            mm(psf[0:96, b1, :], WukP[:, 0, h, :], ckT[:, 0, :], True, False, R=CKT + WUK, W=[("pf", b1)])
            mm(psf[0:96, b1, :], WukP[:, 1, h, :], ckT[:, 1, :], False, False, R=CKT + WUK, W=[("pf", b1)])
            mm(psf[0:96, b1, :], injk[:, :], krT[:, :], False, True, R=["krT", "injk"], W=[("pf", b1)], sig=True)
            cp(ACT if h % 2 else DVE, KT[0:96, h, t0:t0 + 512], psf[0:96, b1, :], R=[("pf", b1)], W=[("KT", h, j)])
        if PROFILE:
            T.stage = "s2d"
        for t in range(4):
            b1 = nf()
            for lc in range(2):
                mm(psf[:, b1, :], ckT[:, lc, t * 128:(t + 1) * 128], Wuv[:, lc, :], lc == 0, lc == 1,
                   R=CKT + WUV, W=[("pf", b1)], sig=(lc == 1))
            kt = j * 4 + t
            pv = psf[:, b1, :].rearrange("p (a two e) -> p a two e", two=2, e=64)
            cp(DVE, Vaug[:, kt, :, 0:64], pv[:, :, 0, :], R=[("pf", b1), "Vaug_init"], W=[("V", kt)])
            cp(ACT, Vaug[:, kt, :, 128:192], pv[:, :, 1, :], R=[("pf", b1), "Vaug_init"], W=[("V", kt)])
        chk(4)
        if PROFILE:
            T.stage = "s2e"
        v2, k2 = wslab("in2")
        v3, k3 = wslab("in3")
        for cc in range(4):
            ba = nf()
            bb = nf()
            for c in range(8):
                mm(psf[:, ba, :], v2[:, c, cc * 128:(cc + 1) * 128], Db[:, c, :], c == 0, c == 7, R=DBK + [k2], W=[("pf", ba)])
                mm(psf[:, bb, :], v3[:, c, cc * 128:(cc + 1) * 128], Db[:, c, :], c == 0, c == 7, R=DBK + [k3], W=[("pf", bb)], sig=(c == 7))
            act(tmpA[:, 0:512], psf[:, bb, :], AF.Sigmoid, R=[("pf", bb)], W=["tmpA"])
            tt(DVE, uTb[:, cc, 30:542], psf[:, ba, :], tmpA[:, 0:512], ALU.mult, R=[("pf", ba), "tmpA", "uT_head"], W=[("uT", cc)])
        if j == nblk - 1:
            b2 = nf2()
            for c in range(8):
                xTt = Db[:, c, 384:512]
                mm(psf[:, b2, :], xTt, v2[:, c, :], c == 0, c == 7, R=DBK + [k2], W=[("pf", b2)])
                mm(psf[:, b2 + 1, :], xTt, v3[:, c, :], c == 0, c == 7, R=DBK + [k3], W=[("pf", b2 + 1)], sig=(c == 7))
            act(tmpA[:, 0:512], psf[:, b2 + 1, :], AF.Sigmoid, R=[("pf", b2 + 1)], W=["tmpA"])
            tt(DVE, tmpA[:, 512:1024], psf[:, b2, :], tmpA[:, 0:512], ALU.mult, R=[("pf", b2), "tmpA"], W=["tmpA"])
            store(conv_p[s, :, :], tmpA[98:128, 512:1024], R=["tmpA"])
        chk(5)
        if PROFILE:
            T.stage = "s3"
        nkt = 4 * j + 4
        for h in range(8):
            bo = nh()
            for kt in range(nkt):
                r = kt - 4 * j
                q0 = 0 if r <= 0 else r * 128
                n = 512 - q0
                bs = nf()
                mm(psf[:, bs, 0:n], KT[0:96, h, kt * 128:(kt + 1) * 128], QT[0:96, h, q0:512], True, True,
                   R=[("KT", h, kt // 4), ("QT", h)], W=[("pf", bs)], sig=True)
                pi = (h * 64 + kt) % 4
                act(Pr[:, pi, 0:n], psf[:, bs, 0:n], AF.Exp, R=[("pf", bs)], W=[("Pr", pi)], scale=MLA_SCALE)
                if r >= 0:
                    tt(POOL, Pr[:, pi, 0:128], Pr[:, pi, 0:128], mask[:, :], ALU.mult, R=[("Pr", pi), "mask"], W=[("Pr", pi)])
                odd = h % 2
                lhs = Vaug[:, kt, h // 2, 64:192] if odd else Vaug[:, kt, h // 2, 0:128]
                mm(psf[:, bo, q0:512], lhs, Pr[:, pi, 0:n], kt == 0, kt == nkt - 1,
                   R=[("Pr", pi), ("V", kt), "Vaug_init"], W=[("pf", bo)], sig=True)
            if h % 2 == 0:
                cp(DVE, tmpA[0:64, 0:512], psf[64:128, bo, :], R=[("pf", bo)], W=["tmpA"])
                T.op(DVE, lambda: nc.vector.reciprocal(tmpA[0:64, 0:512], tmpA[0:64, 0:512]), R=["tmpA"], W=["tmpA"])
                tt(DVE, mixT[0:64, h // 2, :], psf[0:64, bo, :], tmpA[0:64, 0:512], ALU.mult,
                   R=[("pf", bo), "tmpA"], W=[("mixT", h // 2)])
            else:
                cp(DVE, tmpB[64:128, 0:512], psf[0:64, bo, :], R=[("pf", bo)], W=["tmpB"])
                T.op(DVE, lambda: nc.vector.reciprocal(tmpB[64:128, 0:512], tmpB[64:128, 0:512]), R=["tmpB"], W=["tmpB"])
                tt(DVE, mixT[64:128, h // 2, :], psf[64:128, bo, :], tmpB[64:128, 0:512], ALU.mult,
                   R=[("pf", bo), "tmpB"], W=[("mixT", h // 2)])
        chk(6)
        if PROFILE:
            T.stage = "s4"
        for cc in range(4):
            wv, wk = wslab(f"cd{cc}")
            b1 = nf()
            for k in range(31):
                mm(psf[:, b1, :], wv[:, k, :], uTb[:, cc, k:k + 512], k == 0, k == 30,
                   R=[("uT", cc), "uT_head", wk], W=[("pf", b1)], sig=(k == 30))
            act(cacc[:, cc, :], psf[:, b1, :], AF.Identity, R=[("pf", b1)] + CV, W=[("Rb", cc)], bias=cvec[:, 0, cc:cc + 1])
            if j < nblk - 1:
                cp(POOL, tmpA[:, 512 + cc * 32:512 + cc * 32 + 30], uTb[:, cc, 512:542], R=[("uT", cc)], W=[("uhc", cc)])
            cp(ACT, cyb[:, cc, :], cacc[:, cc, :], R=[("Rb", cc)], W=[("QT", cc)])
            act(csq[:, cc, :], cacc[:, cc, :], AF.Square, R=[("Rb", cc)], W=[("Pr", cc)])
        if j < nblk - 1:
            for cc in range(4):
                cp(POOL, uTb[:, cc, 0:30], tmpA[:, 512 + cc * 32:512 + cc * 32 + 30], R=[("uhc", cc)], W=["uT_head"])
        bm = nf()
        bq = nf()
        for cc in range(4):
            mm(psf[:, bm, :], ones[:, :], cyb[:, cc, :], cc == 0, cc == 3, R=[("QT", cc), "ones"], W=[("pf", bm)])
            mm(psf[:, bq, :], ones[:, :], csq[:, cc, :], cc == 0, cc == 3, R=[("Pr", cc), "ones"], W=[("pf", bq)], sig=(cc == 3))
        ts(DVE, tmpA[:, 0:512], psf[:, bm, :], 1.0 / 512.0, None, ALU.mult, None, R=[("pf", bm)], W=["tmpA"])
        tt(DVE, tmpA[:, 512:1024], tmpA[:, 0:512], tmpA[:, 0:512], ALU.mult, R=["tmpA"], W=["tmpA"])
        stt(DVE, tmpA[:, 512:1024], psf[:, bq, :], 1.0 / 512.0, tmpA[:, 512:1024], ALU.mult, ALU.subtract,
            R=[("pf", bq), "tmpA"], W=["tmpA"])
        act(tmpA[:, 512:1024], tmpA[:, 512:1024], AF.Sqrt, R=["tmpA"], W=["tmpA"], bias=EPS, scale=1.0)
        T.op(DVE, lambda: nc.vector.reciprocal(tmpA[:, 512:1024], tmpA[:, 512:1024]), R=["tmpA"], W=["tmpA"])
        for cc in range(4):
            tt(DVE, cacc[:, cc, :], cacc[:, cc, :], tmpA[:, 0:512], ALU.subtract, R=[("Rb", cc), "tmpA"], W=[("Rb", cc)])
            tt(POOL, cacc[:, cc, :], cacc[:, cc, :], tmpA[:, 512:1024], ALU.mult, R=[("Rb", cc), "tmpA"], W=[("Rb", cc)])
            act(mixT[:, 4 + cc, :], cacc[:, cc, :], AF.Silu, R=[("Rb", cc)] + CV, W=[("mixT", 4 + cc)],
                bias=cvec[:, 2, cc:cc + 1], scale=cvec[:, 1, cc:cc + 1])
        chk(7)
        MIX = [("mixT", c) for c in range(8)]
        if PROFILE:
            T.stage = "s5"
        post_ln(s, j, MIX, mixT, "o", 0, 1, x_from_dram=True)
        chk(8)
        if PROFILE:
            T.stage = "s7"
        vq = [wslab("xq0"), wslab("xq1")]
        for e in range(8):
            wv, wk = vq[e // 4]
            b1 = nf()
            for c in range(8):
                mm(psf[:, b1, :], wv[:, c, (e % 4) * 128:(e % 4 + 1) * 128], Db[:, c, :], c == 0, c == 7,
                   R=DBK + [wk], W=[("pf", b1)], sig=(c == 7))
            cp(ACT if e % 2 else DVE, QT[:, e, :], psf[:, b1, :], R=[("pf", b1)], W=[("QT", e)])
        if PROFILE:
            T.stage = "s8"
        for h in range(4):
            pis = []
            for mc in range(2):
                bs = nf()
                for dc in range(2):
                    mm(psf[:, bs, :], mkT[:, 2 * h + dc, mc * 128:(mc + 1) * 128], QT[:, 2 * h + dc, :], dc == 0, dc == 1,
                       R=MKT + [("QT", 2 * h + dc)], W=[("pf", bs)], sig=(dc == 1))
                pi = (h * 2 + mc) % 4
                pis.append(pi)
                act(Pr[:, pi, :], psf[:, bs, :], AF.Exp, R=[("pf", bs)], W=[("Pr", pi)], scale=MEM_SCALE)
            bd = nf()
            for mc in range(2):
                mm(psf[:, bd, :], ones[:, :], Pr[:, pis[mc], :], mc == 0, mc == 1, R=[("Pr", pis[mc]), "ones"], W=[("pf", bd)], sig=(mc == 1))
            T.op(DVE, lambda: nc.vector.reciprocal(tmpA[:, 0:512], psf[:, bd, :]), R=[("pf", bd)], W=["tmpA"])
            for dc in range(2):
                bo = nf()
                for mc in range(2):
                    mm(psf[:, bo, :], mvb[:, mc, h * 256 + dc * 128:h * 256 + (dc + 1) * 128], Pr[:, pis[mc], :], mc == 0, mc == 1,
                       R=MVB + [("Pr", pis[mc])], W=[("pf", bo)], sig=(mc == 1))
                tt(DVE, mixT[:, 2 * h + dc, :], psf[:, bo, :], tmpA[:, 0:512], ALU.mult, R=[("pf", bo), "tmpA"], W=[("mixT", 2 * h + dc)])
        chk(9)
        if PROFILE:
            T.stage = "s9"
        post_ln(s, j, MIX, mixT, "xo", 2, 3, x_from_dram=False)
        chk(10)
        if PROFILE:
            T.stage = "s10"
        nxt = (s, j + 1) if j + 1 < nblk else ((s + 1, 0) if s + 1 < nseq else None)
        if nxt is not None and not (nxt[1] == 0):
            for t in range(2):
                load(Cb[:, t, :], xp[nxt[0], nxt[1] * 512 + t * 128:nxt[1] * 512 + (t + 1) * 128, :], W=[("Cb", t)], q=POOL)
                xpref.add((nxt[0], nxt[1], t))
        T.retire([("qtm", t) for t in range(4)], [("hT", i) for i in range(16)])
        for fh in range(2):
            for g in range(4):
                wv, wk = wslab(f"up{fh * 4 + g}")
                for fc in range(4):
                    b1 = nf()
                    for c in range(8):
                        mm(psf[:, b1, :], wv[:, c, fc * 128:(fc + 1) * 128], Db[:, c, :], c == 0, c == 7,
                           R=DBK + [wk], W=[("pf", b1)], sig=(c == 7))
                    act(tmpA[:, 0:512] if fc % 2 == 0 else tmpB[:, 0:512], psf[:, b1, :], AF.Relu, R=[("pf", b1)],
                        W=["tmpA" if fc % 2 == 0 else "tmpB"])
                    tt(DVE, hT[:, g * 4 + fc, :], tmpA[:, 0:512] if fc % 2 == 0 else tmpB[:, 0:512], psf[:, b1, :], ALU.mult,
                       R=[("pf", b1), "tmpA" if fc % 2 == 0 else "tmpB"], W=[("hT", g * 4 + fc)])
            HK = [("hT", i) for i in range(16)]
            for qd in range(4):
                wv, wk = wslab(f"dn{fh}{qd}")
                for t in range(4):
                    b1 = nf()
                    for fc in range(16):
                        mm(psf[:, b1, 0:256], hT[:, fc, t * 128:(t + 1) * 128], wv[:, fc, :], fc == 0, fc == 15,
                           R=HK + [wk], W=[("pf", b1)], sig=(fc == 15))
                    dstv = Rb[:, t, qd * 256:(qd + 1) * 256]
                    if fh == 0:
                        stt(DVE, dstv, dstv, ALPHA, psf[:, b1, 0:256], ALU.mult, ALU.add, R=[("Rb", t), ("pf", b1)], W=[("Rb", t)])
                    else:
                        tt(DVE, dstv, dstv, psf[:, b1, 0:256], ALU.add, R=[("Rb", t), ("pf", b1)], W=[("Rb", t)])
        bvload(4, 0)
        bvload(5, 1)
        for t in range(4):
            layer_norm_rows(("Rb", t), 128, Rb[:, t, :], Rb[:, t, :], 4, 5, None, None)
            store(y_p[s, t0 + t * 128:t0 + (t + 1) * 128, :], Rb[:, t, :], R=[("Rb", t)])

    def post_ln(s, j, INK, inT, wname, gi, bi, x_from_dram):
        t0 = j * 512
        views = [wslab(wname + "0"), wslab(wname + "1")]
        bvload(gi, 0)
        bvload(bi, 1)
        for t in range(4):
            b2 = nf2()
            for hf in range(2):
                wv, wk = views[hf]
                for c in range(8):
                    mm(psf[:, b2 + hf, :], inT[:, c, t * 128:(t + 1) * 128], wv[:, c, :], c == 0, c == 7,
                       R=INK + [wk], W=[("pf", b2 + hf)], sig=(c == 7))
            if x_from_dram:
                load(xring[:, 0, :], xp[s, t0 + t * 128:t0 + (t + 1) * 128, :], W=[("xring", 0)])
            for hf in range(2):
                src = xring[:, 0, hf * 512:(hf + 1) * 512] if x_from_dram else Rb[:, t, hf * 512:(hf + 1) * 512]
                stt(DVE, Rb[:, t, hf * 512:(hf + 1) * 512], src, ALPHA, psf[:, b2 + hf, :], ALU.mult, ALU.add,
                    R=[("pf", b2 + hf), ("xring", 0), ("Rb", t)], W=[("Rb", t)])
            layer_norm_rows(("Rb", t), 128, Rb[:, t, :], Rb[:, t, :], gi, bi, Cb[:, t % 2, :], ("Cb", t % 2))
            for c in range(8):
                if c % 4 == 0:
                    b = nb()
                tr(psb[:, b, (c % 4) * 128:(c % 4 + 1) * 128], Cb[:, t % 2, c * 128:(c + 1) * 128],
                   R=[("Cb", t % 2)], W=[("pb", b)], sig=(c % 4 == 3))
                if c % 4 == 3:
                    c0 = c - 3
                    cp(ACT if c0 else DVE,
                       Db[:, c0:c0 + 4, t * 128:(t + 1) * 128],
                       psb[:, b, 0:512].rearrange("p (a b) -> p a b", b=128),
                       R=[("pb", b)], W=[("Db", cc) for cc in range(c0, c0 + 4)])


    def barrier():
        engs = (PE, ACT, DVE, POOL, SP)
        dss = st_sems + ld_sems + wr_sems + cvt_sems + g_sems
        for E in engs:
            for F in engs:
                if F is not E and F.count and E.seen.get(F, 0) < F.count:
                    E.eng.wait_ge(F.sem, F.count)
                    E.seen[F] = F.count
            for ds in dss:
                if ds.count and E.seen.get(ds, 0) < ds.count:
                    E.eng.wait_ge(ds.sem, ds.count * 16)
                    E.seen[ds] = ds.count

    g_sems = [T.dsem(f"g{i}") for i in range(4)]

    def sample_phase():
        NS = 16
        KTf = KT[:, :, :].rearrange("p a b -> p (a b)")
        hTf = hT[:, :, :].rearrange("p a b -> p (a b)")
        QTf = QT[:, :, :].rearrange("p a b -> p (a b)")
        Rbf = Rb[:, :, :].rearrange("p a b -> p (a b)")
        uTf = uT[:, :, :].rearrange("p a b -> p (a b)")
        bvf = bvec[:, :, :].rearrange("p a b -> p (a b)")
        mixf = mixT[:, :, :].rearrange("p a b -> p (a b)")
        Dbf = Db[:, :, :].rearrange("p a b -> p (a b)")

        def carver(arena):
            off = [0]
            def carve(n):
                v = arena[:, off[0]:off[0] + n]
                off[0] += n
                return v
            return carve
        ck, ch, cq, cr, cm, cd = carver(KTf), carver(hTf), carver(QTf), carver(Rbf), carver(mixf), carver(Dbf)
        cpg = [ck(2048).rearrange("p (r l) -> p r l", l=256) for _ in range(2)]
        rpg = [ck(256).rearrange("p (r l) -> p r l", l=32) for _ in range(2)]
        cT = [ck(2048).rearrange("p (a b) -> p a b", b=1024) for _ in range(2)]
        rT = [ck(1024) for _ in range(2)]
        cnx = ck(16 * 264).rearrange("p (b l) -> p b l", l=264)
        Pp = ck(512)
        WukT = ch(2048).rearrange("p (h l) -> p h l", l=256)
        ql_bf = ch(2048)
        xs_bf = ch(1024)
        q_s = ch(768)
        cnew_bf = ch(288)
        mix_s = ch(1024)
        xsT = cq(128).rearrange("p (c b) -> p c b", b=16)
        qT_s = cq(128).rearrange("p (h b) -> p h b", b=16)
        qlT = cq(256).rearrange("p (l b h) -> p l b h", b=16, h=8)
        qrT = cq(128).rearrange("p (b h) -> p b h", h=8)
        cnT = cq(32).rearrange("p (l b) -> p l b", b=16)
        krnT = cq(16)
        olT = cq(256).rearrange("p (l h b) -> p l h b", h=8, b=16)
        mixT_s = cq(128).rearrange("p (c b) -> p c b", b=16)
        o2T_s = cq(128).rearrange("p (c b) -> p c b", b=16)
        hT_s = cq(512)
        pnew = cq(8)
        ol_bf = cq(256)
        P2bf = cq(8)
        dhl = cq(16)
        xs_f = cr(1024)
        r_s = cr(1024)
        u_s = cr(512)
        q2_f = cr(1024)
        misc = cr(512)
        xb16 = cm(1024)
        ptb = sb("ptb", [128, 128], I32)
        gqt = sb("gqt", [128, 1], I32)
        idx = sb("idx", [128, 128], I32)
        q2_scr = dscr("q2_scr", [16, 1024], F32)
        onesf = sb("onesf", [128, 1], F32)
        idxc = [sb(f"idxc{i}", [128, 1], I32) for i in range(2)]

        load(ptb[:], ptrep, W=["ptb"])
        load(gqt[:], gq, W=["gqt"])
        T.op(DVE, lambda: nc.vector.memset(onesf[:], 1.0), W=["onesf"])
        ts(DVE, idx[:], ptb[:], 16, gqt[:, 0:1], ALU.mult, ALU.add, R=["ptb", "gqt"], W=["idx"])
        if DEBUG == "idx":
            cp(DVE, tmpA[:, 0:128], idx[:], R=["idx"], W=["tmpA"])
            store(ckv_p[0, 0:128, 0:128], tmpA[:, 0:128], R=["tmpA"])
            barrier()
            raise _Stop()
        T.op(POOL, lambda: nc.gpsimd.memset(cnx[0:1, :, :], 1.0), W=["cnx"])

        def tr16(dst, src16, ncols, key_r, key_w, E=DVE):
            b = nb()
            for c in range(ncols):
                tr(psb[:, b, c * 16:(c + 1) * 16], src16[0:16, c * 128:(c + 1) * 128], R=[key_r], W=[("pb", b)], sig=(c == ncols - 1))
            cp(E, dst, psb[:, b, 0:ncols * 16].rearrange("p (c b) -> p c b", b=16), R=[("pb", b)], W=[key_w])

        if PROFILE:
            T.stage = "smp_pre"
        load(xs_f[0:16, :], xs, W=["xs_f"])
        load(xs_bf[0:16, :], xs, W=["xs_bf"], q=POOL)
        tr16(xsT, xs_bf, 8, "xs_bf", "xsT")
        v0, k0 = wslab("in0")
        v1, k1 = wslab("in1")
        bA, bB, bC = nf(), nf(), nf()
        for (ob, o0, o1, vv, kk, c0_, c1_) in ((bA, 0, 480, v0, k0, 0, 480), (bB, 0, 32, v0, k0, 480, 512),
                                               (bB, 32, 288, v1, k1, 0, 256), (bB, 288, 320, v1, k1, 512, 544),
                                               (bC, 0, 256, v1, k1, 256, 512)):
            for c in range(8):
                mm(psf[0:16, ob, o0:o1], xsT[:, c, :], vv[:, c, c0_:c1_], c == 0, c == 7, R=["xsT", kk], W=[("pf", ob)], sig=(c == 7))
        PQ = [("pf", bA), ("pf", bB), ("pf", bC)]
        qd3 = q_s[0:16, :].rearrange("p (h e) -> p h e", e=96)
        for (bk, h0, nh_) in ((bA, 0, 5), (bB, 5, 3)):
            q3 = psf[0:16, bk, 0:nh_ * 96].rearrange("p (h e) -> p h e", e=96)
            cp(ACT, qd3[:, h0:h0 + nh_, 0:64], q3[:, :, 0:64], R=PQ, W=["q_s"])
            tmp3 = tmpA[0:16, 0:nh_ * 64].rearrange("p (h e) -> p h e", e=64)
            rope_rows(16, q3[:, :, 64:96], qd3[:, h0:h0 + nh_, 64:96],
                      ropeCs[0:16, :].rearrange("p (o e) -> p o e", o=1).to_broadcast([16, nh_, 32]),
                      ropeSs[0:16, :].rearrange("p (o e) -> p o e", o=1).to_broadcast([16, nh_, 32]),
                      tmp3, R=PQ + ["ropes"], W=["q_s"], tmpkey="tmpA")
        cp(DVE, tmpB[0:16, 768:1024], psf[0:16, bC, 0:256], R=PQ, W=["tmpB"])
        zc = tmpB[0:16, 768:1024]
        act(tmpB[0:16, 0:256], zc, AF.Square, R=["tmpB"], W=["tmpB"])
        T.op(DVE, lambda: nc.vector.reduce_sum(small[0:16, 2:3], tmpB[0:16, 0:256], axis=AX.X), R=["tmpB"], W=["tmpB"])
        act(small[0:16, 3:4], small[0:16, 2:3], AF.Sqrt, R=["tmpB"], W=[("small", 3)], bias=EPS, scale=1.0 / 256.0)
        T.op(DVE, lambda: nc.vector.reciprocal(small[0:16, 3:4], small[0:16, 3:4]), R=[("small", 3)], W=[("small", 3)])
        stt(DVE, tmpB[0:16, 256:512], zc, small[0:16, 3:4], kvg_b[0:16, :], ALU.mult, ALU.mult,
            R=[("small", 3), "kvg", "tmpB"], W=["tmpB"])
        store(ckv_s[:, :], tmpB[0:16, 256:512], R=["tmpB"], W=["ckv_s_dram"])
        cp(ACT, cnew_bf[0:16, 0:256], tmpB[0:16, 256:512], R=["tmpB"], W=["cnew_bf"])
        kr3 = tmpB[0:16, 512:544].rearrange("p (h e) -> p h e", e=32)
        rope_rows(16, psf[0:16, bB, 288:320].rearrange("p (h e) -> p h e", e=32), kr3,
                  ropeCs[0:16, :].rearrange("p (o e) -> p o e", o=1), ropeSs[0:16, :].rearrange("p (o e) -> p o e", o=1),
                  tmpB[0:16, 576:640].rearrange("p (h e) -> p h e", e=64),
                  R=PQ + ["ropes"], W=["tmpB"], tmpkey="tmpB")
        store(kr_s[:, :], tmpB[0:16, 512:544], R=["tmpB"])
        cp(ACT, cnew_bf[0:16, 256:288], tmpB[0:16, 512:544], R=["tmpB"], W=["cnew_bf"])
        v2, k2 = wslab("in2")
        v3, k3 = wslab("in3")
        ba, bb = nf(), nf()
        for c in range(8):
            mm(psf[0:16, ba, :], xsT[:, c, :], v2[:, c, :], c == 0, c == 7, R=["xsT", k2], W=[("pf", ba)])
            mm(psf[0:16, bb, :], xsT[:, c, :], v3[:, c, :], c == 0, c == 7, R=["xsT", k3], W=[("pf", bb)], sig=(c == 7))
        act(tmpA[0:16, 0:512], psf[0:16, bb, :], AF.Sigmoid, R=[("pf", bb)], W=["tmpA"])
        tt(DVE, u_s[0:16, :], psf[0:16, ba, :], tmpA[0:16, 0:512], ALU.mult, R=[("pf", ba), "tmpA"], W=["u_s"])
        mk_rest_slabs()
        T.dma(SP, st_sems[0], lambda: nc.sync.dma_start(out=conv_s[:, 0:29 * 512], in_=stc[:, 512:30 * 512]))
        store(conv_s[:, 29 * 512:30 * 512], u_s[0:16, :], R=["u_s"])
        acc = misc[0:16, :]
        first = True
        for k0_ in range(0, 30, 4):
            n = min(4, 30 - k0_)
            ext = uTf[0:16, 0:n * 512]
            cwb = bvf[0:16, 0:n * 512]
            load(ext, stc[:, k0_ * 512:(k0_ + n) * 512], W=["ext"])
            load(cwb, conv_w[k0_:k0_ + n, :].rearrange("(o k) c -> o (k c)", o=1).partition_broadcast(16), W=[("bvec", 0), ("bvec", 1)])
            tt(DVE, ext, ext, cwb, ALU.mult, R=["ext", ("bvec", 0), ("bvec", 1)], W=["ext"])
            for jx in range(n):
                if first:
                    cp(DVE, acc, ext[:, jx * 512:(jx + 1) * 512], R=["ext"], W=["acc"])
                    first = False
                else:
                    tt(DVE, acc, acc, ext[:, jx * 512:(jx + 1) * 512], ALU.add, R=["ext", "acc"], W=["acc"])
        load(bvf[0:16, 0:512], conv_w[30:31, :].partition_broadcast(16), W=[("bvec", 0), ("bvec", 1)])
        tt(DVE, tmpA[0:16, 0:512], u_s[0:16, :], bvf[0:16, 0:512], ALU.mult, R=["u_s", ("bvec", 0), ("bvec", 1)], W=["tmpA"])
        tt(DVE, acc, acc, tmpA[0:16, 0:512], ALU.add, R=["tmpA", "acc"], W=["acc"])
        load(bvf[0:16, 0:512], conv_b.partition_broadcast(16), W=[("bvec", 0), ("bvec", 1)])
        tt(DVE, acc, acc, bvf[0:16, 0:512], ALU.add, R=["acc", ("bvec", 0), ("bvec", 1)], W=["acc"])
        T.op(DVE, lambda: nc.vector.bn_stats(stats[0:16, 0, :], acc), R=["acc"], W=[("stats", 0)])
        T.op(DVE, lambda: nc.vector.bn_aggr(mvar[0:16, :], stats[0:16, 0:1, :]), R=[("stats", 0)], W=["mvar"])
        rstd_from_var(small[0:16, 0:1], mvar[0:16, 1:2], R=["mvar"], W=[("small", 0)])
        stt(DVE, small[0:16, 1:2], mvar[0:16, 0:1], -1.0, small[0:16, 0:1], ALU.mult, ALU.mult,
            R=["mvar", ("small", 0)], W=[("small", 1)])
        act(acc, acc, AF.Identity, R=["acc", ("small", 0), ("small", 1)], W=["acc"], bias=small[0:16, 1:2], scale=small[0:16, 0:1])
        load(bvf[0:16, 0:512], conv_g.partition_broadcast(16), W=[("bvec", 0)])
        load(bvf[0:16, 1024:1536], conv_bb.partition_broadcast(16), W=[("bvec", 1)])
        tt(DVE, acc, acc, bvf[0:16, 0:512], ALU.mult, R=["acc", ("bvec", 0)], W=["acc"])
        tt(DVE, acc, acc, bvf[0:16, 1024:1536], ALU.add, R=["acc", ("bvec", 1)], W=["acc"])
        act(mix_s[0:16, 512:1024], acc, AF.Silu, R=["acc"], W=["mix_s"])
        for h in range(8):
            b = nb()
            for lc in range(2):
                tr(psb[0:64, b, lc * 128:(lc + 1) * 128], WukP[:, lc, h, 0:64], R=WUK, W=[("pb", b)], sig=(lc == 1))
            cp(DVE, WukT[0:64, h, :], psb[0:64, b, 0:256], R=[("pb", b)], W=["WukT"])
        b = nb()
        for h in range(8):
            tr(psb[0:96, b, h * 16:(h + 1) * 16], q_s[0:16, h * 96:(h + 1) * 96], R=["q_s"], W=[("pb", b)], sig=(h == 7))
        cp(DVE, qT_s[0:96, :, :], psb[0:96, b, 0:128].rearrange("p (h b) -> p h b", b=16), R=[("pb", b)], W=["qT_s"])
        cp(DVE, qrT[0:32, :, :].rearrange("p b h -> p h b"), qT_s[64:96, :, :], R=["qT_s"], W=["qrT"])
        qb = [nf() for _ in range(4)]
        for h in range(8):
            mm(psf[0:16, qb[h // 2], (h % 2) * 256:(h % 2 + 1) * 256], qT_s[0:64, h, :], WukT[0:64, h, :], True, True,
               R=["qT_s", "WukT"], W=[("pf", qb[h // 2])], sig=True)
        for i in range(4):
            cp(ACT if i % 2 else DVE, ql_bf[0:16, i * 512:(i + 1) * 512], psf[0:16, qb[i], :], R=[("pf", qb[i])], W=["ql_bf"])
        if DEBUG == "ql":
            cp(DVE, tmpA[0:16, 0:1024], ql_bf[0:16, 1024:2048], R=["ql_bf"], W=["tmpA"])
            store(y_s[:, :], tmpA[0:16, 0:1024], R=["tmpA"])
            barrier()
            raise _Stop()
        b = nb()
        for lc in range(2):
            for h in range(8):
                o = (lc * 8 + h) * 16
                tr(psb[:, b, o:o + 16], ql_bf[0:16, h * 256 + lc * 128:h * 256 + (lc + 1) * 128], R=["ql_bf"], W=[("pb", b)],
                   sig=(lc == 1 and h == 7))
        cp(DVE, qlT[:, :, :, :].rearrange("p l b h -> p l h b"), psb[:, b, 0:256].rearrange("p (l h b) -> p l h b", h=8, b=16),
           R=[("pb", b)], W=["qlT"])
        tr16(cnT, cnew_bf, 2, "cnew_bf", "cnT")
        b = nb()
        tr(psb[0:32, b, 0:16], cnew_bf[0:16, 256:288], R=["cnew_bf"], W=[("pb", b)], sig=True)
        cp(DVE, krnT[0:32, :], psb[0:32, b, 0:16], R=[("pb", b)], W=["krnT"])
        load(cnx[0:1, :, 0:256], ckv_s.rearrange("(o b) l -> o b l", o=1), R=["ckv_s_dram", "cnx"], W=["cnxl"], q=POOL)
        if PROFILE:
            T.stage = "smp_attn"
        gi = [0]
        for bsm in range(NS):
            pS = nh()
            pO = nh()
            for pg in range(8):
                sl = gi[0] % 2
                gi[0] += 1
                col = bsm * 8 + pg
                cp(DVE, idxc[sl][:, 0:1], idx[:, col:col + 1], R=["idx"], W=[("idxc", sl)])
                T.dma(POOL, g_sems[sl], lambda sl=sl, col=col: nc.gpsimd.indirect_dma_start(
                    out=cpg[sl][:, :, :].rearrange("p r l -> p (r l)"), out_offset=None, in_=pool_c,
                    in_offset=bass.IndirectOffsetOnAxis(ap=idxc[sl][:, 0:1], axis=0)), R=[("idxc", sl)], W=[("cpg", sl)])
                T.dma(POOL, g_sems[2 + sl], lambda sl=sl, col=col: nc.gpsimd.indirect_dma_start(
                    out=rpg[sl][:, :, :].rearrange("p r l -> p (r l)"), out_offset=None, in_=pool_r,
                    in_offset=bass.IndirectOffsetOnAxis(ap=idxc[sl][:, 0:1], axis=0)), R=[("idxc", sl)], W=[("rpg", sl)])
                if DEBUG == "cpg":
                    for ii, rr in enumerate((0, 7)):
                        cp(DVE, tmpA[:, ii * 256:(ii + 1) * 256], cpg[sl][:, rr, 0:256], R=[("cpg", sl)], W=["tmpA"])
                        store(ckv_p[0, ii * 128:(ii + 1) * 128, :], tmpA[:, ii * 256:(ii + 1) * 256], R=["tmpA"])
                    cp(DVE, tmpA[:, 512:768], rpg[sl][:, :, :].rearrange("p r l -> p (r l)"), R=[("rpg", sl)], W=["tmpA"])
                    store(ckv_p[0, 256:384, :], tmpA[:, 512:768], R=["tmpA"])
                    barrier()
                    raise _Stop()
                for lc in range(2):
                    b = nb()
                    for r in range(8):
                        tr(psb[:, b, r * 128:(r + 1) * 128], cpg[sl][:, r, lc * 128:(lc + 1) * 128], R=[("cpg", sl)], W=[("pb", b)], sig=(r == 7))
                    cp(ACT if lc else DVE, cT[sl][:, lc, :], psb[:, b, :], R=[("pb", b)], W=[("cT", sl)])
                b = nb()
                for r in range(8):
                    tr(psb[0:32, b, r * 128:(r + 1) * 128], rpg[sl][:, r, :], R=[("rpg", sl)], W=[("pb", b)], sig=(r == 7))
                cp(DVE, rT[sl][0:32, :], psb[0:32, b, :], R=[("pb", b)], W=[("rT", sl)])
                for r in range(8):
                    kt = pg * 8 + r
                    o_ = psf[:, pS, kt * 8:(kt + 1) * 8]
                    RR = [("cT", sl), ("rT", sl), "qlT", "qrT"]
                    mm(o_, cT[sl][:, 0, r * 128:(r + 1) * 128], qlT[:, 0, bsm, :], True, False, R=RR, W=[("pf", pS)])
                    mm(o_, cT[sl][:, 1, r * 128:(r + 1) * 128], qlT[:, 1, bsm, :], False, False, R=RR, W=[("pf", pS)])
                    mm(o_, rT[sl][0:32, r * 128:(r + 1) * 128], qrT[0:32, bsm, :], False, True, R=RR, W=[("pf", pS)], sig=(r == 7))
                act(Pp[:, pg * 64:(pg + 1) * 64], psf[:, pS, pg * 64:(pg + 1) * 64], AF.Exp, R=[("pf", pS)], W=["Pp"], scale=MLA_SCALE)
                for r in range(8):
                    kt = pg * 8 + r
                    mm(psf[0:8, pO, 0:256], Pp[:, kt * 8:(kt + 1) * 8], cpg[sl][:, r, :], kt == 0, False,
                       R=["Pp", ("cpg", sl)], W=[("pf", pO)], sig=(r == 7))
            pN = nf()
            mm(psf[0:1, pN, 0:8], cnT[:, 0, bsm:bsm + 1], qlT[:, 0, bsm, :], True, False, R=["cnT", "qlT"], W=[("pf", pN)])
            mm(psf[0:1, pN, 0:8], cnT[:, 1, bsm:bsm + 1], qlT[:, 1, bsm, :], False, False, R=["cnT", "qlT"], W=[("pf", pN)])
            mm(psf[0:1, pN, 0:8], krnT[0:32, bsm:bsm + 1], qrT[0:32, bsm, :], False, True, R=["krnT", "qrT"], W=[("pf", pN)], sig=True)
            act(pnew[0:1, :], psf[0:1, pN, 0:8], AF.Exp, R=[("pf", pN)], W=["pnew"], scale=MLA_SCALE)
            mm(psf[0:8, pO, 0:256], pnew[0:1, :], cnx[0:1, bsm, 0:256], False, True, R=["pnew", "cnx", "cnxl"], W=[("pf", pO)], sig=True)
            T.op(DVE, lambda: nc.vector.reduce_sum(small[:, 40:48], Pp[:, :].rearrange("p (k h) -> p h k", h=8), axis=AX.X),
                 R=["Pp"], W=[("small", 40)])
            cp(DVE, dhl[:, 0:8], small[:, 40:48], R=[("small", 40)], W=["dhl"])
            tt(DVE, small[:, 48:56], small[:, 40:48], dhl[:, 0:8], ALU.subtract, R=[("small", 40), "dhl"], W=[("small", 48)])
            cp(DVE, dhl[:, 8:16], small[:, 48:56], R=[("small", 48)], W=["dhl"])
            pD = nf()
            mm(psf[0:8, pD, 0:1], dhl[:, 0:8], ones[:, 0:1], True, False, R=["dhl", "ones"], W=[("pf", pD)])
            mm(psf[0:8, pD, 0:1], dhl[:, 8:16], ones[:, 0:1], False, False, R=["dhl", "ones"], W=[("pf", pD)])
            mm(psf[0:8, pD, 0:1], pnew[0:1, :], ones[0:1, 0:1], False, True, R=["pnew", "ones"], W=[("pf", pD)], sig=True)
            if DEBUG == "po" and bsm == 0:
                cp(DVE, tmpA[0:8, 0:256], psf[0:8, pO, 0:256], R=[("pf", pO)], W=["tmpA"])
                store(y_s[0:8, 0:256], tmpA[0:8, 0:256], R=["tmpA"])
                cp(DVE, tmpA[:, 512:1024], Pp[:, :], R=["Pp"], W=["tmpA"])
                store(ckv_p[0, 0:128, :], tmpA[:, 512:768], R=["tmpA"])
                store(ckv_p[0, 128:256, :], tmpA[:, 768:1024], R=["tmpA"])
                barrier()
                raise _Stop()
            T.op(DVE, lambda pD=pD: nc.vector.reciprocal(small[0:8, 8:9], psf[0:8, pD, 0:1]), R=[("pf", pD)], W=[("small", 8)])
            ts(DVE, ol_bf[0:8, :], psf[0:8, pO, 0:256], small[0:8, 8:9], None, ALU.mult, None, R=[("pf", pO), ("small", 8)], W=["ol_bf"])
            b = nb()
            for lc in range(2):
                tr(psb[:, b, lc * 8:(lc + 1) * 8], ol_bf[0:8, lc * 128:(lc + 1) * 128], R=["ol_bf"], W=[("pb", b)], sig=(lc == 1))
            cp(DVE, olT[:, :, :, bsm], psb[:, b, 0:16].rearrange("p (l h) -> p l h", h=8), R=[("pb", b)], W=["olT"])
        if PROFILE:
            T.stage = "smp_rest"
        bk = nf()
        for h in range(8):
            for lc in range(2):
                mm(psf[0:16, bk, h * 64:(h + 1) * 64], olT[:, lc, h, :], Wuv[:, lc, h * 64:(h + 1) * 64], lc == 0, lc == 1,
                   R=["olT"] + WUV, W=[("pf", bk)], sig=(h == 7 and lc == 1))
        cp(DVE, mix_s[0:16, 0:512], psf[0:16, bk, :], R=[("pf", bk)], W=["mix_s"])

        def dbg(name, ap, key):
            if DEBUG == name:
                n = ap.shape[1]
                cp(DVE, tmpA[0:ap.shape[0], 0:n], ap, R=[key], W=["tmpA"])
                store(y_s[0:ap.shape[0], 0:n], tmpA[0:ap.shape[0], 0:n], R=["tmpA"])
                barrier()
                raise _Stop()

        dbg("mix_s", mix_s[0:16, :], "mix_s")

        def post16(inT, inkey, wname, resid, gi_, bi_, outT, outkey):
            views = [wslab(wname + "0"), wslab(wname + "1")]
            b2 = nf2()
            for hf in range(2):
                wv, wk = views[hf]
                for c in range(8):
                    mm(psf[0:16, b2 + hf, :], inT[:, c, :], wv[:, c, :], c == 0, c == 7, R=[inkey, wk], W=[("pf", b2 + hf)], sig=(c == 7))
            for hf in range(2):
                stt(DVE, r_s[0:16, hf * 512:(hf + 1) * 512], resid[0:16, hf * 512:(hf + 1) * 512], ALPHA, psf[0:16, b2 + hf, :],
                    ALU.mult, ALU.add, R=[("pf", b2 + hf), "r_s", "xs_f"], W=["r_s"])
            bvload(gi_, 0)
            bvload(bi_, 1)
            layer_norm_rows("r_s", 16, r_s[0:16, :], r_s[0:16, :], gi_, bi_, xb16[0:16, :] if outT is not None else None, "xb16")
            if outT is not None:
                tr16(outT, xb16, 8, "xb16", outkey)

        cp(ACT, xb16[0:16, :], mix_s[0:16, :], R=["mix_s"], W=["xb16"])
        tr16(mixT_s, xb16, 8, "xb16", "mixT_s")
        x1T_s = cd(128).rearrange("p (c b) -> p c b", b=16)
        x2T_s = cd(128).rearrange("p (c b) -> p c b", b=16)
        post16(mixT_s, "mixT_s", "o", xs_f, 0, 1, x1T_s, "x1T_s")
        dbg("x1", r_s[0:16, :], "r_s")
        if PROFILE:
            T.stage = "smp_xattn"
        vq = [wslab("xq0"), wslab("xq1")]
        b2 = nf2()
        for hf in range(2):
            wv, wk = vq[hf]
            for c in range(8):
                mm(psf[0:16, b2 + hf, :], x1T_s[:, c, :], wv[:, c, :], c == 0, c == 7, R=["x1T_s", wk], W=[("pf", b2 + hf)], sig=(c == 7))
        for hf in range(2):
            cp(DVE, q2_f[0:16, hf * 512:(hf + 1) * 512], psf[0:16, b2 + hf, :], R=[("pf", b2 + hf)], W=["q2_f"])
        store(q2_scr[:, :], q2_f[0:16, :], R=["q2_f"], W=["q2_scr"])
        mkb = uTf[:, 0:2048].rearrange("p (a b) -> p a b", b=1024)
        for bsm in range(NS):
            load(mkb, cmk[bsm].rearrange("(mt p) e -> p mt e", p=128), W=["mkb"])
            load(mvb[:, :, :], cmv[bsm].rearrange("(mt p) e -> p mt e", p=128), W=[("mvb", 0), ("mvb", 1)], q=POOL)
            load(xring[:, 0, :], q2_scr[bsm:bsm + 1, :].partition_broadcast(128), R=["q2_scr"], W=[("xring", 0)])
            for mt in range(2):
                tt(DVE, mkb[:, mt, :], mkb[:, mt, :], xring[:, 0, :], ALU.mult, R=["mkb", ("xring", 0)], W=["mkb"])
            T.op(DVE, lambda: nc.vector.reduce_sum(small[:, 16:24], mkb[:, :, :].rearrange("p a (h d) -> p (a h) d", d=256), axis=AX.X),
                 R=["mkb"], W=[("small", 16)])
            act(P2bf[:, :], small[:, 16:24], AF.Exp, R=[("small", 16)], W=["P2bf"], scale=MEM_SCALE)
            bo = nf()
            for h in range(4):
                for dc in range(2):
                    for mt in range(2):
                        mm(psf[:, bo, (h * 2 + dc):(h * 2 + dc) + 1], mvb[:, mt, h * 256 + dc * 128:h * 256 + (dc + 1) * 128],
                           P2bf[:, mt * 4 + h:mt * 4 + h + 1], mt == 0, mt == 1, R=[("mvb", 0), ("mvb", 1), "P2bf"], W=[("pf", bo)])
            mm(psf[:, bo, 16:24], ones[:, :], P2bf[:, :], True, True, R=["ones", "P2bf"], W=[("pf", bo)], sig=True)
            cp(DVE, small[:, 28:36], psf[:, bo, 16:24], R=[("pf", bo)], W=[("small", 28)])
            tt(DVE, small[:, 24:28], small[:, 28:32], small[:, 32:36], ALU.add, R=[("small", 28)], W=[("small", 24)])
            T.op(DVE, lambda: nc.vector.reciprocal(small[:, 24:28], small[:, 24:28]), R=[("small", 24)], W=[("small", 24)])
            tt(DVE, o2T_s[:, :, bsm].rearrange("p (h d) -> p h d", d=2), psf[:, bo, 0:8].rearrange("p (h d) -> p h d", d=2),
               small[:, 24:28].rearrange("p (h o) -> p h o", o=1).to_broadcast([128, 4, 2]), ALU.mult,
               R=[("pf", bo), ("small", 24)], W=["o2T_s"])
        if PROFILE:
            T.stage = "smp_ffn"
        post16(o2T_s, "o2T_s", "xo", r_s, 2, 3, x2T_s, "x2T_s")
        dbg("x2", r_s[0:16, :], "r_s")
        pH = nh()
        for g in range(8):
            wv, wk = wslab(f"up{g}")
            for fc in range(4):
                o = (g * 4 + fc) * 16
                for c in range(8):
                    mm(psf[:, pH, o:o + 16], wv[:, c, fc * 128:(fc + 1) * 128], x2T_s[:, c, :], c == 0, c == 7,
                       R=["x2T_s", wk], W=[("pf", pH)], sig=(c == 7 and fc == 3))
        act(tmpA[:, 0:512], psf[:, pH, :], AF.Relu, R=[("pf", pH)], W=["tmpA"])
        tt(DVE, hT_s[:, :], tmpA[:, 0:512], psf[:, pH, :], ALU.mult, R=[("pf", pH), "tmpA"], W=["hT_s"])
        b2 = nf2()
        for qd in range(4):
            for fh in range(2):
                wv, wk = wslab(f"dn{fh}{qd}")
                for fc in range(16):
                    o = (fh * 16 + fc) * 16
                    mm(psf[0:16, b2 + qd // 2, (qd % 2) * 256:(qd % 2 + 1) * 256], hT_s[:, o:o + 16], wv[:, fc, :],
                       fh == 0 and fc == 0, fh == 1 and fc == 15, R=["hT_s", wk], W=[("pf", b2 + qd // 2)], sig=(fc == 15))
        for hf in range(2):
            stt(DVE, r_s[0:16, hf * 512:(hf + 1) * 512], r_s[0:16, hf * 512:(hf + 1) * 512], ALPHA, psf[0:16, b2 + hf, :],
                ALU.mult, ALU.add, R=[("pf", b2 + hf), "r_s"], W=["r_s"])
        bvload(4, 0)
        bvload(5, 1)
        layer_norm_rows("r_s", 16, r_s[0:16, :], r_s[0:16, :], 4, 5, None, None)
        store(y_s[:, :], r_s[0:16, :], R=["r_s"])
        barrier()

    try:
        chk(0)
        if do_sample:
            sample_phase()
        mk_rest_slabs()
        chk(0.1)
        if do_prompt:
            for s in range(nseq):
                mem_kv_seq(s)
                chk(1)
                for j in range(nblk):
                    prompt_block(s, j)
    except _Stop:
        pass

    for ds in st_sems + ld_sems + wr_sems + cvt_sems + g_sems:
        if ds.count:
            nc.sync.wait_ge(ds.sem, ds.count * 16)
    for E in (PE, ACT, DVE, POOL):
        if E.count:
            nc.sync.wait_ge(E.sem, E.count)
    return nc


def _consts():
    half = 16
    inv_freq = np.exp(-math.log(10000.0) * np.arange(half, dtype=np.float32) / half).astype(np.float32)
    pos = np.arange(SEQ, dtype=np.float32)
    ang = pos[:, None] * inv_freq[None, :]
    cos, sin = np.cos(ang).astype(np.float32), np.sin(ang).astype(np.float32)
    C2 = np.concatenate([cos, cos], axis=1)
    S2 = np.concatenate([-sin, sin], axis=1)
    ropeC = C2.reshape(16, 128, 32).transpose(1, 0, 2).reshape(128, 512)
    ropeS = S2.reshape(16, 128, 32).transpose(1, 0, 2).reshape(128, 512)
    angs = np.float32(8192.0) * inv_freq
    cs, sn = np.cos(angs).astype(np.float32), np.sin(angs).astype(np.float32)
    ropeCs = np.tile(np.concatenate([cs, cs])[None, :], (128, 1)).astype(np.float32)
    ropeSs = np.tile(np.concatenate([-sn, sn])[None, :], (128, 1)).astype(np.float32)
    ident = np.eye(128, dtype=np.float32).astype(ml_dtypes.bfloat16)
    k = np.arange(128)
    mask = (k[None, :] >= k[:, None]).astype(np.float32).astype(ml_dtypes.bfloat16)
    injk = np.zeros((32, 96), np.float32)
    injk[np.arange(32), 64 + np.arange(32)] = 1.0
    injk = injk.astype(ml_dtypes.bfloat16)
    sel = np.zeros((16, 16, 128), np.float32)
    for b in range(16):
        sel[b, b, :] = 1.0
    gq = (np.arange(128) % 16).astype(np.int32).reshape(128, 1)
    return dict(c_ident=ident, c_mask=mask, c_ropeC=np.ascontiguousarray(ropeC), c_ropeS=np.ascontiguousarray(ropeS),
                c_ropeCs=ropeCs, c_ropeSs=ropeSs, c_injk=injk, c_sel=sel.reshape(16, 16 * 128), gq=gq)


def make_in_maps(inp, cores=range(NCORE), pool_c=None, pool_r=None, page_table=None):
    consts = _consts()
    if pool_c is None:
        pool_c = np.ascontiguousarray(inp["cache_ckv"][0]).reshape(-1, 8 * 256)
        pool_r = np.ascontiguousarray(inp["cache_krope"][0]).reshape(-1, 8 * 32)
        page_table = np.asarray(inp["page_table"])
    shared = dict(
        pool_c=pool_c, pool_r=pool_r,
        w_in=np.ascontiguousarray(inp["w_in"][0]), kvg=np.ascontiguousarray(inp["kv_norm_g"]),
        w_uk=np.ascontiguousarray(inp["w_uk"][0]).reshape(256, 512), w_uv=np.ascontiguousarray(inp["w_uv"][0]).reshape(256, 512),
        conv_w=np.ascontiguousarray(inp["conv_w"][0]), conv_b=np.ascontiguousarray(inp["conv_b"]),
        conv_g=np.ascontiguousarray(inp["conv_ln_g"]), conv_bb=np.ascontiguousarray(inp["conv_ln_b"]),
        w_o=np.ascontiguousarray(inp["w_o"][0]), w_xq=np.ascontiguousarray(inp["w_xq"][0]),
        w_xk=np.ascontiguousarray(inp["w_xk"][0]), w_xv=np.ascontiguousarray(inp["w_xv"][0]),
        w_xo=np.ascontiguousarray(inp["w_xo"][0]), w_up=np.ascontiguousarray(inp["w_up"][0]),
        w_down=np.ascontiguousarray(inp["w_down"][0]),
        ln1_g=np.ascontiguousarray(inp["ln1_g"]), ln1_b=np.ascontiguousarray(inp["ln1_b"]),
        ln2_g=np.ascontiguousarray(inp["ln2_g"]), ln2_b=np.ascontiguousarray(inp["ln2_b"]),
        ln3_g=np.ascontiguousarray(inp["ln3_g"]), ln3_b=np.ascontiguousarray(inp["ln3_b"]),
        **consts,
    )
    maps = []
    for c in cores:
        pt = page_table[c * 16:(c + 1) * 16]
        ptr = pt.reshape(16, 8, 8)[:, :, np.arange(128) // 16]
        ptr = np.ascontiguousarray(ptr.transpose(2, 0, 1).reshape(128, 128)).astype(np.int32)
        m = dict(shared)
        m.update(
            xp=np.ascontiguousarray(inp["x_prompt"][2 * c:2 * c + 2]),
            memp=np.ascontiguousarray(inp["mem_prompt"][2 * c:2 * c + 2]),
            xs=np.ascontiguousarray(inp["x_sample"][16 * c:16 * c + 16, 0]),
            ptrep=ptr,
            stc=np.ascontiguousarray(inp["state_conv"][0, 16 * c:16 * c + 16]).reshape(16, 30 * 512),
            cmk=np.ascontiguousarray(inp["cache_mem_k"][0, 16 * c:16 * c + 16]).reshape(16, 256, 1024),
            cmv=np.ascontiguousarray(inp["cache_mem_v"][0, 16 * c:16 * c + 16]).reshape(16, 256, 1024),
        )
        maps.append(m)
    return maps


def assemble(results):
    cat = lambda k: np.concatenate([r[k] for r in results], axis=0)
    y_p = cat("y_p")
    y_s = cat("y_s")[:, None, :]
    return (y_p, y_s, cat("ckv_p")[None], cat("kr_p")[None], cat("conv_p")[None],
            cat("mk_p").reshape(1, -1, 256, 4, 256), cat("mv_p").reshape(1, -1, 256, 4, 256),
            cat("ckv_s")[None, :, None, :], cat("kr_s")[None, :, None, :], cat("conv_s").reshape(1, -1, 30, 512))


def kernel(**inputs):
    inp = {k: np.asarray(v) for k, v in inputs.items()}
    nc = build()
    maps = make_in_maps(inp)
    res = run_bass_kernel_spmd(nc, maps, core_ids=list(range(NCORE)))
    outs = assemble(res.results)
    return tuple(np.ascontiguousarray(o, dtype=np.float32) for o in outs)
```

```python
import math
import numpy as np
import ml_dtypes
import concourse.bass as bass
import concourse.mybir as mybir
from concourse.bass_utils import run_bass_kernel_spmd

F32 = mybir.dt.float32
BF16 = mybir.dt.bfloat16
I32 = mybir.dt.int32
AF = mybir.ActivationFunctionType
ALU = mybir.AluOpType
AX = mybir.AxisListType

ALPHA = 2.0 ** 0.25
MLA_SCALE = 96.0 ** -0.5
MEM_SCALE = 1.0 / 16.0
EPS = 1e-5
NCORE = 8
SEQ = 2048
NBLK = 4
NPOOL = 10240
DEBUG = False
PROFILE = False


class Eng:
    def __init__(self, name, eng, sem):
        self.name, self.eng, self.sem, self.count, self.seen = name, eng, sem, 0, {}


class DSem:
    def __init__(self, sem):
        self.sem, self.count = sem, 0


_TINY = ("small", "stats", "mvar", "P2bf", "dhl", "pnew", "idx", "idxc", "ptb", "gqt", "onesf", "ol_bf", "krnT", "cnT")


def _tiny(k):
    n = k[0] if isinstance(k, tuple) else k
    return n in _TINY


class Trk:
    def __init__(self, nc):
        self.nc = nc
        self.last_w = {}
        self.readers = {}
        self.nsem = 0
        self.stage = None

    def sem(self, name):
        self.nsem += 1
        return self.nc.alloc_semaphore(name)

    def dsem(self, name):
        return DSem(self.sem(name))

    def _wait(self, E, R, W):
        best = {}
        def add(tk, raw=False):
            obj, val = tk
            if obj is E and not (raw and E.name != "pe" and E.name != "sp"):
                return
            if val > best.get(obj, 0):
                best[obj] = val
        for k in R:
            if k in self.last_w:
                add(self.last_w[k], raw=_tiny(k))
            if isinstance(k, tuple) and k[0] in ("pf", "pb"):
                for obj, val in self.readers.get(k, {}).items():
                    add((obj, val))
        for k in W:
            if k in self.last_w:
                add(self.last_w[k])
            for obj, val in self.readers.get(k, {}).items():
                add((obj, val))
        for obj, val in best.items():
            if E.seen.get(obj, 0) >= val:
                continue
            assert obj.count >= val, "wait on a signal that is not issued yet"
            E.eng.wait_ge(obj.sem, val * (16 if isinstance(obj, DSem) else 1))
            E.seen[obj] = val

    def _commit(self, tk, R, W):
        obj, val = tk
        for k in R:
            d = self.readers.setdefault(k, {})
            if d.get(obj, 0) < val:
                d[obj] = val
        for k in W:
            self.last_w[k] = tk
            self.readers[k] = {}

    def retire(self, old_keys, new_keys):
        merged = {}
        for k in old_keys:
            if k in self.last_w:
                o, v = self.last_w[k]
                merged[o] = max(merged.get(o, 0), v)
            for o, v in self.readers.get(k, {}).items():
                merged[o] = max(merged.get(o, 0), v)
        for nk in new_keys:
            d = self.readers.setdefault(nk, {})
            for o, v in merged.items():
                d[o] = max(d.get(o, 0), v)

    def op(self, E, fn, R=(), W=(), sig=True):
        self._wait(E, R, W)
        ins = fn()
        if self.stage is not None:
            try:
                ins.annotate(self.stage)
            except Exception:
                pass
        if sig:
            E.count += 1
            ins.then_inc(E.sem, 1)
            tk = (E, E.count)
        else:
            tk = (E, E.count + 1)
        self._commit(tk, R, W)

    def dma(self, Q, ds, fn, R=(), W=()):
        self._wait(Q, R, W)
        ins = fn()
        ds.count += 1
        ins.then_inc(ds.sem, 16)
        self._commit((ds, ds.count), R, W)


class _Stop(Exception):
    pass


def build(n_pool=NPOOL, do_prompt=True, do_sample=True, nseq=2, nblk=NBLK, stop=None):
    nc = bass.Bass("TRN2", target_bir_lowering=False)

    def chk(n):
        if stop is not None and n >= stop:
            raise _Stop()

    T = Trk(nc)
    _dummy = [T.sem(f"dummy{i}") for i in range(6)]
    PE = Eng("pe", nc.tensor, T.sem("s_pe"))
    ACT = Eng("act", nc.scalar, T.sem("s_act"))
    DVE = Eng("dve", nc.vector, T.sem("s_dve"))
    POOL = Eng("pool", nc.gpsimd, T.sem("s_pool"))
    SP = Eng("sp", nc.sync, T.sem("s_sp"))

    def din(name, shape, dt=F32):
        return nc.dram_tensor(name, list(shape), dt, kind="ExternalInput").ap()

    def dout(name, shape, dt=F32):
        return nc.dram_tensor(name, list(shape), dt, kind="ExternalOutput").ap()

    def dscr(name, shape, dt=BF16):
        return nc.dram_tensor(name, list(shape), dt, kind="Internal").ap()

    xp = din("xp", [2, SEQ, 1024])
    memp = din("memp", [2, 256, 1024])
    xs = din("xs", [16, 1024])
    pool_c = din("pool_c", [n_pool * 16, 2048])
    pool_r = din("pool_r", [n_pool * 16, 256])
    ptrep = din("ptrep", [128, 128], I32)
    gq = din("gq", [128, 1], I32)
    stc = din("stc", [16, 30 * 512])
    cmk = din("cmk", [16, 256, 1024])
    cmv = din("cmv", [16, 256, 1024])
    w_in = din("w_in", [1024, 2080])
    kvg = din("kvg", [1, 256])
    w_uk = din("w_uk", [256, 512])
    w_uv = din("w_uv", [256, 512])
    conv_w = din("conv_w", [31, 512])
    conv_b = din("conv_b", [1, 512])
    conv_g = din("conv_g", [1, 512])
    conv_bb = din("conv_bb", [1, 512])
    w_o = din("w_o", [1024, 1024])
    w_xq = din("w_xq", [1024, 1024])
    w_xk = din("w_xk", [1024, 1024])
    w_xv = din("w_xv", [1024, 1024])
    w_xo = din("w_xo", [1024, 1024])
    w_up = din("w_up", [1024, 4096])
    w_down = din("w_down", [4096, 1024])
    lnv = [din(n, [1, 1024]) for n in ("ln1_g", "ln1_b", "ln2_g", "ln2_b", "ln3_g", "ln3_b")]
    c_ident = din("c_ident", [128, 128], BF16)
    c_mask = din("c_mask", [128, 128], BF16)
    c_ropeC = din("c_ropeC", [128, 16 * 32])
    c_ropeS = din("c_ropeS", [128, 16 * 32])
    c_ropeCs = din("c_ropeCs", [128, 32])
    c_ropeSs = din("c_ropeSs", [128, 32])
    c_injk = din("c_injk", [32, 96], BF16)
    c_sel = din("c_sel", [16, 16 * 128])

    y_p = dout("y_p", [2, SEQ, 1024])
    y_s = dout("y_s", [16, 1024])
    ckv_p = dout("ckv_p", [2, SEQ, 256])
    kr_p = dout("kr_p", [2, SEQ, 32])
    conv_p = dout("conv_p", [2, 30, 512])
    mk_p = dout("mk_p", [2, 256, 1024])
    mv_p = dout("mv_p", [2, 256, 1024])
    ckv_s = dout("ckv_s", [16, 256])
    kr_s = dout("kr_s", [16, 32])
    conv_s = dout("conv_s", [16, 30 * 512])

    slabs = {}
    cvt_sems = [T.dsem(f"cv{i}") for i in range(4)]
    cvt_i = [0]

    def mk_slab(name, src3):
        a, b = src3.shape[1], src3.shape[2]
        scr = dscr("ws_" + name, [128, a * b])
        ds = cvt_sems[cvt_i[0] % 4]
        cvt_i[0] += 1
        T.dma(POOL, ds, lambda: nc.gpsimd.dma_start(out=scr.rearrange("p (a b) -> p a b", b=b), in_=src3),
              W=[("ws", name)])
        slabs[name] = (scr, a, b)

    def cols(w, c0, c1):
        return w[:, c0:c1].rearrange("(c p) n -> p c n", p=128)

    mk_slab("in0", cols(w_in, 0, 512))
    mk_slab("in1", cols(w_in, 512, 1056))
    mk_slab("in2", cols(w_in, 1056, 1568))
    mk_slab("in3", cols(w_in, 1568, 2080))
    rest_done = [False]

    def mk_rest_slabs():
        if rest_done[0]:
            return
        rest_done[0] = True
        for nm, w in (("o", w_o), ("xq", w_xq), ("xo", w_xo)):
            mk_slab(nm + "0", cols(w, 0, 512))
            mk_slab(nm + "1", cols(w, 512, 1024))
        for g in range(8):
            mk_slab(f"up{g}", cols(w_up, g * 512, (g + 1) * 512))
        for fh in range(2):
            for qd in range(4):
                mk_slab(f"dn{fh}{qd}", w_down[fh * 2048:(fh + 1) * 2048, qd * 256:(qd + 1) * 256]
                        .rearrange("(f p) n -> p f n", p=128))
        for nm, w in (("xk", w_xk), ("xv", w_xv)):
            mk_slab(nm + "0", cols(w, 0, 512))
            mk_slab(nm + "1", cols(w, 512, 1024))

    def sb(name, shape, dt):
        return nc.alloc_sbuf_tensor(name, list(shape), dt)

    ident = sb("ident", [128, 128], BF16)
    mask = sb("mask", [128, 128], BF16)
    ones = sb("ones", [128, 128], BF16)
    injk = sb("injk", [32, 96], BF16)
    ropeC = sb("ropeC", [128, 16, 32], F32)
    ropeS = sb("ropeS", [128, 16, 32], F32)
    ropeCs = sb("ropeCs", [128, 32], F32)
    ropeSs = sb("ropeSs", [128, 32], F32)
    kvg_b = sb("kvg_b", [128, 256], F32)
    WukP = sb("WukP", [128, 2, 8, 96], BF16)
    Wuv = sb("Wuv", [128, 2, 512], BF16)
    cwT = sb("cwT", [128, 4, 31], F32)
    cvec = sb("cvec", [128, 3, 4], F32)
    bvec = sb("bvec", [128, 2, 1024], F32)
    WR_N = 3
    WR = sb("WR", [128, WR_N, 4352], BF16)
    KT = sb("KT", [128, 8, SEQ], BF16)
    Vaug = sb("Vaug", [128, 16, 4, 192], BF16)
    mkT = sb("mkT", [128, 8, 256], BF16)
    mvb = sb("mvb", [128, 2, 1024], BF16)
    Rb = sb("Rb", [128, 4, 1024], F32)
    xring = sb("xring", [128, 1, 1024], F32)
    Cb = sb("Cb", [128, 2, 1024], BF16)
    Db = sb("Db", [128, 8, 512], BF16)
    uT = sb("uT", [128, 4, 30 + 512], F32)
    uTb = sb("uTb", [128, 4, 30 + 512], BF16)
    Pr = sb("Pr", [128, 4, 512], BF16)
    QT = sb("QT", [128, 8, 512], BF16)
    mixT = sb("mixT", [128, 8, 512], BF16)
    hT = sb("hT", [128, 16, 512], BF16)
    qtm = hT[:, 0:6, :].rearrange("p a b -> p (a b)").rearrange("p (t e) -> p t e", e=768)
    ckb = sb("ckb", [128, 4, 288], BF16)
    ckT = sb("ckT", [128, 2, 512], BF16)
    krT = sb("krT", [32, 512], BF16)
    Cbx = hT[:, 0:8, :].rearrange("p a b -> p (a b)").rearrange("p (t e) -> p t e", e=1024)
    tmpA = sb("tmpA", [128, 1024], F32)
    tmpB = sb("tmpB", [128, 1024], F32)
    cacc = Rb[:, :, 0:512]
    csq = Pr
    cyb = QT
    small = sb("small", [128, 64], F32)
    stats = sb("stats", [128, 2, 6], F32)
    mvar = sb("mvar", [128, 2], F32)

    psf = nc.alloc_psum_tensor("psf", [128, 6, 512], F32)
    psb = nc.alloc_psum_tensor("psb", [128, 2, 1024], BF16)
    pf_i = [0]
    pb_i = [0]

    def nf():
        i = pf_i[0] % 4
        pf_i[0] += 1
        return i

    def nf2():
        while pf_i[0] % 2:
            pf_i[0] += 1
        i = pf_i[0] % 4
        pf_i[0] += 2
        return i

    ph_i = [0]

    def nh():
        i = 4 + ph_i[0] % 2
        ph_i[0] += 1
        return i

    def nb():
        i = pb_i[0] % 2
        pb_i[0] += 1
        return i

    ld_sems = [T.dsem(f"ld{i}") for i in range(8)]
    ld_i = [0]
    st_sems = [T.dsem(f"st{i}") for i in range(8)]
    st_i = [0]

    def load(out, in_, W, R=(), q=None):
        ds = ld_sems[ld_i[0] % 8]
        ld_i[0] += 1
        if q is POOL:
            T.dma(POOL, ds, lambda: nc.gpsimd.dma_start(out=out, in_=in_), R=R, W=W)
        else:
            T.dma(SP, ds, lambda: nc.sync.dma_start(out=out, in_=in_), R=R, W=W)

    def store(out, in_, R, W=()):
        ds = st_sems[st_i[0] % 8]
        st_i[0] += 1
        T.dma(SP, ds, lambda: nc.sync.dma_start(out=out, in_=in_), R=R, W=W)

    wr_sems = [T.dsem(f"wr{i}") for i in range(WR_N)]
    wr_i = [0]

    def wslab(name):
        scr, a, b = slabs[name]
        i = wr_i[0] % WR_N
        wr_i[0] += 1
        view = WR[:, i, 0:a * b].rearrange("p (a b) -> p a b", b=b)
        T.dma(SP, wr_sems[i], lambda: nc.sync.dma_start(out=view, in_=scr.rearrange("p (a b) -> p a b", b=b)),
              R=[("ws", name)], W=[("WR", i)])
        return view, ("WR", i)

    def mm(out, lhsT, rhs, start, stop, R, W, sig=False):
        T.op(PE, lambda: nc.tensor.matmul(out, lhsT, rhs, start=start, stop=stop), R=R, W=W, sig=sig)

    def tr(out, in_, R, W, sig=False):
        k = in_.shape[0]
        T.op(PE, lambda: nc.tensor.transpose(out, in_, ident[0:k, 0:k]), R=list(R) + ["ident"], W=W, sig=sig)

    def act(out, in_, func, R, W, bias=0.0, scale=1.0, accum=None):
        if accum is None:
            T.op(ACT, lambda: nc.scalar.activation(out, in_, func, bias=bias, scale=scale), R=R, W=W)
        else:
            T.op(ACT, lambda: nc.scalar.activation(out, in_, func, bias=bias, scale=scale, accum_out=accum), R=R, W=W)

    def tt(E, out, in0, in1, op, R, W):
        T.op(E, lambda: E.eng.tensor_tensor(out, in0, in1, op), R=R, W=W)

    def ts(E, out, in0, s1, s2, op0, op1, R, W):
        if op1 is None:
            T.op(E, lambda: E.eng.tensor_scalar(out, in0, s1, None, op0), R=R, W=W)
        else:
            T.op(E, lambda: E.eng.tensor_scalar(out, in0, s1, s2, op0, op1), R=R, W=W)

    def stt(E, out, in0, scalar, in1, op0, op1, R, W):
        T.op(E, lambda: E.eng.scalar_tensor_tensor(out, in0, scalar, in1, op0, op1), R=R, W=W)

    def cp(E, out, in_, R, W):
        if E is ACT:
            T.op(ACT, lambda: nc.scalar.activation(out, in_, AF.Identity), R=R, W=W)
        else:
            T.op(E, lambda: E.eng.tensor_copy(out, in_), R=R, W=W)

    def bvload(idx, slot):
        load(bvec[:, slot, :], lnv[idx].partition_broadcast(128), W=[("bvec", slot)])

    load(ident[:], c_ident, W=["ident"])
    load(mask[:], c_mask, W=["mask"])
    load(injk[:], c_injk, W=["injk"])
    load(ropeC[:], c_ropeC.rearrange("p (a b) -> p a b", b=32), W=["rope"])
    load(ropeS[:], c_ropeS.rearrange("p (a b) -> p a b", b=32), W=["rope"])
    load(ropeCs[:], c_ropeCs, W=["ropes"])
    load(ropeSs[:], c_ropeSs, W=["ropes"])
    load(kvg_b[:], kvg.partition_broadcast(128), W=["kvg"])
    T.op(DVE, lambda: nc.vector.memset(ones[:], 1.0), W=["ones"])
    T.op(DVE, lambda: nc.vector.memset(WukP[:], 0.0), W=["WukP"])
    T.op(DVE, lambda: nc.vector.memset(Vaug[:], 1.0), W=["Vaug_init"])
    for lc in range(2):
        load(WukP[:, lc, :, 0:64], w_uk[lc * 128:(lc + 1) * 128, :].rearrange("p (h e) -> p h e", e=64),
             R=["WukP"], W=[("WukPl", lc)], q=POOL)
        load(Wuv[:, lc, :], w_uv[lc * 128:(lc + 1) * 128, :], W=[("Wuv", lc)], q=POOL)
    WUK = [("WukPl", 0), ("WukPl", 1), "WukP"]
    WUV = [("Wuv", 0), ("Wuv", 1)]
    with nc.allow_non_contiguous_dma(reason="tiny transposed parameter loads"):
        for cc in range(4):
            load(cwT[:, cc, :], conv_w[:, cc * 128:(cc + 1) * 128].rearrange("k p -> p k"), W=["cwT"])
            for i, v in enumerate((conv_b, conv_g, conv_bb)):
                load(cvec[:, i, cc:cc + 1], v[:, cc * 128:(cc + 1) * 128].rearrange("o p -> p o"), W=[("cvec", i)])
    CV = [("cvec", 0), ("cvec", 1), ("cvec", 2)]
    for cc in range(4):
        stg = hT[:, (cc % 2) * 8:(cc % 2) * 8 + 8, :].rearrange("p a b -> p (a b)")[:, 0:31 * 128].rearrange("p (k c) -> p k c", c=128)
        for k in range(31):
            ts(DVE, stg[:, k, :], ident[:, :], cwT[:, cc, k:k + 1], None, ALU.mult, None, R=["ident", "cwT"], W=[("cdstg", cc % 2)])
        scr = dscr(f"ws_cd{cc}", [128, 31 * 128])
        store(scr.rearrange("p (k c) -> p k c", c=128), stg, R=[("cdstg", cc % 2)], W=[("ws", f"cd{cc}")])
        slabs[f"cd{cc}"] = (scr, 31, 128)

    def rstd_from_var(out, var_ap, R, W):
        act(out, var_ap, AF.Sqrt, R=R, W=W, bias=EPS, scale=1.0)
        T.op(DVE, lambda: nc.vector.reciprocal(out, out), R=W, W=W)

    def layer_norm_rows(t_key, rows, src, dst, gi, bi, bf_out, bf_key):
        for hlf in range(2):
            T.op(DVE, lambda h=hlf: nc.vector.bn_stats(stats[0:rows, h, :], src[:, h * 512:(h + 1) * 512]),
                 R=[t_key], W=[("stats", hlf)])
        T.op(DVE, lambda: nc.vector.bn_aggr(mvar[0:rows, :], stats[0:rows, :, :]),
             R=[("stats", 0), ("stats", 1)], W=["mvar"])
        rstd_from_var(small[0:rows, 0:1], mvar[0:rows, 1:2], R=["mvar"], W=[("small", 0)])
        stt(DVE, small[0:rows, 1:2], mvar[0:rows, 0:1], -1.0, small[0:rows, 0:1], ALU.mult, ALU.mult,
            R=["mvar", ("small", 0)], W=[("small", 1)])
        act(dst, src, AF.Identity, R=[t_key, ("small", 0), ("small", 1)], W=[t_key],
            bias=small[0:rows, 1:2], scale=small[0:rows, 0:1])
        tt(POOL, dst, dst, bvec[0:rows, 0, :], ALU.mult, R=[t_key, ("bvec", 0)], W=[t_key])
        tt(POOL, dst, dst, bvec[0:rows, 1, :], ALU.add, R=[t_key, ("bvec", 1)], W=[t_key])
        if bf_out is not None:
            cp(ACT, bf_out, dst, R=[t_key], W=[bf_key])

    def rope_rows(rows, src3, dst3, C2, S2, tmp3, R, W, tmpkey):
        tt(DVE, tmp3[:, :, 0:16], src3[:, :, 16:32], S2[:, :, 0:16], ALU.mult, R=R, W=[tmpkey])
        tt(DVE, tmp3[:, :, 16:32], src3[:, :, 0:16], S2[:, :, 16:32], ALU.mult, R=R, W=[tmpkey])
        tt(DVE, tmp3[:, :, 32:64], src3, C2, ALU.mult, R=R, W=[tmpkey])
        tt(POOL, dst3, tmp3[:, :, 0:32], tmp3[:, :, 32:64], ALU.add, R=[tmpkey], W=W)

    def mem_kv_seq(s):
        memT = Db
        for mt in range(2):
            load(Cb[:, mt, :], memp[s, mt * 128:(mt + 1) * 128, :], W=[("Cb", mt)], q=POOL)
        for c in range(8):
            b = nb()
            for mt in range(2):
                tr(psb[:, b, mt * 128:(mt + 1) * 128], Cb[:, mt, c * 128:(c + 1) * 128],
                   R=[("Cb", mt)], W=[("pb", b)], sig=(mt == 1))
            cp(DVE, memT[:, c, 0:256], psb[:, b, 0:256], R=[("pb", b)], W=[("Db", c)])
        DBK = [("Db", c) for c in range(8)]
        chk(0.3)
        for which, dst in (("xk", mk_p), ("xv", mv_p)):
            cf = (lambda x: x) if which == "xk" else (lambda x: 0.97 + 0.03 * x)
            views = [wslab(which + "0"), wslab(which + "1")]
            chk(cf(0.5))
            for mt in range(2):
                b2 = nf2()
                for hf in range(2):
                    wv, wk = views[hf]
                    for c in range(8):
                        mm(psf[:, b2 + hf, :], memT[:, c, mt * 128:(mt + 1) * 128], wv[:, c, :], c == 0, c == 7,
                           R=DBK + [wk], W=[("pf", b2 + hf)], sig=(c == 7))
                chk(cf(0.7))
                t_ = tmpA if mt == 0 else tmpB
                tk = "tmpA" if mt == 0 else "tmpB"
                for hf in range(2):
                    cp(ACT, t_[:, hf * 512:(hf + 1) * 512], psf[:, b2 + hf, :], R=[("pf", b2 + hf)], W=[tk])
                chk(cf(0.8))
                store(dst[s, mt * 128:(mt + 1) * 128, :], t_[:, :], R=[tk])
                chk(cf(0.9))
                if which == "xv":
                    for hf in range(2):
                        cp(DVE, mvb[:, mt, hf * 512:(hf + 1) * 512], t_[:, hf * 512:(hf + 1) * 512],
                           R=[tk], W=[("mvb", mt)])
            chk(cf(0.95))
            if which == "xk":
                for e in range(8):
                    wv, wk = views[e // 4]
                    b1 = nf()
                    for c in range(8):
                        mm(psf[:, b1, 0:256], wv[:, c, (e % 4) * 128:(e % 4 + 1) * 128], memT[:, c, 0:256], c == 0, c == 7,
                           R=DBK + [wk], W=[("pf", b1)], sig=(c == 7))
                    cp(DVE, mkT[:, e, :], psf[:, b1, 0:256], R=[("pf", b1)], W=[("mkT", e)])
                chk(0.97)

    xpref = set()
    MKT = [("mkT", e) for e in range(8)]
    MVB = [("mvb", 0), ("mvb", 1)]

    def prompt_block(s, j):
        t0 = j * 512
        DBK = [("Db", c) for c in range(8)]
        T.retire([("hT", i) for i in range(16)] + [("Cbx", t) for t in range(4)], [("qtm", t) for t in range(4)])
        if PROFILE:
            T.stage = "s01"
        if j == 0:
            T.op(POOL, lambda: nc.gpsimd.memset(uTb[:, :, 0:30], 0.0), W=["uT_head"])
        for t in range(4):
            if (s, j, t) not in xpref:
                load(Cb[:, t % 2, :], xp[s, t0 + t * 128:t0 + (t + 1) * 128, :], W=[("Cb", t % 2)], q=POOL)
            for c in range(8):
                if c % 4 == 0:
                    b = nb()
                tr(psb[:, b, (c % 4) * 128:(c % 4 + 1) * 128], Cb[:, t % 2, c * 128:(c + 1) * 128],
                   R=[("Cb", t % 2)], W=[("pb", b)], sig=(c % 4 == 3))
                if c % 4 == 3:
                    c0 = c - 3
                    cp(ACT if c0 else DVE,
                       Db[:, c0:c0 + 4, t * 128:(t + 1) * 128],
                       psb[:, b, 0:512].rearrange("p (a b) -> p a b", b=128),
                       R=[("pb", b)], W=[("Db", cc) for cc in range(c0, c0 + 4)])
        chk(2)
        if PROFILE:
            T.stage = "s2a"
        v0, k0 = wslab("in0")
        v1, k1 = wslab("in1")
        for t in range(4):
            bA = nf()
            bB = nf()
            bC = nf()
            for (ob, o0, o1, vv, kk, c0_, c1_) in (
                    (bA, 0, 480, v0, k0, 0, 480), (bB, 0, 32, v0, k0, 480, 512), (bB, 32, 288, v1, k1, 0, 256),
                    (bB, 288, 320, v1, k1, 512, 544), (bC, 0, 256, v1, k1, 256, 512)):
                for c in range(8):
                    mm(psf[:, ob, o0:o1], Db[:, c, t * 128:(t + 1) * 128], vv[:, c, c0_:c1_], c == 0, c == 7,
                       R=DBK + [kk], W=[("pf", ob)], sig=(c == 7))
            PQ = [("pf", bA), ("pf", bB), ("pf", bC)]
            tile_i = j * 4 + t
            qd3 = qtm[:, t, :].rearrange("p (h e) -> p h e", e=96)
            for (bk, h0, nh_) in ((bA, 0, 5), (bB, 5, 3)):
                q3 = psf[:, bk, 0:nh_ * 96].rearrange("p (h e) -> p h e", e=96)
                cp(ACT, qd3[:, h0:h0 + nh_, 0:64], q3[:, :, 0:64], R=PQ, W=[("qtm", t)])
                tmp3 = tmpA[:, 0:nh_ * 64].rearrange("p (h e) -> p h e", e=64)
                rope_rows(128, q3[:, :, 64:96], qd3[:, h0:h0 + nh_, 64:96],
                          ropeC[:, tile_i:tile_i + 1, :].to_broadcast([128, nh_, 32]),
                          ropeS[:, tile_i:tile_i + 1, :].to_broadcast([128, nh_, 32]),
                          tmp3, R=PQ + ["rope"], W=[("qtm", t)], tmpkey="tmpA")
            pckv = psf[:, bC, 0:256]
            b1 = bB
            cp(DVE, tmpB[:, 768:1024], pckv, R=PQ, W=["tmpB"])
            pckv = tmpB[:, 768:1024]
            act(tmpB[:, 0:256], pckv, AF.Square, R=["tmpB"], W=["tmpB"])
            T.op(DVE, lambda: nc.vector.reduce_sum(small[:, 2:3], tmpB[:, 0:256], axis=AX.X), R=["tmpB"], W=["tmpB"])
            act(small[:, 3:4], small[:, 2:3], AF.Sqrt, R=["tmpB"], W=[("small", 3)], bias=EPS, scale=1.0 / 256.0)
            T.op(DVE, lambda: nc.vector.reciprocal(small[:, 3:4], small[:, 3:4]), R=[("small", 3)], W=[("small", 3)])
            stt(DVE, tmpB[:, 256:512], pckv, small[:, 3:4], kvg_b[:, :], ALU.mult, ALU.mult,
                R=PQ + [("small", 3), "kvg", "tmpB"], W=["tmpB"])
            store(ckv_p[s, t0 + t * 128:t0 + (t + 1) * 128, :], tmpB[:, 256:512], R=["tmpB"])
            if DEBUG and t == 1:
                store(y_p[s, 0:128, 0:64], small[:, :], R=["tmpB", ("small", 3)])
                store(y_p[s, 0:128, 64:320], tmpB[:, 768:1024], R=["tmpB"])
                store(y_p[s, 0:128, 320:576], tmpB[:, 256:512], R=["tmpB"])
            cp(ACT, ckb[:, t, 0:256], tmpB[:, 256:512], R=["tmpB"], W=[("ckb", t)])
            kr3 = tmpB[:, 512:544].rearrange("p (h e) -> p h e", e=32)
            rope_rows(128, psf[:, b1, 288:320].rearrange("p (h e) -> p h e", e=32), kr3,
                      ropeC[:, tile_i:tile_i + 1, :], ropeS[:, tile_i:tile_i + 1, :],
                      tmpB[:, 576:640].rearrange("p (h e) -> p h e", e=64),
                      R=[("pf", b1), "rope"], W=["tmpB"], tmpkey="tmpB")
            store(kr_p[s, t0 + t * 128:t0 + (t + 1) * 128, :], tmpB[:, 512:544], R=["tmpB"])
            cp(ACT, ckb[:, t, 256:288], tmpB[:, 512:544], R=["tmpB"], W=[("ckb", t)])
        chk(3)
        QTM = [("qtm", t) for t in range(4)]
        CKB = [("ckb", t) for t in range(4)]
        if PROFILE:
            T.stage = "s2b"
        for h in range(8):
            b = nb()
            for t in range(4):
                tr(psb[0:96, b, t * 128:(t + 1) * 128], qtm[:, t, h * 96:(h + 1) * 96], R=QTM, W=[("pb", b)], sig=(t == 3))
            cp(ACT if h % 2 else DVE, QT[0:96, h, :], psb[0:96, b, 0:512], R=[("pb", b)], W=[("QT", h)])
        for lc in range(2):
            b = nb()
            for t in range(4):
                tr(psb[:, b, t * 128:(t + 1) * 128], ckb[:, t, lc * 128:(lc + 1) * 128], R=CKB, W=[("pb", b)], sig=(t == 3))
            cp(DVE, ckT[:, lc, :], psb[:, b, 0:512], R=[("pb", b)], W=[("ckT", lc)])
        b = nb()
        for t in range(4):
            tr(psb[0:32, b, t * 128:(t + 1) * 128], ckb[:, t, 256:288], R=CKB, W=[("pb", b)], sig=(t == 3))
        cp(DVE, krT[:, :], psb[0:32, b, 0:512], R=[("pb", b)], W=["krT"])
        CKT = [("ckT", 0), ("ckT", 1)]
        if PROFILE:
            T.stage = "s2c"
        for h in range(8):
            b1 = nf()
            mm(psf[0:96, b1, :], WukP[:, 0, h, :], ckT[:, 0, :], True, False, R=CKT + WUK, W=[("pf", b1)])
            mm(psf[0:96, b1, :], WukP[:, 1, h, :], ckT[:, 1, :], False, False, R=CKT + WUK, W=[("pf", b1)])
            mm(psf[0:96, b1, :], injk[:, :], krT[:, :], False, True, R=["krT", "injk"], W=[("pf", b1)], sig=True)
            cp(ACT if h % 2 else DVE, KT[0:96, h, t0:t0 + 512], psf[0:96, b1, :], R=[("pf", b1)], W=[("KT", h, j)])
        if PROFILE:
            T.stage = "s2d"
        for t in range(4):
            b1 = nf()
            for lc in range(2):
                mm(psf[:, b1, :], ckT[:, lc, t * 128:(t + 1) * 128], Wuv[:, lc, :], lc == 0, lc == 1,
                   R=CKT + WUV, W=[("pf", b1)], sig=(lc == 1))
            kt = j * 4 + t
            pv = psf[:, b1, :].rearrange("p (a two e) -> p a two e", two=2, e=64)
            cp(DVE, Vaug[:, kt, :, 0:64], pv[:, :, 0, :], R=[("pf", b1), "Vaug_init"], W=[("V", kt)])
            cp(ACT, Vaug[:, kt, :, 128:192], pv[:, :, 1, :], R=[("pf", b1), "Vaug_init"], W=[("V", kt)])
        chk(4)
        if PROFILE:
            T.stage = "s2e"
        v2, k2 = wslab("in2")
        v3, k3 = wslab("in3")
        for cc in range(4):
            ba = nf()
            bb = nf()
            for c in range(8):
                mm(psf[:, ba, :], v2[:, c, cc * 128:(cc + 1) * 128], Db[:, c, :], c == 0, c == 7, R=DBK + [k2], W=[("pf", ba)])
                mm(psf[:, bb, :], v3[:, c, cc * 128:(cc + 1) * 128], Db[:, c, :], c == 0, c == 7, R=DBK + [k3], W=[("pf", bb)], sig=(c == 7))
            act(tmpA[:, 0:512], psf[:, bb, :], AF.Sigmoid, R=[("pf", bb)], W=["tmpA"])
            tt(DVE, uTb[:, cc, 30:542], psf[:, ba, :], tmpA[:, 0:512], ALU.mult, R=[("pf", ba), "tmpA", "uT_head"], W=[("uT", cc)])
        if j == nblk - 1:
            b2 = nf2()
            for c in range(8):
                xTt = Db[:, c, 384:512]
                mm(psf[:, b2, :], xTt, v2[:, c, :], c == 0, c == 7, R=DBK + [k2], W=[("pf", b2)])
                mm(psf[:, b2 + 1, :], xTt, v3[:, c, :], c == 0, c == 7, R=DBK + [k3], W=[("pf", b2 + 1)], sig=(c == 7))
            act(tmpA[:, 0:512], psf[:, b2 + 1, :], AF.Sigmoid, R=[("pf", b2 + 1)], W=["tmpA"])
            tt(DVE, tmpA[:, 512:1024], psf[:, b2, :], tmpA[:, 0:512], ALU.mult, R=[("pf", b2), "tmpA"], W=["tmpA"])
            store(conv_p[s, :, :], tmpA[98:128, 512:1024], R=["tmpA"])
        chk(5)
        if PROFILE:
            T.stage = "s3"
        nkt = 4 * j + 4
        pairs = [(h, kt) for h in range(8) for kt in range(nkt)]
        bos = {}
        st_info = {}

        def emit_S(h, kt):
            r = kt - 4 * j
            q0 = 0 if r <= 0 else r * 128
            n = 512 - q0
            bs = nf()
            mm(psf[:, bs, 0:n], KT[0:96, h, kt * 128:(kt + 1) * 128], QT[0:96, h, q0:512], True, True,
               R=[("KT", h, kt // 4), ("QT", h)], W=[("pf", bs)], sig=True)
            pi = (h * 64 + kt) % 4
            act(Pr[:, pi, 0:n], psf[:, bs, 0:n], AF.Exp, R=[("pf", bs)], W=[("Pr", pi)], scale=MLA_SCALE)
            if r >= 0:
                tt(POOL, Pr[:, pi, 0:128], Pr[:, pi, 0:128], mask[:, :], ALU.mult, R=[("Pr", pi), "mask"], W=[("Pr", pi)])
            st_info[(h, kt)] = (pi, q0, n)

        def emit_PV(h, kt):
            pi, q0, n = st_info.pop((h, kt))
            if kt == 0:
                bos[h] = nh()
            bo = bos[h]
            odd = h % 2
            lhs = Vaug[:, kt, h // 2, 64:192] if odd else Vaug[:, kt, h // 2, 0:128]
            mm(psf[:, bo, q0:512], lhs, Pr[:, pi, 0:n], kt == 0, kt == nkt - 1,
               R=[("Pr", pi), ("V", kt), "Vaug_init"], W=[("pf", bo)], sig=True)
            if kt == nkt - 1:
                if h % 2 == 0:
                    cp(DVE, tmpA[0:64, 0:512], psf[64:128, bo, :], R=[("pf", bo)], W=["tmpA"])
                    T.op(DVE, lambda: nc.vector.reciprocal(tmpA[0:64, 0:512], tmpA[0:64, 0:512]), R=["tmpA"], W=["tmpA"])
                    tt(DVE, mixT[0:64, h // 2, :], psf[0:64, bo, :], tmpA[0:64, 0:512], ALU.mult,
                       R=[("pf", bo), "tmpA"], W=[("mixT", h // 2)])
                else:
                    cp(DVE, tmpB[64:128, 0:512], psf[0:64, bo, :], R=[("pf", bo)], W=["tmpB"])
                    T.op(DVE, lambda: nc.vector.reciprocal(tmpB[64:128, 0:512], tmpB[64:128, 0:512]), R=["tmpB"], W=["tmpB"])
                    tt(DVE, mixT[64:128, h // 2, :], psf[64:128, bo, :], tmpB[64:128, 0:512], ALU.mult,
                       R=[("pf", bo), "tmpB"], W=[("mixT", h // 2)])

        emit_S(*pairs[0])
        for i in range(len(pairs)):
            if i + 1 < len(pairs):
                emit_S(*pairs[i + 1])
            emit_PV(*pairs[i])
        chk(6)
        if PROFILE:
            T.stage = "s4"
        for cc in range(4):
            wv, wk = wslab(f"cd{cc}")
            b1 = nf()
            for k in range(31):
                mm(psf[:, b1, :], wv[:, k, :], uTb[:, cc, k:k + 512], k == 0, k == 30,
                   R=[("uT", cc), "uT_head", wk], W=[("pf", b1)], sig=(k == 30))
            act(cacc[:, cc, :], psf[:, b1, :], AF.Identity, R=[("pf", b1)] + CV, W=[("Rb", cc)], bias=cvec[:, 0, cc:cc + 1])
            if j < nblk - 1:
                cp(POOL, tmpA[:, 512 + cc * 32:512 + cc * 32 + 30], uTb[:, cc, 512:542], R=[("uT", cc)], W=[("uhc", cc)])
            cp(ACT, cyb[:, cc, :], cacc[:, cc, :], R=[("Rb", cc)], W=[("QT", cc)])
            act(csq[:, cc, :], cacc[:, cc, :], AF.Square, R=[("Rb", cc)], W=[("Pr", cc)])
        if j < nblk - 1:
            for cc in range(4):
                cp(POOL, uTb[:, cc, 0:30], tmpA[:, 512 + cc * 32:512 + cc * 32 + 30], R=[("uhc", cc)], W=["uT_head"])
        bm = nf()
        bq = nf()
        for cc in range(4):
            mm(psf[:, bm, :], ones[:, :], cyb[:, cc, :], cc == 0, cc == 3, R=[("QT", cc), "ones"], W=[("pf", bm)])
            mm(psf[:, bq, :], ones[:, :], csq[:, cc, :], cc == 0, cc == 3, R=[("Pr", cc), "ones"], W=[("pf", bq)], sig=(cc == 3))
        ts(DVE, tmpA[:, 0:512], psf[:, bm, :], 1.0 / 512.0, None, ALU.mult, None, R=[("pf", bm)], W=["tmpA"])
        tt(DVE, tmpA[:, 512:1024], tmpA[:, 0:512], tmpA[:, 0:512], ALU.mult, R=["tmpA"], W=["tmpA"])
        stt(DVE, tmpA[:, 512:1024], psf[:, bq, :], 1.0 / 512.0, tmpA[:, 512:1024], ALU.mult, ALU.subtract,
            R=[("pf", bq), "tmpA"], W=["tmpA"])
        act(tmpA[:, 512:1024], tmpA[:, 512:1024], AF.Sqrt, R=["tmpA"], W=["tmpA"], bias=EPS, scale=1.0)
        T.op(DVE, lambda: nc.vector.reciprocal(tmpA[:, 512:1024], tmpA[:, 512:1024]), R=["tmpA"], W=["tmpA"])
        for cc in range(4):
            tt(DVE, cacc[:, cc, :], cacc[:, cc, :], tmpA[:, 0:512], ALU.subtract, R=[("Rb", cc), "tmpA"], W=[("Rb", cc)])
            tt(POOL, cacc[:, cc, :], cacc[:, cc, :], tmpA[:, 512:1024], ALU.mult, R=[("Rb", cc), "tmpA"], W=[("Rb", cc)])
            act(mixT[:, 4 + cc, :], cacc[:, cc, :], AF.Silu, R=[("Rb", cc)] + CV, W=[("mixT", 4 + cc)],
                bias=cvec[:, 2, cc:cc + 1], scale=cvec[:, 1, cc:cc + 1])
        chk(7)
        MIX = [("mixT", c) for c in range(8)]
        if PROFILE:
            T.stage = "s5"
        T.retire([("qtm", t) for t in range(4)], [("Cbx", t) for t in range(4)])
        post_ln(s, j, MIX, mixT, "o", 0, 1, x_from_dram=True)
        chk(8)
        if PROFILE:
            T.stage = "s7"
        vq = [wslab("xq0"), wslab("xq1")]
        for e in range(8):
            wv, wk = vq[e // 4]
            b1 = nf()
            for c in range(8):
                mm(psf[:, b1, :], wv[:, c, (e % 4) * 128:(e % 4 + 1) * 128], Db[:, c, :], c == 0, c == 7,
                   R=DBK + [wk], W=[("pf", b1)], sig=(c == 7))
            cp(ACT if e % 2 else DVE, QT[:, e, :], psf[:, b1, :], R=[("pf", b1)], W=[("QT", e)])
        if PROFILE:
            T.stage = "s8"
        for h in range(4):
            pis = []
            for mc in range(2):
                bs = nf()
                for dc in range(2):
                    mm(psf[:, bs, :], mkT[:, 2 * h + dc, mc * 128:(mc + 1) * 128], QT[:, 2 * h + dc, :], dc == 0, dc == 1,
                       R=MKT + [("QT", 2 * h + dc)], W=[("pf", bs)], sig=(dc == 1))
                pi = (h * 2 + mc) % 4
                pis.append(pi)
                act(Pr[:, pi, :], psf[:, bs, :], AF.Exp, R=[("pf", bs)], W=[("Pr", pi)], scale=MEM_SCALE)
            bd = nf()
            for mc in range(2):
                mm(psf[:, bd, :], ones[:, :], Pr[:, pis[mc], :], mc == 0, mc == 1, R=[("Pr", pis[mc]), "ones"], W=[("pf", bd)], sig=(mc == 1))
            T.op(DVE, lambda: nc.vector.reciprocal(tmpA[:, 0:512], psf[:, bd, :]), R=[("pf", bd)], W=["tmpA"])
            for dc in range(2):
                bo = nf()
                for mc in range(2):
                    mm(psf[:, bo, :], mvb[:, mc, h * 256 + dc * 128:h * 256 + (dc + 1) * 128], Pr[:, pis[mc], :], mc == 0, mc == 1,
                       R=MVB + [("Pr", pis[mc])], W=[("pf", bo)], sig=(mc == 1))
                tt(DVE, mixT[:, 2 * h + dc, :], psf[:, bo, :], tmpA[:, 0:512], ALU.mult, R=[("pf", bo), "tmpA"], W=[("mixT", 2 * h + dc)])
        chk(9)
        if PROFILE:
            T.stage = "s9"
        post_ln(s, j, MIX, mixT, "xo", 2, 3, x_from_dram=False)
        chk(10)
        if PROFILE:
            T.stage = "s10"
        nxt = (s, j + 1) if j + 1 < nblk else ((s + 1, 0) if s + 1 < nseq else None)
        if nxt is not None and not (nxt[1] == 0):
            for t in range(2):
                load(Cb[:, t, :], xp[nxt[0], nxt[1] * 512 + t * 128:nxt[1] * 512 + (t + 1) * 128, :], W=[("Cb", t)], q=POOL)
                xpref.add((nxt[0], nxt[1], t))
        T.retire([("qtm", t) for t in range(4)] + [("Cbx", t) for t in range(4)], [("hT", i) for i in range(16)])
        for fh in range(2):
            for g in range(4):
                wv, wk = wslab(f"up{fh * 4 + g}")
                for fc in range(4):
                    b1 = nf()
                    for c in range(8):
                        mm(psf[:, b1, :], wv[:, c, fc * 128:(fc + 1) * 128], Db[:, c, :], c == 0, c == 7,
                           R=DBK + [wk], W=[("pf", b1)], sig=(c == 7))
                    act(tmpA[:, 0:512] if fc % 2 == 0 else tmpB[:, 0:512], psf[:, b1, :], AF.Relu, R=[("pf", b1)],
                        W=["tmpA" if fc % 2 == 0 else "tmpB"])
                    tt(DVE, hT[:, g * 4 + fc, :], tmpA[:, 0:512] if fc % 2 == 0 else tmpB[:, 0:512], psf[:, b1, :], ALU.mult,
                       R=[("pf", b1), "tmpA" if fc % 2 == 0 else "tmpB"], W=[("hT", g * 4 + fc)])
            HK = [("hT", i) for i in range(16)]
            for qd in range(4):
                wv, wk = wslab(f"dn{fh}{qd}")
                for t in range(4):
                    b1 = nf()
                    for fc in range(16):
                        mm(psf[:, b1, 0:256], hT[:, fc, t * 128:(t + 1) * 128], wv[:, fc, :], fc == 0, fc == 15,
                           R=HK + [wk], W=[("pf", b1)], sig=(fc == 15))
                    dstv = Rb[:, t, qd * 256:(qd + 1) * 256]
                    if fh == 0:
                        stt(DVE, dstv, dstv, ALPHA, psf[:, b1, 0:256], ALU.mult, ALU.add, R=[("Rb", t), ("pf", b1)], W=[("Rb", t)])
                    else:
                        tt(DVE, dstv, dstv, psf[:, b1, 0:256], ALU.add, R=[("Rb", t), ("pf", b1)], W=[("Rb", t)])
        bvload(4, 0)
        bvload(5, 1)
        for t in range(4):
            layer_norm_rows(("Rb", t), 128, Rb[:, t, :], Rb[:, t, :], 4, 5, None, None)
            store(y_p[s, t0 + t * 128:t0 + (t + 1) * 128, :], Rb[:, t, :], R=[("Rb", t)])

    def post_ln(s, j, INK, inT, wname, gi, bi, x_from_dram):
        t0 = j * 512
        views = [wslab(wname + "0"), wslab(wname + "1")]
        bvload(gi, 0)
        bvload(bi, 1)

        def mm_ln(t):
            b2 = nf2()
            for hf in range(2):
                wv, wk = views[hf]
                for c in range(8):
                    mm(psf[:, b2 + hf, :], inT[:, c, t * 128:(t + 1) * 128], wv[:, c, :], c == 0, c == 7,
                       R=INK + [wk], W=[("pf", b2 + hf)], sig=(c == 7))
            if x_from_dram:
                load(xring[:, 0, :], xp[s, t0 + t * 128:t0 + (t + 1) * 128, :], W=[("xring", 0)])
            for hf in range(2):
                src = xring[:, 0, hf * 512:(hf + 1) * 512] if x_from_dram else Rb[:, t, hf * 512:(hf + 1) * 512]
                stt(DVE, Rb[:, t, hf * 512:(hf + 1) * 512], src, ALPHA, psf[:, b2 + hf, :], ALU.mult, ALU.add,
                    R=[("pf", b2 + hf), ("xring", 0), ("Rb", t)], W=[("Rb", t)])
            layer_norm_rows(("Rb", t), 128, Rb[:, t, :], Rb[:, t, :], gi, bi, Cbx[:, t, :], ("Cbx", t))

        def tr_ev(t):
            for c in range(8):
                if c % 4 == 0:
                    b = nb()
                tr(psb[:, b, (c % 4) * 128:(c % 4 + 1) * 128], Cbx[:, t, c * 128:(c + 1) * 128],
                   R=[("Cbx", t)], W=[("pb", b)], sig=(c % 4 == 3))
                if c % 4 == 3:
                    c0 = c - 3
                    cp(ACT if c0 else DVE,
                       Db[:, c0:c0 + 4, t * 128:(t + 1) * 128],
                       psb[:, b, 0:512].rearrange("p (a b) -> p a b", b=128),
                       R=[("pb", b)], W=[("Db", cc) for cc in range(c0, c0 + 4)])

        mm_ln(0)
        mm_ln(1)
        mm_ln(2)
        tr_ev(0)
        mm_ln(3)
        tr_ev(1)
        tr_ev(2)
        tr_ev(3)

    def barrier():
        engs = (PE, ACT, DVE, POOL, SP)
        dss = st_sems + ld_sems + wr_sems + cvt_sems + g_sems
        for E in engs:
            for F in engs:
                if F is not E and F.count and E.seen.get(F, 0) < F.count:
                    E.eng.wait_ge(F.sem, F.count)
                    E.seen[F] = F.count
            for ds in dss:
                if ds.count and E.seen.get(ds, 0) < ds.count:
                    E.eng.wait_ge(ds.sem, ds.count * 16)
                    E.seen[ds] = ds.count

    g_sems = [T.dsem(f"g{i}") for i in range(4)]

    def sample_phase():
        NS = 16
        KTf = KT[:, :, :].rearrange("p a b -> p (a b)")
        hTf = hT[:, :, :].rearrange("p a b -> p (a b)")
        QTf = QT[:, :, :].rearrange("p a b -> p (a b)")
        Rbf = Rb[:, :, :].rearrange("p a b -> p (a b)")
        uTf = uT[:, :, :].rearrange("p a b -> p (a b)")
        bvf = bvec[:, :, :].rearrange("p a b -> p (a b)")
        mixf = mixT[:, :, :].rearrange("p a b -> p (a b)")
        Dbf = Db[:, :, :].rearrange("p a b -> p (a b)")

        def carver(arena):
            off = [0]
            def carve(n):
                v = arena[:, off[0]:off[0] + n]
                off[0] += n
                return v
            return carve
        ck, ch, cq, cr, cm, cd = carver(KTf), carver(hTf), carver(QTf), carver(Rbf), carver(mixf), carver(Dbf)
        cpg = [ck(2048).rearrange("p (r l) -> p r l", l=256) for _ in range(2)]
        rpg = [ck(256).rearrange("p (r l) -> p r l", l=32) for _ in range(2)]
        cT = [ck(2048).rearrange("p (a b) -> p a b", b=1024) for _ in range(2)]
        rT = [ck(1024) for _ in range(2)]
        cnx = ck(16 * 264).rearrange("p (b l) -> p b l", l=264)
        Pp = ck(512)
        WukT = ch(2048).rearrange("p (h l) -> p h l", l=256)
        ql_bf = ch(2048)
        xs_bf = ch(1024)
        q_s = ch(768)
        cnew_bf = ch(288)
        mix_s = ch(1024)
        xsT = cq(128).rearrange("p (c b) -> p c b", b=16)
        qT_s = cq(128).rearrange("p (h b) -> p h b", b=16)
        qlT = cq(256).rearrange("p (l b h) -> p l b h", b=16, h=8)
        qrT = cq(128).rearrange("p (b h) -> p b h", h=8)
        cnT = cq(32).rearrange("p (l b) -> p l b", b=16)
        krnT = cq(16)
        olT = cq(256).rearrange("p (l h b) -> p l h b", h=8, b=16)
        mixT_s = cq(128).rearrange("p (c b) -> p c b", b=16)
        o2T_s = cq(128).rearrange("p (c b) -> p c b", b=16)
        hT_s = cq(512)
        pnew = cq(8)
        ol_bf = cq(256)
        P2bf = cq(8)
        dhl = cq(16)
        xs_f = cr(1024)
        r_s = cr(1024)
        u_s = cr(512)
        q2_f = cr(1024)
        misc = cr(512)
        xb16 = cm(1024)
        ptb = sb("ptb", [128, 128], I32)
        gqt = sb("gqt", [128, 1], I32)
        idx = sb("idx", [128, 128], I32)
        q2_scr = dscr("q2_scr", [16, 1024], F32)
        onesf = sb("onesf", [128, 1], F32)
        idxc = [sb(f"idxc{i}", [128, 1], I32) for i in range(2)]

        load(ptb[:], ptrep, W=["ptb"])
        load(gqt[:], gq, W=["gqt"])
        T.op(DVE, lambda: nc.vector.memset(onesf[:], 1.0), W=["onesf"])
        ts(DVE, idx[:], ptb[:], 16, gqt[:, 0:1], ALU.mult, ALU.add, R=["ptb", "gqt"], W=["idx"])
        if DEBUG == "idx":
            cp(DVE, tmpA[:, 0:128], idx[:], R=["idx"], W=["tmpA"])
            store(ckv_p[0, 0:128, 0:128], tmpA[:, 0:128], R=["tmpA"])
            barrier()
            raise _Stop()
        T.op(POOL, lambda: nc.gpsimd.memset(cnx[0:1, :, :], 1.0), W=["cnx"])

        def tr16(dst, src16, ncols, key_r, key_w, E=DVE):
            b = nb()
            for c in range(ncols):
                tr(psb[:, b, c * 16:(c + 1) * 16], src16[0:16, c * 128:(c + 1) * 128], R=[key_r], W=[("pb", b)], sig=(c == ncols - 1))
            cp(E, dst, psb[:, b, 0:ncols * 16].rearrange("p (c b) -> p c b", b=16), R=[("pb", b)], W=[key_w])

        if PROFILE:
            T.stage = "smp_pre"
        load(xs_f[0:16, :], xs, W=["xs_f"])
        load(xs_bf[0:16, :], xs, W=["xs_bf"], q=POOL)
        tr16(xsT, xs_bf, 8, "xs_bf", "xsT")
        v0, k0 = wslab("in0")
        v1, k1 = wslab("in1")
        bA, bB, bC = nf(), nf(), nf()
        for (ob, o0, o1, vv, kk, c0_, c1_) in ((bA, 0, 480, v0, k0, 0, 480), (bB, 0, 32, v0, k0, 480, 512),
                                               (bB, 32, 288, v1, k1, 0, 256), (bB, 288, 320, v1, k1, 512, 544),
                                               (bC, 0, 256, v1, k1, 256, 512)):
            for c in range(8):
                mm(psf[0:16, ob, o0:o1], xsT[:, c, :], vv[:, c, c0_:c1_], c == 0, c == 7, R=["xsT", kk], W=[("pf", ob)], sig=(c == 7))
        PQ = [("pf", bA), ("pf", bB), ("pf", bC)]
        qd3 = q_s[0:16, :].rearrange("p (h e) -> p h e", e=96)
        for (bk, h0, nh_) in ((bA, 0, 5), (bB, 5, 3)):
            q3 = psf[0:16, bk, 0:nh_ * 96].rearrange("p (h e) -> p h e", e=96)
            cp(ACT, qd3[:, h0:h0 + nh_, 0:64], q3[:, :, 0:64], R=PQ, W=["q_s"])
            tmp3 = tmpA[0:16, 0:nh_ * 64].rearrange("p (h e) -> p h e", e=64)
            rope_rows(16, q3[:, :, 64:96], qd3[:, h0:h0 + nh_, 64:96],
                      ropeCs[0:16, :].rearrange("p (o e) -> p o e", o=1).to_broadcast([16, nh_, 32]),
                      ropeSs[0:16, :].rearrange("p (o e) -> p o e", o=1).to_broadcast([16, nh_, 32]),
                      tmp3, R=PQ + ["ropes"], W=["q_s"], tmpkey="tmpA")
        cp(DVE, tmpB[0:16, 768:1024], psf[0:16, bC, 0:256], R=PQ, W=["tmpB"])
        zc = tmpB[0:16, 768:1024]
        act(tmpB[0:16, 0:256], zc, AF.Square, R=["tmpB"], W=["tmpB"])
        T.op(DVE, lambda: nc.vector.reduce_sum(small[0:16, 2:3], tmpB[0:16, 0:256], axis=AX.X), R=["tmpB"], W=["tmpB"])
        act(small[0:16, 3:4], small[0:16, 2:3], AF.Sqrt, R=["tmpB"], W=[("small", 3)], bias=EPS, scale=1.0 / 256.0)
        T.op(DVE, lambda: nc.vector.reciprocal(small[0:16, 3:4], small[0:16, 3:4]), R=[("small", 3)], W=[("small", 3)])
        stt(DVE, tmpB[0:16, 256:512], zc, small[0:16, 3:4], kvg_b[0:16, :], ALU.mult, ALU.mult,
            R=[("small", 3), "kvg", "tmpB"], W=["tmpB"])
        store(ckv_s[:, :], tmpB[0:16, 256:512], R=["tmpB"], W=["ckv_s_dram"])
        cp(ACT, cnew_bf[0:16, 0:256], tmpB[0:16, 256:512], R=["tmpB"], W=["cnew_bf"])
        kr3 = tmpB[0:16, 512:544].rearrange("p (h e) -> p h e", e=32)
        rope_rows(16, psf[0:16, bB, 288:320].rearrange("p (h e) -> p h e", e=32), kr3,
                  ropeCs[0:16, :].rearrange("p (o e) -> p o e", o=1), ropeSs[0:16, :].rearrange("p (o e) -> p o e", o=1),
                  tmpB[0:16, 576:640].rearrange("p (h e) -> p h e", e=64),
                  R=PQ + ["ropes"], W=["tmpB"], tmpkey="tmpB")
        store(kr_s[:, :], tmpB[0:16, 512:544], R=["tmpB"])
        cp(ACT, cnew_bf[0:16, 256:288], tmpB[0:16, 512:544], R=["tmpB"], W=["cnew_bf"])
        v2, k2 = wslab("in2")
        v3, k3 = wslab("in3")
        ba, bb = nf(), nf()
        for c in range(8):
            mm(psf[0:16, ba, :], xsT[:, c, :], v2[:, c, :], c == 0, c == 7, R=["xsT", k2], W=[("pf", ba)])
            mm(psf[0:16, bb, :], xsT[:, c, :], v3[:, c, :], c == 0, c == 7, R=["xsT", k3], W=[("pf", bb)], sig=(c == 7))
        act(tmpA[0:16, 0:512], psf[0:16, bb, :], AF.Sigmoid, R=[("pf", bb)], W=["tmpA"])
        tt(DVE, u_s[0:16, :], psf[0:16, ba, :], tmpA[0:16, 0:512], ALU.mult, R=[("pf", ba), "tmpA"], W=["u_s"])
        mk_rest_slabs()
        T.dma(SP, st_sems[0], lambda: nc.sync.dma_start(out=conv_s[:, 0:29 * 512], in_=stc[:, 512:30 * 512]))
        store(conv_s[:, 29 * 512:30 * 512], u_s[0:16, :], R=["u_s"])
        acc = misc[0:16, :]
        first = True
        for k0_ in range(0, 30, 4):
            n = min(4, 30 - k0_)
            ext = uTf[0:16, 0:n * 512]
            cwb = bvf[0:16, 0:n * 512]
            load(ext, stc[:, k0_ * 512:(k0_ + n) * 512], W=["ext"])
            load(cwb, conv_w[k0_:k0_ + n, :].rearrange("(o k) c -> o (k c)", o=1).partition_broadcast(16), W=[("bvec", 0), ("bvec", 1)])
            tt(DVE, ext, ext, cwb, ALU.mult, R=["ext", ("bvec", 0), ("bvec", 1)], W=["ext"])
            for jx in range(n):
                if first:
                    cp(DVE, acc, ext[:, jx * 512:(jx + 1) * 512], R=["ext"], W=["acc"])
                    first = False
                else:
                    tt(DVE, acc, acc, ext[:, jx * 512:(jx + 1) * 512], ALU.add, R=["ext", "acc"], W=["acc"])
        load(bvf[0:16, 0:512], conv_w[30:31, :].partition_broadcast(16), W=[("bvec", 0), ("bvec", 1)])
        tt(DVE, tmpA[0:16, 0:512], u_s[0:16, :], bvf[0:16, 0:512], ALU.mult, R=["u_s", ("bvec", 0), ("bvec", 1)], W=["tmpA"])
        tt(DVE, acc, acc, tmpA[0:16, 0:512], ALU.add, R=["tmpA", "acc"], W=["acc"])
        load(bvf[0:16, 0:512], conv_b.partition_broadcast(16), W=[("bvec", 0), ("bvec", 1)])
        tt(DVE, acc, acc, bvf[0:16, 0:512], ALU.add, R=["acc", ("bvec", 0), ("bvec", 1)], W=["acc"])
        T.op(DVE, lambda: nc.vector.bn_stats(stats[0:16, 0, :], acc), R=["acc"], W=[("stats", 0)])
        T.op(DVE, lambda: nc.vector.bn_aggr(mvar[0:16, :], stats[0:16, 0:1, :]), R=[("stats", 0)], W=["mvar"])
        rstd_from_var(small[0:16, 0:1], mvar[0:16, 1:2], R=["mvar"], W=[("small", 0)])
        stt(DVE, small[0:16, 1:2], mvar[0:16, 0:1], -1.0, small[0:16, 0:1], ALU.mult, ALU.mult,
            R=["mvar", ("small", 0)], W=[("small", 1)])
        act(acc, acc, AF.Identity, R=["acc", ("small", 0), ("small", 1)], W=["acc"], bias=small[0:16, 1:2], scale=small[0:16, 0:1])
        load(bvf[0:16, 0:512], conv_g.partition_broadcast(16), W=[("bvec", 0)])
        load(bvf[0:16, 1024:1536], conv_bb.partition_broadcast(16), W=[("bvec", 1)])
        tt(DVE, acc, acc, bvf[0:16, 0:512], ALU.mult, R=["acc", ("bvec", 0)], W=["acc"])
        tt(DVE, acc, acc, bvf[0:16, 1024:1536], ALU.add, R=["acc", ("bvec", 1)], W=["acc"])
        act(mix_s[0:16, 512:1024], acc, AF.Silu, R=["acc"], W=["mix_s"])
        for h in range(8):
            b = nb()
            for lc in range(2):
                tr(psb[0:64, b, lc * 128:(lc + 1) * 128], WukP[:, lc, h, 0:64], R=WUK, W=[("pb", b)], sig=(lc == 1))
            cp(DVE, WukT[0:64, h, :], psb[0:64, b, 0:256], R=[("pb", b)], W=["WukT"])
        b = nb()
        for h in range(8):
            tr(psb[0:96, b, h * 16:(h + 1) * 16], q_s[0:16, h * 96:(h + 1) * 96], R=["q_s"], W=[("pb", b)], sig=(h == 7))
        cp(DVE, qT_s[0:96, :, :], psb[0:96, b, 0:128].rearrange("p (h b) -> p h b", b=16), R=[("pb", b)], W=["qT_s"])
        cp(DVE, qrT[0:32, :, :].rearrange("p b h -> p h b"), qT_s[64:96, :, :], R=["qT_s"], W=["qrT"])
        qb = [nf() for _ in range(4)]
        for h in range(8):
            mm(psf[0:16, qb[h // 2], (h % 2) * 256:(h % 2 + 1) * 256], qT_s[0:64, h, :], WukT[0:64, h, :], True, True,
               R=["qT_s", "WukT"], W=[("pf", qb[h // 2])], sig=True)
        for i in range(4):
            cp(ACT if i % 2 else DVE, ql_bf[0:16, i * 512:(i + 1) * 512], psf[0:16, qb[i], :], R=[("pf", qb[i])], W=["ql_bf"])
        if DEBUG == "ql":
            cp(DVE, tmpA[0:16, 0:1024], ql_bf[0:16, 1024:2048], R=["ql_bf"], W=["tmpA"])
            store(y_s[:, :], tmpA[0:16, 0:1024], R=["tmpA"])
            barrier()
            raise _Stop()
        b = nb()
        for lc in range(2):
            for h in range(8):
                o = (lc * 8 + h) * 16
                tr(psb[:, b, o:o + 16], ql_bf[0:16, h * 256 + lc * 128:h * 256 + (lc + 1) * 128], R=["ql_bf"], W=[("pb", b)],
                   sig=(lc == 1 and h == 7))
        cp(DVE, qlT[:, :, :, :].rearrange("p l b h -> p l h b"), psb[:, b, 0:256].rearrange("p (l h b) -> p l h b", h=8, b=16),
           R=[("pb", b)], W=["qlT"])
        tr16(cnT, cnew_bf, 2, "cnew_bf", "cnT")
        b = nb()
        tr(psb[0:32, b, 0:16], cnew_bf[0:16, 256:288], R=["cnew_bf"], W=[("pb", b)], sig=True)
        cp(DVE, krnT[0:32, :], psb[0:32, b, 0:16], R=[("pb", b)], W=["krnT"])
        load(cnx[0:1, :, 0:256], ckv_s.rearrange("(o b) l -> o b l", o=1), R=["ckv_s_dram", "cnx"], W=["cnxl"], q=POOL)
        if PROFILE:
            T.stage = "smp_attn"
        gi = [0]
        for bsm in range(NS):
            pS = nh()
            pO = nh()
            for pg in range(8):
                sl = gi[0] % 2
                gi[0] += 1
                col = bsm * 8 + pg
                cp(DVE, idxc[sl][:, 0:1], idx[:, col:col + 1], R=["idx"], W=[("idxc", sl)])
                T.dma(POOL, g_sems[sl], lambda sl=sl, col=col: nc.gpsimd.indirect_dma_start(
                    out=cpg[sl][:, :, :].rearrange("p r l -> p (r l)"), out_offset=None, in_=pool_c,
                    in_offset=bass.IndirectOffsetOnAxis(ap=idxc[sl][:, 0:1], axis=0)), R=[("idxc", sl)], W=[("cpg", sl)])
                T.dma(POOL, g_sems[2 + sl], lambda sl=sl, col=col: nc.gpsimd.indirect_dma_start(
                    out=rpg[sl][:, :, :].rearrange("p r l -> p (r l)"), out_offset=None, in_=pool_r,
                    in_offset=bass.IndirectOffsetOnAxis(ap=idxc[sl][:, 0:1], axis=0)), R=[("idxc", sl)], W=[("rpg", sl)])
                if DEBUG == "cpg":
                    for ii, rr in enumerate((0, 7)):
                        cp(DVE, tmpA[:, ii * 256:(ii + 1) * 256], cpg[sl][:, rr, 0:256], R=[("cpg", sl)], W=["tmpA"])
                        store(ckv_p[0, ii * 128:(ii + 1) * 128, :], tmpA[:, ii * 256:(ii + 1) * 256], R=["tmpA"])
                    cp(DVE, tmpA[:, 512:768], rpg[sl][:, :, :].rearrange("p r l -> p (r l)"), R=[("rpg", sl)], W=["tmpA"])
                    store(ckv_p[0, 256:384, :], tmpA[:, 512:768], R=["tmpA"])
                    barrier()
                    raise _Stop()
                for lc in range(2):
                    b = nb()
                    for r in range(8):
                        tr(psb[:, b, r * 128:(r + 1) * 128], cpg[sl][:, r, lc * 128:(lc + 1) * 128], R=[("cpg", sl)], W=[("pb", b)], sig=(r == 7))
                    cp(ACT if lc else DVE, cT[sl][:, lc, :], psb[:, b, :], R=[("pb", b)], W=[("cT", sl)])
                b = nb()
                for r in range(8):
                    tr(psb[0:32, b, r * 128:(r + 1) * 128], rpg[sl][:, r, :], R=[("rpg", sl)], W=[("pb", b)], sig=(r == 7))
                cp(DVE, rT[sl][0:32, :], psb[0:32, b, :], R=[("pb", b)], W=[("rT", sl)])
                for r in range(8):
                    kt = pg * 8 + r
                    o_ = psf[:, pS, kt * 8:(kt + 1) * 8]
                    RR = [("cT", sl), ("rT", sl), "qlT", "qrT"]
                    mm(o_, cT[sl][:, 0, r * 128:(r + 1) * 128], qlT[:, 0, bsm, :], True, False, R=RR, W=[("pf", pS)])
                    mm(o_, cT[sl][:, 1, r * 128:(r + 1) * 128], qlT[:, 1, bsm, :], False, False, R=RR, W=[("pf", pS)])
                    mm(o_, rT[sl][0:32, r * 128:(r + 1) * 128], qrT[0:32, bsm, :], False, True, R=RR, W=[("pf", pS)], sig=(r == 7))
                act(Pp[:, pg * 64:(pg + 1) * 64], psf[:, pS, pg * 64:(pg + 1) * 64], AF.Exp, R=[("pf", pS)], W=["Pp"], scale=MLA_SCALE)
                for r in range(8):
                    kt = pg * 8 + r
                    mm(psf[0:8, pO, 0:256], Pp[:, kt * 8:(kt + 1) * 8], cpg[sl][:, r, :], kt == 0, False,
                       R=["Pp", ("cpg", sl)], W=[("pf", pO)], sig=(r == 7))
            pN = nf()
            mm(psf[0:1, pN, 0:8], cnT[:, 0, bsm:bsm + 1], qlT[:, 0, bsm, :], True, False, R=["cnT", "qlT"], W=[("pf", pN)])
            mm(psf[0:1, pN, 0:8], cnT[:, 1, bsm:bsm + 1], qlT[:, 1, bsm, :], False, False, R=["cnT", "qlT"], W=[("pf", pN)])
            mm(psf[0:1, pN, 0:8], krnT[0:32, bsm:bsm + 1], qrT[0:32, bsm, :], False, True, R=["krnT", "qrT"], W=[("pf", pN)], sig=True)
            act(pnew[0:1, :], psf[0:1, pN, 0:8], AF.Exp, R=[("pf", pN)], W=["pnew"], scale=MLA_SCALE)
            mm(psf[0:8, pO, 0:256], pnew[0:1, :], cnx[0:1, bsm, 0:256], False, True, R=["pnew", "cnx", "cnxl"], W=[("pf", pO)], sig=True)
            T.op(DVE, lambda: nc.vector.reduce_sum(small[:, 40:48], Pp[:, :].rearrange("p (k h) -> p h k", h=8), axis=AX.X),
                 R=["Pp"], W=[("small", 40)])
            cp(DVE, dhl[:, 0:8], small[:, 40:48], R=[("small", 40)], W=["dhl"])
            tt(DVE, small[:, 48:56], small[:, 40:48], dhl[:, 0:8], ALU.subtract, R=[("small", 40), "dhl"], W=[("small", 48)])
            cp(DVE, dhl[:, 8:16], small[:, 48:56], R=[("small", 48)], W=["dhl"])
            pD = nf()
            mm(psf[0:8, pD, 0:1], dhl[:, 0:8], ones[:, 0:1], True, False, R=["dhl", "ones"], W=[("pf", pD)])
            mm(psf[0:8, pD, 0:1], dhl[:, 8:16], ones[:, 0:1], False, False, R=["dhl", "ones"], W=[("pf", pD)])
            mm(psf[0:8, pD, 0:1], pnew[0:1, :], ones[0:1, 0:1], False, True, R=["pnew", "ones"], W=[("pf", pD)], sig=True)
            if DEBUG == "po" and bsm == 0:
                cp(DVE, tmpA[0:8, 0:256], psf[0:8, pO, 0:256], R=[("pf", pO)], W=["tmpA"])
                store(y_s[0:8, 0:256], tmpA[0:8, 0:256], R=["tmpA"])
                cp(DVE, tmpA[:, 512:1024], Pp[:, :], R=["Pp"], W=["tmpA"])
                store(ckv_p[0, 0:128, :], tmpA[:, 512:768], R=["tmpA"])
                store(ckv_p[0, 128:256, :], tmpA[:, 768:1024], R=["tmpA"])
                barrier()
                raise _Stop()
            T.op(DVE, lambda pD=pD: nc.vector.reciprocal(small[0:8, 8:9], psf[0:8, pD, 0:1]), R=[("pf", pD)], W=[("small", 8)])
            ts(DVE, ol_bf[0:8, :], psf[0:8, pO, 0:256], small[0:8, 8:9], None, ALU.mult, None, R=[("pf", pO), ("small", 8)], W=["ol_bf"])
            b = nb()
            for lc in range(2):
                tr(psb[:, b, lc * 8:(lc + 1) * 8], ol_bf[0:8, lc * 128:(lc + 1) * 128], R=["ol_bf"], W=[("pb", b)], sig=(lc == 1))
            cp(DVE, olT[:, :, :, bsm], psb[:, b, 0:16].rearrange("p (l h) -> p l h", h=8), R=[("pb", b)], W=["olT"])
        if PROFILE:
            T.stage = "smp_rest"
        bk = nf()
        for h in range(8):
            for lc in range(2):
                mm(psf[0:16, bk, h * 64:(h + 1) * 64], olT[:, lc, h, :], Wuv[:, lc, h * 64:(h + 1) * 64], lc == 0, lc == 1,
                   R=["olT"] + WUV, W=[("pf", bk)], sig=(h == 7 and lc == 1))
        cp(DVE, mix_s[0:16, 0:512], psf[0:16, bk, :], R=[("pf", bk)], W=["mix_s"])

        def dbg(name, ap, key):
            if DEBUG == name:
                n = ap.shape[1]
                cp(DVE, tmpA[0:ap.shape[0], 0:n], ap, R=[key], W=["tmpA"])
                store(y_s[0:ap.shape[0], 0:n], tmpA[0:ap.shape[0], 0:n], R=["tmpA"])
                barrier()
                raise _Stop()

        dbg("mix_s", mix_s[0:16, :], "mix_s")

        def post16(inT, inkey, wname, resid, gi_, bi_, outT, outkey):
            views = [wslab(wname + "0"), wslab(wname + "1")]
            b2 = nf2()
            for hf in range(2):
                wv, wk = views[hf]
                for c in range(8):
                    mm(psf[0:16, b2 + hf, :], inT[:, c, :], wv[:, c, :], c == 0, c == 7, R=[inkey, wk], W=[("pf", b2 + hf)], sig=(c == 7))
            for hf in range(2):
                stt(DVE, r_s[0:16, hf * 512:(hf + 1) * 512], resid[0:16, hf * 512:(hf + 1) * 512], ALPHA, psf[0:16, b2 + hf, :],
                    ALU.mult, ALU.add, R=[("pf", b2 + hf), "r_s", "xs_f"], W=["r_s"])
            bvload(gi_, 0)
            bvload(bi_, 1)
            layer_norm_rows("r_s", 16, r_s[0:16, :], r_s[0:16, :], gi_, bi_, xb16[0:16, :] if outT is not None else None, "xb16")
            if outT is not None:
                tr16(outT, xb16, 8, "xb16", outkey)

        cp(ACT, xb16[0:16, :], mix_s[0:16, :], R=["mix_s"], W=["xb16"])
        tr16(mixT_s, xb16, 8, "xb16", "mixT_s")
        x1T_s = cd(128).rearrange("p (c b) -> p c b", b=16)
        x2T_s = cd(128).rearrange("p (c b) -> p c b", b=16)
        post16(mixT_s, "mixT_s", "o", xs_f, 0, 1, x1T_s, "x1T_s")
        dbg("x1", r_s[0:16, :], "r_s")
        if PROFILE:
            T.stage = "smp_xattn"
        vq = [wslab("xq0"), wslab("xq1")]
        b2 = nf2()
        for hf in range(2):
            wv, wk = vq[hf]
            for c in range(8):
                mm(psf[0:16, b2 + hf, :], x1T_s[:, c, :], wv[:, c, :], c == 0, c == 7, R=["x1T_s", wk], W=[("pf", b2 + hf)], sig=(c == 7))
        for hf in range(2):
            cp(DVE, q2_f[0:16, hf * 512:(hf + 1) * 512], psf[0:16, b2 + hf, :], R=[("pf", b2 + hf)], W=["q2_f"])
        store(q2_scr[:, :], q2_f[0:16, :], R=["q2_f"], W=["q2_scr"])
        mkb = uTf[:, 0:2048].rearrange("p (a b) -> p a b", b=1024)
        for bsm in range(NS):
            load(mkb, cmk[bsm].rearrange("(mt p) e -> p mt e", p=128), W=["mkb"])
            load(mvb[:, :, :], cmv[bsm].rearrange("(mt p) e -> p mt e", p=128), W=[("mvb", 0), ("mvb", 1)], q=POOL)
            load(xring[:, 0, :], q2_scr[bsm:bsm + 1, :].partition_broadcast(128), R=["q2_scr"], W=[("xring", 0)])
            for mt in range(2):
                tt(DVE, mkb[:, mt, :], mkb[:, mt, :], xring[:, 0, :], ALU.mult, R=["mkb", ("xring", 0)], W=["mkb"])
            T.op(DVE, lambda: nc.vector.reduce_sum(small[:, 16:24], mkb[:, :, :].rearrange("p a (h d) -> p (a h) d", d=256), axis=AX.X),
                 R=["mkb"], W=[("small", 16)])
            act(P2bf[:, :], small[:, 16:24], AF.Exp, R=[("small", 16)], W=["P2bf"], scale=MEM_SCALE)
            bo = nf()
            for h in range(4):
                for dc in range(2):
                    for mt in range(2):
                        mm(psf[:, bo, (h * 2 + dc):(h * 2 + dc) + 1], mvb[:, mt, h * 256 + dc * 128:h * 256 + (dc + 1) * 128],
                           P2bf[:, mt * 4 + h:mt * 4 + h + 1], mt == 0, mt == 1, R=[("mvb", 0), ("mvb", 1), "P2bf"], W=[("pf", bo)])
            mm(psf[:, bo, 16:24], ones[:, :], P2bf[:, :], True, True, R=["ones", "P2bf"], W=[("pf", bo)], sig=True)
            cp(DVE, small[:, 28:36], psf[:, bo, 16:24], R=[("pf", bo)], W=[("small", 28)])
            tt(DVE, small[:, 24:28], small[:, 28:32], small[:, 32:36], ALU.add, R=[("small", 28)], W=[("small", 24)])
            T.op(DVE, lambda: nc.vector.reciprocal(small[:, 24:28], small[:, 24:28]), R=[("small", 24)], W=[("small", 24)])
            tt(DVE, o2T_s[:, :, bsm].rearrange("p (h d) -> p h d", d=2), psf[:, bo, 0:8].rearrange("p (h d) -> p h d", d=2),
               small[:, 24:28].rearrange("p (h o) -> p h o", o=1).to_broadcast([128, 4, 2]), ALU.mult,
               R=[("pf", bo), ("small", 24)], W=["o2T_s"])
        if PROFILE:
            T.stage = "smp_ffn"
        post16(o2T_s, "o2T_s", "xo", r_s, 2, 3, x2T_s, "x2T_s")
        dbg("x2", r_s[0:16, :], "r_s")
        pH = nh()
        for g in range(8):
            wv, wk = wslab(f"up{g}")
            for fc in range(4):
                o = (g * 4 + fc) * 16
                for c in range(8):
                    mm(psf[:, pH, o:o + 16], wv[:, c, fc * 128:(fc + 1) * 128], x2T_s[:, c, :], c == 0, c == 7,
                       R=["x2T_s", wk], W=[("pf", pH)], sig=(c == 7 and fc == 3))
        act(tmpA[:, 0:512], psf[:, pH, :], AF.Relu, R=[("pf", pH)], W=["tmpA"])
        tt(DVE, hT_s[:, :], tmpA[:, 0:512], psf[:, pH, :], ALU.mult, R=[("pf", pH), "tmpA"], W=["hT_s"])
        b2 = nf2()
        for qd in range(4):
            for fh in range(2):
                wv, wk = wslab(f"dn{fh}{qd}")
                for fc in range(16):
                    o = (fh * 16 + fc) * 16
                    mm(psf[0:16, b2 + qd // 2, (qd % 2) * 256:(qd % 2 + 1) * 256], hT_s[:, o:o + 16], wv[:, fc, :],
                       fh == 0 and fc == 0, fh == 1 and fc == 15, R=["hT_s", wk], W=[("pf", b2 + qd // 2)], sig=(fc == 15))
        for hf in range(2):
            stt(DVE, r_s[0:16, hf * 512:(hf + 1) * 512], r_s[0:16, hf * 512:(hf + 1) * 512], ALPHA, psf[0:16, b2 + hf, :],
                ALU.mult, ALU.add, R=[("pf", b2 + hf), "r_s"], W=["r_s"])
        bvload(4, 0)
        bvload(5, 1)
        layer_norm_rows("r_s", 16, r_s[0:16, :], r_s[0:16, :], 4, 5, None, None)
        store(y_s[:, :], r_s[0:16, :], R=["r_s"])
        barrier()

    try:
        barrier()
        chk(0)
        if do_sample:
            sample_phase()
        mk_rest_slabs()
        chk(0.1)
        if do_prompt:
            for s in range(nseq):
                mem_kv_seq(s)
                chk(1)
                for j in range(nblk):
                    prompt_block(s, j)
    except _Stop:
        pass

    for ds in st_sems + ld_sems + wr_sems + cvt_sems + g_sems:
        if ds.count:
            nc.sync.wait_ge(ds.sem, ds.count * 16)
    for E in (PE, ACT, DVE, POOL):
        if E.count:
            nc.sync.wait_ge(E.sem, E.count)
    return nc


def _consts():
    half = 16
    inv_freq = np.exp(-math.log(10000.0) * np.arange(half, dtype=np.float32) / half).astype(np.float32)
    pos = np.arange(SEQ, dtype=np.float32)
    ang = pos[:, None] * inv_freq[None, :]
    cos, sin = np.cos(ang).astype(np.float32), np.sin(ang).astype(np.float32)
    C2 = np.concatenate([cos, cos], axis=1)
    S2 = np.concatenate([-sin, sin], axis=1)
    ropeC = C2.reshape(16, 128, 32).transpose(1, 0, 2).reshape(128, 512)
    ropeS = S2.reshape(16, 128, 32).transpose(1, 0, 2).reshape(128, 512)
    angs = np.float32(8192.0) * inv_freq
    cs, sn = np.cos(angs).astype(np.float32), np.sin(angs).astype(np.float32)
    ropeCs = np.tile(np.concatenate([cs, cs])[None, :], (128, 1)).astype(np.float32)
    ropeSs = np.tile(np.concatenate([-sn, sn])[None, :], (128, 1)).astype(np.float32)
    ident = np.eye(128, dtype=np.float32).astype(ml_dtypes.bfloat16)
    k = np.arange(128)
    mask = (k[None, :] >= k[:, None]).astype(np.float32).astype(ml_dtypes.bfloat16)
    injk = np.zeros((32, 96), np.float32)
    injk[np.arange(32), 64 + np.arange(32)] = 1.0
    injk = injk.astype(ml_dtypes.bfloat16)
    sel = np.zeros((16, 16, 128), np.float32)
    for b in range(16):
        sel[b, b, :] = 1.0
    gq = (np.arange(128) % 16).astype(np.int32).reshape(128, 1)
    return dict(c_ident=ident, c_mask=mask, c_ropeC=np.ascontiguousarray(ropeC), c_ropeS=np.ascontiguousarray(ropeS),
                c_ropeCs=ropeCs, c_ropeSs=ropeSs, c_injk=injk, c_sel=sel.reshape(16, 16 * 128), gq=gq)


def make_in_maps(inp, cores=range(NCORE), pool_c=None, pool_r=None, page_table=None):
    consts = _consts()
    if pool_c is None:
        pool_c = np.ascontiguousarray(inp["cache_ckv"][0]).reshape(-1, 8 * 256)
        pool_r = np.ascontiguousarray(inp["cache_krope"][0]).reshape(-1, 8 * 32)
        page_table = np.asarray(inp["page_table"])
    shared = dict(
        pool_c=pool_c, pool_r=pool_r,
        w_in=np.ascontiguousarray(inp["w_in"][0]), kvg=np.ascontiguousarray(inp["kv_norm_g"]),
        w_uk=np.ascontiguousarray(inp["w_uk"][0]).reshape(256, 512), w_uv=np.ascontiguousarray(inp["w_uv"][0]).reshape(256, 512),
        conv_w=np.ascontiguousarray(inp["conv_w"][0]), conv_b=np.ascontiguousarray(inp["conv_b"]),
        conv_g=np.ascontiguousarray(inp["conv_ln_g"]), conv_bb=np.ascontiguousarray(inp["conv_ln_b"]),
        w_o=np.ascontiguousarray(inp["w_o"][0]), w_xq=np.ascontiguousarray(inp["w_xq"][0]),
        w_xk=np.ascontiguousarray(inp["w_xk"][0]), w_xv=np.ascontiguousarray(inp["w_xv"][0]),
        w_xo=np.ascontiguousarray(inp["w_xo"][0]), w_up=np.ascontiguousarray(inp["w_up"][0]),
        w_down=np.ascontiguousarray(inp["w_down"][0]),
        ln1_g=np.ascontiguousarray(inp["ln1_g"]), ln1_b=np.ascontiguousarray(inp["ln1_b"]),
        ln2_g=np.ascontiguousarray(inp["ln2_g"]), ln2_b=np.ascontiguousarray(inp["ln2_b"]),
        ln3_g=np.ascontiguousarray(inp["ln3_g"]), ln3_b=np.ascontiguousarray(inp["ln3_b"]),
        **consts,
    )
    maps = []
    for c in cores:
        pt = page_table[c * 16:(c + 1) * 16]
        ptr = pt.reshape(16, 8, 8)[:, :, np.arange(128) // 16]
        ptr = np.ascontiguousarray(ptr.transpose(2, 0, 1).reshape(128, 128)).astype(np.int32)
        m = dict(shared)
        m.update(
            xp=np.ascontiguousarray(inp["x_prompt"][2 * c:2 * c + 2]),
            memp=np.ascontiguousarray(inp["mem_prompt"][2 * c:2 * c + 2]),
            xs=np.ascontiguousarray(inp["x_sample"][16 * c:16 * c + 16, 0]),
            ptrep=ptr,
            stc=np.ascontiguousarray(inp["state_conv"][0, 16 * c:16 * c + 16]).reshape(16, 30 * 512),
            cmk=np.ascontiguousarray(inp["cache_mem_k"][0, 16 * c:16 * c + 16]).reshape(16, 256, 1024),
            cmv=np.ascontiguousarray(inp["cache_mem_v"][0, 16 * c:16 * c + 16]).reshape(16, 256, 1024),
        )
        maps.append(m)
    return maps


def assemble(results):
    cat = lambda k: np.concatenate([r[k] for r in results], axis=0)
    y_p = cat("y_p")
    y_s = cat("y_s")[:, None, :]
    return (y_p, y_s, cat("ckv_p")[None], cat("kr_p")[None], cat("conv_p")[None],
            cat("mk_p").reshape(1, -1, 256, 4, 256), cat("mv_p").reshape(1, -1, 256, 4, 256),
            cat("ckv_s")[None, :, None, :], cat("kr_s")[None, :, None, :], cat("conv_s").reshape(1, -1, 30, 512))


def kernel(**inputs):
    inp = {k: np.asarray(v) for k, v in inputs.items()}
    nc = build()
    maps = make_in_maps(inp)
    res = run_bass_kernel_spmd(nc, maps, core_ids=list(range(NCORE)))
    outs = assemble(res.results)
    return tuple(np.ascontiguousarray(o, dtype=np.float32) for o in outs)
```

```python
import math
import numpy as np
import ml_dtypes
import concourse.bass as bass
import concourse.mybir as mybir
from concourse.bass_utils import run_bass_kernel_spmd

F32 = mybir.dt.float32
BF16 = mybir.dt.bfloat16
I32 = mybir.dt.int32
AF = mybir.ActivationFunctionType
ALU = mybir.AluOpType
AX = mybir.AxisListType

ALPHA = 2.0 ** 0.25
MLA_SCALE = 96.0 ** -0.5
MEM_SCALE = 1.0 / 16.0
EPS = 1e-5
NCORE = 8
SEQ = 2048
NBLK = 4
NPOOL = 10240
DEBUG = False
PROFILE = False


class Eng:
    def __init__(self, name, eng, sem):
        self.name, self.eng, self.sem, self.count, self.seen = name, eng, sem, 0, {}


class DSem:
    def __init__(self, sem):
        self.sem, self.count = sem, 0


_TINY = ("small", "stats", "mvar", "P2bf", "dhl", "pnew", "idx", "idxc", "ptb", "gqt", "onesf", "ol_bf", "krnT", "cnT")


def _tiny(k):
    n = k[0] if isinstance(k, tuple) else k
    return n in _TINY


class Trk:
    def __init__(self, nc):
        self.nc = nc
        self.last_w = {}
        self.readers = {}
        self.nsem = 0
        self.stage = None

    def sem(self, name):
        self.nsem += 1
        return self.nc.alloc_semaphore(name)

    def dsem(self, name):
        return DSem(self.sem(name))

    def _wait(self, E, R, W):
        best = {}
        def add(tk, raw=False):
            obj, val = tk
            if obj is E and not (raw and E.name != "pe" and E.name != "sp"):
                return
            if val > best.get(obj, 0):
                best[obj] = val
        for k in R:
            if k in self.last_w:
                add(self.last_w[k], raw=_tiny(k))
            if isinstance(k, tuple) and k[0] in ("pf", "pb"):
                for obj, val in self.readers.get(k, {}).items():
                    add((obj, val))
        for k in W:
            if k in self.last_w:
                add(self.last_w[k])
            for obj, val in self.readers.get(k, {}).items():
                add((obj, val))
        for obj, val in best.items():
            if E.seen.get(obj, 0) >= val:
                continue
            assert obj.count >= val, "wait on a signal that is not issued yet"
            E.eng.wait_ge(obj.sem, val * (16 if isinstance(obj, DSem) else 1))
            E.seen[obj] = val

    def _commit(self, tk, R, W):
        obj, val = tk
        for k in R:
            d = self.readers.setdefault(k, {})
            if d.get(obj, 0) < val:
                d[obj] = val
        for k in W:
            self.last_w[k] = tk
            self.readers[k] = {}

    def retire(self, old_keys, new_keys):
        merged = {}
        for k in old_keys:
            if k in self.last_w:
                o, v = self.last_w[k]
                merged[o] = max(merged.get(o, 0), v)
            for o, v in self.readers.get(k, {}).items():
                merged[o] = max(merged.get(o, 0), v)
        for nk in new_keys:
            d = self.readers.setdefault(nk, {})
            for o, v in merged.items():
                d[o] = max(d.get(o, 0), v)

    def op(self, E, fn, R=(), W=(), sig=True):
        self._wait(E, R, W)
        ins = fn()
        if self.stage is not None:
            try:
                ins.annotate(self.stage)
            except Exception:
                pass
        if sig:
            E.count += 1
            ins.then_inc(E.sem, 1)
            tk = (E, E.count)
        else:
            tk = (E, E.count + 1)
        self._commit(tk, R, W)

    def dma(self, Q, ds, fn, R=(), W=()):
        self._wait(Q, R, W)
        ins = fn()
        ds.count += 1
        ins.then_inc(ds.sem, 16)
        self._commit((ds, ds.count), R, W)


class _Stop(Exception):
    pass


def build(n_pool=NPOOL, do_prompt=True, do_sample=True, nseq=2, nblk=NBLK, stop=None):
    nc = bass.Bass("TRN2", target_bir_lowering=False)

    def chk(n):
        if stop is not None and n >= stop:
            raise _Stop()

    T = Trk(nc)
    _dummy = [T.sem(f"dummy{i}") for i in range(6)]
    PE = Eng("pe", nc.tensor, T.sem("s_pe"))
    ACT = Eng("act", nc.scalar, T.sem("s_act"))
    DVE = Eng("dve", nc.vector, T.sem("s_dve"))
    POOL = Eng("pool", nc.gpsimd, T.sem("s_pool"))
    SP = Eng("sp", nc.sync, T.sem("s_sp"))

    def din(name, shape, dt=F32):
        return nc.dram_tensor(name, list(shape), dt, kind="ExternalInput").ap()

    def dout(name, shape, dt=F32):
        return nc.dram_tensor(name, list(shape), dt, kind="ExternalOutput").ap()

    def dscr(name, shape, dt=BF16):
        return nc.dram_tensor(name, list(shape), dt, kind="Internal").ap()

    xp = din("xp", [2, SEQ, 1024])
    memp = din("memp", [2, 256, 1024])
    xs = din("xs", [16, 1024])
    pool_c = din("pool_c", [n_pool * 16, 2048])
    pool_r = din("pool_r", [n_pool * 16, 256])
    ptrep = din("ptrep", [128, 128], I32)
    gq = din("gq", [128, 1], I32)
    stc = din("stc", [16, 30 * 512])
    cmk = din("cmk", [16, 256, 1024])
    cmv = din("cmv", [16, 256, 1024])
    w_in = din("w_in", [1024, 2080])
    kvg = din("kvg", [1, 256])
    w_uk = din("w_uk", [256, 512])
    w_uv = din("w_uv", [256, 512])
    conv_w = din("conv_w", [31, 512])
    conv_b = din("conv_b", [1, 512])
    conv_g = din("conv_g", [1, 512])
    conv_bb = din("conv_bb", [1, 512])
    w_o = din("w_o", [1024, 1024])
    w_xq = din("w_xq", [1024, 1024])
    w_xk = din("w_xk", [1024, 1024])
    w_xv = din("w_xv", [1024, 1024])
    w_xo = din("w_xo", [1024, 1024])
    w_up = din("w_up", [1024, 4096])
    w_down = din("w_down", [4096, 1024])
    lnv = [din(n, [1, 1024]) for n in ("ln1_g", "ln1_b", "ln2_g", "ln2_b", "ln3_g", "ln3_b")]
    c_ident = din("c_ident", [128, 128], BF16)
    c_mask = din("c_mask", [128, 128], BF16)
    c_ropeC = din("c_ropeC", [128, 16 * 32])
    c_ropeS = din("c_ropeS", [128, 16 * 32])
    c_ropeCs = din("c_ropeCs", [128, 32])
    c_ropeSs = din("c_ropeSs", [128, 32])
    c_injk = din("c_injk", [32, 96], BF16)
    c_sel = din("c_sel", [16, 16 * 128])

    y_p = dout("y_p", [2, SEQ, 1024])
    y_s = dout("y_s", [16, 1024])
    ckv_p = dout("ckv_p", [2, SEQ, 256])
    kr_p = dout("kr_p", [2, SEQ, 32])
    conv_p = dout("conv_p", [2, 30, 512])
    mk_p = dout("mk_p", [2, 256, 1024])
    mv_p = dout("mv_p", [2, 256, 1024])
    ckv_s = dout("ckv_s", [16, 256])
    kr_s = dout("kr_s", [16, 32])
    conv_s = dout("conv_s", [16, 30 * 512])

    slabs = {}
    cvt_sems = [T.dsem(f"cv{i}") for i in range(4)]
    cvt_i = [0]

    def mk_slab(name, src3):
        a, b = src3.shape[1], src3.shape[2]
        scr = dscr("ws_" + name, [128, a * b])
        ds = cvt_sems[cvt_i[0] % 4]
        cvt_i[0] += 1
        T.dma(POOL, ds, lambda: nc.gpsimd.dma_start(out=scr.rearrange("p (a b) -> p a b", b=b), in_=src3),
              W=[("ws", name)])
        slabs[name] = (scr, a, b)

    def cols(w, c0, c1):
        return w[:, c0:c1].rearrange("(c p) n -> p c n", p=128)

    mk_slab("in0", cols(w_in, 0, 512))
    mk_slab("in1", cols(w_in, 512, 1056))
    mk_slab("in2", cols(w_in, 1056, 1568))
    mk_slab("in3", cols(w_in, 1568, 2080))
    rest_done = [False]

    def mk_rest_slabs():
        if rest_done[0]:
            return
        rest_done[0] = True
        for nm, w in (("o", w_o), ("xq", w_xq), ("xo", w_xo)):
            mk_slab(nm + "0", cols(w, 0, 512))
            mk_slab(nm + "1", cols(w, 512, 1024))
        for g in range(8):
            mk_slab(f"up{g}", cols(w_up, g * 512, (g + 1) * 512))
        for fh in range(2):
            for qd in range(4):
                mk_slab(f"dn{fh}{qd}", w_down[fh * 2048:(fh + 1) * 2048, qd * 256:(qd + 1) * 256]
                        .rearrange("(f p) n -> p f n", p=128))
        for nm, w in (("xk", w_xk), ("xv", w_xv)):
            mk_slab(nm + "0", cols(w, 0, 512))
            mk_slab(nm + "1", cols(w, 512, 1024))

    def sb(name, shape, dt):
        return nc.alloc_sbuf_tensor(name, list(shape), dt)

    ident = sb("ident", [128, 128], BF16)
    mask = sb("mask", [128, 128], BF16)
    ones = sb("ones", [128, 128], BF16)
    injk = sb("injk", [32, 96], BF16)
    ropeC = sb("ropeC", [128, 16, 32], F32)
    ropeS = sb("ropeS", [128, 16, 32], F32)
    ropeCs = sb("ropeCs", [128, 32], F32)
    ropeSs = sb("ropeSs", [128, 32], F32)
    kvg_b = sb("kvg_b", [128, 256], F32)
    WukP = sb("WukP", [128, 2, 8, 96], BF16)
    Wuv = sb("Wuv", [128, 2, 512], BF16)
    cwT = sb("cwT", [128, 4, 31], F32)
    cvec = sb("cvec", [128, 3, 4], F32)
    bvec = sb("bvec", [128, 2, 1024], F32)
    WR_N = 3
    WR = sb("WR", [128, WR_N, 4352], BF16)
    KT = sb("KT", [128, 8, SEQ], BF16)
    Vaug = sb("Vaug", [128, 16, 4, 192], BF16)
    mkT = sb("mkT", [128, 8, 256], BF16)
    mvb = sb("mvb", [128, 2, 1024], BF16)
    Rb = sb("Rb", [128, 4, 1024], F32)
    xring = sb("xring", [128, 1, 1024], F32)
    Cb = sb("Cb", [128, 2, 1024], BF16)
    Db = sb("Db", [128, 8, 512], BF16)
    uT = sb("uT", [128, 4, 30 + 512], F32)
    uTb = sb("uTb", [128, 4, 30 + 512], BF16)
    Pr = sb("Pr", [128, 4, 512], BF16)
    QT = sb("QT", [128, 8, 512], BF16)
    mixT = sb("mixT", [128, 8, 512], BF16)
    hT = sb("hT", [128, 16, 512], BF16)
    qtm = hT[:, 0:6, :].rearrange("p a b -> p (a b)").rearrange("p (t e) -> p t e", e=768)
    ckb = sb("ckb", [128, 4, 288], BF16)
    ckT = sb("ckT", [128, 2, 512], BF16)
    krT = sb("krT", [32, 512], BF16)
    Cbx = hT[:, 0:8, :].rearrange("p a b -> p (a b)").rearrange("p (t e) -> p t e", e=1024)
    tmpA = sb("tmpA", [128, 1024], F32)
    tmpB = sb("tmpB", [128, 1024], F32)
    cacc = Rb[:, :, 0:512]
    csq = Pr
    cyb = QT
    small = sb("small", [128, 64], F32)
    stats = sb("stats", [128, 2, 6], F32)
    mvar = sb("mvar", [128, 2], F32)

    psf = nc.alloc_psum_tensor("psf", [128, 6, 512], F32)
    psb = nc.alloc_psum_tensor("psb", [128, 2, 1024], BF16)
    pf_i = [0]
    pb_i = [0]

    nf_mod = [4]

    def nf():
        i = pf_i[0] % nf_mod[0]
        pf_i[0] += 1
        return i

    def nf2():
        while pf_i[0] % 2:
            pf_i[0] += 1
        i = pf_i[0] % 4
        pf_i[0] += 2
        return i

    ph_i = [0]

    def nh():
        i = 4 + ph_i[0] % 2
        ph_i[0] += 1
        return i

    def nb():
        i = pb_i[0] % 2
        pb_i[0] += 1
        return i

    ld_sems = [T.dsem(f"ld{i}") for i in range(8)]
    ld_i = [0]
    st_sems = [T.dsem(f"st{i}") for i in range(8)]
    st_i = [0]

    def load(out, in_, W, R=(), q=None):
        ds = ld_sems[ld_i[0] % 8]
        ld_i[0] += 1
        if q is POOL:
            T.dma(POOL, ds, lambda: nc.gpsimd.dma_start(out=out, in_=in_), R=R, W=W)
        else:
            T.dma(SP, ds, lambda: nc.sync.dma_start(out=out, in_=in_), R=R, W=W)

    def store(out, in_, R, W=()):
        ds = st_sems[st_i[0] % 8]
        st_i[0] += 1
        T.dma(SP, ds, lambda: nc.sync.dma_start(out=out, in_=in_), R=R, W=W)

    wr_sems = [T.dsem(f"wr{i}") for i in range(WR_N)]
    wr_i = [0]

    def wslab(name):
        scr, a, b = slabs[name]
        i = wr_i[0] % WR_N
        wr_i[0] += 1
        view = WR[:, i, 0:a * b].rearrange("p (a b) -> p a b", b=b)
        T.dma(SP, wr_sems[i], lambda: nc.sync.dma_start(out=view, in_=scr.rearrange("p (a b) -> p a b", b=b)),
              R=[("ws", name)], W=[("WR", i)])
        return view, ("WR", i)

    def mm(out, lhsT, rhs, start, stop, R, W, sig=False):
        T.op(PE, lambda: nc.tensor.matmul(out, lhsT, rhs, start=start, stop=stop), R=R, W=W, sig=sig)

    def tr(out, in_, R, W, sig=False):
        k = in_.shape[0]
        T.op(PE, lambda: nc.tensor.transpose(out, in_, ident[0:k, 0:k]), R=list(R) + ["ident"], W=W, sig=sig)

    def act(out, in_, func, R, W, bias=0.0, scale=1.0, accum=None):
        if accum is None:
            T.op(ACT, lambda: nc.scalar.activation(out, in_, func, bias=bias, scale=scale), R=R, W=W)
        else:
            T.op(ACT, lambda: nc.scalar.activation(out, in_, func, bias=bias, scale=scale, accum_out=accum), R=R, W=W)

    def tt(E, out, in0, in1, op, R, W):
        T.op(E, lambda: E.eng.tensor_tensor(out, in0, in1, op), R=R, W=W)

    def ts(E, out, in0, s1, s2, op0, op1, R, W):
        if op1 is None:
            T.op(E, lambda: E.eng.tensor_scalar(out, in0, s1, None, op0), R=R, W=W)
        else:
            T.op(E, lambda: E.eng.tensor_scalar(out, in0, s1, s2, op0, op1), R=R, W=W)

    def stt(E, out, in0, scalar, in1, op0, op1, R, W):
        T.op(E, lambda: E.eng.scalar_tensor_tensor(out, in0, scalar, in1, op0, op1), R=R, W=W)

    def cp(E, out, in_, R, W):
        if E is ACT:
            T.op(ACT, lambda: nc.scalar.activation(out, in_, AF.Identity), R=R, W=W)
        else:
            T.op(E, lambda: E.eng.tensor_copy(out, in_), R=R, W=W)

    def bvload(idx, slot):
        load(bvec[:, slot, :], lnv[idx].partition_broadcast(128), W=[("bvec", slot)])

    load(ident[:], c_ident, W=["ident"])
    load(mask[:], c_mask, W=["mask"])
    load(injk[:], c_injk, W=["injk"])
    load(ropeC[:], c_ropeC.rearrange("p (a b) -> p a b", b=32), W=["rope"])
    load(ropeS[:], c_ropeS.rearrange("p (a b) -> p a b", b=32), W=["rope"])
    load(ropeCs[:], c_ropeCs, W=["ropes"])
    load(ropeSs[:], c_ropeSs, W=["ropes"])
    load(kvg_b[:], kvg.partition_broadcast(128), W=["kvg"])
    T.op(DVE, lambda: nc.vector.memset(ones[:], 1.0), W=["ones"])
    T.op(DVE, lambda: nc.vector.memset(WukP[:], 0.0), W=["WukP"])
    T.op(DVE, lambda: nc.vector.memset(Vaug[:], 1.0), W=["Vaug_init"])
    for lc in range(2):
        load(WukP[:, lc, :, 0:64], w_uk[lc * 128:(lc + 1) * 128, :].rearrange("p (h e) -> p h e", e=64),
             R=["WukP"], W=[("WukPl", lc)], q=POOL)
        load(Wuv[:, lc, :], w_uv[lc * 128:(lc + 1) * 128, :], W=[("Wuv", lc)], q=POOL)
    WUK = [("WukPl", 0), ("WukPl", 1), "WukP"]
    WUV = [("Wuv", 0), ("Wuv", 1)]
    with nc.allow_non_contiguous_dma(reason="tiny transposed parameter loads"):
        for cc in range(4):
            load(cwT[:, cc, :], conv_w[:, cc * 128:(cc + 1) * 128].rearrange("k p -> p k"), W=["cwT"])
            for i, v in enumerate((conv_b, conv_g, conv_bb)):
                load(cvec[:, i, cc:cc + 1], v[:, cc * 128:(cc + 1) * 128].rearrange("o p -> p o"), W=[("cvec", i)])
    CV = [("cvec", 0), ("cvec", 1), ("cvec", 2)]
    for cc in range(4):
        stg = hT[:, (cc % 2) * 8:(cc % 2) * 8 + 8, :].rearrange("p a b -> p (a b)")[:, 0:31 * 128].rearrange("p (k c) -> p k c", c=128)
        for k in range(31):
            ts(DVE, stg[:, k, :], ident[:, :], cwT[:, cc, k:k + 1], None, ALU.mult, None, R=["ident", "cwT"], W=[("cdstg", cc % 2)])
        scr = dscr(f"ws_cd{cc}", [128, 31 * 128])
        store(scr.rearrange("p (k c) -> p k c", c=128), stg, R=[("cdstg", cc % 2)], W=[("ws", f"cd{cc}")])
        slabs[f"cd{cc}"] = (scr, 31, 128)

    def rstd_from_var(out, var_ap, R, W):
        act(out, var_ap, AF.Sqrt, R=R, W=W, bias=EPS, scale=1.0)
        T.op(DVE, lambda: nc.vector.reciprocal(out, out), R=W, W=W)

    def layer_norm_rows(t_key, rows, src, dst, gi, bi, bf_out, bf_key):
        for hlf in range(2):
            T.op(DVE, lambda h=hlf: nc.vector.bn_stats(stats[0:rows, h, :], src[:, h * 512:(h + 1) * 512]),
                 R=[t_key], W=[("stats", hlf)])
        T.op(DVE, lambda: nc.vector.bn_aggr(mvar[0:rows, :], stats[0:rows, :, :]),
             R=[("stats", 0), ("stats", 1)], W=["mvar"])
        rstd_from_var(small[0:rows, 0:1], mvar[0:rows, 1:2], R=["mvar"], W=[("small", 0)])
        stt(DVE, small[0:rows, 1:2], mvar[0:rows, 0:1], -1.0, small[0:rows, 0:1], ALU.mult, ALU.mult,
            R=["mvar", ("small", 0)], W=[("small", 1)])
        act(dst, src, AF.Identity, R=[t_key, ("small", 0), ("small", 1)], W=[t_key],
            bias=small[0:rows, 1:2], scale=small[0:rows, 0:1])
        tt(POOL, dst, dst, bvec[0:rows, 0, :], ALU.mult, R=[t_key, ("bvec", 0)], W=[t_key])
        tt(POOL, dst, dst, bvec[0:rows, 1, :], ALU.add, R=[t_key, ("bvec", 1)], W=[t_key])
        if bf_out is not None:
            cp(ACT, bf_out, dst, R=[t_key], W=[bf_key])

    def rope_rows(rows, src3, dst3, C2, S2, tmp3, R, W, tmpkey):
        tt(DVE, tmp3[:, :, 0:16], src3[:, :, 16:32], S2[:, :, 0:16], ALU.mult, R=R, W=[tmpkey])
        tt(DVE, tmp3[:, :, 16:32], src3[:, :, 0:16], S2[:, :, 16:32], ALU.mult, R=R, W=[tmpkey])
        tt(DVE, tmp3[:, :, 32:64], src3, C2, ALU.mult, R=R, W=[tmpkey])
        tt(POOL, dst3, tmp3[:, :, 0:32], tmp3[:, :, 32:64], ALU.add, R=[tmpkey], W=W)

    def mem_kv_seq(s):
        memT = Db
        for mt in range(2):
            load(Cb[:, mt, :], memp[s, mt * 128:(mt + 1) * 128, :], W=[("Cb", mt)], q=POOL)
        for c in range(8):
            b = nb()
            for mt in range(2):
                tr(psb[:, b, mt * 128:(mt + 1) * 128], Cb[:, mt, c * 128:(c + 1) * 128],
                   R=[("Cb", mt)], W=[("pb", b)], sig=(mt == 1))
            cp(DVE, memT[:, c, 0:256], psb[:, b, 0:256], R=[("pb", b)], W=[("Db", c)])
        DBK = [("Db", c) for c in range(8)]
        chk(0.3)
        for which, dst in (("xk", mk_p), ("xv", mv_p)):
            cf = (lambda x: x) if which == "xk" else (lambda x: 0.97 + 0.03 * x)
            views = [wslab(which + "0"), wslab(which + "1")]
            chk(cf(0.5))
            for mt in range(2):
                b2 = nf2()
                for hf in range(2):
                    wv, wk = views[hf]
                    for c in range(8):
                        mm(psf[:, b2 + hf, :], memT[:, c, mt * 128:(mt + 1) * 128], wv[:, c, :], c == 0, c == 7,
                           R=DBK + [wk], W=[("pf", b2 + hf)], sig=(c == 7))
                chk(cf(0.7))
                t_ = tmpA if mt == 0 else tmpB
                tk = "tmpA" if mt == 0 else "tmpB"
                for hf in range(2):
                    cp(ACT, t_[:, hf * 512:(hf + 1) * 512], psf[:, b2 + hf, :], R=[("pf", b2 + hf)], W=[tk])
                chk(cf(0.8))
                store(dst[s, mt * 128:(mt + 1) * 128, :], t_[:, :], R=[tk])
                chk(cf(0.9))
                if which == "xv":
                    for hf in range(2):
                        cp(DVE, mvb[:, mt, hf * 512:(hf + 1) * 512], t_[:, hf * 512:(hf + 1) * 512],
                           R=[tk], W=[("mvb", mt)])
            chk(cf(0.95))
            if which == "xk":
                for e in range(8):
                    wv, wk = views[e // 4]
                    b1 = nf()
                    for c in range(8):
                        mm(psf[:, b1, 0:256], wv[:, c, (e % 4) * 128:(e % 4 + 1) * 128], memT[:, c, 0:256], c == 0, c == 7,
                           R=DBK + [wk], W=[("pf", b1)], sig=(c == 7))
                    cp(DVE, mkT[:, e, :], psf[:, b1, 0:256], R=[("pf", b1)], W=[("mkT", e)])
                chk(0.97)

    xpref = set()
    MKT = [("mkT", e) for e in range(8)]
    MVB = [("mvb", 0), ("mvb", 1)]

    def prompt_block(s, j):
        t0 = j * 512
        DBK = [("Db", c) for c in range(8)]
        T.retire([("hT", i) for i in range(16)] + [("Cbx", t) for t in range(4)], [("qtm", t) for t in range(4)])
        if PROFILE:
            T.stage = "s01"
        if j == 0:
            T.op(POOL, lambda: nc.gpsimd.memset(uTb[:, :, 0:30], 0.0), W=["uT_head"])
        for t in range(4):
            if (s, j, t) not in xpref:
                load(Cb[:, t % 2, :], xp[s, t0 + t * 128:t0 + (t + 1) * 128, :], W=[("Cb", t % 2)], q=POOL)
            for c in range(8):
                if c % 4 == 0:
                    b = nb()
                tr(psb[:, b, (c % 4) * 128:(c % 4 + 1) * 128], Cb[:, t % 2, c * 128:(c + 1) * 128],
                   R=[("Cb", t % 2)], W=[("pb", b)], sig=(c % 4 == 3))
                if c % 4 == 3:
                    c0 = c - 3
                    cp(ACT if c0 else DVE,
                       Db[:, c0:c0 + 4, t * 128:(t + 1) * 128],
                       psb[:, b, 0:512].rearrange("p (a b) -> p a b", b=128),
                       R=[("pb", b)], W=[("Db", cc) for cc in range(c0, c0 + 4)])
        chk(2)
        if PROFILE:
            T.stage = "s2a"
        v0, k0 = wslab("in0")
        v1, k1 = wslab("in1")
        for t in range(4):
            bA = nf()
            bB = nf()
            bC = nf()
            for (ob, o0, o1, vv, kk, c0_, c1_) in (
                    (bA, 0, 480, v0, k0, 0, 480), (bB, 0, 32, v0, k0, 480, 512), (bB, 32, 288, v1, k1, 0, 256),
                    (bB, 288, 320, v1, k1, 512, 544), (bC, 0, 256, v1, k1, 256, 512)):
                for c in range(8):
                    mm(psf[:, ob, o0:o1], Db[:, c, t * 128:(t + 1) * 128], vv[:, c, c0_:c1_], c == 0, c == 7,
                       R=DBK + [kk], W=[("pf", ob)], sig=(c == 7))
            PQ = [("pf", bA), ("pf", bB), ("pf", bC)]
            tile_i = j * 4 + t
            qd3 = qtm[:, t, :].rearrange("p (h e) -> p h e", e=96)
            for (bk, h0, nh_) in ((bA, 0, 5), (bB, 5, 3)):
                q3 = psf[:, bk, 0:nh_ * 96].rearrange("p (h e) -> p h e", e=96)
                cp(ACT, qd3[:, h0:h0 + nh_, 0:64], q3[:, :, 0:64], R=PQ, W=[("qtm", t)])
                tmp3 = tmpA[:, 0:nh_ * 64].rearrange("p (h e) -> p h e", e=64)
                rope_rows(128, q3[:, :, 64:96], qd3[:, h0:h0 + nh_, 64:96],
                          ropeC[:, tile_i:tile_i + 1, :].to_broadcast([128, nh_, 32]),
                          ropeS[:, tile_i:tile_i + 1, :].to_broadcast([128, nh_, 32]),
                          tmp3, R=PQ + ["rope"], W=[("qtm", t)], tmpkey="tmpA")
            pckv = psf[:, bC, 0:256]
            b1 = bB
            cp(DVE, tmpB[:, 768:1024], pckv, R=PQ, W=["tmpB"])
            pckv = tmpB[:, 768:1024]
            act(tmpB[:, 0:256], pckv, AF.Square, R=["tmpB"], W=["tmpB"])
            T.op(DVE, lambda: nc.vector.reduce_sum(small[:, 2:3], tmpB[:, 0:256], axis=AX.X), R=["tmpB"], W=["tmpB"])
            act(small[:, 3:4], small[:, 2:3], AF.Sqrt, R=["tmpB"], W=[("small", 3)], bias=EPS, scale=1.0 / 256.0)
            T.op(DVE, lambda: nc.vector.reciprocal(small[:, 3:4], small[:, 3:4]), R=[("small", 3)], W=[("small", 3)])
            stt(DVE, tmpB[:, 256:512], pckv, small[:, 3:4], kvg_b[:, :], ALU.mult, ALU.mult,
                R=PQ + [("small", 3), "kvg", "tmpB"], W=["tmpB"])
            store(ckv_p[s, t0 + t * 128:t0 + (t + 1) * 128, :], tmpB[:, 256:512], R=["tmpB"])
            if DEBUG and t == 1:
                store(y_p[s, 0:128, 0:64], small[:, :], R=["tmpB", ("small", 3)])
                store(y_p[s, 0:128, 64:320], tmpB[:, 768:1024], R=["tmpB"])
                store(y_p[s, 0:128, 320:576], tmpB[:, 256:512], R=["tmpB"])
            cp(ACT, ckb[:, t, 0:256], tmpB[:, 256:512], R=["tmpB"], W=[("ckb", t)])
            kr3 = tmpB[:, 512:544].rearrange("p (h e) -> p h e", e=32)
            rope_rows(128, psf[:, b1, 288:320].rearrange("p (h e) -> p h e", e=32), kr3,
                      ropeC[:, tile_i:tile_i + 1, :], ropeS[:, tile_i:tile_i + 1, :],
                      tmpB[:, 576:640].rearrange("p (h e) -> p h e", e=64),
                      R=[("pf", b1), "rope"], W=["tmpB"], tmpkey="tmpB")
            store(kr_p[s, t0 + t * 128:t0 + (t + 1) * 128, :], tmpB[:, 512:544], R=["tmpB"])
            cp(ACT, ckb[:, t, 256:288], tmpB[:, 512:544], R=["tmpB"], W=[("ckb", t)])
        chk(3)
        QTM = [("qtm", t) for t in range(4)]
        CKB = [("ckb", t) for t in range(4)]
        if PROFILE:
            T.stage = "s2b"
        for h in range(8):
            b = nb()
            for t in range(4):
                tr(psb[0:96, b, t * 128:(t + 1) * 128], qtm[:, t, h * 96:(h + 1) * 96], R=QTM, W=[("pb", b)], sig=(t == 3))
            cp(ACT if h % 2 else DVE, QT[0:96, h, :], psb[0:96, b, 0:512], R=[("pb", b)], W=[("QT", h)])
        for lc in range(2):
            b = nb()
            for t in range(4):
                tr(psb[:, b, t * 128:(t + 1) * 128], ckb[:, t, lc * 128:(lc + 1) * 128], R=CKB, W=[("pb", b)], sig=(t == 3))
            cp(DVE, ckT[:, lc, :], psb[:, b, 0:512], R=[("pb", b)], W=[("ckT", lc)])
        b = nb()
        for t in range(4):
            tr(psb[0:32, b, t * 128:(t + 1) * 128], ckb[:, t, 256:288], R=CKB, W=[("pb", b)], sig=(t == 3))
        cp(DVE, krT[:, :], psb[0:32, b, 0:512], R=[("pb", b)], W=["krT"])
        CKT = [("ckT", 0), ("ckT", 1)]
        if PROFILE:
            T.stage = "s2c"
        for h in range(8):
            b1 = nf()
            mm(psf[0:96, b1, :], WukP[:, 0, h, :], ckT[:, 0, :], True, False, R=CKT + WUK, W=[("pf", b1)])
            mm(psf[0:96, b1, :], WukP[:, 1, h, :], ckT[:, 1, :], False, False, R=CKT + WUK, W=[("pf", b1)])
            mm(psf[0:96, b1, :], injk[:, :], krT[:, :], False, True, R=["krT", "injk"], W=[("pf", b1)], sig=True)
            cp(ACT if h % 2 else DVE, KT[0:96, h, t0:t0 + 512], psf[0:96, b1, :], R=[("pf", b1)], W=[("KT", h, j)])
        if PROFILE:
            T.stage = "s2d"
        for t in range(4):
            b1 = nf()
            for lc in range(2):
                mm(psf[:, b1, :], ckT[:, lc, t * 128:(t + 1) * 128], Wuv[:, lc, :], lc == 0, lc == 1,
                   R=CKT + WUV, W=[("pf", b1)], sig=(lc == 1))
            kt = j * 4 + t
            pv = psf[:, b1, :].rearrange("p (a two e) -> p a two e", two=2, e=64)
            cp(DVE, Vaug[:, kt, :, 0:64], pv[:, :, 0, :], R=[("pf", b1), "Vaug_init"], W=[("V", kt)])
            cp(ACT, Vaug[:, kt, :, 128:192], pv[:, :, 1, :], R=[("pf", b1), "Vaug_init"], W=[("V", kt)])
        chk(4)
        if PROFILE:
            T.stage = "s2e"
        v2, k2 = wslab("in2")
        v3, k3 = wslab("in3")
        for cc in range(4):
            ba = nf()
            bb = nf()
            for c in range(8):
                mm(psf[:, ba, :], v2[:, c, cc * 128:(cc + 1) * 128], Db[:, c, :], c == 0, c == 7, R=DBK + [k2], W=[("pf", ba)])
                mm(psf[:, bb, :], v3[:, c, cc * 128:(cc + 1) * 128], Db[:, c, :], c == 0, c == 7, R=DBK + [k3], W=[("pf", bb)], sig=(c == 7))
            act(tmpA[:, 0:512], psf[:, bb, :], AF.Sigmoid, R=[("pf", bb)], W=["tmpA"])
            tt(DVE, uTb[:, cc, 30:542], psf[:, ba, :], tmpA[:, 0:512], ALU.mult, R=[("pf", ba), "tmpA", "uT_head"], W=[("uT", cc)])
        if j == nblk - 1:
            b2 = nf2()
            for c in range(8):
                xTt = Db[:, c, 384:512]
                mm(psf[:, b2, :], xTt, v2[:, c, :], c == 0, c == 7, R=DBK + [k2], W=[("pf", b2)])
                mm(psf[:, b2 + 1, :], xTt, v3[:, c, :], c == 0, c == 7, R=DBK + [k3], W=[("pf", b2 + 1)], sig=(c == 7))
            act(tmpA[:, 0:512], psf[:, b2 + 1, :], AF.Sigmoid, R=[("pf", b2 + 1)], W=["tmpA"])
            tt(DVE, tmpA[:, 512:1024], psf[:, b2, :], tmpA[:, 0:512], ALU.mult, R=[("pf", b2), "tmpA"], W=["tmpA"])
            store(conv_p[s, :, :], tmpA[98:128, 512:1024], R=["tmpA"])
        chk(5)
        if PROFILE:
            T.stage = "s3"
        nkt = 4 * j + 4
        pairs = [(h, kt) for h in range(8) for kt in range(nkt)]
        bos = {}
        st_info = {}

        def emit_S(h, kt):
            r = kt - 4 * j
            q0 = 0 if r <= 0 else r * 128
            n = 512 - q0
            bs = nf()
            mm(psf[:, bs, 0:n], KT[0:96, h, kt * 128:(kt + 1) * 128], QT[0:96, h, q0:512], True, True,
               R=[("KT", h, kt // 4), ("QT", h)], W=[("pf", bs)], sig=True)
            pi = (h * 64 + kt) % 4
            act(Pr[:, pi, 0:n], psf[:, bs, 0:n], AF.Exp, R=[("pf", bs)], W=[("Pr", pi)], scale=MLA_SCALE)
            if r >= 0:
                tt(POOL, Pr[:, pi, 0:128], Pr[:, pi, 0:128], mask[:, :], ALU.mult, R=[("Pr", pi), "mask"], W=[("Pr", pi)])
            st_info[(h, kt)] = (pi, q0, n)

        def emit_PV(h, kt):
            pi, q0, n = st_info.pop((h, kt))
            if kt == 0:
                bos[h] = nh()
            bo = bos[h]
            odd = h % 2
            lhs = Vaug[:, kt, h // 2, 64:192] if odd else Vaug[:, kt, h // 2, 0:128]
            mm(psf[:, bo, q0:512], lhs, Pr[:, pi, 0:n], kt == 0, kt == nkt - 1,
               R=[("Pr", pi), ("V", kt), "Vaug_init"], W=[("pf", bo)], sig=True)
            if kt == nkt - 1:
                if h % 2 == 0:
                    cp(DVE, tmpA[0:64, 0:512], psf[64:128, bo, :], R=[("pf", bo)], W=["tmpA"])
                    T.op(DVE, lambda: nc.vector.reciprocal(tmpA[0:64, 0:512], tmpA[0:64, 0:512]), R=["tmpA"], W=["tmpA"])
                    tt(DVE, mixT[0:64, h // 2, :], psf[0:64, bo, :], tmpA[0:64, 0:512], ALU.mult,
                       R=[("pf", bo), "tmpA"], W=[("mixT", h // 2)])
                else:
                    cp(DVE, tmpB[64:128, 0:512], psf[0:64, bo, :], R=[("pf", bo)], W=["tmpB"])
                    T.op(DVE, lambda: nc.vector.reciprocal(tmpB[64:128, 0:512], tmpB[64:128, 0:512]), R=["tmpB"], W=["tmpB"])
                    tt(DVE, mixT[64:128, h // 2, :], psf[64:128, bo, :], tmpB[64:128, 0:512], ALU.mult,
                       R=[("pf", bo), "tmpB"], W=[("mixT", h // 2)])

        emit_S(*pairs[0])
        for i in range(len(pairs)):
            if i + 1 < len(pairs):
                emit_S(*pairs[i + 1])
            emit_PV(*pairs[i])
        chk(6)
        if PROFILE:
            T.stage = "s4"
        for cc in range(4):
            wv, wk = wslab(f"cd{cc}")
            b1 = nf()
            for k in range(31):
                mm(psf[:, b1, :], wv[:, k, :], uTb[:, cc, k:k + 512], k == 0, k == 30,
                   R=[("uT", cc), "uT_head", wk], W=[("pf", b1)], sig=(k == 30))
            act(cacc[:, cc, :], psf[:, b1, :], AF.Identity, R=[("pf", b1)] + CV, W=[("Rb", cc)], bias=cvec[:, 0, cc:cc + 1])
            if j < nblk - 1:
                cp(POOL, tmpA[:, 512 + cc * 32:512 + cc * 32 + 30], uTb[:, cc, 512:542], R=[("uT", cc)], W=[("uhc", cc)])
            cp(ACT, cyb[:, cc, :], cacc[:, cc, :], R=[("Rb", cc)], W=[("QT", cc)])
            act(csq[:, cc, :], cacc[:, cc, :], AF.Square, R=[("Rb", cc)], W=[("Pr", cc)])
        if j < nblk - 1:
            for cc in range(4):
                cp(POOL, uTb[:, cc, 0:30], tmpA[:, 512 + cc * 32:512 + cc * 32 + 30], R=[("uhc", cc)], W=["uT_head"])
        bm = nf()
        bq = nf()
        for cc in range(4):
            mm(psf[:, bm, :], ones[:, :], cyb[:, cc, :], cc == 0, cc == 3, R=[("QT", cc), "ones"], W=[("pf", bm)])
            mm(psf[:, bq, :], ones[:, :], csq[:, cc, :], cc == 0, cc == 3, R=[("Pr", cc), "ones"], W=[("pf", bq)], sig=(cc == 3))
        ts(DVE, tmpA[:, 0:512], psf[:, bm, :], 1.0 / 512.0, None, ALU.mult, None, R=[("pf", bm)], W=["tmpA"])
        tt(DVE, tmpA[:, 512:1024], tmpA[:, 0:512], tmpA[:, 0:512], ALU.mult, R=["tmpA"], W=["tmpA"])
        stt(DVE, tmpA[:, 512:1024], psf[:, bq, :], 1.0 / 512.0, tmpA[:, 512:1024], ALU.mult, ALU.subtract,
            R=[("pf", bq), "tmpA"], W=["tmpA"])
        act(tmpA[:, 512:1024], tmpA[:, 512:1024], AF.Sqrt, R=["tmpA"], W=["tmpA"], bias=EPS, scale=1.0)
        T.op(DVE, lambda: nc.vector.reciprocal(tmpA[:, 512:1024], tmpA[:, 512:1024]), R=["tmpA"], W=["tmpA"])
        for cc in range(4):
            tt(DVE, cacc[:, cc, :], cacc[:, cc, :], tmpA[:, 0:512], ALU.subtract, R=[("Rb", cc), "tmpA"], W=[("Rb", cc)])
            tt(POOL, cacc[:, cc, :], cacc[:, cc, :], tmpA[:, 512:1024], ALU.mult, R=[("Rb", cc), "tmpA"], W=[("Rb", cc)])
            act(mixT[:, 4 + cc, :], cacc[:, cc, :], AF.Silu, R=[("Rb", cc)] + CV, W=[("mixT", 4 + cc)],
                bias=cvec[:, 2, cc:cc + 1], scale=cvec[:, 1, cc:cc + 1])
        chk(7)
        MIX = [("mixT", c) for c in range(8)]
        if PROFILE:
            T.stage = "s5"
        T.retire([("qtm", t) for t in range(4)], [("Cbx", t) for t in range(4)])
        post_ln(s, j, MIX, mixT, "o", 0, 1, x_from_dram=True)
        chk(8)
        if PROFILE:
            T.stage = "s7"
        vq = [wslab("xq0"), wslab("xq1")]
        for e in range(8):
            wv, wk = vq[e // 4]
            b1 = nf()
            for c in range(8):
                mm(psf[:, b1, :], wv[:, c, (e % 4) * 128:(e % 4 + 1) * 128], Db[:, c, :], c == 0, c == 7,
                   R=DBK + [wk], W=[("pf", b1)], sig=(c == 7))
            cp(ACT if e % 2 else DVE, QT[:, e, :], psf[:, b1, :], R=[("pf", b1)], W=[("QT", e)])
        if PROFILE:
            T.stage = "s8"
        for h in range(4):
            pis = []
            for mc in range(2):
                bs = nf()
                for dc in range(2):
                    mm(psf[:, bs, :], mkT[:, 2 * h + dc, mc * 128:(mc + 1) * 128], QT[:, 2 * h + dc, :], dc == 0, dc == 1,
                       R=MKT + [("QT", 2 * h + dc)], W=[("pf", bs)], sig=(dc == 1))
                pi = (h * 2 + mc) % 4
                pis.append(pi)
                act(Pr[:, pi, :], psf[:, bs, :], AF.Exp, R=[("pf", bs)], W=[("Pr", pi)], scale=MEM_SCALE)
            bd = nf()
            for mc in range(2):
                mm(psf[:, bd, :], ones[:, :], Pr[:, pis[mc], :], mc == 0, mc == 1, R=[("Pr", pis[mc]), "ones"], W=[("pf", bd)], sig=(mc == 1))
            T.op(DVE, lambda: nc.vector.reciprocal(tmpA[:, 0:512], psf[:, bd, :]), R=[("pf", bd)], W=["tmpA"])
            for dc in range(2):
                bo = nf()
                for mc in range(2):
                    mm(psf[:, bo, :], mvb[:, mc, h * 256 + dc * 128:h * 256 + (dc + 1) * 128], Pr[:, pis[mc], :], mc == 0, mc == 1,
                       R=MVB + [("Pr", pis[mc])], W=[("pf", bo)], sig=(mc == 1))
                tt(DVE, mixT[:, 2 * h + dc, :], psf[:, bo, :], tmpA[:, 0:512], ALU.mult, R=[("pf", bo), "tmpA"], W=[("mixT", 2 * h + dc)])
        chk(9)
        if PROFILE:
            T.stage = "s9"
        post_ln(s, j, MIX, mixT, "xo", 2, 3, x_from_dram=False)
        chk(10)
        if PROFILE:
            T.stage = "s10"
        nxt = (s, j + 1) if j + 1 < nblk else ((s + 1, 0) if s + 1 < nseq else None)
        if nxt is not None and not (nxt[1] == 0):
            for t in range(2):
                load(Cb[:, t, :], xp[nxt[0], nxt[1] * 512 + t * 128:nxt[1] * 512 + (t + 1) * 128, :], W=[("Cb", t)], q=POOL)
                xpref.add((nxt[0], nxt[1], t))
        T.retire([("qtm", t) for t in range(4)] + [("Cbx", t) for t in range(4)], [("hT", i) for i in range(16)])
        for fh in range(2):
            for g in range(4):
                wv, wk = wslab(f"up{fh * 4 + g}")
                for fc in range(4):
                    b1 = nf()
                    for c in range(8):
                        mm(psf[:, b1, :], wv[:, c, fc * 128:(fc + 1) * 128], Db[:, c, :], c == 0, c == 7,
                           R=DBK + [wk], W=[("pf", b1)], sig=(c == 7))
                    act(tmpA[:, 0:512] if fc % 2 == 0 else tmpB[:, 0:512], psf[:, b1, :], AF.Relu, R=[("pf", b1)],
                        W=["tmpA" if fc % 2 == 0 else "tmpB"])
                    tt(DVE, hT[:, g * 4 + fc, :], tmpA[:, 0:512] if fc % 2 == 0 else tmpB[:, 0:512], psf[:, b1, :], ALU.mult,
                       R=[("pf", b1), "tmpA" if fc % 2 == 0 else "tmpB"], W=[("hT", g * 4 + fc)])
            HK = [("hT", i) for i in range(16)]
            for qd in range(4):
                wv, wk = wslab(f"dn{fh}{qd}")
                for t in range(4):
                    b1 = nf()
                    for fc in range(16):
                        mm(psf[:, b1, 0:256], hT[:, fc, t * 128:(t + 1) * 128], wv[:, fc, :], fc == 0, fc == 15,
                           R=HK + [wk], W=[("pf", b1)], sig=(fc == 15))
                    dstv = Rb[:, t, qd * 256:(qd + 1) * 256]
                    if fh == 0:
                        stt(DVE, dstv, dstv, ALPHA, psf[:, b1, 0:256], ALU.mult, ALU.add, R=[("Rb", t), ("pf", b1)], W=[("Rb", t)])
                    else:
                        tt(DVE, dstv, dstv, psf[:, b1, 0:256], ALU.add, R=[("Rb", t), ("pf", b1)], W=[("Rb", t)])
        bvload(4, 0)
        bvload(5, 1)
        for t in range(4):
            layer_norm_rows(("Rb", t), 128, Rb[:, t, :], Rb[:, t, :], 4, 5, None, None)
            store(y_p[s, t0 + t * 128:t0 + (t + 1) * 128, :], Rb[:, t, :], R=[("Rb", t)])

    def post_ln(s, j, INK, inT, wname, gi, bi, x_from_dram):
        t0 = j * 512
        views = [wslab(wname + "0"), wslab(wname + "1")]
        bvload(gi, 0)
        bvload(bi, 1)

        def mm_ln(t):
            b2 = nf2()
            for hf in range(2):
                wv, wk = views[hf]
                for c in range(8):
                    mm(psf[:, b2 + hf, :], inT[:, c, t * 128:(t + 1) * 128], wv[:, c, :], c == 0, c == 7,
                       R=INK + [wk], W=[("pf", b2 + hf)], sig=(c == 7))
            if x_from_dram:
                load(xring[:, 0, :], xp[s, t0 + t * 128:t0 + (t + 1) * 128, :], W=[("xring", 0)])
            for hf in range(2):
                src = xring[:, 0, hf * 512:(hf + 1) * 512] if x_from_dram else Rb[:, t, hf * 512:(hf + 1) * 512]
                stt(DVE, Rb[:, t, hf * 512:(hf + 1) * 512], src, ALPHA, psf[:, b2 + hf, :], ALU.mult, ALU.add,
                    R=[("pf", b2 + hf), ("xring", 0), ("Rb", t)], W=[("Rb", t)])
            layer_norm_rows(("Rb", t), 128, Rb[:, t, :], Rb[:, t, :], gi, bi, Cbx[:, t, :], ("Cbx", t))

        def tr_ev(t):
            for c in range(8):
                if c % 4 == 0:
                    b = nb()
                tr(psb[:, b, (c % 4) * 128:(c % 4 + 1) * 128], Cbx[:, t, c * 128:(c + 1) * 128],
                   R=[("Cbx", t)], W=[("pb", b)], sig=(c % 4 == 3))
                if c % 4 == 3:
                    c0 = c - 3
                    cp(ACT if c0 else DVE,
                       Db[:, c0:c0 + 4, t * 128:(t + 1) * 128],
                       psb[:, b, 0:512].rearrange("p (a b) -> p a b", b=128),
                       R=[("pb", b)], W=[("Db", cc) for cc in range(c0, c0 + 4)])

        mm_ln(0)
        mm_ln(1)
        mm_ln(2)
        tr_ev(0)
        mm_ln(3)
        tr_ev(1)
        tr_ev(2)
        tr_ev(3)

    def barrier():
        engs = (PE, ACT, DVE, POOL, SP)
        dss = st_sems + ld_sems + wr_sems + cvt_sems + g_sems
        for E in engs:
            for F in engs:
                if F is not E and F.count and E.seen.get(F, 0) < F.count:
                    E.eng.wait_ge(F.sem, F.count)
                    E.seen[F] = F.count
            for ds in dss:
                if ds.count and E.seen.get(ds, 0) < ds.count:
                    E.eng.wait_ge(ds.sem, ds.count * 16)
                    E.seen[ds] = ds.count

    g_sems = [T.dsem(f"g{i}") for i in range(4)]

    def sample_phase():
        NS = 16
        KTf = KT[:, :, :].rearrange("p a b -> p (a b)")
        hTf = hT[:, :, :].rearrange("p a b -> p (a b)")
        QTf = QT[:, :, :].rearrange("p a b -> p (a b)")
        Rbf = Rb[:, :, :].rearrange("p a b -> p (a b)")
        uTf = uT[:, :, :].rearrange("p a b -> p (a b)")
        bvf = bvec[:, :, :].rearrange("p a b -> p (a b)")
        mixf = mixT[:, :, :].rearrange("p a b -> p (a b)")
        Dbf = Db[:, :, :].rearrange("p a b -> p (a b)")

        def carver(arena):
            off = [0]
            def carve(n):
                v = arena[:, off[0]:off[0] + n]
                off[0] += n
                return v
            return carve
        ck, ch, cq, cr, cm, cd = carver(KTf), carver(hTf), carver(QTf), carver(Rbf), carver(mixf), carver(Dbf)
        cpg = [ck(2048).rearrange("p (r l) -> p r l", l=256) for _ in range(2)]
        rpg = [ck(256).rearrange("p (r l) -> p r l", l=32) for _ in range(2)]
        cT = [ck(2048).rearrange("p (a b) -> p a b", b=1024) for _ in range(2)]
        rT = [ck(1024) for _ in range(2)]
        cnx = ck(16 * 264).rearrange("p (b l) -> p b l", l=264)
        Pp = ck(512)
        WukT = ch(2048).rearrange("p (h l) -> p h l", l=256)
        ql_bf = ch(2048)
        xs_bf = ch(1024)
        q_s = ch(768)
        cnew_bf = ch(288)
        mix_s = ch(1024)
        xsT = cq(128).rearrange("p (c b) -> p c b", b=16)
        qT_s = cq(128).rearrange("p (h b) -> p h b", b=16)
        qlT = cq(256).rearrange("p (l b h) -> p l b h", b=16, h=8)
        qrT = cq(128).rearrange("p (b h) -> p b h", h=8)
        cnT = cq(32).rearrange("p (l b) -> p l b", b=16)
        krnT = cq(16)
        olT = cq(256).rearrange("p (l h b) -> p l h b", h=8, b=16)
        mixT_s = cq(128).rearrange("p (c b) -> p c b", b=16)
        o2T_s = cq(128).rearrange("p (c b) -> p c b", b=16)
        hT_s = cq(512)
        pnew = cq(8)
        ol_bf = cq(256)
        P2bf = cq(8)
        dhl = cq(16)
        xs_f = cr(1024)
        r_s = cr(1024)
        u_s = cr(512)
        q2_f = cr(1024)
        misc = cr(512)
        xb16 = cm(1024)
        ptb = sb("ptb", [128, 128], I32)
        gqt = sb("gqt", [128, 1], I32)
        idx = sb("idx", [128, 128], I32)
        q2_scr = dscr("q2_scr", [16, 1024], F32)
        onesf = sb("onesf", [128, 1], F32)
        idxc = [sb(f"idxc{i}", [128, 1], I32) for i in range(2)]

        load(ptb[:], ptrep, W=["ptb"])
        load(gqt[:], gq, W=["gqt"])
        T.op(DVE, lambda: nc.vector.memset(onesf[:], 1.0), W=["onesf"])
        ts(DVE, idx[:], ptb[:], 16, gqt[:, 0:1], ALU.mult, ALU.add, R=["ptb", "gqt"], W=["idx"])
        if DEBUG == "idx":
            cp(DVE, tmpA[:, 0:128], idx[:], R=["idx"], W=["tmpA"])
            store(ckv_p[0, 0:128, 0:128], tmpA[:, 0:128], R=["tmpA"])
            barrier()
            raise _Stop()
        T.op(POOL, lambda: nc.gpsimd.memset(cnx[0:1, :, :], 1.0), W=["cnx"])

        def tr16(dst, src16, ncols, key_r, key_w, E=DVE):
            b = nb()
            for c in range(ncols):
                tr(psb[:, b, c * 16:(c + 1) * 16], src16[0:16, c * 128:(c + 1) * 128], R=[key_r], W=[("pb", b)], sig=(c == ncols - 1))
            cp(E, dst, psb[:, b, 0:ncols * 16].rearrange("p (c b) -> p c b", b=16), R=[("pb", b)], W=[key_w])

        if PROFILE:
            T.stage = "smp_pre"
        load(xs_f[0:16, :], xs, W=["xs_f"])
        load(xs_bf[0:16, :], xs, W=["xs_bf"], q=POOL)
        tr16(xsT, xs_bf, 8, "xs_bf", "xsT")
        v0, k0 = wslab("in0")
        v1, k1 = wslab("in1")
        bA, bB, bC = nf(), nf(), nf()
        for (ob, o0, o1, vv, kk, c0_, c1_) in ((bA, 0, 480, v0, k0, 0, 480), (bB, 0, 32, v0, k0, 480, 512),
                                               (bB, 32, 288, v1, k1, 0, 256), (bB, 288, 320, v1, k1, 512, 544),
                                               (bC, 0, 256, v1, k1, 256, 512)):
            for c in range(8):
                mm(psf[0:16, ob, o0:o1], xsT[:, c, :], vv[:, c, c0_:c1_], c == 0, c == 7, R=["xsT", kk], W=[("pf", ob)], sig=(c == 7))
        PQ = [("pf", bA), ("pf", bB), ("pf", bC)]
        qd3 = q_s[0:16, :].rearrange("p (h e) -> p h e", e=96)
        for (bk, h0, nh_) in ((bA, 0, 5), (bB, 5, 3)):
            q3 = psf[0:16, bk, 0:nh_ * 96].rearrange("p (h e) -> p h e", e=96)
            cp(ACT, qd3[:, h0:h0 + nh_, 0:64], q3[:, :, 0:64], R=PQ, W=["q_s"])
            tmp3 = tmpA[0:16, 0:nh_ * 64].rearrange("p (h e) -> p h e", e=64)
            rope_rows(16, q3[:, :, 64:96], qd3[:, h0:h0 + nh_, 64:96],
                      ropeCs[0:16, :].rearrange("p (o e) -> p o e", o=1).to_broadcast([16, nh_, 32]),
                      ropeSs[0:16, :].rearrange("p (o e) -> p o e", o=1).to_broadcast([16, nh_, 32]),
                      tmp3, R=PQ + ["ropes"], W=["q_s"], tmpkey="tmpA")
        cp(DVE, tmpB[0:16, 768:1024], psf[0:16, bC, 0:256], R=PQ, W=["tmpB"])
        zc = tmpB[0:16, 768:1024]
        act(tmpB[0:16, 0:256], zc, AF.Square, R=["tmpB"], W=["tmpB"])
        T.op(DVE, lambda: nc.vector.reduce_sum(small[0:16, 2:3], tmpB[0:16, 0:256], axis=AX.X), R=["tmpB"], W=["tmpB"])
        act(small[0:16, 3:4], small[0:16, 2:3], AF.Sqrt, R=["tmpB"], W=[("small", 3)], bias=EPS, scale=1.0 / 256.0)
        T.op(DVE, lambda: nc.vector.reciprocal(small[0:16, 3:4], small[0:16, 3:4]), R=[("small", 3)], W=[("small", 3)])
        stt(DVE, tmpB[0:16, 256:512], zc, small[0:16, 3:4], kvg_b[0:16, :], ALU.mult, ALU.mult,
            R=[("small", 3), "kvg", "tmpB"], W=["tmpB"])
        store(ckv_s[:, :], tmpB[0:16, 256:512], R=["tmpB"], W=["ckv_s_dram"])
        cp(ACT, cnew_bf[0:16, 0:256], tmpB[0:16, 256:512], R=["tmpB"], W=["cnew_bf"])
        kr3 = tmpB[0:16, 512:544].rearrange("p (h e) -> p h e", e=32)
        rope_rows(16, psf[0:16, bB, 288:320].rearrange("p (h e) -> p h e", e=32), kr3,
                  ropeCs[0:16, :].rearrange("p (o e) -> p o e", o=1), ropeSs[0:16, :].rearrange("p (o e) -> p o e", o=1),
                  tmpB[0:16, 576:640].rearrange("p (h e) -> p h e", e=64),
                  R=PQ + ["ropes"], W=["tmpB"], tmpkey="tmpB")
        store(kr_s[:, :], tmpB[0:16, 512:544], R=["tmpB"])
        cp(ACT, cnew_bf[0:16, 256:288], tmpB[0:16, 512:544], R=["tmpB"], W=["cnew_bf"])
        v2, k2 = wslab("in2")
        v3, k3 = wslab("in3")
        ba, bb = nf(), nf()
        for c in range(8):
            mm(psf[0:16, ba, :], xsT[:, c, :], v2[:, c, :], c == 0, c == 7, R=["xsT", k2], W=[("pf", ba)])
            mm(psf[0:16, bb, :], xsT[:, c, :], v3[:, c, :], c == 0, c == 7, R=["xsT", k3], W=[("pf", bb)], sig=(c == 7))
        act(tmpA[0:16, 0:512], psf[0:16, bb, :], AF.Sigmoid, R=[("pf", bb)], W=["tmpA"])
        tt(DVE, u_s[0:16, :], psf[0:16, ba, :], tmpA[0:16, 0:512], ALU.mult, R=[("pf", ba), "tmpA"], W=["u_s"])
        mk_rest_slabs()
        T.dma(SP, st_sems[0], lambda: nc.sync.dma_start(out=conv_s[:, 0:29 * 512], in_=stc[:, 512:30 * 512]))
        store(conv_s[:, 29 * 512:30 * 512], u_s[0:16, :], R=["u_s"])
        acc = misc[0:16, :]
        first = True
        for k0_ in range(0, 30, 4):
            n = min(4, 30 - k0_)
            ext = uTf[0:16, 0:n * 512]
            cwb = bvf[0:16, 0:n * 512]
            load(ext, stc[:, k0_ * 512:(k0_ + n) * 512], W=["ext"])
            load(cwb, conv_w[k0_:k0_ + n, :].rearrange("(o k) c -> o (k c)", o=1).partition_broadcast(16), W=[("bvec", 0), ("bvec", 1)])
            tt(DVE, ext, ext, cwb, ALU.mult, R=["ext", ("bvec", 0), ("bvec", 1)], W=["ext"])
            for jx in range(n):
                if first:
                    cp(DVE, acc, ext[:, jx * 512:(jx + 1) * 512], R=["ext"], W=["acc"])
                    first = False
                else:
                    tt(DVE, acc, acc, ext[:, jx * 512:(jx + 1) * 512], ALU.add, R=["ext", "acc"], W=["acc"])
        load(bvf[0:16, 0:512], conv_w[30:31, :].partition_broadcast(16), W=[("bvec", 0), ("bvec", 1)])
        tt(DVE, tmpA[0:16, 0:512], u_s[0:16, :], bvf[0:16, 0:512], ALU.mult, R=["u_s", ("bvec", 0), ("bvec", 1)], W=["tmpA"])
        tt(DVE, acc, acc, tmpA[0:16, 0:512], ALU.add, R=["tmpA", "acc"], W=["acc"])
        load(bvf[0:16, 0:512], conv_b.partition_broadcast(16), W=[("bvec", 0), ("bvec", 1)])
        tt(DVE, acc, acc, bvf[0:16, 0:512], ALU.add, R=["acc", ("bvec", 0), ("bvec", 1)], W=["acc"])
        T.op(DVE, lambda: nc.vector.bn_stats(stats[0:16, 0, :], acc), R=["acc"], W=[("stats", 0)])
        T.op(DVE, lambda: nc.vector.bn_aggr(mvar[0:16, :], stats[0:16, 0:1, :]), R=[("stats", 0)], W=["mvar"])
        rstd_from_var(small[0:16, 0:1], mvar[0:16, 1:2], R=["mvar"], W=[("small", 0)])
        stt(DVE, small[0:16, 1:2], mvar[0:16, 0:1], -1.0, small[0:16, 0:1], ALU.mult, ALU.mult,
            R=["mvar", ("small", 0)], W=[("small", 1)])
        act(acc, acc, AF.Identity, R=["acc", ("small", 0), ("small", 1)], W=["acc"], bias=small[0:16, 1:2], scale=small[0:16, 0:1])
        load(bvf[0:16, 0:512], conv_g.partition_broadcast(16), W=[("bvec", 0)])
        load(bvf[0:16, 1024:1536], conv_bb.partition_broadcast(16), W=[("bvec", 1)])
        tt(DVE, acc, acc, bvf[0:16, 0:512], ALU.mult, R=["acc", ("bvec", 0)], W=["acc"])
        tt(DVE, acc, acc, bvf[0:16, 1024:1536], ALU.add, R=["acc", ("bvec", 1)], W=["acc"])
        act(mix_s[0:16, 512:1024], acc, AF.Silu, R=["acc"], W=["mix_s"])
        for h in range(8):
            b = nb()
            for lc in range(2):
                tr(psb[0:64, b, lc * 128:(lc + 1) * 128], WukP[:, lc, h, 0:64], R=WUK, W=[("pb", b)], sig=(lc == 1))
            cp(DVE, WukT[0:64, h, :], psb[0:64, b, 0:256], R=[("pb", b)], W=["WukT"])
        b = nb()
        for h in range(8):
            tr(psb[0:96, b, h * 16:(h + 1) * 16], q_s[0:16, h * 96:(h + 1) * 96], R=["q_s"], W=[("pb", b)], sig=(h == 7))
        cp(DVE, qT_s[0:96, :, :], psb[0:96, b, 0:128].rearrange("p (h b) -> p h b", b=16), R=[("pb", b)], W=["qT_s"])
        cp(DVE, qrT[0:32, :, :].rearrange("p b h -> p h b"), qT_s[64:96, :, :], R=["qT_s"], W=["qrT"])
        qb = [nf() for _ in range(4)]
        for h in range(8):
            mm(psf[0:16, qb[h // 2], (h % 2) * 256:(h % 2 + 1) * 256], qT_s[0:64, h, :], WukT[0:64, h, :], True, True,
               R=["qT_s", "WukT"], W=[("pf", qb[h // 2])], sig=True)
        for i in range(4):
            cp(ACT if i % 2 else DVE, ql_bf[0:16, i * 512:(i + 1) * 512], psf[0:16, qb[i], :], R=[("pf", qb[i])], W=["ql_bf"])
        if DEBUG == "ql":
            cp(DVE, tmpA[0:16, 0:1024], ql_bf[0:16, 1024:2048], R=["ql_bf"], W=["tmpA"])
            store(y_s[:, :], tmpA[0:16, 0:1024], R=["tmpA"])
            barrier()
            raise _Stop()
        b = nb()
        for lc in range(2):
            for h in range(8):
                o = (lc * 8 + h) * 16
                tr(psb[:, b, o:o + 16], ql_bf[0:16, h * 256 + lc * 128:h * 256 + (lc + 1) * 128], R=["ql_bf"], W=[("pb", b)],
                   sig=(lc == 1 and h == 7))
        cp(DVE, qlT[:, :, :, :].rearrange("p l b h -> p l h b"), psb[:, b, 0:256].rearrange("p (l h b) -> p l h b", h=8, b=16),
           R=[("pb", b)], W=["qlT"])
        tr16(cnT, cnew_bf, 2, "cnew_bf", "cnT")
        b = nb()
        tr(psb[0:32, b, 0:16], cnew_bf[0:16, 256:288], R=["cnew_bf"], W=[("pb", b)], sig=True)
        cp(DVE, krnT[0:32, :], psb[0:32, b, 0:16], R=[("pb", b)], W=["krnT"])
        load(cnx[0:1, :, 0:256], ckv_s.rearrange("(o b) l -> o b l", o=1), R=["ckv_s_dram", "cnx"], W=["cnxl"], q=POOL)
        if PROFILE:
            T.stage = "smp_attn"
        gi = [0]
        nf_mod[0] = 2
        trb = [(psb[:, 0, :], ("pb", 0)), (psb[:, 1, :], ("pb", 1)),
               (psf[:, 2, :].bitcast(BF16), ("pf", 2)), (psf[:, 3, :].bitcast(BF16), ("pf", 3))]
        trb_i = [0]

        def nbx():
            v = trb[trb_i[0] % 4]
            trb_i[0] += 1
            return v
        for bsm in range(NS):
            pS = nh()
            pO = nh()
            for pg in range(8):
                sl = gi[0] % 2
                gi[0] += 1
                col = bsm * 8 + pg
                cp(DVE, idxc[sl][:, 0:1], idx[:, col:col + 1], R=["idx"], W=[("idxc", sl)])
                T.dma(POOL, g_sems[sl], lambda sl=sl, col=col: nc.gpsimd.indirect_dma_start(
                    out=cpg[sl][:, :, :].rearrange("p r l -> p (r l)"), out_offset=None, in_=pool_c,
                    in_offset=bass.IndirectOffsetOnAxis(ap=idxc[sl][:, 0:1], axis=0)), R=[("idxc", sl)], W=[("cpg", sl)])
                T.dma(POOL, g_sems[2 + sl], lambda sl=sl, col=col: nc.gpsimd.indirect_dma_start(
                    out=rpg[sl][:, :, :].rearrange("p r l -> p (r l)"), out_offset=None, in_=pool_r,
                    in_offset=bass.IndirectOffsetOnAxis(ap=idxc[sl][:, 0:1], axis=0)), R=[("idxc", sl)], W=[("rpg", sl)])
                if DEBUG == "cpg":
                    for ii, rr in enumerate((0, 7)):
                        cp(DVE, tmpA[:, ii * 256:(ii + 1) * 256], cpg[sl][:, rr, 0:256], R=[("cpg", sl)], W=["tmpA"])
                        store(ckv_p[0, ii * 128:(ii + 1) * 128, :], tmpA[:, ii * 256:(ii + 1) * 256], R=["tmpA"])
                    cp(DVE, tmpA[:, 512:768], rpg[sl][:, :, :].rearrange("p r l -> p (r l)"), R=[("rpg", sl)], W=["tmpA"])
                    store(ckv_p[0, 256:384, :], tmpA[:, 512:768], R=["tmpA"])
                    barrier()
                    raise _Stop()
                for lc in range(2):
                    bap, bkey = nbx()
                    for r in range(8):
                        tr(bap[:, r * 128:(r + 1) * 128], cpg[sl][:, r, lc * 128:(lc + 1) * 128], R=[("cpg", sl)], W=[bkey], sig=(r == 7))
                    cp(ACT if lc else DVE, cT[sl][:, lc, :], bap[:, :], R=[bkey], W=[("cT", sl)])
                bap, bkey = nbx()
                for r in range(8):
                    tr(bap[0:32, r * 128:(r + 1) * 128], rpg[sl][:, r, :], R=[("rpg", sl)], W=[bkey], sig=(r == 7))
                cp(DVE, rT[sl][0:32, :], bap[0:32, :], R=[bkey], W=[("rT", sl)])
                for r in range(8):
                    kt = pg * 8 + r
                    o_ = psf[:, pS, kt * 8:(kt + 1) * 8]
                    RR = [("cT", sl), ("rT", sl), "qlT", "qrT"]
                    mm(o_, cT[sl][:, 0, r * 128:(r + 1) * 128], qlT[:, 0, bsm, :], True, False, R=RR, W=[("pf", pS)])
                    mm(o_, cT[sl][:, 1, r * 128:(r + 1) * 128], qlT[:, 1, bsm, :], False, False, R=RR, W=[("pf", pS)])
                    mm(o_, rT[sl][0:32, r * 128:(r + 1) * 128], qrT[0:32, bsm, :], False, True, R=RR, W=[("pf", pS)], sig=(r == 7))
                act(Pp[:, pg * 64:(pg + 1) * 64], psf[:, pS, pg * 64:(pg + 1) * 64], AF.Exp, R=[("pf", pS)], W=["Pp"], scale=MLA_SCALE)
                for r in range(8):
                    kt = pg * 8 + r
                    mm(psf[0:8, pO, 0:256], Pp[:, kt * 8:(kt + 1) * 8], cpg[sl][:, r, :], kt == 0, False,
                       R=["Pp", ("cpg", sl)], W=[("pf", pO)], sig=(r == 7))
            pN = nf()
            mm(psf[0:1, pN, 0:8], cnT[:, 0, bsm:bsm + 1], qlT[:, 0, bsm, :], True, False, R=["cnT", "qlT"], W=[("pf", pN)])
            mm(psf[0:1, pN, 0:8], cnT[:, 1, bsm:bsm + 1], qlT[:, 1, bsm, :], False, False, R=["cnT", "qlT"], W=[("pf", pN)])
            mm(psf[0:1, pN, 0:8], krnT[0:32, bsm:bsm + 1], qrT[0:32, bsm, :], False, True, R=["krnT", "qrT"], W=[("pf", pN)], sig=True)
            act(pnew[0:1, :], psf[0:1, pN, 0:8], AF.Exp, R=[("pf", pN)], W=["pnew"], scale=MLA_SCALE)
            mm(psf[0:8, pO, 0:256], pnew[0:1, :], cnx[0:1, bsm, 0:256], False, True, R=["pnew", "cnx", "cnxl"], W=[("pf", pO)], sig=True)
            T.op(DVE, lambda: nc.vector.reduce_sum(small[:, 40:48], Pp[:, :].rearrange("p (k h) -> p h k", h=8), axis=AX.X),
                 R=["Pp"], W=[("small", 40)])
            cp(DVE, dhl[:, 0:8], small[:, 40:48], R=[("small", 40)], W=["dhl"])
            tt(DVE, small[:, 48:56], small[:, 40:48], dhl[:, 0:8], ALU.subtract, R=[("small", 40), "dhl"], W=[("small", 48)])
            cp(DVE, dhl[:, 8:16], small[:, 48:56], R=[("small", 48)], W=["dhl"])
            pD = nf()
            mm(psf[0:8, pD, 0:1], dhl[:, 0:8], ones[:, 0:1], True, False, R=["dhl", "ones"], W=[("pf", pD)])
            mm(psf[0:8, pD, 0:1], dhl[:, 8:16], ones[:, 0:1], False, False, R=["dhl", "ones"], W=[("pf", pD)])
            mm(psf[0:8, pD, 0:1], pnew[0:1, :], ones[0:1, 0:1], False, True, R=["pnew", "ones"], W=[("pf", pD)], sig=True)
            if DEBUG == "po" and bsm == 0:
                cp(DVE, tmpA[0:8, 0:256], psf[0:8, pO, 0:256], R=[("pf", pO)], W=["tmpA"])
                store(y_s[0:8, 0:256], tmpA[0:8, 0:256], R=["tmpA"])
                cp(DVE, tmpA[:, 512:1024], Pp[:, :], R=["Pp"], W=["tmpA"])
                store(ckv_p[0, 0:128, :], tmpA[:, 512:768], R=["tmpA"])
                store(ckv_p[0, 128:256, :], tmpA[:, 768:1024], R=["tmpA"])
                barrier()
                raise _Stop()
            T.op(DVE, lambda pD=pD: nc.vector.reciprocal(small[0:8, 8:9], psf[0:8, pD, 0:1]), R=[("pf", pD)], W=[("small", 8)])
            ts(DVE, ol_bf[0:8, :], psf[0:8, pO, 0:256], small[0:8, 8:9], None, ALU.mult, None, R=[("pf", pO), ("small", 8)], W=["ol_bf"])
            b = nb()
            for lc in range(2):
                tr(psb[:, b, lc * 8:(lc + 1) * 8], ol_bf[0:8, lc * 128:(lc + 1) * 128], R=["ol_bf"], W=[("pb", b)], sig=(lc == 1))
            cp(DVE, olT[:, :, :, bsm], psb[:, b, 0:16].rearrange("p (l h) -> p l h", h=8), R=[("pb", b)], W=["olT"])
        if PROFILE:
            T.stage = "smp_rest"
        nf_mod[0] = 4
        bk = nf()
        for h in range(8):
            for lc in range(2):
                mm(psf[0:16, bk, h * 64:(h + 1) * 64], olT[:, lc, h, :], Wuv[:, lc, h * 64:(h + 1) * 64], lc == 0, lc == 1,
                   R=["olT"] + WUV, W=[("pf", bk)], sig=(h == 7 and lc == 1))
        cp(DVE, mix_s[0:16, 0:512], psf[0:16, bk, :], R=[("pf", bk)], W=["mix_s"])

        def dbg(name, ap, key):
            if DEBUG == name:
                n = ap.shape[1]
                cp(DVE, tmpA[0:ap.shape[0], 0:n], ap, R=[key], W=["tmpA"])
                store(y_s[0:ap.shape[0], 0:n], tmpA[0:ap.shape[0], 0:n], R=["tmpA"])
                barrier()
                raise _Stop()

        dbg("mix_s", mix_s[0:16, :], "mix_s")

        def post16(inT, inkey, wname, resid, gi_, bi_, outT, outkey):
            views = [wslab(wname + "0"), wslab(wname + "1")]
            b2 = nf2()
            for hf in range(2):
                wv, wk = views[hf]
                for c in range(8):
                    mm(psf[0:16, b2 + hf, :], inT[:, c, :], wv[:, c, :], c == 0, c == 7, R=[inkey, wk], W=[("pf", b2 + hf)], sig=(c == 7))
            for hf in range(2):
                stt(DVE, r_s[0:16, hf * 512:(hf + 1) * 512], resid[0:16, hf * 512:(hf + 1) * 512], ALPHA, psf[0:16, b2 + hf, :],
                    ALU.mult, ALU.add, R=[("pf", b2 + hf), "r_s", "xs_f"], W=["r_s"])
            bvload(gi_, 0)
            bvload(bi_, 1)
            layer_norm_rows("r_s", 16, r_s[0:16, :], r_s[0:16, :], gi_, bi_, xb16[0:16, :] if outT is not None else None, "xb16")
            if outT is not None:
                tr16(outT, xb16, 8, "xb16", outkey)

        cp(ACT, xb16[0:16, :], mix_s[0:16, :], R=["mix_s"], W=["xb16"])
        tr16(mixT_s, xb16, 8, "xb16", "mixT_s")
        x1T_s = cd(128).rearrange("p (c b) -> p c b", b=16)
        x2T_s = cd(128).rearrange("p (c b) -> p c b", b=16)
        post16(mixT_s, "mixT_s", "o", xs_f, 0, 1, x1T_s, "x1T_s")
        dbg("x1", r_s[0:16, :], "r_s")
        if PROFILE:
            T.stage = "smp_xattn"
        vq = [wslab("xq0"), wslab("xq1")]
        b2 = nf2()
        for hf in range(2):
            wv, wk = vq[hf]
            for c in range(8):
                mm(psf[0:16, b2 + hf, :], x1T_s[:, c, :], wv[:, c, :], c == 0, c == 7, R=["x1T_s", wk], W=[("pf", b2 + hf)], sig=(c == 7))
        for hf in range(2):
            cp(DVE, q2_f[0:16, hf * 512:(hf + 1) * 512], psf[0:16, b2 + hf, :], R=[("pf", b2 + hf)], W=["q2_f"])
        store(q2_scr[:, :], q2_f[0:16, :], R=["q2_f"], W=["q2_scr"])
        mkb = uTf[:, 0:2048].rearrange("p (a b) -> p a b", b=1024)
        for bsm in range(NS):
            load(mkb, cmk[bsm].rearrange("(mt p) e -> p mt e", p=128), W=["mkb"])
            load(mvb[:, :, :], cmv[bsm].rearrange("(mt p) e -> p mt e", p=128), W=[("mvb", 0), ("mvb", 1)], q=POOL)
            load(xring[:, 0, :], q2_scr[bsm:bsm + 1, :].partition_broadcast(128), R=["q2_scr"], W=[("xring", 0)])
            for mt in range(2):
                tt(DVE, mkb[:, mt, :], mkb[:, mt, :], xring[:, 0, :], ALU.mult, R=["mkb", ("xring", 0)], W=["mkb"])
            T.op(DVE, lambda: nc.vector.reduce_sum(small[:, 16:24], mkb[:, :, :].rearrange("p a (h d) -> p (a h) d", d=256), axis=AX.X),
                 R=["mkb"], W=[("small", 16)])
            act(P2bf[:, :], small[:, 16:24], AF.Exp, R=[("small", 16)], W=["P2bf"], scale=MEM_SCALE)
            bo = nf()
            for h in range(4):
                for dc in range(2):
                    for mt in range(2):
                        mm(psf[:, bo, (h * 2 + dc):(h * 2 + dc) + 1], mvb[:, mt, h * 256 + dc * 128:h * 256 + (dc + 1) * 128],
                           P2bf[:, mt * 4 + h:mt * 4 + h + 1], mt == 0, mt == 1, R=[("mvb", 0), ("mvb", 1), "P2bf"], W=[("pf", bo)])
            mm(psf[:, bo, 16:24], ones[:, :], P2bf[:, :], True, True, R=["ones", "P2bf"], W=[("pf", bo)], sig=True)
            cp(DVE, small[:, 28:36], psf[:, bo, 16:24], R=[("pf", bo)], W=[("small", 28)])
            tt(DVE, small[:, 24:28], small[:, 28:32], small[:, 32:36], ALU.add, R=[("small", 28)], W=[("small", 24)])
            T.op(DVE, lambda: nc.vector.reciprocal(small[:, 24:28], small[:, 24:28]), R=[("small", 24)], W=[("small", 24)])
            tt(DVE, o2T_s[:, :, bsm].rearrange("p (h d) -> p h d", d=2), psf[:, bo, 0:8].rearrange("p (h d) -> p h d", d=2),
               small[:, 24:28].rearrange("p (h o) -> p h o", o=1).to_broadcast([128, 4, 2]), ALU.mult,
               R=[("pf", bo), ("small", 24)], W=["o2T_s"])
        if PROFILE:
            T.stage = "smp_ffn"
        post16(o2T_s, "o2T_s", "xo", r_s, 2, 3, x2T_s, "x2T_s")
        dbg("x2", r_s[0:16, :], "r_s")
        pH = nh()
        for g in range(8):
            wv, wk = wslab(f"up{g}")
            for fc in range(4):
                o = (g * 4 + fc) * 16
                for c in range(8):
                    mm(psf[:, pH, o:o + 16], wv[:, c, fc * 128:(fc + 1) * 128], x2T_s[:, c, :], c == 0, c == 7,
                       R=["x2T_s", wk], W=[("pf", pH)], sig=(c == 7 and fc == 3))
        act(tmpA[:, 0:512], psf[:, pH, :], AF.Relu, R=[("pf", pH)], W=["tmpA"])
        tt(DVE, hT_s[:, :], tmpA[:, 0:512], psf[:, pH, :], ALU.mult, R=[("pf", pH), "tmpA"], W=["hT_s"])
        b2 = nf2()
        for qd in range(4):
            for fh in range(2):
                wv, wk = wslab(f"dn{fh}{qd}")
                for fc in range(16):
                    o = (fh * 16 + fc) * 16
                    mm(psf[0:16, b2 + qd // 2, (qd % 2) * 256:(qd % 2 + 1) * 256], hT_s[:, o:o + 16], wv[:, fc, :],
                       fh == 0 and fc == 0, fh == 1 and fc == 15, R=["hT_s", wk], W=[("pf", b2 + qd // 2)], sig=(fc == 15))
        for hf in range(2):
            stt(DVE, r_s[0:16, hf * 512:(hf + 1) * 512], r_s[0:16, hf * 512:(hf + 1) * 512], ALPHA, psf[0:16, b2 + hf, :],
                ALU.mult, ALU.add, R=[("pf", b2 + hf), "r_s"], W=["r_s"])
        bvload(4, 0)
        bvload(5, 1)
        layer_norm_rows("r_s", 16, r_s[0:16, :], r_s[0:16, :], 4, 5, None, None)
        store(y_s[:, :], r_s[0:16, :], R=["r_s"])
        barrier()

    try:
        barrier()
        chk(0)
        if do_sample:
            sample_phase()
        mk_rest_slabs()
        chk(0.1)
        if do_prompt:
            for s in range(nseq):
                mem_kv_seq(s)
                chk(1)
                for j in range(nblk):
                    prompt_block(s, j)
    except _Stop:
        pass

    for ds in st_sems + ld_sems + wr_sems + cvt_sems + g_sems:
        if ds.count:
            nc.sync.wait_ge(ds.sem, ds.count * 16)
    for E in (PE, ACT, DVE, POOL):
        if E.count:
            nc.sync.wait_ge(E.sem, E.count)
    return nc


def _consts():
    half = 16
    inv_freq = np.exp(-math.log(10000.0) * np.arange(half, dtype=np.float32) / half).astype(np.float32)
    pos = np.arange(SEQ, dtype=np.float32)
    ang = pos[:, None] * inv_freq[None, :]
    cos, sin = np.cos(ang).astype(np.float32), np.sin(ang).astype(np.float32)
    C2 = np.concatenate([cos, cos], axis=1)
    S2 = np.concatenate([-sin, sin], axis=1)
    ropeC = C2.reshape(16, 128, 32).transpose(1, 0, 2).reshape(128, 512)
    ropeS = S2.reshape(16, 128, 32).transpose(1, 0, 2).reshape(128, 512)
    angs = np.float32(8192.0) * inv_freq
    cs, sn = np.cos(angs).astype(np.float32), np.sin(angs).astype(np.float32)
    ropeCs = np.tile(np.concatenate([cs, cs])[None, :], (128, 1)).astype(np.float32)
    ropeSs = np.tile(np.concatenate([-sn, sn])[None, :], (128, 1)).astype(np.float32)
    ident = np.eye(128, dtype=np.float32).astype(ml_dtypes.bfloat16)
    k = np.arange(128)
    mask = (k[None, :] >= k[:, None]).astype(np.float32).astype(ml_dtypes.bfloat16)
    injk = np.zeros((32, 96), np.float32)
    injk[np.arange(32), 64 + np.arange(32)] = 1.0
    injk = injk.astype(ml_dtypes.bfloat16)
    sel = np.zeros((16, 16, 128), np.float32)
    for b in range(16):
        sel[b, b, :] = 1.0
    gq = (np.arange(128) % 16).astype(np.int32).reshape(128, 1)
    return dict(c_ident=ident, c_mask=mask, c_ropeC=np.ascontiguousarray(ropeC), c_ropeS=np.ascontiguousarray(ropeS),
                c_ropeCs=ropeCs, c_ropeSs=ropeSs, c_injk=injk, c_sel=sel.reshape(16, 16 * 128), gq=gq)


def make_in_maps(inp, cores=range(NCORE), pool_c=None, pool_r=None, page_table=None):
    consts = _consts()
    if pool_c is None:
        pool_c = np.ascontiguousarray(inp["cache_ckv"][0]).reshape(-1, 8 * 256)
        pool_r = np.ascontiguousarray(inp["cache_krope"][0]).reshape(-1, 8 * 32)
        page_table = np.asarray(inp["page_table"])
    shared = dict(
        pool_c=pool_c, pool_r=pool_r,
        w_in=np.ascontiguousarray(inp["w_in"][0]), kvg=np.ascontiguousarray(inp["kv_norm_g"]),
        w_uk=np.ascontiguousarray(inp["w_uk"][0]).reshape(256, 512), w_uv=np.ascontiguousarray(inp["w_uv"][0]).reshape(256, 512),
        conv_w=np.ascontiguousarray(inp["conv_w"][0]), conv_b=np.ascontiguousarray(inp["conv_b"]),
        conv_g=np.ascontiguousarray(inp["conv_ln_g"]), conv_bb=np.ascontiguousarray(inp["conv_ln_b"]),
        w_o=np.ascontiguousarray(inp["w_o"][0]), w_xq=np.ascontiguousarray(inp["w_xq"][0]),
        w_xk=np.ascontiguousarray(inp["w_xk"][0]), w_xv=np.ascontiguousarray(inp["w_xv"][0]),
        w_xo=np.ascontiguousarray(inp["w_xo"][0]), w_up=np.ascontiguousarray(inp["w_up"][0]),
        w_down=np.ascontiguousarray(inp["w_down"][0]),
        ln1_g=np.ascontiguousarray(inp["ln1_g"]), ln1_b=np.ascontiguousarray(inp["ln1_b"]),
        ln2_g=np.ascontiguousarray(inp["ln2_g"]), ln2_b=np.ascontiguousarray(inp["ln2_b"]),
        ln3_g=np.ascontiguousarray(inp["ln3_g"]), ln3_b=np.ascontiguousarray(inp["ln3_b"]),
        **consts,
    )
    maps = []
    for c in cores:
        pt = page_table[c * 16:(c + 1) * 16]
        ptr = pt.reshape(16, 8, 8)[:, :, np.arange(128) // 16]
        ptr = np.ascontiguousarray(ptr.transpose(2, 0, 1).reshape(128, 128)).astype(np.int32)
        m = dict(shared)
        m.update(
            xp=np.ascontiguousarray(inp["x_prompt"][2 * c:2 * c + 2]),
            memp=np.ascontiguousarray(inp["mem_prompt"][2 * c:2 * c + 2]),
            xs=np.ascontiguousarray(inp["x_sample"][16 * c:16 * c + 16, 0]),
            ptrep=ptr,
            stc=np.ascontiguousarray(inp["state_conv"][0, 16 * c:16 * c + 16]).reshape(16, 30 * 512),
            cmk=np.ascontiguousarray(inp["cache_mem_k"][0, 16 * c:16 * c + 16]).reshape(16, 256, 1024),
            cmv=np.ascontiguousarray(inp["cache_mem_v"][0, 16 * c:16 * c + 16]).reshape(16, 256, 1024),
        )
        maps.append(m)
    return maps


def assemble(results):
    cat = lambda k: np.concatenate([r[k] for r in results], axis=0)
    y_p = cat("y_p")
    y_s = cat("y_s")[:, None, :]
    return (y_p, y_s, cat("ckv_p")[None], cat("kr_p")[None], cat("conv_p")[None],
            cat("mk_p").reshape(1, -1, 256, 4, 256), cat("mv_p").reshape(1, -1, 256, 4, 256),
            cat("ckv_s")[None, :, None, :], cat("kr_s")[None, :, None, :], cat("conv_s").reshape(1, -1, 30, 512))


def kernel(**inputs):
    inp = {k: np.asarray(v) for k, v in inputs.items()}
    nc = build()
    maps = make_in_maps(inp)
    res = run_bass_kernel_spmd(nc, maps, core_ids=list(range(NCORE)))
    outs = assemble(res.results)
    return tuple(np.ascontiguousarray(o, dtype=np.float32) for o in outs)
```

```python
import math
import numpy as np
import ml_dtypes
import concourse.bass as bass
import concourse.mybir as mybir
from concourse.bass_utils import run_bass_kernel_spmd

F32 = mybir.dt.float32
BF16 = mybir.dt.bfloat16
I32 = mybir.dt.int32
AF = mybir.ActivationFunctionType
ALU = mybir.AluOpType
AX = mybir.AxisListType

ALPHA = 2.0 ** 0.25
MLA_SCALE = 96.0 ** -0.5
MEM_SCALE = 1.0 / 16.0
EPS = 1e-5
NCORE = 8
SEQ = 2048
NBLK = 4
NPOOL = 10240
DEBUG = False
PROFILE = False


class Eng:
    def __init__(self, name, eng, sem):
        self.name, self.eng, self.sem, self.count, self.seen = name, eng, sem, 0, {}


class DSem:
    def __init__(self, sem):
        self.sem, self.count = sem, 0


_TINY = ("small", "stats", "mvar", "P2bf", "dhl", "pnew", "idx", "idxc", "ptb", "gqt", "onesf", "ol_bf", "krnT", "cnT")


def _tiny(k):
    n = k[0] if isinstance(k, tuple) else k
    return n in _TINY


class Trk:
    def __init__(self, nc):
        self.nc = nc
        self.last_w = {}
        self.readers = {}
        self.nsem = 0
        self.stage = None

    def sem(self, name):
        self.nsem += 1
        return self.nc.alloc_semaphore(name)

    def dsem(self, name):
        return DSem(self.sem(name))

    def _wait(self, E, R, W):
        best = {}
        def add(tk, raw=False):
            obj, val = tk
            if obj is E and not (raw and E.name != "pe" and E.name != "sp"):
                return
            if val > best.get(obj, 0):
                best[obj] = val
        for k in R:
            if k in self.last_w:
                add(self.last_w[k], raw=_tiny(k))
            if isinstance(k, tuple) and k[0] in ("pf", "pb"):
                for obj, val in self.readers.get(k, {}).items():
                    add((obj, val))
        for k in W:
            if k in self.last_w:
                add(self.last_w[k])
            for obj, val in self.readers.get(k, {}).items():
                add((obj, val))
        for obj, val in best.items():
            if E.seen.get(obj, 0) >= val:
                continue
            assert obj.count >= val, "wait on a signal that is not issued yet"
            E.eng.wait_ge(obj.sem, val * (16 if isinstance(obj, DSem) else 1))
            E.seen[obj] = val

    def _commit(self, tk, R, W):
        obj, val = tk
        for k in R:
            d = self.readers.setdefault(k, {})
            if d.get(obj, 0) < val:
                d[obj] = val
        for k in W:
            self.last_w[k] = tk
            self.readers[k] = {}

    def retire(self, old_keys, new_keys):
        merged = {}
        for k in old_keys:
            if k in self.last_w:
                o, v = self.last_w[k]
                merged[o] = max(merged.get(o, 0), v)
            for o, v in self.readers.get(k, {}).items():
                merged[o] = max(merged.get(o, 0), v)
        for nk in new_keys:
            d = self.readers.setdefault(nk, {})
            for o, v in merged.items():
                d[o] = max(d.get(o, 0), v)

    def op(self, E, fn, R=(), W=(), sig=True):
        self._wait(E, R, W)
        ins = fn()
        if self.stage is not None:
            try:
                ins.annotate(self.stage)
            except Exception:
                pass
        if sig:
            E.count += 1
            ins.then_inc(E.sem, 1)
            tk = (E, E.count)
        else:
            tk = (E, E.count + 1)
        self._commit(tk, R, W)

    def dma(self, Q, ds, fn, R=(), W=()):
        self._wait(Q, R, W)
        ins = fn()
        ds.count += 1
        ins.then_inc(ds.sem, 16)
        self._commit((ds, ds.count), R, W)


class _Stop(Exception):
    pass


def build(n_pool=NPOOL, do_prompt=True, do_sample=True, nseq=2, nblk=NBLK, stop=None):
    nc = bass.Bass("TRN2", target_bir_lowering=False)

    def chk(n):
        if stop is not None and n >= stop:
            raise _Stop()

    T = Trk(nc)
    _dummy = [T.sem(f"dummy{i}") for i in range(6)]
    PE = Eng("pe", nc.tensor, T.sem("s_pe"))
    ACT = Eng("act", nc.scalar, T.sem("s_act"))
    DVE = Eng("dve", nc.vector, T.sem("s_dve"))
    POOL = Eng("pool", nc.gpsimd, T.sem("s_pool"))
    SP = Eng("sp", nc.sync, T.sem("s_sp"))

    def din(name, shape, dt=F32):
        return nc.dram_tensor(name, list(shape), dt, kind="ExternalInput").ap()

    def dout(name, shape, dt=F32):
        return nc.dram_tensor(name, list(shape), dt, kind="ExternalOutput").ap()

    def dscr(name, shape, dt=BF16):
        return nc.dram_tensor(name, list(shape), dt, kind="Internal").ap()

    xp = din("xp", [2, SEQ, 1024])
    memp = din("memp", [2, 256, 1024])
    xs = din("xs", [16, 1024])
    pool_c = din("pool_c", [n_pool * 16, 2048])
    pool_r = din("pool_r", [n_pool * 16, 256])
    ptrep = din("ptrep", [128, 128], I32)
    gq = din("gq", [128, 1], I32)
    stc = din("stc", [16, 30 * 512])
    cmk = din("cmk", [16, 256, 1024])
    cmv = din("cmv", [16, 256, 1024])
    w_in = din("w_in", [1024, 2080])
    kvg = din("kvg", [1, 256])
    w_uk = din("w_uk", [256, 512])
    w_uv = din("w_uv", [256, 512])
    conv_w = din("conv_w", [31, 512])
    conv_b = din("conv_b", [1, 512])
    conv_g = din("conv_g", [1, 512])
    conv_bb = din("conv_bb", [1, 512])
    w_o = din("w_o", [1024, 1024])
    w_xq = din("w_xq", [1024, 1024])
    w_xk = din("w_xk", [1024, 1024])
    w_xv = din("w_xv", [1024, 1024])
    w_xo = din("w_xo", [1024, 1024])
    w_up = din("w_up", [1024, 4096])
    w_down = din("w_down", [4096, 1024])
    lnv = [din(n, [1, 1024]) for n in ("ln1_g", "ln1_b", "ln2_g", "ln2_b", "ln3_g", "ln3_b")]
    c_ident = din("c_ident", [128, 128], BF16)
    c_mask = din("c_mask", [128, 128], BF16)
    c_ropeC = din("c_ropeC", [128, 16 * 32])
    c_ropeS = din("c_ropeS", [128, 16 * 32])
    c_ropeCs = din("c_ropeCs", [128, 32])
    c_ropeSs = din("c_ropeSs", [128, 32])
    c_injk = din("c_injk", [32, 96], BF16)
    c_sel = din("c_sel", [16, 16 * 128])

    y_p = dout("y_p", [2, SEQ, 1024])
    y_s = dout("y_s", [16, 1024])
    ckv_p = dout("ckv_p", [2, SEQ, 256])
    kr_p = dout("kr_p", [2, SEQ, 32])
    conv_p = dout("conv_p", [2, 30, 512])
    mk_p = dout("mk_p", [2, 256, 1024])
    mv_p = dout("mv_p", [2, 256, 1024])
    ckv_s = dout("ckv_s", [16, 256])
    kr_s = dout("kr_s", [16, 32])
    conv_s = dout("conv_s", [16, 30 * 512])

    slabs = {}
    cvt_sems = [T.dsem(f"cv{i}") for i in range(4)]
    cvt_i = [0]

    def mk_slab(name, src3):
        a, b = src3.shape[1], src3.shape[2]
        scr = dscr("ws_" + name, [128, a * b])
        ds = cvt_sems[cvt_i[0] % 4]
        cvt_i[0] += 1
        T.dma(POOL, ds, lambda: nc.gpsimd.dma_start(out=scr.rearrange("p (a b) -> p a b", b=b), in_=src3),
              W=[("ws", name)])
        slabs[name] = (scr, a, b)

    def cols(w, c0, c1):
        return w[:, c0:c1].rearrange("(c p) n -> p c n", p=128)

    mk_slab("in0", cols(w_in, 0, 512))
    mk_slab("in1", cols(w_in, 512, 1056))
    mk_slab("in2", cols(w_in, 1056, 1568))
    mk_slab("in3", cols(w_in, 1568, 2080))
    rest_done = [False]

    def mk_rest_slabs():
        if rest_done[0]:
            return
        rest_done[0] = True
        for nm, w in (("o", w_o), ("xq", w_xq), ("xo", w_xo)):
            mk_slab(nm + "0", cols(w, 0, 512))
            mk_slab(nm + "1", cols(w, 512, 1024))
        for g in range(8):
            mk_slab(f"up{g}", cols(w_up, g * 512, (g + 1) * 512))
        for fh in range(2):
            for qd in range(4):
                mk_slab(f"dn{fh}{qd}", w_down[fh * 2048:(fh + 1) * 2048, qd * 256:(qd + 1) * 256]
                        .rearrange("(f p) n -> p f n", p=128))
        for nm, w in (("xk", w_xk), ("xv", w_xv)):
            mk_slab(nm + "0", cols(w, 0, 512))
            mk_slab(nm + "1", cols(w, 512, 1024))

    def sb(name, shape, dt):
        return nc.alloc_sbuf_tensor(name, list(shape), dt)

    ident = sb("ident", [128, 128], BF16)
    mask = sb("mask", [128, 128], BF16)
    ones = sb("ones", [128, 128], BF16)
    injk = sb("injk", [32, 96], BF16)
    ropeC = sb("ropeC", [128, 16, 32], F32)
    ropeS = sb("ropeS", [128, 16, 32], F32)
    ropeCs = sb("ropeCs", [128, 32], F32)
    ropeSs = sb("ropeSs", [128, 32], F32)
    kvg_b = sb("kvg_b", [128, 256], F32)
    WukP = sb("WukP", [128, 2, 8, 96], BF16)
    Wuv = sb("Wuv", [128, 2, 512], BF16)
    cwT = sb("cwT", [128, 4, 31], F32)
    cvec = sb("cvec", [128, 3, 4], F32)
    bvec = sb("bvec", [128, 2, 1024], F32)
    WR_N = 3
    WR = sb("WR", [128, WR_N, 4352], BF16)
    KT = sb("KT", [128, 8, SEQ], BF16)
    Vaug = sb("Vaug", [128, 16, 4, 192], BF16)
    mkT = sb("mkT", [128, 8, 256], BF16)
    mvb = sb("mvb", [128, 2, 1024], BF16)
    Rb = sb("Rb", [128, 4, 1024], F32)
    xring = sb("xring", [128, 1, 1024], F32)
    Cb = sb("Cb", [128, 2, 1024], BF16)
    Db = sb("Db", [128, 8, 512], BF16)
    uT = sb("uT", [128, 4, 30 + 512], F32)
    uTb = sb("uTb", [128, 4, 30 + 512], BF16)
    Pr = sb("Pr", [128, 4, 512], BF16)
    QT = sb("QT", [128, 8, 512], BF16)
    mixT = sb("mixT", [128, 8, 512], BF16)
    hT = sb("hT", [128, 16, 512], BF16)
    qtm = hT[:, 0:6, :].rearrange("p a b -> p (a b)").rearrange("p (t e) -> p t e", e=768)
    ckb = sb("ckb", [128, 4, 288], BF16)
    ckT = sb("ckT", [128, 2, 512], BF16)
    krT = sb("krT", [32, 512], BF16)
    Cbx = hT[:, 0:8, :].rearrange("p a b -> p (a b)").rearrange("p (t e) -> p t e", e=1024)
    tmpA = sb("tmpA", [128, 1024], F32)
    tmpB = sb("tmpB", [128, 1024], F32)
    cacc = Rb[:, :, 0:512]
    csq = Pr
    cyb = QT
    small = sb("small", [128, 64], F32)
    stats = sb("stats", [128, 2, 6], F32)
    mvar = sb("mvar", [128, 2], F32)

    psf = nc.alloc_psum_tensor("psf", [128, 6, 512], F32)
    psb = nc.alloc_psum_tensor("psb", [128, 2, 1024], BF16)
    pf_i = [0]
    pb_i = [0]

    def nf():
        i = pf_i[0] % 4
        pf_i[0] += 1
        return i

    def nf2():
        while pf_i[0] % 2:
            pf_i[0] += 1
        i = pf_i[0] % 4
        pf_i[0] += 2
        return i

    ph_i = [0]

    def nh():
        i = 4 + ph_i[0] % 2
        ph_i[0] += 1
        return i

    def nb():
        i = pb_i[0] % 2
        pb_i[0] += 1
        return i

    ld_sems = [T.dsem(f"ld{i}") for i in range(8)]
    ld_i = [0]
    st_sems = [T.dsem(f"st{i}") for i in range(8)]
    st_i = [0]

    def load(out, in_, W, R=(), q=None):
        ds = ld_sems[ld_i[0] % 8]
        ld_i[0] += 1
        if q is POOL:
            T.dma(POOL, ds, lambda: nc.gpsimd.dma_start(out=out, in_=in_), R=R, W=W)
        else:
            T.dma(SP, ds, lambda: nc.sync.dma_start(out=out, in_=in_), R=R, W=W)

    def store(out, in_, R, W=()):
        ds = st_sems[st_i[0] % 8]
        st_i[0] += 1
        T.dma(SP, ds, lambda: nc.sync.dma_start(out=out, in_=in_), R=R, W=W)

    wr_sems = [T.dsem(f"wr{i}") for i in range(WR_N)]
    wr_i = [0]

    def wslab(name):
        scr, a, b = slabs[name]
        i = wr_i[0] % WR_N
        wr_i[0] += 1
        view = WR[:, i, 0:a * b].rearrange("p (a b) -> p a b", b=b)
        T.dma(SP, wr_sems[i], lambda: nc.sync.dma_start(out=view, in_=scr.rearrange("p (a b) -> p a b", b=b)),
              R=[("ws", name)], W=[("WR", i)])
        return view, ("WR", i)

    def mm(out, lhsT, rhs, start, stop, R, W, sig=False):
        T.op(PE, lambda: nc.tensor.matmul(out, lhsT, rhs, start=start, stop=stop), R=R, W=W, sig=sig)

    def tr(out, in_, R, W, sig=False):
        k = in_.shape[0]
        T.op(PE, lambda: nc.tensor.transpose(out, in_, ident[0:k, 0:k]), R=list(R) + ["ident"], W=W, sig=sig)

    def act(out, in_, func, R, W, bias=0.0, scale=1.0, accum=None):
        if accum is None:
            T.op(ACT, lambda: nc.scalar.activation(out, in_, func, bias=bias, scale=scale), R=R, W=W)
        else:
            T.op(ACT, lambda: nc.scalar.activation(out, in_, func, bias=bias, scale=scale, accum_out=accum), R=R, W=W)

    def tt(E, out, in0, in1, op, R, W):
        T.op(E, lambda: E.eng.tensor_tensor(out, in0, in1, op), R=R, W=W)

    def ts(E, out, in0, s1, s2, op0, op1, R, W):
        if op1 is None:
            T.op(E, lambda: E.eng.tensor_scalar(out, in0, s1, None, op0), R=R, W=W)
        else:
            T.op(E, lambda: E.eng.tensor_scalar(out, in0, s1, s2, op0, op1), R=R, W=W)

    def stt(E, out, in0, scalar, in1, op0, op1, R, W):
        T.op(E, lambda: E.eng.scalar_tensor_tensor(out, in0, scalar, in1, op0, op1), R=R, W=W)

    def cp(E, out, in_, R, W):
        if E is ACT:
            T.op(ACT, lambda: nc.scalar.activation(out, in_, AF.Identity), R=R, W=W)
        else:
            T.op(E, lambda: E.eng.tensor_copy(out, in_), R=R, W=W)

    def bvload(idx, slot):
        load(bvec[:, slot, :], lnv[idx].partition_broadcast(128), W=[("bvec", slot)])

    load(ident[:], c_ident, W=["ident"])
    load(mask[:], c_mask, W=["mask"])
    load(injk[:], c_injk, W=["injk"])
    load(ropeC[:], c_ropeC.rearrange("p (a b) -> p a b", b=32), W=["rope"])
    load(ropeS[:], c_ropeS.rearrange("p (a b) -> p a b", b=32), W=["rope"])
    load(ropeCs[:], c_ropeCs, W=["ropes"])
    load(ropeSs[:], c_ropeSs, W=["ropes"])
    load(kvg_b[:], kvg.partition_broadcast(128), W=["kvg"])
    T.op(DVE, lambda: nc.vector.memset(ones[:], 1.0), W=["ones"])
    T.op(DVE, lambda: nc.vector.memset(WukP[:], 0.0), W=["WukP"])
    T.op(DVE, lambda: nc.vector.memset(Vaug[:], 1.0), W=["Vaug_init"])
    for lc in range(2):
        load(WukP[:, lc, :, 0:64], w_uk[lc * 128:(lc + 1) * 128, :].rearrange("p (h e) -> p h e", e=64),
             R=["WukP"], W=[("WukPl", lc)], q=POOL)
        load(Wuv[:, lc, :], w_uv[lc * 128:(lc + 1) * 128, :], W=[("Wuv", lc)], q=POOL)
    WUK = [("WukPl", 0), ("WukPl", 1), "WukP"]
    WUV = [("Wuv", 0), ("Wuv", 1)]
    with nc.allow_non_contiguous_dma(reason="tiny transposed parameter loads"):
        for cc in range(4):
            load(cwT[:, cc, :], conv_w[:, cc * 128:(cc + 1) * 128].rearrange("k p -> p k"), W=["cwT"])
            for i, v in enumerate((conv_b, conv_g, conv_bb)):
                load(cvec[:, i, cc:cc + 1], v[:, cc * 128:(cc + 1) * 128].rearrange("o p -> p o"), W=[("cvec", i)])
    CV = [("cvec", 0), ("cvec", 1), ("cvec", 2)]
    for cc in range(4):
        stg = hT[:, (cc % 2) * 8:(cc % 2) * 8 + 8, :].rearrange("p a b -> p (a b)")[:, 0:31 * 128].rearrange("p (k c) -> p k c", c=128)
        for k in range(31):
            ts(DVE, stg[:, k, :], ident[:, :], cwT[:, cc, k:k + 1], None, ALU.mult, None, R=["ident", "cwT"], W=[("cdstg", cc % 2)])
        scr = dscr(f"ws_cd{cc}", [128, 31 * 128])
        store(scr.rearrange("p (k c) -> p k c", c=128), stg, R=[("cdstg", cc % 2)], W=[("ws", f"cd{cc}")])
        slabs[f"cd{cc}"] = (scr, 31, 128)

    def rstd_from_var(out, var_ap, R, W):
        act(out, var_ap, AF.Sqrt, R=R, W=W, bias=EPS, scale=1.0)
        T.op(DVE, lambda: nc.vector.reciprocal(out, out), R=W, W=W)

    def layer_norm_rows(t_key, rows, src, dst, gi, bi, bf_out, bf_key):
        for hlf in range(2):
            T.op(DVE, lambda h=hlf: nc.vector.bn_stats(stats[0:rows, h, :], src[:, h * 512:(h + 1) * 512]),
                 R=[t_key], W=[("stats", hlf)])
        T.op(DVE, lambda: nc.vector.bn_aggr(mvar[0:rows, :], stats[0:rows, :, :]),
             R=[("stats", 0), ("stats", 1)], W=["mvar"])
        rstd_from_var(small[0:rows, 0:1], mvar[0:rows, 1:2], R=["mvar"], W=[("small", 0)])
        stt(DVE, small[0:rows, 1:2], mvar[0:rows, 0:1], -1.0, small[0:rows, 0:1], ALU.mult, ALU.mult,
            R=["mvar", ("small", 0)], W=[("small", 1)])
        act(dst, src, AF.Identity, R=[t_key, ("small", 0), ("small", 1)], W=[t_key],
            bias=small[0:rows, 1:2], scale=small[0:rows, 0:1])
        tt(POOL, dst, dst, bvec[0:rows, 0, :], ALU.mult, R=[t_key, ("bvec", 0)], W=[t_key])
        tt(POOL, dst, dst, bvec[0:rows, 1, :], ALU.add, R=[t_key, ("bvec", 1)], W=[t_key])
        if bf_out is not None:
            cp(ACT, bf_out, dst, R=[t_key], W=[bf_key])

    def rope_rows(rows, src3, dst3, C2, S2, tmp3, R, W, tmpkey):
        tt(DVE, tmp3[:, :, 0:16], src3[:, :, 16:32], S2[:, :, 0:16], ALU.mult, R=R, W=[tmpkey])
        tt(DVE, tmp3[:, :, 16:32], src3[:, :, 0:16], S2[:, :, 16:32], ALU.mult, R=R, W=[tmpkey])
        tt(DVE, tmp3[:, :, 32:64], src3, C2, ALU.mult, R=R, W=[tmpkey])
        tt(POOL, dst3, tmp3[:, :, 0:32], tmp3[:, :, 32:64], ALU.add, R=[tmpkey], W=W)

    def mem_kv_seq(s):
        memT = Db
        for mt in range(2):
            load(Cb[:, mt, :], memp[s, mt * 128:(mt + 1) * 128, :], W=[("Cb", mt)], q=POOL)
        for c in range(8):
            b = nb()
            for mt in range(2):
                tr(psb[:, b, mt * 128:(mt + 1) * 128], Cb[:, mt, c * 128:(c + 1) * 128],
                   R=[("Cb", mt)], W=[("pb", b)], sig=(mt == 1))
            cp(DVE, memT[:, c, 0:256], psb[:, b, 0:256], R=[("pb", b)], W=[("Db", c)])
        DBK = [("Db", c) for c in range(8)]
        chk(0.3)
        for which, dst in (("xk", mk_p), ("xv", mv_p)):
            cf = (lambda x: x) if which == "xk" else (lambda x: 0.97 + 0.03 * x)
            views = [wslab(which + "0"), wslab(which + "1")]
            chk(cf(0.5))
            for mt in range(2):
                b2 = nf2()
                for hf in range(2):
                    wv, wk = views[hf]
                    for c in range(8):
                        mm(psf[:, b2 + hf, :], memT[:, c, mt * 128:(mt + 1) * 128], wv[:, c, :], c == 0, c == 7,
                           R=DBK + [wk], W=[("pf", b2 + hf)], sig=(c == 7))
                chk(cf(0.7))
                t_ = tmpA if mt == 0 else tmpB
                tk = "tmpA" if mt == 0 else "tmpB"
                for hf in range(2):
                    cp(ACT, t_[:, hf * 512:(hf + 1) * 512], psf[:, b2 + hf, :], R=[("pf", b2 + hf)], W=[tk])
                chk(cf(0.8))
                store(dst[s, mt * 128:(mt + 1) * 128, :], t_[:, :], R=[tk])
                chk(cf(0.9))
                if which == "xv":
                    for hf in range(2):
                        cp(DVE, mvb[:, mt, hf * 512:(hf + 1) * 512], t_[:, hf * 512:(hf + 1) * 512],
                           R=[tk], W=[("mvb", mt)])
            chk(cf(0.95))
            if which == "xk":
                for e in range(8):
                    wv, wk = views[e // 4]
                    b1 = nf()
                    for c in range(8):
                        mm(psf[:, b1, 0:256], wv[:, c, (e % 4) * 128:(e % 4 + 1) * 128], memT[:, c, 0:256], c == 0, c == 7,
                           R=DBK + [wk], W=[("pf", b1)], sig=(c == 7))
                    cp(DVE, mkT[:, e, :], psf[:, b1, 0:256], R=[("pf", b1)], W=[("mkT", e)])
                chk(0.97)

    xpref = set()
    MKT = [("mkT", e) for e in range(8)]
    MVB = [("mvb", 0), ("mvb", 1)]

    def prompt_block(s, j):
        t0 = j * 512
        DBK = [("Db", c) for c in range(8)]
        T.retire([("hT", i) for i in range(16)] + [("Cbx", t) for t in range(4)], [("qtm", t) for t in range(4)])
        if PROFILE:
            T.stage = "s01"
        if j == 0:
            T.op(POOL, lambda: nc.gpsimd.memset(uTb[:, :, 0:30], 0.0), W=["uT_head"])
        for t in range(4):
            if (s, j, t) not in xpref:
                load(Cb[:, t % 2, :], xp[s, t0 + t * 128:t0 + (t + 1) * 128, :], W=[("Cb", t % 2)], q=POOL)
            for c in range(8):
                if c % 4 == 0:
                    b = nb()
                tr(psb[:, b, (c % 4) * 128:(c % 4 + 1) * 128], Cb[:, t % 2, c * 128:(c + 1) * 128],
                   R=[("Cb", t % 2)], W=[("pb", b)], sig=(c % 4 == 3))
                if c % 4 == 3:
                    c0 = c - 3
                    cp(ACT if c0 else DVE,
                       Db[:, c0:c0 + 4, t * 128:(t + 1) * 128],
                       psb[:, b, 0:512].rearrange("p (a b) -> p a b", b=128),
                       R=[("pb", b)], W=[("Db", cc) for cc in range(c0, c0 + 4)])
        chk(2)
        if PROFILE:
            T.stage = "s2a"
        v0, k0 = wslab("in0")
        v1, k1 = wslab("in1")
        for t in range(4):
            bA = nf()
            bB = nf()
            bC = nf()
            for (ob, o0, o1, vv, kk, c0_, c1_) in (
                    (bA, 0, 480, v0, k0, 0, 480), (bB, 0, 32, v0, k0, 480, 512), (bB, 32, 288, v1, k1, 0, 256),
                    (bB, 288, 320, v1, k1, 512, 544), (bC, 0, 256, v1, k1, 256, 512)):
                for c in range(8):
                    mm(psf[:, ob, o0:o1], Db[:, c, t * 128:(t + 1) * 128], vv[:, c, c0_:c1_], c == 0, c == 7,
                       R=DBK + [kk], W=[("pf", ob)], sig=(c == 7))
            PQ = [("pf", bA), ("pf", bB), ("pf", bC)]
            tile_i = j * 4 + t
            qd3 = qtm[:, t, :].rearrange("p (h e) -> p h e", e=96)
            for (bk, h0, nh_) in ((bA, 0, 5), (bB, 5, 3)):
                q3 = psf[:, bk, 0:nh_ * 96].rearrange("p (h e) -> p h e", e=96)
                cp(ACT, qd3[:, h0:h0 + nh_, 0:64], q3[:, :, 0:64], R=PQ, W=[("qtm", t)])
                tmp3 = tmpA[:, 0:nh_ * 64].rearrange("p (h e) -> p h e", e=64)
                rope_rows(128, q3[:, :, 64:96], qd3[:, h0:h0 + nh_, 64:96],
                          ropeC[:, tile_i:tile_i + 1, :].to_broadcast([128, nh_, 32]),
                          ropeS[:, tile_i:tile_i + 1, :].to_broadcast([128, nh_, 32]),
                          tmp3, R=PQ + ["rope"], W=[("qtm", t)], tmpkey="tmpA")
            pckv = psf[:, bC, 0:256]
            b1 = bB
            cp(DVE, tmpB[:, 768:1024], pckv, R=PQ, W=["tmpB"])
            pckv = tmpB[:, 768:1024]
            act(tmpB[:, 0:256], pckv, AF.Square, R=["tmpB"], W=["tmpB"])
            T.op(DVE, lambda: nc.vector.reduce_sum(small[:, 2:3], tmpB[:, 0:256], axis=AX.X), R=["tmpB"], W=["tmpB"])
            act(small[:, 3:4], small[:, 2:3], AF.Sqrt, R=["tmpB"], W=[("small", 3)], bias=EPS, scale=1.0 / 256.0)
            T.op(DVE, lambda: nc.vector.reciprocal(small[:, 3:4], small[:, 3:4]), R=[("small", 3)], W=[("small", 3)])
            stt(DVE, tmpB[:, 256:512], pckv, small[:, 3:4], kvg_b[:, :], ALU.mult, ALU.mult,
                R=PQ + [("small", 3), "kvg", "tmpB"], W=["tmpB"])
            store(ckv_p[s, t0 + t * 128:t0 + (t + 1) * 128, :], tmpB[:, 256:512], R=["tmpB"])
            if DEBUG and t == 1:
                store(y_p[s, 0:128, 0:64], small[:, :], R=["tmpB", ("small", 3)])
                store(y_p[s, 0:128, 64:320], tmpB[:, 768:1024], R=["tmpB"])
                store(y_p[s, 0:128, 320:576], tmpB[:, 256:512], R=["tmpB"])
            cp(ACT, ckb[:, t, 0:256], tmpB[:, 256:512], R=["tmpB"], W=[("ckb", t)])
            kr3 = tmpB[:, 512:544].rearrange("p (h e) -> p h e", e=32)
            rope_rows(128, psf[:, b1, 288:320].rearrange("p (h e) -> p h e", e=32), kr3,
                      ropeC[:, tile_i:tile_i + 1, :], ropeS[:, tile_i:tile_i + 1, :],
                      tmpB[:, 576:640].rearrange("p (h e) -> p h e", e=64),
                      R=[("pf", b1), "rope"], W=["tmpB"], tmpkey="tmpB")
            store(kr_p[s, t0 + t * 128:t0 + (t + 1) * 128, :], tmpB[:, 512:544], R=["tmpB"])
            cp(ACT, ckb[:, t, 256:288], tmpB[:, 512:544], R=["tmpB"], W=[("ckb", t)])
        chk(3)
        QTM = [("qtm", t) for t in range(4)]
        CKB = [("ckb", t) for t in range(4)]
        if PROFILE:
            T.stage = "s2b"
        for h in range(8):
            b = nb()
            for t in range(4):
                tr(psb[0:96, b, t * 128:(t + 1) * 128], qtm[:, t, h * 96:(h + 1) * 96], R=QTM, W=[("pb", b)], sig=(t == 3))
            cp(ACT if h % 2 else DVE, QT[0:96, h, :], psb[0:96, b, 0:512], R=[("pb", b)], W=[("QT", h)])
        for lc in range(2):
            b = nb()
            for t in range(4):
                tr(psb[:, b, t * 128:(t + 1) * 128], ckb[:, t, lc * 128:(lc + 1) * 128], R=CKB, W=[("pb", b)], sig=(t == 3))
            cp(DVE, ckT[:, lc, :], psb[:, b, 0:512], R=[("pb", b)], W=[("ckT", lc)])
        b = nb()
        for t in range(4):
            tr(psb[0:32, b, t * 128:(t + 1) * 128], ckb[:, t, 256:288], R=CKB, W=[("pb", b)], sig=(t == 3))
        cp(DVE, krT[:, :], psb[0:32, b, 0:512], R=[("pb", b)], W=["krT"])
        CKT = [("ckT", 0), ("ckT", 1)]
        if PROFILE:
            T.stage = "s2c"
        for h in range(8):
            b1 = nf()
            mm(psf[0:96, b1, :], WukP[:, 0, h, :], ckT[:, 0, :], True, False, R=CKT + WUK, W=[("pf", b1)])
            mm(psf[0:96, b1, :], WukP[:, 1, h, :], ckT[:, 1, :], False, False, R=CKT + WUK, W=[("pf", b1)])
            mm(psf[0:96, b1, :], injk[:, :], krT[:, :], False, True, R=["krT", "injk"], W=[("pf", b1)], sig=True)
            cp(ACT if h % 2 else DVE, KT[0:96, h, t0:t0 + 512], psf[0:96, b1, :], R=[("pf", b1)], W=[("KT", h, j)])
        if PROFILE:
            T.stage = "s2d"
        for t in range(4):
            b1 = nf()
            for lc in range(2):
                mm(psf[:, b1, :], ckT[:, lc, t * 128:(t + 1) * 128], Wuv[:, lc, :], lc == 0, lc == 1,
                   R=CKT + WUV, W=[("pf", b1)], sig=(lc == 1))
            kt = j * 4 + t
            pv = psf[:, b1, :].rearrange("p (a two e) -> p a two e", two=2, e=64)
            cp(DVE, Vaug[:, kt, :, 0:64], pv[:, :, 0, :], R=[("pf", b1), "Vaug_init"], W=[("V", kt)])
            cp(ACT, Vaug[:, kt, :, 128:192], pv[:, :, 1, :], R=[("pf", b1), "Vaug_init"], W=[("V", kt)])
        chk(4)
        if PROFILE:
            T.stage = "s2e"
        v2, k2 = wslab("in2")
        v3, k3 = wslab("in3")
        for cc in range(4):
            ba = nf()
            bb = nf()
            for c in range(8):
                mm(psf[:, ba, :], v2[:, c, cc * 128:(cc + 1) * 128], Db[:, c, :], c == 0, c == 7, R=DBK + [k2], W=[("pf", ba)])
                mm(psf[:, bb, :], v3[:, c, cc * 128:(cc + 1) * 128], Db[:, c, :], c == 0, c == 7, R=DBK + [k3], W=[("pf", bb)], sig=(c == 7))
            act(tmpA[:, 0:512], psf[:, bb, :], AF.Sigmoid, R=[("pf", bb)], W=["tmpA"])
            tt(DVE, uTb[:, cc, 30:542], psf[:, ba, :], tmpA[:, 0:512], ALU.mult, R=[("pf", ba), "tmpA", "uT_head"], W=[("uT", cc)])
        if j == nblk - 1:
            b2 = nf2()
            for c in range(8):
                xTt = Db[:, c, 384:512]
                mm(psf[:, b2, :], xTt, v2[:, c, :], c == 0, c == 7, R=DBK + [k2], W=[("pf", b2)])
                mm(psf[:, b2 + 1, :], xTt, v3[:, c, :], c == 0, c == 7, R=DBK + [k3], W=[("pf", b2 + 1)], sig=(c == 7))
            act(tmpA[:, 0:512], psf[:, b2 + 1, :], AF.Sigmoid, R=[("pf", b2 + 1)], W=["tmpA"])
            tt(DVE, tmpA[:, 512:1024], psf[:, b2, :], tmpA[:, 0:512], ALU.mult, R=[("pf", b2), "tmpA"], W=["tmpA"])
            store(conv_p[s, :, :], tmpA[98:128, 512:1024], R=["tmpA"])
        chk(5)
        if PROFILE:
            T.stage = "s3"
        nkt = 4 * j + 4
        pairs = [(h, kt) for h in range(8) for kt in range(nkt)]
        bos = {}
        st_info = {}

        def emit_S(h, kt):
            r = kt - 4 * j
            q0 = 0 if r <= 0 else r * 128
            n = 512 - q0
            bs = nf()
            mm(psf[:, bs, 0:n], KT[0:96, h, kt * 128:(kt + 1) * 128], QT[0:96, h, q0:512], True, True,
               R=[("KT", h, kt // 4), ("QT", h)], W=[("pf", bs)], sig=True)
            pi = (h * 64 + kt) % 4
            act(Pr[:, pi, 0:n], psf[:, bs, 0:n], AF.Exp, R=[("pf", bs)], W=[("Pr", pi)], scale=MLA_SCALE)
            if r >= 0:
                tt(POOL, Pr[:, pi, 0:128], Pr[:, pi, 0:128], mask[:, :], ALU.mult, R=[("Pr", pi), "mask"], W=[("Pr", pi)])
            st_info[(h, kt)] = (pi, q0, n)

        def emit_PV(h, kt):
            pi, q0, n = st_info.pop((h, kt))
            if kt == 0:
                bos[h] = nh()
            bo = bos[h]
            odd = h % 2
            lhs = Vaug[:, kt, h // 2, 64:192] if odd else Vaug[:, kt, h // 2, 0:128]
            mm(psf[:, bo, q0:512], lhs, Pr[:, pi, 0:n], kt == 0, kt == nkt - 1,
               R=[("Pr", pi), ("V", kt), "Vaug_init"], W=[("pf", bo)], sig=True)
            if kt == nkt - 1:
                if h % 2 == 0:
                    cp(DVE, tmpA[0:64, 0:512], psf[64:128, bo, :], R=[("pf", bo)], W=["tmpA"])
                    T.op(DVE, lambda: nc.vector.reciprocal(tmpA[0:64, 0:512], tmpA[0:64, 0:512]), R=["tmpA"], W=["tmpA"])
                    tt(DVE, mixT[0:64, h // 2, :], psf[0:64, bo, :], tmpA[0:64, 0:512], ALU.mult,
                       R=[("pf", bo), "tmpA"], W=[("mixT", h // 2)])
                else:
                    cp(DVE, tmpB[64:128, 0:512], psf[0:64, bo, :], R=[("pf", bo)], W=["tmpB"])
                    T.op(DVE, lambda: nc.vector.reciprocal(tmpB[64:128, 0:512], tmpB[64:128, 0:512]), R=["tmpB"], W=["tmpB"])
                    tt(DVE, mixT[64:128, h // 2, :], psf[64:128, bo, :], tmpB[64:128, 0:512], ALU.mult,
                       R=[("pf", bo), "tmpB"], W=[("mixT", h // 2)])

        emit_S(*pairs[0])
        for i in range(len(pairs)):
            if i + 1 < len(pairs):
                emit_S(*pairs[i + 1])
            emit_PV(*pairs[i])
        chk(6)
        if PROFILE:
            T.stage = "s4"
        for cc in range(4):
            wv, wk = wslab(f"cd{cc}")
            b1 = nf()
            for k in range(31):
                mm(psf[:, b1, :], wv[:, k, :], uTb[:, cc, k:k + 512], k == 0, k == 30,
                   R=[("uT", cc), "uT_head", wk], W=[("pf", b1)], sig=(k == 30))
            act(cacc[:, cc, :], psf[:, b1, :], AF.Identity, R=[("pf", b1)] + CV, W=[("Rb", cc)], bias=cvec[:, 0, cc:cc + 1])
            if j < nblk - 1:
                cp(POOL, tmpA[:, 512 + cc * 32:512 + cc * 32 + 30], uTb[:, cc, 512:542], R=[("uT", cc)], W=[("uhc", cc)])
            cp(ACT, cyb[:, cc, :], cacc[:, cc, :], R=[("Rb", cc)], W=[("QT", cc)])
            act(csq[:, cc, :], cacc[:, cc, :], AF.Square, R=[("Rb", cc)], W=[("Pr", cc)])
        if j < nblk - 1:
            for cc in range(4):
                cp(POOL, uTb[:, cc, 0:30], tmpA[:, 512 + cc * 32:512 + cc * 32 + 30], R=[("uhc", cc)], W=["uT_head"])
        bm = nf()
        bq = nf()
        for cc in range(4):
            mm(psf[:, bm, :], ones[:, :], cyb[:, cc, :], cc == 0, cc == 3, R=[("QT", cc), "ones"], W=[("pf", bm)])
            mm(psf[:, bq, :], ones[:, :], csq[:, cc, :], cc == 0, cc == 3, R=[("Pr", cc), "ones"], W=[("pf", bq)], sig=(cc == 3))
        ts(DVE, tmpA[:, 0:512], psf[:, bm, :], 1.0 / 512.0, None, ALU.mult, None, R=[("pf", bm)], W=["tmpA"])
        tt(DVE, tmpA[:, 512:1024], tmpA[:, 0:512], tmpA[:, 0:512], ALU.mult, R=["tmpA"], W=["tmpA"])
        stt(DVE, tmpA[:, 512:1024], psf[:, bq, :], 1.0 / 512.0, tmpA[:, 512:1024], ALU.mult, ALU.subtract,
            R=[("pf", bq), "tmpA"], W=["tmpA"])
        act(tmpA[:, 512:1024], tmpA[:, 512:1024], AF.Sqrt, R=["tmpA"], W=["tmpA"], bias=EPS, scale=1.0)
        T.op(DVE, lambda: nc.vector.reciprocal(tmpA[:, 512:1024], tmpA[:, 512:1024]), R=["tmpA"], W=["tmpA"])
        for cc in range(4):
            tt(DVE, cacc[:, cc, :], cacc[:, cc, :], tmpA[:, 0:512], ALU.subtract, R=[("Rb", cc), "tmpA"], W=[("Rb", cc)])
            tt(POOL, cacc[:, cc, :], cacc[:, cc, :], tmpA[:, 512:1024], ALU.mult, R=[("Rb", cc), "tmpA"], W=[("Rb", cc)])
            act(mixT[:, 4 + cc, :], cacc[:, cc, :], AF.Silu, R=[("Rb", cc)] + CV, W=[("mixT", 4 + cc)],
                bias=cvec[:, 2, cc:cc + 1], scale=cvec[:, 1, cc:cc + 1])
        chk(7)
        MIX = [("mixT", c) for c in range(8)]
        if PROFILE:
            T.stage = "s5"
        T.retire([("qtm", t) for t in range(4)], [("Cbx", t) for t in range(4)])
        post_ln(s, j, MIX, mixT, "o", 0, 1, x_from_dram=True)
        chk(8)
        if PROFILE:
            T.stage = "s7"
        vq = [wslab("xq0"), wslab("xq1")]
        for e in range(8):
            wv, wk = vq[e // 4]
            b1 = nf()
            for c in range(8):
                mm(psf[:, b1, :], wv[:, c, (e % 4) * 128:(e % 4 + 1) * 128], Db[:, c, :], c == 0, c == 7,
                   R=DBK + [wk], W=[("pf", b1)], sig=(c == 7))
            cp(ACT if e % 2 else DVE, QT[:, e, :], psf[:, b1, :], R=[("pf", b1)], W=[("QT", e)])
        if PROFILE:
            T.stage = "s8"
        for h in range(4):
            pis = []
            for mc in range(2):
                bs = nf()
                for dc in range(2):
                    mm(psf[:, bs, :], mkT[:, 2 * h + dc, mc * 128:(mc + 1) * 128], QT[:, 2 * h + dc, :], dc == 0, dc == 1,
                       R=MKT + [("QT", 2 * h + dc)], W=[("pf", bs)], sig=(dc == 1))
                pi = (h * 2 + mc) % 4
                pis.append(pi)
                act(Pr[:, pi, :], psf[:, bs, :], AF.Exp, R=[("pf", bs)], W=[("Pr", pi)], scale=MEM_SCALE)
            bd = nf()
            for mc in range(2):
                mm(psf[:, bd, :], ones[:, :], Pr[:, pis[mc], :], mc == 0, mc == 1, R=[("Pr", pis[mc]), "ones"], W=[("pf", bd)], sig=(mc == 1))
            T.op(DVE, lambda: nc.vector.reciprocal(tmpA[:, 0:512], psf[:, bd, :]), R=[("pf", bd)], W=["tmpA"])
            for dc in range(2):
                bo = nf()
                for mc in range(2):
                    mm(psf[:, bo, :], mvb[:, mc, h * 256 + dc * 128:h * 256 + (dc + 1) * 128], Pr[:, pis[mc], :], mc == 0, mc == 1,
                       R=MVB + [("Pr", pis[mc])], W=[("pf", bo)], sig=(mc == 1))
                tt(DVE, mixT[:, 2 * h + dc, :], psf[:, bo, :], tmpA[:, 0:512], ALU.mult, R=[("pf", bo), "tmpA"], W=[("mixT", 2 * h + dc)])
        chk(9)
        if PROFILE:
            T.stage = "s9"
        post_ln(s, j, MIX, mixT, "xo", 2, 3, x_from_dram=False)
        chk(10)
        if PROFILE:
            T.stage = "s10"
        nxt = (s, j + 1) if j + 1 < nblk else ((s + 1, 0) if s + 1 < nseq else None)
        if nxt is not None and not (nxt[1] == 0):
            for t in range(2):
                load(Cb[:, t, :], xp[nxt[0], nxt[1] * 512 + t * 128:nxt[1] * 512 + (t + 1) * 128, :], W=[("Cb", t)], q=POOL)
                xpref.add((nxt[0], nxt[1], t))
        T.retire([("qtm", t) for t in range(4)] + [("Cbx", t) for t in range(4)], [("hT", i) for i in range(16)])
        for fh in range(2):
            for g in range(4):
                wv, wk = wslab(f"up{fh * 4 + g}")
                for fc in range(4):
                    b1 = nf()
                    for c in range(8):
                        mm(psf[:, b1, :], wv[:, c, fc * 128:(fc + 1) * 128], Db[:, c, :], c == 0, c == 7,
                           R=DBK + [wk], W=[("pf", b1)], sig=(c == 7))
                    act(tmpA[:, 0:512] if fc % 2 == 0 else tmpB[:, 0:512], psf[:, b1, :], AF.Relu, R=[("pf", b1)],
                        W=["tmpA" if fc % 2 == 0 else "tmpB"])
                    tt(DVE, hT[:, g * 4 + fc, :], tmpA[:, 0:512] if fc % 2 == 0 else tmpB[:, 0:512], psf[:, b1, :], ALU.mult,
                       R=[("pf", b1), "tmpA" if fc % 2 == 0 else "tmpB"], W=[("hT", g * 4 + fc)])
            HK = [("hT", i) for i in range(16)]
            for qd in range(4):
                wv, wk = wslab(f"dn{fh}{qd}")
                for t in range(4):
                    b1 = nf()
                    for fc in range(16):
                        mm(psf[:, b1, 0:256], hT[:, fc, t * 128:(t + 1) * 128], wv[:, fc, :], fc == 0, fc == 15,
                           R=HK + [wk], W=[("pf", b1)], sig=(fc == 15))
                    dstv = Rb[:, t, qd * 256:(qd + 1) * 256]
                    if fh == 0:
                        stt(DVE, dstv, dstv, ALPHA, psf[:, b1, 0:256], ALU.mult, ALU.add, R=[("Rb", t), ("pf", b1)], W=[("Rb", t)])
                    else:
                        tt(DVE, dstv, dstv, psf[:, b1, 0:256], ALU.add, R=[("Rb", t), ("pf", b1)], W=[("Rb", t)])
        bvload(4, 0)
        bvload(5, 1)
        for t in range(4):
            layer_norm_rows(("Rb", t), 128, Rb[:, t, :], Rb[:, t, :], 4, 5, None, None)
            store(y_p[s, t0 + t * 128:t0 + (t + 1) * 128, :], Rb[:, t, :], R=[("Rb", t)])

    def post_ln(s, j, INK, inT, wname, gi, bi, x_from_dram):
        t0 = j * 512
        views = [wslab(wname + "0"), wslab(wname + "1")]
        bvload(gi, 0)
        bvload(bi, 1)

        def mm_ln(t):
            b2 = nf2()
            for hf in range(2):
                wv, wk = views[hf]
                for c in range(8):
                    mm(psf[:, b2 + hf, :], inT[:, c, t * 128:(t + 1) * 128], wv[:, c, :], c == 0, c == 7,
                       R=INK + [wk], W=[("pf", b2 + hf)], sig=(c == 7))
            if x_from_dram:
                load(xring[:, 0, :], xp[s, t0 + t * 128:t0 + (t + 1) * 128, :], W=[("xring", 0)])
            for hf in range(2):
                src = xring[:, 0, hf * 512:(hf + 1) * 512] if x_from_dram else Rb[:, t, hf * 512:(hf + 1) * 512]
                stt(DVE, Rb[:, t, hf * 512:(hf + 1) * 512], src, ALPHA, psf[:, b2 + hf, :], ALU.mult, ALU.add,
                    R=[("pf", b2 + hf), ("xring", 0), ("Rb", t)], W=[("Rb", t)])
            layer_norm_rows(("Rb", t), 128, Rb[:, t, :], Rb[:, t, :], gi, bi, Cbx[:, t, :], ("Cbx", t))

        def tr_ev(t):
            for c in range(8):
                if c % 4 == 0:
                    b = nb()
                tr(psb[:, b, (c % 4) * 128:(c % 4 + 1) * 128], Cbx[:, t, c * 128:(c + 1) * 128],
                   R=[("Cbx", t)], W=[("pb", b)], sig=(c % 4 == 3))
                if c % 4 == 3:
                    c0 = c - 3
                    cp(ACT if c0 else DVE,
                       Db[:, c0:c0 + 4, t * 128:(t + 1) * 128],
                       psb[:, b, 0:512].rearrange("p (a b) -> p a b", b=128),
                       R=[("pb", b)], W=[("Db", cc) for cc in range(c0, c0 + 4)])

        mm_ln(0)
        mm_ln(1)
        mm_ln(2)
        tr_ev(0)
        mm_ln(3)
        tr_ev(1)
        tr_ev(2)
        tr_ev(3)

    def barrier():
        engs = (PE, ACT, DVE, POOL, SP)
        dss = st_sems + ld_sems + wr_sems + cvt_sems + g_sems
        for E in engs:
            for F in engs:
                if F is not E and F.count and E.seen.get(F, 0) < F.count:
                    E.eng.wait_ge(F.sem, F.count)
                    E.seen[F] = F.count
            for ds in dss:
                if ds.count and E.seen.get(ds, 0) < ds.count:
                    E.eng.wait_ge(ds.sem, ds.count * 16)
                    E.seen[ds] = ds.count

    g_sems = [T.dsem(f"g{i}") for i in range(4)]

    def sample_phase():
        NS = 16
        KTf = KT[:, :, :].rearrange("p a b -> p (a b)")
        hTf = hT[:, :, :].rearrange("p a b -> p (a b)")
        QTf = QT[:, :, :].rearrange("p a b -> p (a b)")
        Rbf = Rb[:, :, :].rearrange("p a b -> p (a b)")
        uTf = uT[:, :, :].rearrange("p a b -> p (a b)")
        bvf = bvec[:, :, :].rearrange("p a b -> p (a b)")
        mixf = mixT[:, :, :].rearrange("p a b -> p (a b)")
        Dbf = Db[:, :, :].rearrange("p a b -> p (a b)")

        def carver(arena):
            off = [0]
            def carve(n):
                v = arena[:, off[0]:off[0] + n]
                off[0] += n
                return v
            return carve
        ck, ch, cq, cr, cm, cd = carver(KTf), carver(hTf), carver(QTf), carver(Rbf), carver(mixf), carver(Dbf)
        cpg = [ck(2048).rearrange("p (r l) -> p r l", l=256) for _ in range(2)]
        rpg = [ck(256).rearrange("p (r l) -> p r l", l=32) for _ in range(2)]
        cT = [ck(2048).rearrange("p (a b) -> p a b", b=1024) for _ in range(2)]
        rT = [ck(1024) for _ in range(2)]
        cnx = ck(16 * 264).rearrange("p (b l) -> p b l", l=264)
        Pp = ck(512)
        WukT = ch(2048).rearrange("p (h l) -> p h l", l=256)
        ql_bf = ch(2048)
        xs_bf = ch(1024)
        q_s = ch(768)
        cnew_bf = ch(288)
        mix_s = ch(1024)
        xsT = cq(128).rearrange("p (c b) -> p c b", b=16)
        qT_s = cq(128).rearrange("p (h b) -> p h b", b=16)
        qlT = cq(256).rearrange("p (l b h) -> p l b h", b=16, h=8)
        qrT = cq(128).rearrange("p (b h) -> p b h", h=8)
        cnT = cq(32).rearrange("p (l b) -> p l b", b=16)
        krnT = cq(16)
        olT = cq(256).rearrange("p (l h b) -> p l h b", h=8, b=16)
        mixT_s = cq(128).rearrange("p (c b) -> p c b", b=16)
        o2T_s = cq(128).rearrange("p (c b) -> p c b", b=16)
        hT_s = cq(512)
        pnew = cq(8)
        ol_bf = cq(256)
        P2bf = cq(8)
        dhl = cq(16)
        xs_f = cr(1024)
        r_s = cr(1024)
        u_s = cr(512)
        q2_f = cr(1024)
        misc = cr(512)
        xb16 = cm(1024)
        ptb = sb("ptb", [128, 128], I32)
        gqt = sb("gqt", [128, 1], I32)
        idx = sb("idx", [128, 128], I32)
        q2_scr = dscr("q2_scr", [16, 1024], F32)
        onesf = sb("onesf", [128, 1], F32)
        idxc = [sb(f"idxc{i}", [128, 1], I32) for i in range(2)]

        load(ptb[:], ptrep, W=["ptb"])
        load(gqt[:], gq, W=["gqt"])
        T.op(DVE, lambda: nc.vector.memset(onesf[:], 1.0), W=["onesf"])
        ts(DVE, idx[:], ptb[:], 16, gqt[:, 0:1], ALU.mult, ALU.add, R=["ptb", "gqt"], W=["idx"])
        if DEBUG == "idx":
            cp(DVE, tmpA[:, 0:128], idx[:], R=["idx"], W=["tmpA"])
            store(ckv_p[0, 0:128, 0:128], tmpA[:, 0:128], R=["tmpA"])
            barrier()
            raise _Stop()
        T.op(POOL, lambda: nc.gpsimd.memset(cnx[0:1, :, :], 1.0), W=["cnx"])

        def tr16(dst, src16, ncols, key_r, key_w, E=DVE):
            b = nb()
            for c in range(ncols):
                tr(psb[:, b, c * 16:(c + 1) * 16], src16[0:16, c * 128:(c + 1) * 128], R=[key_r], W=[("pb", b)], sig=(c == ncols - 1))
            cp(E, dst, psb[:, b, 0:ncols * 16].rearrange("p (c b) -> p c b", b=16), R=[("pb", b)], W=[key_w])

        if PROFILE:
            T.stage = "smp_pre"
        load(xs_f[0:16, :], xs, W=["xs_f"])
        load(xs_bf[0:16, :], xs, W=["xs_bf"], q=POOL)
        tr16(xsT, xs_bf, 8, "xs_bf", "xsT")
        v0, k0 = wslab("in0")
        v1, k1 = wslab("in1")
        bA, bB, bC = nf(), nf(), nf()
        for (ob, o0, o1, vv, kk, c0_, c1_) in ((bA, 0, 480, v0, k0, 0, 480), (bB, 0, 32, v0, k0, 480, 512),
                                               (bB, 32, 288, v1, k1, 0, 256), (bB, 288, 320, v1, k1, 512, 544),
                                               (bC, 0, 256, v1, k1, 256, 512)):
            for c in range(8):
                mm(psf[0:16, ob, o0:o1], xsT[:, c, :], vv[:, c, c0_:c1_], c == 0, c == 7, R=["xsT", kk], W=[("pf", ob)], sig=(c == 7))
        PQ = [("pf", bA), ("pf", bB), ("pf", bC)]
        qd3 = q_s[0:16, :].rearrange("p (h e) -> p h e", e=96)
        for (bk, h0, nh_) in ((bA, 0, 5), (bB, 5, 3)):
            q3 = psf[0:16, bk, 0:nh_ * 96].rearrange("p (h e) -> p h e", e=96)
            cp(ACT, qd3[:, h0:h0 + nh_, 0:64], q3[:, :, 0:64], R=PQ, W=["q_s"])
            tmp3 = tmpA[0:16, 0:nh_ * 64].rearrange("p (h e) -> p h e", e=64)
            rope_rows(16, q3[:, :, 64:96], qd3[:, h0:h0 + nh_, 64:96],
                      ropeCs[0:16, :].rearrange("p (o e) -> p o e", o=1).to_broadcast([16, nh_, 32]),
                      ropeSs[0:16, :].rearrange("p (o e) -> p o e", o=1).to_broadcast([16, nh_, 32]),
                      tmp3, R=PQ + ["ropes"], W=["q_s"], tmpkey="tmpA")
        cp(DVE, tmpB[0:16, 768:1024], psf[0:16, bC, 0:256], R=PQ, W=["tmpB"])
        zc = tmpB[0:16, 768:1024]
        act(tmpB[0:16, 0:256], zc, AF.Square, R=["tmpB"], W=["tmpB"])
        T.op(DVE, lambda: nc.vector.reduce_sum(small[0:16, 2:3], tmpB[0:16, 0:256], axis=AX.X), R=["tmpB"], W=["tmpB"])
        act(small[0:16, 3:4], small[0:16, 2:3], AF.Sqrt, R=["tmpB"], W=[("small", 3)], bias=EPS, scale=1.0 / 256.0)
        T.op(DVE, lambda: nc.vector.reciprocal(small[0:16, 3:4], small[0:16, 3:4]), R=[("small", 3)], W=[("small", 3)])
        stt(DVE, tmpB[0:16, 256:512], zc, small[0:16, 3:4], kvg_b[0:16, :], ALU.mult, ALU.mult,
            R=[("small", 3), "kvg", "tmpB"], W=["tmpB"])
        store(ckv_s[:, :], tmpB[0:16, 256:512], R=["tmpB"], W=["ckv_s_dram"])
        cp(ACT, cnew_bf[0:16, 0:256], tmpB[0:16, 256:512], R=["tmpB"], W=["cnew_bf"])
        kr3 = tmpB[0:16, 512:544].rearrange("p (h e) -> p h e", e=32)
        rope_rows(16, psf[0:16, bB, 288:320].rearrange("p (h e) -> p h e", e=32), kr3,
                  ropeCs[0:16, :].rearrange("p (o e) -> p o e", o=1), ropeSs[0:16, :].rearrange("p (o e) -> p o e", o=1),
                  tmpB[0:16, 576:640].rearrange("p (h e) -> p h e", e=64),
                  R=PQ + ["ropes"], W=["tmpB"], tmpkey="tmpB")
        store(kr_s[:, :], tmpB[0:16, 512:544], R=["tmpB"])
        cp(ACT, cnew_bf[0:16, 256:288], tmpB[0:16, 512:544], R=["tmpB"], W=["cnew_bf"])
        v2, k2 = wslab("in2")
        v3, k3 = wslab("in3")
        ba, bb = nf(), nf()
        for c in range(8):
            mm(psf[0:16, ba, :], xsT[:, c, :], v2[:, c, :], c == 0, c == 7, R=["xsT", k2], W=[("pf", ba)])
            mm(psf[0:16, bb, :], xsT[:, c, :], v3[:, c, :], c == 0, c == 7, R=["xsT", k3], W=[("pf", bb)], sig=(c == 7))
        act(tmpA[0:16, 0:512], psf[0:16, bb, :], AF.Sigmoid, R=[("pf", bb)], W=["tmpA"])
        tt(DVE, u_s[0:16, :], psf[0:16, ba, :], tmpA[0:16, 0:512], ALU.mult, R=[("pf", ba), "tmpA"], W=["u_s"])
        mk_rest_slabs()
        T.dma(SP, st_sems[0], lambda: nc.sync.dma_start(out=conv_s[:, 0:29 * 512], in_=stc[:, 512:30 * 512]))
        store(conv_s[:, 29 * 512:30 * 512], u_s[0:16, :], R=["u_s"])
        acc = misc[0:16, :]
        first = True
        for k0_ in range(0, 30, 4):
            n = min(4, 30 - k0_)
            ext = uTf[0:16, 0:n * 512]
            cwb = bvf[0:16, 0:n * 512]
            load(ext, stc[:, k0_ * 512:(k0_ + n) * 512], W=["ext"])
            load(cwb, conv_w[k0_:k0_ + n, :].rearrange("(o k) c -> o (k c)", o=1).partition_broadcast(16), W=[("bvec", 0), ("bvec", 1)])
            tt(DVE, ext, ext, cwb, ALU.mult, R=["ext", ("bvec", 0), ("bvec", 1)], W=["ext"])
            for jx in range(n):
                if first:
                    cp(DVE, acc, ext[:, jx * 512:(jx + 1) * 512], R=["ext"], W=["acc"])
                    first = False
                else:
                    tt(DVE, acc, acc, ext[:, jx * 512:(jx + 1) * 512], ALU.add, R=["ext", "acc"], W=["acc"])
        load(bvf[0:16, 0:512], conv_w[30:31, :].partition_broadcast(16), W=[("bvec", 0), ("bvec", 1)])
        tt(DVE, tmpA[0:16, 0:512], u_s[0:16, :], bvf[0:16, 0:512], ALU.mult, R=["u_s", ("bvec", 0), ("bvec", 1)], W=["tmpA"])
        tt(DVE, acc, acc, tmpA[0:16, 0:512], ALU.add, R=["tmpA", "acc"], W=["acc"])
        load(bvf[0:16, 0:512], conv_b.partition_broadcast(16), W=[("bvec", 0), ("bvec", 1)])
        tt(DVE, acc, acc, bvf[0:16, 0:512], ALU.add, R=["acc", ("bvec", 0), ("bvec", 1)], W=["acc"])
        T.op(DVE, lambda: nc.vector.bn_stats(stats[0:16, 0, :], acc), R=["acc"], W=[("stats", 0)])
        T.op(DVE, lambda: nc.vector.bn_aggr(mvar[0:16, :], stats[0:16, 0:1, :]), R=[("stats", 0)], W=["mvar"])
        rstd_from_var(small[0:16, 0:1], mvar[0:16, 1:2], R=["mvar"], W=[("small", 0)])
        stt(DVE, small[0:16, 1:2], mvar[0:16, 0:1], -1.0, small[0:16, 0:1], ALU.mult, ALU.mult,
            R=["mvar", ("small", 0)], W=[("small", 1)])
        act(acc, acc, AF.Identity, R=["acc", ("small", 0), ("small", 1)], W=["acc"], bias=small[0:16, 1:2], scale=small[0:16, 0:1])
        load(bvf[0:16, 0:512], conv_g.partition_broadcast(16), W=[("bvec", 0)])
        load(bvf[0:16, 1024:1536], conv_bb.partition_broadcast(16), W=[("bvec", 1)])
        tt(DVE, acc, acc, bvf[0:16, 0:512], ALU.mult, R=["acc", ("bvec", 0)], W=["acc"])
        tt(DVE, acc, acc, bvf[0:16, 1024:1536], ALU.add, R=["acc", ("bvec", 1)], W=["acc"])
        act(mix_s[0:16, 512:1024], acc, AF.Silu, R=["acc"], W=["mix_s"])
        for h in range(8):
            b = nb()
            for lc in range(2):
                tr(psb[0:64, b, lc * 128:(lc + 1) * 128], WukP[:, lc, h, 0:64], R=WUK, W=[("pb", b)], sig=(lc == 1))
            cp(DVE, WukT[0:64, h, :], psb[0:64, b, 0:256], R=[("pb", b)], W=["WukT"])
        b = nb()
        for h in range(8):
            tr(psb[0:96, b, h * 16:(h + 1) * 16], q_s[0:16, h * 96:(h + 1) * 96], R=["q_s"], W=[("pb", b)], sig=(h == 7))
        cp(DVE, qT_s[0:96, :, :], psb[0:96, b, 0:128].rearrange("p (h b) -> p h b", b=16), R=[("pb", b)], W=["qT_s"])
        cp(DVE, qrT[0:32, :, :].rearrange("p b h -> p h b"), qT_s[64:96, :, :], R=["qT_s"], W=["qrT"])
        qb = [nf() for _ in range(4)]
        for h in range(8):
            mm(psf[0:16, qb[h // 2], (h % 2) * 256:(h % 2 + 1) * 256], qT_s[0:64, h, :], WukT[0:64, h, :], True, True,
               R=["qT_s", "WukT"], W=[("pf", qb[h // 2])], sig=True)
        for i in range(4):
            cp(ACT if i % 2 else DVE, ql_bf[0:16, i * 512:(i + 1) * 512], psf[0:16, qb[i], :], R=[("pf", qb[i])], W=["ql_bf"])
        if DEBUG == "ql":
            cp(DVE, tmpA[0:16, 0:1024], ql_bf[0:16, 1024:2048], R=["ql_bf"], W=["tmpA"])
            store(y_s[:, :], tmpA[0:16, 0:1024], R=["tmpA"])
            barrier()
            raise _Stop()
        b = nb()
        for lc in range(2):
            for h in range(8):
                o = (lc * 8 + h) * 16
                tr(psb[:, b, o:o + 16], ql_bf[0:16, h * 256 + lc * 128:h * 256 + (lc + 1) * 128], R=["ql_bf"], W=[("pb", b)],
                   sig=(lc == 1 and h == 7))
        cp(DVE, qlT[:, :, :, :].rearrange("p l b h -> p l h b"), psb[:, b, 0:256].rearrange("p (l h b) -> p l h b", h=8, b=16),
           R=[("pb", b)], W=["qlT"])
        tr16(cnT, cnew_bf, 2, "cnew_bf", "cnT")
        b = nb()
        tr(psb[0:32, b, 0:16], cnew_bf[0:16, 256:288], R=["cnew_bf"], W=[("pb", b)], sig=True)
        cp(DVE, krnT[0:32, :], psb[0:32, b, 0:16], R=[("pb", b)], W=["krnT"])
        load(cnx[0:1, :, 0:256], ckv_s.rearrange("(o b) l -> o b l", o=1), R=["ckv_s_dram", "cnx"], W=["cnxl"], q=POOL)
        if PROFILE:
            T.stage = "smp_attn"
        gi = [0]
        for bsm in range(NS):
            pS = nh()
            pO = nh()
            for pg in range(8):
                sl = gi[0] % 2
                gi[0] += 1
                col = bsm * 8 + pg
                cp(DVE, idxc[sl][:, 0:1], idx[:, col:col + 1], R=["idx"], W=[("idxc", sl)])
                T.dma(POOL, g_sems[sl], lambda sl=sl, col=col: nc.gpsimd.indirect_dma_start(
                    out=cpg[sl][:, :, :].rearrange("p r l -> p (r l)"), out_offset=None, in_=pool_c,
                    in_offset=bass.IndirectOffsetOnAxis(ap=idxc[sl][:, 0:1], axis=0)), R=[("idxc", sl)], W=[("cpg", sl)])
                T.dma(POOL, g_sems[2 + sl], lambda sl=sl, col=col: nc.gpsimd.indirect_dma_start(
                    out=rpg[sl][:, :, :].rearrange("p r l -> p (r l)"), out_offset=None, in_=pool_r,
                    in_offset=bass.IndirectOffsetOnAxis(ap=idxc[sl][:, 0:1], axis=0)), R=[("idxc", sl)], W=[("rpg", sl)])
                if DEBUG == "cpg":
                    for ii, rr in enumerate((0, 7)):
                        cp(DVE, tmpA[:, ii * 256:(ii + 1) * 256], cpg[sl][:, rr, 0:256], R=[("cpg", sl)], W=["tmpA"])
                        store(ckv_p[0, ii * 128:(ii + 1) * 128, :], tmpA[:, ii * 256:(ii + 1) * 256], R=["tmpA"])
                    cp(DVE, tmpA[:, 512:768], rpg[sl][:, :, :].rearrange("p r l -> p (r l)"), R=[("rpg", sl)], W=["tmpA"])
                    store(ckv_p[0, 256:384, :], tmpA[:, 512:768], R=["tmpA"])
                    barrier()
                    raise _Stop()
                for lc in range(2):
                    b = nb()
                    for r in range(8):
                        tr(psb[:, b, r * 128:(r + 1) * 128], cpg[sl][:, r, lc * 128:(lc + 1) * 128], R=[("cpg", sl)], W=[("pb", b)], sig=(r == 7))
                    cp(ACT if lc else DVE, cT[sl][:, lc, :], psb[:, b, :], R=[("pb", b)], W=[("cT", sl)])
                b = nb()
                for r in range(8):
                    tr(psb[0:32, b, r * 128:(r + 1) * 128], rpg[sl][:, r, :], R=[("rpg", sl)], W=[("pb", b)], sig=(r == 7))
                cp(DVE, rT[sl][0:32, :], psb[0:32, b, :], R=[("pb", b)], W=[("rT", sl)])
                for r in range(8):
                    kt = pg * 8 + r
                    o_ = psf[:, pS, kt * 8:(kt + 1) * 8]
                    RR = [("cT", sl), ("rT", sl), "qlT", "qrT"]
                    mm(o_, cT[sl][:, 0, r * 128:(r + 1) * 128], qlT[:, 0, bsm, :], True, False, R=RR, W=[("pf", pS)])
                    mm(o_, cT[sl][:, 1, r * 128:(r + 1) * 128], qlT[:, 1, bsm, :], False, False, R=RR, W=[("pf", pS)])
                    mm(o_, rT[sl][0:32, r * 128:(r + 1) * 128], qrT[0:32, bsm, :], False, True, R=RR, W=[("pf", pS)], sig=(r == 7))
                act(Pp[:, pg * 64:(pg + 1) * 64], psf[:, pS, pg * 64:(pg + 1) * 64], AF.Exp, R=[("pf", pS)], W=["Pp"], scale=MLA_SCALE)
                for r in range(8):
                    kt = pg * 8 + r
                    mm(psf[0:8, pO, 0:256], Pp[:, kt * 8:(kt + 1) * 8], cpg[sl][:, r, :], kt == 0, False,
                       R=["Pp", ("cpg", sl)], W=[("pf", pO)], sig=(r == 7))
            pN = nf()
            mm(psf[0:1, pN, 0:8], cnT[:, 0, bsm:bsm + 1], qlT[:, 0, bsm, :], True, False, R=["cnT", "qlT"], W=[("pf", pN)])
            mm(psf[0:1, pN, 0:8], cnT[:, 1, bsm:bsm + 1], qlT[:, 1, bsm, :], False, False, R=["cnT", "qlT"], W=[("pf", pN)])
            mm(psf[0:1, pN, 0:8], krnT[0:32, bsm:bsm + 1], qrT[0:32, bsm, :], False, True, R=["krnT", "qrT"], W=[("pf", pN)], sig=True)
            act(pnew[0:1, :], psf[0:1, pN, 0:8], AF.Exp, R=[("pf", pN)], W=["pnew"], scale=MLA_SCALE)
            mm(psf[0:8, pO, 0:256], pnew[0:1, :], cnx[0:1, bsm, 0:256], False, True, R=["pnew", "cnx", "cnxl"], W=[("pf", pO)], sig=True)
            T.op(DVE, lambda: nc.vector.reduce_sum(small[:, 40:48], Pp[:, :].rearrange("p (k h) -> p h k", h=8), axis=AX.X),
                 R=["Pp"], W=[("small", 40)])
            cp(DVE, dhl[:, 0:8], small[:, 40:48], R=[("small", 40)], W=["dhl"])
            tt(DVE, small[:, 48:56], small[:, 40:48], dhl[:, 0:8], ALU.subtract, R=[("small", 40), "dhl"], W=[("small", 48)])
            cp(DVE, dhl[:, 8:16], small[:, 48:56], R=[("small", 48)], W=["dhl"])
            pD = nf()
            mm(psf[0:8, pD, 0:1], dhl[:, 0:8], ones[:, 0:1], True, False, R=["dhl", "ones"], W=[("pf", pD)])
            mm(psf[0:8, pD, 0:1], dhl[:, 8:16], ones[:, 0:1], False, False, R=["dhl", "ones"], W=[("pf", pD)])
            mm(psf[0:8, pD, 0:1], pnew[0:1, :], ones[0:1, 0:1], False, True, R=["pnew", "ones"], W=[("pf", pD)], sig=True)
            if DEBUG == "po" and bsm == 0:
                cp(DVE, tmpA[0:8, 0:256], psf[0:8, pO, 0:256], R=[("pf", pO)], W=["tmpA"])
                store(y_s[0:8, 0:256], tmpA[0:8, 0:256], R=["tmpA"])
                cp(DVE, tmpA[:, 512:1024], Pp[:, :], R=["Pp"], W=["tmpA"])
                store(ckv_p[0, 0:128, :], tmpA[:, 512:768], R=["tmpA"])
                store(ckv_p[0, 128:256, :], tmpA[:, 768:1024], R=["tmpA"])
                barrier()
                raise _Stop()
            T.op(DVE, lambda pD=pD: nc.vector.reciprocal(small[0:8, 8:9], psf[0:8, pD, 0:1]), R=[("pf", pD)], W=[("small", 8)])
            ts(DVE, ol_bf[0:8, :], psf[0:8, pO, 0:256], small[0:8, 8:9], None, ALU.mult, None, R=[("pf", pO), ("small", 8)], W=["ol_bf"])
            b = nb()
            for lc in range(2):
                tr(psb[:, b, lc * 8:(lc + 1) * 8], ol_bf[0:8, lc * 128:(lc + 1) * 128], R=["ol_bf"], W=[("pb", b)], sig=(lc == 1))
            cp(DVE, olT[:, :, :, bsm], psb[:, b, 0:16].rearrange("p (l h) -> p l h", h=8), R=[("pb", b)], W=["olT"])
        if PROFILE:
            T.stage = "smp_rest"
        bk = nf()
        for h in range(8):
            for lc in range(2):
                mm(psf[0:16, bk, h * 64:(h + 1) * 64], olT[:, lc, h, :], Wuv[:, lc, h * 64:(h + 1) * 64], lc == 0, lc == 1,
                   R=["olT"] + WUV, W=[("pf", bk)], sig=(h == 7 and lc == 1))
        cp(DVE, mix_s[0:16, 0:512], psf[0:16, bk, :], R=[("pf", bk)], W=["mix_s"])

        def dbg(name, ap, key):
            if DEBUG == name:
                n = ap.shape[1]
                cp(DVE, tmpA[0:ap.shape[0], 0:n], ap, R=[key], W=["tmpA"])
                store(y_s[0:ap.shape[0], 0:n], tmpA[0:ap.shape[0], 0:n], R=["tmpA"])
                barrier()
                raise _Stop()

        dbg("mix_s", mix_s[0:16, :], "mix_s")

        def post16(inT, inkey, wname, resid, gi_, bi_, outT, outkey):
            views = [wslab(wname + "0"), wslab(wname + "1")]
            b2 = nf2()
            for hf in range(2):
                wv, wk = views[hf]
                for c in range(8):
                    mm(psf[0:16, b2 + hf, :], inT[:, c, :], wv[:, c, :], c == 0, c == 7, R=[inkey, wk], W=[("pf", b2 + hf)], sig=(c == 7))
            for hf in range(2):
                stt(DVE, r_s[0:16, hf * 512:(hf + 1) * 512], resid[0:16, hf * 512:(hf + 1) * 512], ALPHA, psf[0:16, b2 + hf, :],
                    ALU.mult, ALU.add, R=[("pf", b2 + hf), "r_s", "xs_f"], W=["r_s"])
            bvload(gi_, 0)
            bvload(bi_, 1)
            layer_norm_rows("r_s", 16, r_s[0:16, :], r_s[0:16, :], gi_, bi_, xb16[0:16, :] if outT is not None else None, "xb16")
            if outT is not None:
                tr16(outT, xb16, 8, "xb16", outkey)

        cp(ACT, xb16[0:16, :], mix_s[0:16, :], R=["mix_s"], W=["xb16"])
        tr16(mixT_s, xb16, 8, "xb16", "mixT_s")
        x1T_s = cd(128).rearrange("p (c b) -> p c b", b=16)
        x2T_s = cd(128).rearrange("p (c b) -> p c b", b=16)
        post16(mixT_s, "mixT_s", "o", xs_f, 0, 1, x1T_s, "x1T_s")
        dbg("x1", r_s[0:16, :], "r_s")
        if PROFILE:
            T.stage = "smp_xattn"
        vq = [wslab("xq0"), wslab("xq1")]
        b2 = nf2()
        for hf in range(2):
            wv, wk = vq[hf]
            for c in range(8):
                mm(psf[0:16, b2 + hf, :], x1T_s[:, c, :], wv[:, c, :], c == 0, c == 7, R=["x1T_s", wk], W=[("pf", b2 + hf)], sig=(c == 7))
        for hf in range(2):
            cp(DVE, q2_f[0:16, hf * 512:(hf + 1) * 512], psf[0:16, b2 + hf, :], R=[("pf", b2 + hf)], W=["q2_f"])
        store(q2_scr[:, :], q2_f[0:16, :], R=["q2_f"], W=["q2_scr"])
        barrier()
        mkbs = [uTf[:, 0:2048].rearrange("p (a b) -> p a b", b=1024),
                KTf[:, 0:4096].bitcast(F32).rearrange("p (a b) -> p a b", b=1024)]
        mvbs = [mvb[:, :, :], KTf[:, 8192:10240].rearrange("p (a b) -> p a b", b=1024)]
        q2bs = [(xring[:, 0, :], ("xring", 0)), (tmpB[:, :], "tmpB")]
        for bsm in range(NS):
            db = bsm % 2
            mkb, mvb_, (q2b, q2k) = mkbs[db], mvbs[db], q2bs[db]
            MK, MV = ("mkb", db), ("mvbs", db)
            load(mkb, cmk[bsm].rearrange("(mt p) e -> p mt e", p=128), W=[MK])
            load(mvb_, cmv[bsm].rearrange("(mt p) e -> p mt e", p=128), W=[MV], q=POOL)
            load(q2b, q2_scr[bsm:bsm + 1, :].partition_broadcast(128), R=["q2_scr"], W=[q2k])
            for mt in range(2):
                tt(DVE, mkb[:, mt, :], mkb[:, mt, :], q2b, ALU.mult, R=[MK, q2k], W=[MK])
            T.op(DVE, lambda mkb=mkb: nc.vector.reduce_sum(small[:, 16:24], mkb[:, :, :].rearrange("p a (h d) -> p (a h) d", d=256), axis=AX.X),
                 R=[MK], W=[("small", 16)])
            act(P2bf[:, :], small[:, 16:24], AF.Exp, R=[("small", 16)], W=["P2bf"], scale=MEM_SCALE)
            bo = nf()
            for h in range(4):
                for dc in range(2):
                    for mt in range(2):
                        mm(psf[:, bo, (h * 2 + dc):(h * 2 + dc) + 1], mvb_[:, mt, h * 256 + dc * 128:h * 256 + (dc + 1) * 128],
                           P2bf[:, mt * 4 + h:mt * 4 + h + 1], mt == 0, mt == 1, R=[MV, "P2bf"], W=[("pf", bo)])
            mm(psf[:, bo, 16:24], ones[:, :], P2bf[:, :], True, True, R=["ones", "P2bf"], W=[("pf", bo)], sig=True)
            cp(DVE, small[:, 28:36], psf[:, bo, 16:24], R=[("pf", bo)], W=[("small", 28)])
            tt(DVE, small[:, 24:28], small[:, 28:32], small[:, 32:36], ALU.add, R=[("small", 28)], W=[("small", 24)])
            T.op(DVE, lambda: nc.vector.reciprocal(small[:, 24:28], small[:, 24:28]), R=[("small", 24)], W=[("small", 24)])
            tt(DVE, o2T_s[:, :, bsm].rearrange("p (h d) -> p h d", d=2), psf[:, bo, 0:8].rearrange("p (h d) -> p h d", d=2),
               small[:, 24:28].rearrange("p (h o) -> p h o", o=1).to_broadcast([128, 4, 2]), ALU.mult,
               R=[("pf", bo), ("small", 24)], W=["o2T_s"])
        if PROFILE:
            T.stage = "smp_ffn"
        post16(o2T_s, "o2T_s", "xo", r_s, 2, 3, x2T_s, "x2T_s")
        dbg("x2", r_s[0:16, :], "r_s")
        pH = nh()
        for g in range(8):
            wv, wk = wslab(f"up{g}")
            for fc in range(4):
                o = (g * 4 + fc) * 16
                for c in range(8):
                    mm(psf[:, pH, o:o + 16], wv[:, c, fc * 128:(fc + 1) * 128], x2T_s[:, c, :], c == 0, c == 7,
                       R=["x2T_s", wk], W=[("pf", pH)], sig=(c == 7 and fc == 3))
        act(tmpA[:, 0:512], psf[:, pH, :], AF.Relu, R=[("pf", pH)], W=["tmpA"])
        tt(DVE, hT_s[:, :], tmpA[:, 0:512], psf[:, pH, :], ALU.mult, R=[("pf", pH), "tmpA"], W=["hT_s"])
        b2 = nf2()
        for qd in range(4):
            for fh in range(2):
                wv, wk = wslab(f"dn{fh}{qd}")
                for fc in range(16):
                    o = (fh * 16 + fc) * 16
                    mm(psf[0:16, b2 + qd // 2, (qd % 2) * 256:(qd % 2 + 1) * 256], hT_s[:, o:o + 16], wv[:, fc, :],
                       fh == 0 and fc == 0, fh == 1 and fc == 15, R=["hT_s", wk], W=[("pf", b2 + qd // 2)], sig=(fc == 15))
        for hf in range(2):
            stt(DVE, r_s[0:16, hf * 512:(hf + 1) * 512], r_s[0:16, hf * 512:(hf + 1) * 512], ALPHA, psf[0:16, b2 + hf, :],
                ALU.mult, ALU.add, R=[("pf", b2 + hf), "r_s"], W=["r_s"])
        bvload(4, 0)
        bvload(5, 1)
        layer_norm_rows("r_s", 16, r_s[0:16, :], r_s[0:16, :], 4, 5, None, None)
        store(y_s[:, :], r_s[0:16, :], R=["r_s"])
        barrier()

    try:
        barrier()
        chk(0)
        if do_sample:
            sample_phase()
        mk_rest_slabs()
        chk(0.1)
        if do_prompt:
            for s in range(nseq):
                mem_kv_seq(s)
                chk(1)
                for j in range(nblk):
                    prompt_block(s, j)
    except _Stop:
        pass

    for ds in st_sems + ld_sems + wr_sems + cvt_sems + g_sems:
        if ds.count:
            nc.sync.wait_ge(ds.sem, ds.count * 16)
    for E in (PE, ACT, DVE, POOL):
        if E.count:
            nc.sync.wait_ge(E.sem, E.count)
    return nc


def _consts():
    half = 16
    inv_freq = np.exp(-math.log(10000.0) * np.arange(half, dtype=np.float32) / half).astype(np.float32)
    pos = np.arange(SEQ, dtype=np.float32)
    ang = pos[:, None] * inv_freq[None, :]
    cos, sin = np.cos(ang).astype(np.float32), np.sin(ang).astype(np.float32)
    C2 = np.concatenate([cos, cos], axis=1)
    S2 = np.concatenate([-sin, sin], axis=1)
    ropeC = C2.reshape(16, 128, 32).transpose(1, 0, 2).reshape(128, 512)
    ropeS = S2.reshape(16, 128, 32).transpose(1, 0, 2).reshape(128, 512)
    angs = np.float32(8192.0) * inv_freq
    cs, sn = np.cos(angs).astype(np.float32), np.sin(angs).astype(np.float32)
    ropeCs = np.tile(np.concatenate([cs, cs])[None, :], (128, 1)).astype(np.float32)
    ropeSs = np.tile(np.concatenate([-sn, sn])[None, :], (128, 1)).astype(np.float32)
    ident = np.eye(128, dtype=np.float32).astype(ml_dtypes.bfloat16)
    k = np.arange(128)
    mask = (k[None, :] >= k[:, None]).astype(np.float32).astype(ml_dtypes.bfloat16)
    injk = np.zeros((32, 96), np.float32)
    injk[np.arange(32), 64 + np.arange(32)] = 1.0
    injk = injk.astype(ml_dtypes.bfloat16)
    sel = np.zeros((16, 16, 128), np.float32)
    for b in range(16):
        sel[b, b, :] = 1.0
    gq = (np.arange(128) % 16).astype(np.int32).reshape(128, 1)
    return dict(c_ident=ident, c_mask=mask, c_ropeC=np.ascontiguousarray(ropeC), c_ropeS=np.ascontiguousarray(ropeS),
                c_ropeCs=ropeCs, c_ropeSs=ropeSs, c_injk=injk, c_sel=sel.reshape(16, 16 * 128), gq=gq)


def make_in_maps(inp, cores=range(NCORE), pool_c=None, pool_r=None, page_table=None):
    consts = _consts()
    if pool_c is None:
        pool_c = np.ascontiguousarray(inp["cache_ckv"][0]).reshape(-1, 8 * 256)
        pool_r = np.ascontiguousarray(inp["cache_krope"][0]).reshape(-1, 8 * 32)
        page_table = np.asarray(inp["page_table"])
    shared = dict(
        pool_c=pool_c, pool_r=pool_r,
        w_in=np.ascontiguousarray(inp["w_in"][0]), kvg=np.ascontiguousarray(inp["kv_norm_g"]),
        w_uk=np.ascontiguousarray(inp["w_uk"][0]).reshape(256, 512), w_uv=np.ascontiguousarray(inp["w_uv"][0]).reshape(256, 512),
        conv_w=np.ascontiguousarray(inp["conv_w"][0]), conv_b=np.ascontiguousarray(inp["conv_b"]),
        conv_g=np.ascontiguousarray(inp["conv_ln_g"]), conv_bb=np.ascontiguousarray(inp["conv_ln_b"]),
        w_o=np.ascontiguousarray(inp["w_o"][0]), w_xq=np.ascontiguousarray(inp["w_xq"][0]),
        w_xk=np.ascontiguousarray(inp["w_xk"][0]), w_xv=np.ascontiguousarray(inp["w_xv"][0]),
        w_xo=np.ascontiguousarray(inp["w_xo"][0]), w_up=np.ascontiguousarray(inp["w_up"][0]),
        w_down=np.ascontiguousarray(inp["w_down"][0]),
        ln1_g=np.ascontiguousarray(inp["ln1_g"]), ln1_b=np.ascontiguousarray(inp["ln1_b"]),
        ln2_g=np.ascontiguousarray(inp["ln2_g"]), ln2_b=np.ascontiguousarray(inp["ln2_b"]),
        ln3_g=np.ascontiguousarray(inp["ln3_g"]), ln3_b=np.ascontiguousarray(inp["ln3_b"]),
        **consts,
    )
    maps = []
    for c in cores:
        pt = page_table[c * 16:(c + 1) * 16]
        ptr = pt.reshape(16, 8, 8)[:, :, np.arange(128) // 16]
        ptr = np.ascontiguousarray(ptr.transpose(2, 0, 1).reshape(128, 128)).astype(np.int32)
        m = dict(shared)
        m.update(
            xp=np.ascontiguousarray(inp["x_prompt"][2 * c:2 * c + 2]),
            memp=np.ascontiguousarray(inp["mem_prompt"][2 * c:2 * c + 2]),
            xs=np.ascontiguousarray(inp["x_sample"][16 * c:16 * c + 16, 0]),
            ptrep=ptr,
            stc=np.ascontiguousarray(inp["state_conv"][0, 16 * c:16 * c + 16]).reshape(16, 30 * 512),
            cmk=np.ascontiguousarray(inp["cache_mem_k"][0, 16 * c:16 * c + 16]).reshape(16, 256, 1024),
            cmv=np.ascontiguousarray(inp["cache_mem_v"][0, 16 * c:16 * c + 16]).reshape(16, 256, 1024),
        )
        maps.append(m)
    return maps


def assemble(results):
    cat = lambda k: np.concatenate([r[k] for r in results], axis=0)
    y_p = cat("y_p")
    y_s = cat("y_s")[:, None, :]
    return (y_p, y_s, cat("ckv_p")[None], cat("kr_p")[None], cat("conv_p")[None],
            cat("mk_p").reshape(1, -1, 256, 4, 256), cat("mv_p").reshape(1, -1, 256, 4, 256),
            cat("ckv_s")[None, :, None, :], cat("kr_s")[None, :, None, :], cat("conv_s").reshape(1, -1, 30, 512))


def kernel(**inputs):
    inp = {k: np.asarray(v) for k, v in inputs.items()}
    nc = build()
    maps = make_in_maps(inp)
    res = run_bass_kernel_spmd(nc, maps, core_ids=list(range(NCORE)))
    outs = assemble(res.results)
    return tuple(np.ascontiguousarray(o, dtype=np.float32) for o in outs)
```
